# Optimizing a Trainium2 kernel written in Bass

```python
import math
import jax
import jax.numpy as jnp
from jax import lax
import numpy as np

D_MODEL = 1024
BATCH = 4
SEQ = 4096
DEPTH = 1

GRID_W = 64
CTX_LEN = 256
D_MIX = D_MODEL
D_RWKV = D_MIX // 2
RWKV_HEAD = 64
RWKV_HEADS = D_RWKV // RWKV_HEAD
D_S5 = D_MIX - D_RWKV
S5_GROUP = 16
S5_GROUPS = D_S5 // S5_GROUP
S5_STATE = 64
DECAY_LORA = 32
AAA_LORA = 32
GATE_LORA = 96
D_FF = ((8 * D_MODEL + 3 * 256 - 1) // (3 * 256)) * 256
IN_SPLITS = (D_RWKV, D_RWKV, D_RWKV, DECAY_LORA, DECAY_LORA, AAA_LORA, AAA_LORA, GATE_LORA, D_S5)
D_IN = sum(IN_SPLITS)
N_MOD = 6
NORM_EPS = 1e-6
RWKV_LN_EPS = 64e-5

kernel_name = "hybrid_rwkv7_s5_prefix_dit_block"


def rmsnorm(x, g):
    xf = x.astype(jnp.float32)
    y = xf * lax.rsqrt(jnp.mean(xf * xf, axis=-1, keepdims=True) + NORM_EPS)
    return (y * g.astype(jnp.float32)).astype(x.dtype)


def modulate(h, shift, scale):
    return h * (1 + scale) + shift


def split_proj(z):
    return jnp.split(z, np.cumsum(IN_SPLITS)[:-1].tolist(), axis=-1)


def swiglu(h, w1, w3, w2):
    return (jax.nn.silu(h @ w1) * (h @ w3)) @ w2


def dwconv3x3(x, w, rows):
    bsz, length, ch = x.shape
    xg = x.reshape(bsz, rows, length // rows, ch)
    y = lax.conv_general_dilated(xg, w[:, :, None, :].astype(x.dtype), (1, 1), "SAME",
                                 dimension_numbers=("NHWC", "HWIO", "NHWC"),
                                 feature_group_count=ch)
    return y.reshape(bsz, length, ch)


def _rwkv_step(state, inp):
    r_t, w_t, k_t, v_t, kk_t, a_t = inp
    sa = jnp.einsum("bhvk,bhk->bhv", state, -kk_t)
    state = (state * w_t[:, :, None, :]
             + sa[..., None] * (kk_t * a_t)[:, :, None, :]
             + v_t[..., None] * k_t[:, :, None, :])
    return state, jnp.einsum("bhvk,bhk->bhv", state, r_t)


def rwkv_scan(r, w, k, v, kk, a, s0, reverse):
    xs = tuple(jnp.moveaxis(t, 1, 0) for t in (r, w, k, v, kk, a))
    s_fin, y = lax.scan(_rwkv_step, s0, xs, reverse=reverse)
    return jnp.moveaxis(y, 0, 1), s_fin


def rwkv_mixer(zr, zk, zv, wd, ad, gd, rows, s0, conv_w, w0, w2, a0, a2, g2, k_k, k_a, r_k,
               ln_w, ln_b, readout):
    f32 = jnp.float32
    bsz, length, _ = zr.shape

    def heads(t):
        return t.reshape(bsz, length, RWKV_HEADS, RWKV_HEAD)

    rkv = dwconv3x3(jnp.concatenate([zr, zk, zv], axis=-1).astype(f32), conv_w.astype(f32), rows)
    r, k, v = jnp.split(rkv, 3, axis=-1)
    kk = heads(k * k_k.astype(f32))
    kk = kk * lax.rsqrt(jnp.maximum(jnp.sum(kk * kk, axis=-1, keepdims=True), 1e-12))
    rh, vh = heads(r), heads(v)
    ys, kds, finals = [], [], []
    for d, reverse in ((0, False), (1, True)):
        logw = -jax.nn.softplus(-(w0[d] + jnp.tanh(wd[d]) @ w2[d]).astype(f32)) - 0.5
        decay = jnp.exp(-jnp.exp(logw))
        a = jax.nn.sigmoid((a0[d] + ad[d] @ a2[d]).astype(f32))
        kd = heads(k * (1 + (a - 1) * k_a.astype(f32)))
        y, s_fin = rwkv_scan(rh, heads(decay), kd, vh, kk, heads(a), s0[d], reverse)
        ys.append(y)
        kds.append(kd)
        finals.append(s_fin)
    if not readout:
        return None, (finals[0], finals[1])
    y = ys[0] + ys[1]
    mu = jnp.mean(y, axis=-1, keepdims=True)
    var = jnp.mean(jnp.square(y - mu), axis=-1, keepdims=True)
    yn = ((y - mu) * lax.rsqrt(var + RWKV_LN_EPS)).reshape(bsz, length, D_RWKV)
    yn = yn * ln_w.astype(f32) + ln_b.astype(f32)
    rk = r_k.astype(f32)
    bonus = jnp.sum(rh * (kds[0] + kds[1]) * rk, axis=-1, keepdims=True)
    yn = yn + (bonus * vh).reshape(bsz, length, D_RWKV)
    g = jax.nn.sigmoid(gd.astype(f32)) @ g2.astype(f32)
    return (yn * g).astype(zr.dtype), (finals[0], finals[1])


def _complex_affine_combine(e1, e2):
    a1r, a1i, b1r, b1i = e1
    a2r, a2i, b2r, b2i = e2
    return (a2r * a1r - a2i * a1i,
            a2r * a1i + a2i * a1r,
            a2r * b1r - a2i * b1i + b2r,
            a2r * b1i + a2i * b1r + b2i)


def s5_discretize(lam_re, lam_im, log_step, b_re, b_im):
    step = jnp.exp(log_step)[:, None]
    mag = jnp.exp(lam_re * step)
    lb_re = mag * jnp.cos(lam_im * step)
    lb_im = mag * jnp.sin(lam_im * step)
    den = lam_re * lam_re + lam_im * lam_im
    nr = lb_re - 1
    q_re = (nr * lam_re + lb_im * lam_im) / den
    q_im = (lb_im * lam_re - nr * lam_im) / den
    bb_re = q_re[..., None] * b_re - q_im[..., None] * b_im
    bb_im = q_re[..., None] * b_im + q_im[..., None] * b_re
    return lb_re, lb_im, bb_re, bb_im


def s5_mixer(u, h0, lam_re, lam_im, log_step, b_re, b_im, c_re, c_im, d_skip, glu_w, glu_b, readout):
    f32 = jnp.float32
    bsz, length, _ = u.shape
    uf = u.astype(f32)
    ug = uf.reshape(bsz, length, S5_GROUPS, S5_GROUP)
    bre, bim = b_re.astype(f32), b_im.astype(f32)
    states, finals = [], []
    for d, reverse in ((0, False), (1, True)):
        lb_re, lb_im, bb_re, bb_im = s5_discretize(lam_re[d].astype(f32), lam_im[d].astype(f32),
                                                   log_step[d].astype(f32), bre, bim)
        bu_re = jnp.einsum("blgh,gph->blgp", ug, bb_re)
        bu_im = jnp.einsum("blgh,gph->blgp", ug, bb_im)
        first, final = (length - 1, 0) if reverse else (0, length - 1)
        h0_re, h0_im = h0[d]
        bu_re = bu_re.at[:, first].add(lb_re * h0_re - lb_im * h0_im)
        bu_im = bu_im.at[:, first].add(lb_re * h0_im + lb_im * h0_re)
        a_re = jnp.broadcast_to(lb_re, bu_re.shape)
        a_im = jnp.broadcast_to(lb_im, bu_im.shape)
        _, _, h_re, h_im = lax.associative_scan(_complex_affine_combine, (a_re, a_im, bu_re, bu_im),
                                                reverse=reverse, axis=1)
        states.append((h_re, h_im))
        finals.append((h_re[:, final], h_im[:, final]))
    if not readout:
        return None, (finals[0], finals[1])
    h_re = states[0][0] + states[1][0]
    h_im = states[0][1] + states[1][1]
    y = (jnp.einsum("blgp,ghp->blgh", h_re, c_re.astype(f32))
         - jnp.einsum("blgp,ghp->blgh", h_im, c_im.astype(f32)))
    y = y.reshape(bsz, length, D_S5) + d_skip.astype(f32) * uf
    z = jax.nn.gelu(y)
    out = z * jax.nn.sigmoid(z @ glu_w.astype(f32) + glu_b.astype(f32))
    return out.astype(u.dtype), (finals[0], finals[1])


def setup_inputs(seed: int = 0) -> dict:
    key = jax.random.key(seed)
    ks = jax.random.split(key, 40)
    f32 = jnp.float32

    def nrm(i, shape, scale):
        return scale * jax.random.normal(ks[i], shape, f32)

    conv_center = jnp.zeros((DEPTH, 3, 3, 3 * D_RWKV), f32).at[:, 1, 1].set(1.0)
    return {
        "x": nrm(0, (BATCH, SEQ, D_MODEL), 1.0),
        "c": nrm(1, (BATCH, D_MODEL), 1.0),
        "ctx": nrm(2, (BATCH, CTX_LEN, D_MODEL), 1.0),
        "c_ctx": nrm(3, (D_MODEL,), 1.0),
        "mod_w": nrm(4, (DEPTH, D_MODEL, N_MOD * D_MODEL), 0.5 * D_MODEL ** -0.5),
        "mod_b": nrm(5, (DEPTH, N_MOD * D_MODEL), 0.02),
        "norm1_g": 1.0 + nrm(6, (DEPTH, D_MODEL), 0.05),
        "norm2_g": 1.0 + nrm(7, (DEPTH, D_MODEL), 0.05),
        "w_in": nrm(8, (DEPTH, D_MODEL, D_IN), D_MODEL ** -0.5),
        "w_out": nrm(9, (DEPTH, D_MIX, D_MODEL), D_MIX ** -0.5),
        "rwkv_conv": conv_center + nrm(10, (DEPTH, 3, 3, 3 * D_RWKV), 0.1),
        "rwkv_w0": -6.0 + 5.0 * jnp.linspace(0.0, 1.0, D_RWKV, dtype=f32) + nrm(11, (DEPTH, 2, D_RWKV), 0.2),
        "rwkv_w2": nrm(12, (DEPTH, 2, DECAY_LORA, D_RWKV), 0.5 * DECAY_LORA ** -0.5),
        "rwkv_a0": nrm(13, (DEPTH, 2, D_RWKV), 0.3),
        "rwkv_a2": nrm(14, (DEPTH, 2, AAA_LORA, D_RWKV), 0.5 * AAA_LORA ** -0.5),
        "rwkv_g2": nrm(15, (DEPTH, GATE_LORA, D_RWKV), GATE_LORA ** -0.5),
        "rwkv_kk": 0.85 + nrm(16, (DEPTH, D_RWKV), 0.05),
        "rwkv_ka": 1.0 + nrm(17, (DEPTH, D_RWKV), 0.05),
        "rwkv_rk": nrm(18, (DEPTH, RWKV_HEADS, RWKV_HEAD), 0.1),
        "rwkv_ln_w": 1.0 + nrm(19, (DEPTH, D_RWKV), 0.05),
        "rwkv_ln_b": nrm(20, (DEPTH, D_RWKV), 0.01),
        "s5_lam_re": -0.5 + nrm(21, (DEPTH, 2, S5_GROUPS, S5_STATE), 0.01),
        "s5_lam_im": math.pi * jnp.arange(S5_STATE, dtype=f32) + nrm(22, (DEPTH, 2, S5_GROUPS, S5_STATE), 0.01),
        "s5_log_step": jax.random.uniform(ks[23], (DEPTH, 2, S5_GROUPS), f32,
                                          minval=math.log(1e-3), maxval=math.log(1e-1)),
        "s5_b_re": nrm(24, (DEPTH, S5_GROUPS, S5_STATE, S5_GROUP), (2 * S5_GROUP) ** -0.5),
        "s5_b_im": nrm(25, (DEPTH, S5_GROUPS, S5_STATE, S5_GROUP), (2 * S5_GROUP) ** -0.5),
        "s5_c_re": nrm(26, (DEPTH, S5_GROUPS, S5_GROUP, S5_STATE), S5_STATE ** -0.5),
        "s5_c_im": nrm(27, (DEPTH, S5_GROUPS, S5_GROUP, S5_STATE), S5_STATE ** -0.5),
        "s5_d": nrm(28, (DEPTH, D_S5), 1.0),
        "s5_glu_w": nrm(29, (DEPTH, D_S5, D_S5), D_S5 ** -0.5),
        "s5_glu_b": nrm(30, (DEPTH, D_S5), 0.01),
        "ffn_w1": nrm(31, (DEPTH, D_MODEL, D_FF), D_MODEL ** -0.5),
        "ffn_w3": nrm(32, (DEPTH, D_MODEL, D_FF), D_MODEL ** -0.5),
        "ffn_w2": nrm(33, (DEPTH, D_FF, D_MODEL), D_FF ** -0.5),
        "final_g": 1.0 + nrm(34, (D_MODEL,), 0.05),
    }


def reference(x, c, ctx, c_ctx, mod_w, mod_b, norm1_g, norm2_g, w_in, w_out, rwkv_conv, rwkv_w0,
              rwkv_w2, rwkv_a0, rwkv_a2, rwkv_g2, rwkv_kk, rwkv_ka, rwkv_rk, rwkv_ln_w, rwkv_ln_b,
              s5_lam_re, s5_lam_im, s5_log_step, s5_b_re, s5_b_im, s5_c_re, s5_c_im, s5_d, s5_glu_w,
              s5_glu_b, ffn_w1, ffn_w3, ffn_w2, final_g):
    f32 = jnp.float32
    bsz, length, _ = x.shape
    rows = length // GRID_W
    zero_rwkv = jnp.zeros((bsz, RWKV_HEADS, RWKV_HEAD, RWKV_HEAD), f32)
    zero_s5 = jnp.zeros((bsz, S5_GROUPS, S5_STATE), f32)
    for layer in range(DEPTH):
        last = layer == DEPTH - 1
        mod_x = jnp.split((jax.nn.silu(c) @ mod_w[layer] + mod_b[layer])[:, None, :], N_MOD, axis=-1)
        mod_c = jnp.split(jax.nn.silu(c_ctx) @ mod_w[layer] + mod_b[layer], N_MOD, axis=-1)
        z_x = split_proj(modulate(rmsnorm(x, norm1_g[layer]), mod_x[0], mod_x[1]) @ w_in[layer])
        z_c = split_proj(modulate(rmsnorm(ctx, norm1_g[layer]), mod_c[0], mod_c[1]) @ w_in[layer])

        def run_rwkv(z, grid_rows, s0, readout):
            return rwkv_mixer(z[0], z[1], z[2], (z[3], z[4]), (z[5], z[6]), z[7], grid_rows, s0,
                              rwkv_conv[layer], rwkv_w0[layer], rwkv_w2[layer], rwkv_a0[layer],
                              rwkv_a2[layer], rwkv_g2[layer], rwkv_kk[layer], rwkv_ka[layer],
                              rwkv_rk[layer], rwkv_ln_w[layer], rwkv_ln_b[layer], readout)

        def run_s5(z, h0, readout):
            return s5_mixer(z[8], h0, s5_lam_re[layer], s5_lam_im[layer], s5_log_step[layer],
                            s5_b_re[layer], s5_b_im[layer], s5_c_re[layer], s5_c_im[layer],
                            s5_d[layer], s5_glu_w[layer], s5_glu_b[layer], readout)

        rwkv_c, rwkv_state = run_rwkv(z_c, 1, (zero_rwkv, zero_rwkv), not last)
        s5_c, s5_state = run_s5(z_c, ((zero_s5, zero_s5), (zero_s5, zero_s5)), not last)
        rwkv_x, _ = run_rwkv(z_x, rows, rwkv_state, True)
        s5_x, _ = run_s5(z_x, s5_state, True)

        x = x + mod_x[2] * (jnp.concatenate([rwkv_x, s5_x], axis=-1) @ w_out[layer])
        x = x + mod_x[5] * swiglu(modulate(rmsnorm(x, norm2_g[layer]), mod_x[3], mod_x[4]),
                                  ffn_w1[layer], ffn_w3[layer], ffn_w2[layer])
        if not last:
            ctx = ctx + mod_c[2] * (jnp.concatenate([rwkv_c, s5_c], axis=-1) @ w_out[layer])
            ctx = ctx + mod_c[5] * swiglu(modulate(rmsnorm(ctx, norm2_g[layer]), mod_c[3], mod_c[4]),
                                          ffn_w1[layer], ffn_w3[layer], ffn_w2[layer])
    return rmsnorm(x, final_g)
```

```python
import contextlib
import numpy as np
import concourse.bass as bass
import concourse.mybir as mybir
from concourse.bass_utils import run_bass_kernel_spmd

F32 = mybir.dt.float32
BF16 = mybir.dt.bfloat16
ALU = mybir.AluOpType
AF = mybir.ActivationFunctionType
AXX = mybir.AxisListType.X

SEM_CAP = 16000
N_DMA_SEM = 24
SCHED_WINDOW_STREAMS = 1
SAME_ENG_WAIT = True

T_LAT, T_CTX, D, DFF = 4096, 256, 1024, 2816
OWN = 2048
NOWN = OWN // 128
PI = float(np.pi)
KDEC = 0.6065306597126334


class KB:
    ENGS = ("pe", "dve", "act", "pool", "sp")

    def __init__(self, nc):
        self.nc = nc
        self.ops = {e: [] for e in self.ENGS}
        self.lastw = {}
        self.readers = {}
        self.ndma = 0
        self.rr = 0

    @staticmethod
    def _norm(reads, writes):
        r2 = [k for k in reads if not k.startswith("pX")]
        w2 = [k for k in writes if not k.startswith("pX")]
        banks = {"bank" + k[2:].split("_")[0] for k in list(reads) + list(writes) if k.startswith("pX")}
        return r2, w2 + sorted(banks)

    def _deps(self, me, reads, writes):
        reads, writes = self._norm(reads, writes)
        deps = set()
        for k in reads:
            w = self.lastw.get(k)
            if w is not None:
                deps.add(w)
        for k in writes:
            w = self.lastw.get(k)
            if w is not None:
                deps.add(w)
            for r in self.readers.get(k, ()):
                deps.add(r)
        deps.discard(me)
        for k in reads:
            self.readers.setdefault(k, []).append(me)
        for k in writes:
            self.lastw[k] = me
            self.readers[k] = []
        return deps

    mute = False
    rec = None
    ksuf = None
    KGLOBAL = ("pX", "scr_", "o_", "fence", "rows", "tw", "modt", "ident", "A2")

    def _sfx(self, keys):
        if not self.ksuf:
            return list(keys)
        return [k if k.startswith(self.KGLOBAL) else k + self.ksuf for k in keys]

    @staticmethod
    def _est(op):
        def nfree(ap):
            n = 1
            for d in ap.shape[1:]:
                n *= d
            return n
        if op[0] == "D":
            ap = op[1]
            nbytes = nfree(ap) * ap.shape[0] * (2 if ap.dtype == BF16 else 4)
            return "sp", 0.08, 2.2 + nbytes / 1.0e5
        eng, meth, args = op[1], op[2], op[3]
        n = nfree(args[0])
        if eng == "pe":
            f32 = args[1].dtype == F32
            d = 0.09 + n * (0.0017 if f32 else 0.00045)
        elif eng == "dve":
            d = 0.25 + n * 0.00104 * (6.0 if meth == "reciprocal" else 1.0)
        elif eng == "act":
            d = 0.2 + n * 0.00104
        else:
            d = 0.45 + n * 0.0026
        return eng, d, d

    def merge(self, *streams):
        ops = [op for st_ in streams for op in st_]
        n = len(ops)
        lastw, readers = {}, {}
        preds = [set() for _ in range(n)]
        for i, op in enumerate(ops):
            r_, w_ = (op[5], op[6]) if op[0] == "I" else (op[4], op[5])
            r_, w_ = self._norm(r_, w_)
            for k in r_:
                if k in lastw:
                    preds[i].add(lastw[k])
            for k in w_:
                if k in lastw:
                    preds[i].add(lastw[k])
                preds[i].update(readers.get(k, ()))
            preds[i].discard(i)
            for k in r_:
                readers.setdefault(k, []).append(i)
            for k in w_:
                lastw[k] = i
                readers[k] = []
        succs = [[] for _ in range(n)]
        for i in range(n):
            for p in preds[i]:
                succs[p].append(i)
        est = [self._est(op) for op in ops]
        cp = [0.0] * n
        for i in range(n - 1, -1, -1):
            cp[i] = est[i][2] + max((cp[j] for j in succs[i]), default=0.0)
        npred = [len(p) for p in preds]
        ready = [i for i in range(n) if npred[i] == 0]
        fin = [0.0] * n
        eng_free = {e: 0.0 for e in self.ENGS}
        LAT = 1.6
        while ready:
            best, bkey = None, None
            for i in ready:
                e = est[i][0]
                t = eng_free[e]
                for p in preds[i]:
                    tp = fin[p] + (0.05 if est[p][0] == e else LAT)
                    if tp > t:
                        t = tp
                key = (t - 0.02 * cp[i], i)
                if bkey is None or key < bkey:
                    best, bkey, bt = i, key, t
            i = best
            ready.remove(i)
            e = est[i][0]
            eng_free[e] = bt + est[i][1]
            fin[i] = bt + est[i][2]
            op = ops[i]
            if op[0] == "I":
                self.I(op[1], op[2], *op[3], reads=op[5], writes=op[6], **op[4])
            else:
                self.D(op[1], op[2], reads=op[4], writes=op[5], **op[3])
            for j in succs[i]:
                npred[j] -= 1
                if npred[j] == 0:
                    ready.append(j)

    def I(self, eng, meth, *args, reads=(), writes=(), **kw):
        if self.mute:
            return
        if self.rec is not None:
            self.rec.append(("I", eng, meth, args, kw, self._sfx(reads), self._sfx(writes)))
            return
        idx = len(self.ops[eng])
        deps = self._deps((eng, idx), list(reads), list(writes))
        self.ops[eng].append(((meth, args, kw), deps, None))

    def D(self, out, in_, reads=(), writes=(), **kw):
        if self.mute:
            return
        if self.rec is not None:
            self.rec.append(("D", out, in_, dict(kw), self._sfx(reads), self._sfx(writes)))
            return
        k = self.ndma
        self.ndma += 1
        deps = self._deps(("dma", k), list(reads), list(writes))
        kw = dict(kw)
        kw["out"] = out
        kw["in_"] = in_
        self.ops["sp"].append((("dma_start", (), kw), deps, k))

    def cp(self, eng, out, in_, reads=(), writes=()):
        self.I(eng, "copy" if eng == "act" else "tensor_copy", out, in_, reads=reads, writes=writes)

    def ts1(self, eng, out, in0, s, op, reads=(), writes=()):
        self.I(eng, "tensor_scalar", out, in0, s, 0.0, op, ALU.add, reads=reads, writes=writes)

    def ew(self):
        self.rr += 1
        return "dve" if (self.rr % 3) else "pool"

    def ev(self):
        self.rr += 1
        return "dve" if (self.rr % 2) else "act"

    def emit(self, final_keys=()):
        nc = self.nc
        me = ("sp", len(self.ops["sp"]))
        deps = self._deps(me, list(final_keys), [])
        self.ops["sp"].append((None, deps, None))
        nsem = {e: (len(self.ops[e]) + SEM_CAP - 1) // SEM_CAP + 1 for e in self.ENGS}
        with contextlib.ExitStack() as st:
            sems = {e: [st.enter_context(nc.semaphore(f"s_{e}{i}")) for i in range(nsem[e])]
                    for e in self.ENGS}
            dsems = [st.enter_context(nc.semaphore(f"s_dma{i}")) for i in range(N_DMA_SEM)]
            block = st.enter_context(nc.Block())

            def waitspec(p):
                if p[0] == "dma":
                    k = p[1]
                    return ("d", k % N_DMA_SEM), dsems[k % N_DMA_SEM], 16 * (k // N_DMA_SEM + 1)
                e, i = p
                return (e, i // SEM_CAP), sems[e][i // SEM_CAP], i % SEM_CAP + 1

            def run(ename, eobj):
                waited = {}
                for idx, (fn, deps, dk) in enumerate(self.ops[ename]):
                    specs = []
                    for p in deps:
                        if p[0] == ename and (ename == "pe" or not SAME_ENG_WAIT):
                            continue
                        specs.append(waitspec(p))
                    if dk is not None and dk >= N_DMA_SEM:
                        specs.append((("d", dk % N_DMA_SEM), dsems[dk % N_DMA_SEM],
                                      16 * (dk // N_DMA_SEM)))
                    for sid, sem, val in specs:
                        if waited.get(sid, 0) >= val:
                            continue
                        waited[sid] = val
                        eobj.wait_ge(sem, val)
                    if fn is None:
                        continue
                    meth, args, kw = fn
                    ins = getattr(eobj, meth)(*args, **kw)
                    if dk is not None:
                        ins.then_inc(dsems[dk % N_DMA_SEM], 16)
                    else:
                        ins.then_inc(sems[ename][idx // SEM_CAP], 1)

            @block.tensor
            def _(e):
                run("pe", e)

            @block.vector
            def _(e):
                run("dve", e)

            @block.scalar
            def _(e):
                run("act", e)

            @block.gpsimd
            def _(e):
                run("pool", e)

            @block.sync
            def _(e):
                run("sp", e)


IN_SHAPES = {
    "x": [T_LAT, D], "ctx": [T_CTX, D], "cc": [2, D], "mod_w": [D, 6 * D], "mod_b": [1, 6 * D],
    "g1": [1, D], "g2n": [1, D], "gf": [1, D], "w_in": [D, 2272], "w_out": [D, D],
    "conv": [9, 1536], "w0": [2, 512], "w2": [2, 32, 512], "a0": [2, 512], "a2": [2, 32, 512],
    "g2": [96, 512], "kkv": [512], "kav": [512], "rkv": [512], "lnw": [1, 512], "lnb": [1, 512],
    "lam_re": [2, 32, 64], "lam_im": [2, 32, 64], "lstep": [2, 32],
    "b_re": [32, 64, 16], "b_im": [32, 64, 16], "c_re": [32, 16, 64], "c_im": [32, 16, 64],
    "s5d": [512], "gluw": [512, 512], "glub": [1, 512],
    "w1": [D, DFF], "w3": [D, DFF], "w2f": [DFF, D],
    "ident": [128, 128], "antiI": [128, 128], "mask_b": [128, 256], "mask_k": [128, 256],
    "maskT": [128, 128], "bones": [128, 128],
}

DBG_SHAPES = {"d_mod": [128, 6 * D], "d_AB": [128, 32], "d_z": [128, 3 * 256], "d_zu": [128, 512],
              "d_rkvc": [128, 3 * 128], "d_yT": [128, 512], "d_ST": [128, 256], "d_ys": [128, 512],
              "d_x1": [128, D], "d_car": [128, 32], "d_F": [128, 2048], "d_Bw": [128, 2048]}


def build_nc(n_lat_chunks=(16, 16), do_tail=True, debug=False, upto="full", n_ctx=2):
    nc = bass.Bass("TRN2", target_bir_lowering=False)
    din = {k: nc.dram_tensor(k, s, F32, kind="ExternalInput").ap() for k, s in IN_SHAPES.items()}
    out_d = nc.dram_tensor("out", [OWN, D], F32, kind="ExternalOutput").ap()
    scr = {}
    for nm in ("yr0", "yr1", "ys0", "ys1", "bv0", "bv1"):
        scr[nm] = nc.dram_tensor("scr_" + nm, [OWN, 512], F32, kind="Internal").ap()
    scr["sg"] = nc.dram_tensor("scr_sg", [NOWN * 128, 128], F32, kind="Internal").ap()
    scr["x1"] = nc.dram_tensor("scr_x1", [OWN, D], F32, kind="Internal").ap()
    scr["mod"] = nc.dram_tensor("scr_mod", [128, 6 * D], F32, kind="Internal").ap()
    scr["dg"] = nc.dram_tensor("scr_dg", [12, 128, 9 * 128], BF16, kind="Internal").ap()
    scr["fr"] = nc.dram_tensor("scr_fr", [NOWN * 4 * 128, 512], F32, kind="Internal").ap()
    scr["fz"] = nc.dram_tensor("scr_fz", [NOWN * 128, 128], F32, kind="Internal").ap()
    scr["fu"] = nc.dram_tensor("scr_fu", [NOWN * 128, 512], BF16, kind="Internal").ap()
    scr["cc_in"] = nc.dram_tensor("scr_cc_in", [128, 288], F32, kind="Internal").ap()
    scr["cc_out"] = nc.dram_tensor("scr_cc_out", [128, 288], F32, kind="Internal").ap()
    dbg = {}
    if debug:
        for nm, shp in DBG_SHAPES.items():
            dbg[nm] = nc.dram_tensor(nm, shp, F32, kind="ExternalOutput").ap()
    kb = KB(nc)
    cnt = [0]

    with contextlib.ExitStack() as g:
        def sb(st, name, shape, dt=F32):
            cnt[0] += 1
            return st.enter_context(nc.sbuf_tensor(f"sb{cnt[0]}_{name}", shape, dt))

        ident = sb(g, "ident", [128, 128]); antiI = sb(g, "antiI", [128, 128])
        mask_b = sb(g, "mask_b", [128, 256]); mask_k = sb(g, "mask_k", [128, 256])
        maskT = sb(g, "maskT", [128, 128]); bones = sb(g, "bones", [128, 128])
        ones = sb(g, "ones", [128, 128])
        for nm, t in (("ident", ident), ("antiI", antiI), ("mask_b", mask_b), ("mask_k", mask_k),
                      ("maskT", maskT), ("bones", bones)):
            kb.D(t[:], din[nm], writes=[nm])
        kb.I("pool", "memset", ones[:], 1.0, writes=["ones"])
        AB = sb(g, "AB", [128, 4, 8])
        cnt[0] += 1
        PS = [g.enter_context(nc.psum_tensor(f"psbank{i}", [128, 512], F32)) for i in range(8)]

        def bk(i):
            return [f"pX{i}_{j}" for j in range(4)]

        wscope = contextlib.ExitStack()
        win = sb(wscope, "win", [128, 8, 2272], BF16)
        car_re = sb(wscope, "car_re", [128, 16]); car_im = sb(wscope, "car_im", [128, 16])
        gs_re = sb(wscope, "gs_re", [128, 16]); gs_im = sb(wscope, "gs_im", [128, 16]); gta = sb(wscope, "gta", [128, 16]); gtb = sb(wscope, "gtb", [128, 16])
        Bw_re = sb(wscope, "Bw_re", [128, 16, 128], BF16); Bw_im = sb(wscope, "Bw_im", [128, 16, 128], BF16)
        Cw_re = sb(wscope, "Cw_re", [128, 16, 32]); Cw_imn = sb(wscope, "Cw_imn", [128, 16, 32])
        E_re = sb(wscope, "E_re", [128, 16, 128]); E_im = sb(wscope, "E_im", [128, 16, 128])
        F_re = sb(wscope, "F_re", [128, 16, 128]); F_im = sb(wscope, "F_im", [128, 16, 128])
        s5t = {nm: sb(wscope, "s5" + nm, [128, 512]) for nm in ("t1", "t2", "Xre", "Xim", "Gre", "Gim", "Hre", "Him")}

        def s5_setup(sd, F0=()):
            K = "s5s"
            S5K = ["s5t1", "s5t2", "s5Xre", "s5Xim", "s5Gre", "s5Gim", "s5Hre", "s5Him", "s5tab", K]
            with contextlib.ExitStack() as t_:
                sm = sb(t_, "s5sm", [128, 20, 16])
                (lre, lim, stp, th, th2, tmpc, mag, imag, sn, csn, lbr, lbi, den, nr, qre, qim, u1, ivr, ivi) = [sm[:, i_, :] for i_ in range(19)]
                bre = s5t["Gre"][:, 0:256].rearrange("p (q h) -> p q h", h=16); bim = s5t["Gre"][:, 256:512].rearrange("p (q h) -> p q h", h=16)
                bbr = s5t["Gim"][:, 0:256].rearrange("p (q h) -> p q h", h=16); bbi = s5t["Gim"][:, 256:512].rearrange("p (q h) -> p q h", h=16)
                v1 = s5t["Hre"][:, 0:256].rearrange("p (q h) -> p q h", h=16); cst = s5t["Hre"][:, 256:512].rearrange("p (q h) -> p q h", h=16)
                bdw = s5t["Xre"][:].rearrange("p (r c) -> p r c", c=128)

                def V(meth, *args, eng="dve", **kw):
                    kb.I(eng, meth, *args, reads=S5K, writes=S5K, **kw)

                kb.D(lre, din["lam_re"][sd].rearrange("(q g) p -> (g p) q", g=2), reads=list(F0) + S5K, writes=S5K, allow_slow_non_contiguous=True)
                kb.D(lim, din["lam_im"][sd].rearrange("(q g) p -> (g p) q", g=2), reads=list(F0), writes=[K], allow_slow_non_contiguous=True)
                for g2_ in range(2):
                    kb.D(stp[64 * g2_:64 * g2_ + 64, :],
                         din["lstep"][sd:sd + 1, :].rearrange("o (q g) -> o q g", g=2)[:, :, g2_].partition_broadcast(64),
                         reads=list(F0), writes=[K], allow_slow_non_contiguous=True)
                kb.D(bre, din["b_re"].rearrange("(q g) p h -> (g p) q h", g=2), reads=list(F0), writes=[K])
                kb.D(bim, din["b_im"].rearrange("(q g) p h -> (g p) q h", g=2), reads=list(F0), writes=[K])
                V("activation", stp, stp, AF.Exp, eng="act")
                V("tensor_tensor", mag, lre, stp, ALU.mult)
                V("tensor_tensor", th, lim, stp, ALU.mult)
                V("activation", imag, mag, AF.Exp, eng="act", scale=-1.0)
                V("activation", mag, mag, AF.Exp, eng="act")
                V("tensor_copy", u1, th)
                for m in (PI, 3 * PI, 5 * PI):
                    V("tensor_scalar", tmpc, th, m, -2 * PI, ALU.is_ge, ALU.mult)
                    V("tensor_tensor", u1, u1, tmpc, ALU.add)
                V("tensor_scalar", u1, u1, 0.125, 0.0, ALU.mult, ALU.add)
                V("tensor_tensor", th2, u1, u1, ALU.mult)
                V("tensor_scalar", sn, th2, -1.0 / 5040, 1.0 / 120, ALU.mult, ALU.add)
                V("tensor_tensor", sn, sn, th2, ALU.mult)
                V("tensor_scalar", sn, sn, -1.0 / 6, 0.0, ALU.add, ALU.add)
                V("tensor_tensor", sn, sn, th2, ALU.mult)
                V("tensor_scalar", sn, sn, 1.0, 0.0, ALU.add, ALU.add)
                V("tensor_tensor", sn, sn, u1, ALU.mult)
                V("tensor_scalar", csn, th2, 1.0 / 40320, -1.0 / 720, ALU.mult, ALU.add)
                V("tensor_tensor", csn, csn, th2, ALU.mult)
                V("tensor_scalar", csn, csn, 1.0 / 24, 0.0, ALU.add, ALU.add)
                V("tensor_tensor", csn, csn, th2, ALU.mult)
                V("tensor_scalar", csn, csn, -0.5, 0.0, ALU.add, ALU.add)
                V("tensor_tensor", csn, csn, th2, ALU.mult)
                V("tensor_scalar", csn, csn, 1.0, 0.0, ALU.add, ALU.add)
                for _ in range(3):
                    V("tensor_tensor", tmpc, csn, sn, ALU.mult)
                    V("tensor_tensor", th2, sn, sn, ALU.mult)
                    V("tensor_tensor", csn, csn, csn, ALU.mult)
                    V("tensor_tensor", csn, csn, th2, ALU.subtract)
                    V("tensor_scalar", sn, tmpc, 2.0, 0.0, ALU.mult, ALU.add)
                V("tensor_tensor", lbr, mag, csn, ALU.mult)
                V("tensor_tensor", lbi, mag, sn, ALU.mult)
                V("tensor_tensor", ivr, imag, csn, ALU.mult)
                V("scalar_tensor_tensor", ivi, imag, -1.0, sn, ALU.mult, ALU.mult)
                V("tensor_tensor", den, lre, lre, ALU.mult)
                V("tensor_tensor", u1, lim, lim, ALU.mult)
                V("tensor_tensor", den, den, u1, ALU.add)
                V("reciprocal", den, den)
                V("tensor_scalar", nr, lbr, -1.0, 0.0, ALU.add, ALU.add)
                V("tensor_tensor", qre, nr, lre, ALU.mult)
                V("tensor_tensor", u1, lbi, lim, ALU.mult)
                V("tensor_tensor", qre, qre, u1, ALU.add)
                V("tensor_tensor", qre, qre, den, ALU.mult)
                V("tensor_tensor", qim, lbi, lre, ALU.mult)
                V("tensor_tensor", u1, nr, lim, ALU.mult)
                V("tensor_tensor", qim, qim, u1, ALU.subtract)
                V("tensor_tensor", qim, qim, den, ALU.mult)
                qreb = qre.unsqueeze(2).to_broadcast([128, 16, 16]); qimb = qim.unsqueeze(2).to_broadcast([128, 16, 16])
                V("tensor_tensor", bbr, bre, qreb, ALU.mult)
                V("tensor_tensor", v1, bim, qimb, ALU.mult)
                V("tensor_tensor", bbr, bbr, v1, ALU.subtract)
                V("tensor_tensor", bbi, bim, qreb, ALU.mult)
                V("tensor_tensor", v1, bre, qimb, ALU.mult)
                V("tensor_tensor", bbi, bbi, v1, ALU.add)
                for src_, dstT in ((bbr, Bw_re), (bbi, Bw_im)):
                    for qq in range(4):
                        V("memset", bdw, 0.0)
                        for r_ in range(4):
                            for g2_ in range(2):
                                c0 = 32 * r_ + 16 * g2_
                                V("tensor_copy", bdw[64 * g2_:64 * g2_ + 64, r_, c0:c0 + 16], src_[64 * g2_:64 * g2_ + 64, qq * 4 + r_, :])
                        for r_ in range(4):
                            kb.I("pe", "transpose", PS[4][:, r_ * 128:(r_ + 1) * 128], bdw[:, r_, :], ident[:],
                                 reads=S5K + ["ident"], writes=[f"pX4_{r_}"])
                        kb.I("dve", "tensor_copy", dstT[:, qq * 4:(qq + 1) * 4, :], PS[4][:].rearrange("p (r c) -> p r c", r=4),
                             reads=bk(4) + S5K, writes=S5K)
                cpad = s5t["Xim"][:].rearrange("p (j c) -> p j c", c=128)
                for nm, dstC, sc in (("c_re", Cw_re, 1.0), ("c_im", Cw_imn, -1.0)):
                    csrc = din[nm].rearrange("g h p -> (g h) p").rearrange("(j r) p -> r j p", r=128)
                    for hh_ in range(2):
                        kb.D(cpad[:, :, 64 * hh_:64 * hh_ + 64], csrc, reads=S5K, writes=S5K)
                    for j_ in range(4):
                        kb.I("pe", "transpose", PS[4][:, j_ * 128:(j_ + 1) * 128], cpad[:, j_, :], ident[:], reads=S5K + ["ident"], writes=[f"pX4_{j_}"])
                    V("memset", dstC[:], 0.0)
                    for g2_ in range(2):
                        srcv = PS[4][64 * g2_:64 * g2_ + 64, :].rearrange("p (q g h) -> p q g h", g=2, h=16)[:, :, g2_, :]
                        kb.I("dve", "tensor_scalar", dstC[64 * g2_:64 * g2_ + 64, :, 16 * g2_:16 * g2_ + 16], srcv, sc, 0.0, ALU.mult, ALU.add,
                             reads=bk(4) + S5K, writes=S5K)
                for (Tr, Ti, sr, si) in ((F_re, F_im, lbr, lbi), (E_re, E_im, ivr, ivi)):
                    V("tensor_copy", Tr[:, :, 0:1], sr.unsqueeze(2))
                    V("tensor_copy", Ti[:, :, 0:1], si.unsqueeze(2))
                    n_ = 1
                    while n_ < 128:
                        for qh in range(2):
                            qsl = slice(8 * qh, 8 * qh + 8)
                            pr = Tr[:, qsl, n_ - 1:n_].to_broadcast([128, 8, n_]); pim = Ti[:, qsl, n_ - 1:n_].to_broadcast([128, 8, n_])
                            t1_ = s5t["t1"][:, 0:8 * n_].rearrange("p (q n) -> p q n", q=8)
                            t2_ = s5t["t2"][:, 0:8 * n_].rearrange("p (q n) -> p q n", q=8)
                            V("tensor_tensor", t1_, Tr[:, qsl, 0:n_], pr, ALU.mult)
                            V("tensor_tensor", t2_, Ti[:, qsl, 0:n_], pim, ALU.mult)
                            V("tensor_tensor", Tr[:, qsl, n_:2 * n_], t1_, t2_, ALU.subtract)
                            V("tensor_tensor", t1_, Tr[:, qsl, 0:n_], pim, ALU.mult)
                            V("tensor_tensor", t2_, Ti[:, qsl, 0:n_], pr, ALU.mult)
                            V("tensor_tensor", Ti[:, qsl, n_:2 * n_], t1_, t2_, ALU.add)
                        n_ *= 2
                if debug and sd == 0:
                    kb.D(dbg["d_F"], F_re[:].rearrange("p q t -> p (q t)"), reads=S5K, writes=["o_d_F"])
                V("tensor_copy", lre[:, 0:1], lre[:, 0:1])

        with contextlib.ExitStack() as p0:
            rec0 = []
            kb.rec = rec0
            stage = [sb(p0, f"stage{i}", [128, 2272]) for i in range(2)]
            for kt in range(8):
                s_ = stage[kt % 2]; sk = f"stage{kt % 2}"
                kb.D(s_[:], din["w_in"][kt * 128:(kt + 1) * 128, :], writes=[sk])
                kb.cp("act" if kt % 2 else "pool", win[:, kt, :], s_[:], reads=[sk], writes=["win"])
            modb = sb(p0, "modb", [128, 6 * D])
            cT = sb(p0, "cT", [128, 2, 8]); cs = sb(p0, "cs", [128, 2, 8])
            lbx = sb(p0, "lbx", [128, 8, 128]); lbc = sb(p0, "lbc", [128, 8, 128])
            modc = sb(p0, "modc", [128, 2 * D]); mbias = sb(p0, "mbias", [128, 6 * D])
            g1b = sb(p0, "g1b", [128, D]); tmpA = sb(p0, "tmpA", [128, D])
            wbuf = [sb(p0, f"wbuf{i}", [128, 512]) for i in range(4)]
            for j_ in range(2):
                kb.D(cT[:, j_, :], din["cc"][j_].rearrange("(k p) -> p k", p=128), writes=["cT"], allow_slow_non_contiguous=True)
            kb.D(mbias[:], din["mod_b"].partition_broadcast(128), writes=["mbias"])
            kb.D(g1b[:], din["g1"].partition_broadcast(128), writes=["g1b"])
            kb.I("act", "activation", cs[:], cT[:], AF.Silu, reads=["cT"], writes=["cs"])
            for kt in range(8):
                kb.I("dve", "tensor_copy", lbx[:, kt, :], cs[:, 0, kt:kt + 1].to_broadcast([128, 128]), reads=["cs"], writes=["lbx"])
                kb.I("pool", "tensor_copy", lbc[:, kt, :], cs[:, 1, kt:kt + 1].to_broadcast([128, 128]), reads=["cs"], writes=["lbc"])
            i = 0
            for n in range(12 if upto != "p0a" else 0):
                ns = slice(n * 512, (n + 1) * 512)
                for kt in range(8):
                    wb = wbuf[i % 4]; wk = f"wbuf{i % 4}"; i += 1
                    kb.D(wb[:], din["mod_w"][kt * 128:(kt + 1) * 128, ns], writes=[wk])
                    kb.I("pe", "matmul", PS[0][:], lbx[:, kt, :], wb[:], start=(kt == 0), stop=(kt == 7), reads=[wk, "lbx"], writes=bk(0))
                    if n < 4:
                        kb.I("pe", "matmul", PS[1][:], lbc[:, kt, :], wb[:], start=(kt == 0), stop=(kt == 7), reads=[wk, "lbc"], writes=bk(1))
                kb.I("dve", "tensor_tensor", modb[:, ns], PS[0][:], mbias[:, ns], ALU.add, reads=bk(0) + ["mbias"], writes=["modb"])
                if n < 4:
                    kb.I("dve", "tensor_tensor", modc[:, ns], PS[1][:], mbias[:, ns], ALU.add, reads=bk(1) + ["mbias"], writes=["modc"])
            for which, src_ in ((0, modb), (1, modc)) if upto not in ("p0a", "p0b") else ():
                kb.I("dve", "scalar_tensor_tensor", tmpA[:], src_[:, D:2 * D], 1.0, g1b[:], ALU.add, ALU.mult,
                     reads=["modb", "modc", "g1b"], writes=["tmpA"])
                for half in range(2):
                    for j in range(4):
                        kt = half * 4 + j
                        kb.I("pe", "transpose", PS[2][:, j * 128:(j + 1) * 128], tmpA[:, kt * 128:(kt + 1) * 128], ident[:],
                             reads=["tmpA", "ident"], writes=[f"pX2_{j}"])
                        kb.I("pe", "transpose", PS[3][:, j * 128:(j + 1) * 128], src_[:, kt * 128:(kt + 1) * 128], ident[:],
                             reads=["modb", "modc", "ident"], writes=[f"pX3_{j}"])
                    for j in range(4):
                        kt = half * 4 + j
                        kb.I("dve", "tensor_copy", AB[:, 2 * which, kt:kt + 1], PS[2][:, j * 128:j * 128 + 1], reads=[f"pX2_{j}"], writes=["AB"])
                        kb.I("dve", "tensor_copy", AB[:, 2 * which + 1, kt:kt + 1], PS[3][:, j * 128:j * 128 + 1], reads=[f"pX3_{j}"], writes=["AB"])
            if upto not in ("p0a", "p0b", "p0c"):
                kb.D(scr["mod"], modb[:], reads=["modb"], writes=["scr_mod"])
            if debug and upto not in ("p0a", "p0b", "p0c"):
                kb.D(dbg["d_mod"], modb[:], reads=["modb"], writes=["o_d_mod"])
                kb.D(dbg["d_AB"], AB[:].rearrange("p a k -> p (a k)"), reads=["AB"], writes=["o_d_AB"])
            kb.rec = None
            recS = []
            if not upto.startswith("p0"):
                kb.rec = recS
                s5_setup(0)
                kb.rec = None
            kb.merge(rec0, recS)
            kb.I("dve", "tensor_copy", tmpA[:, 0:1], tmpA[:, 0:1],
                 reads=list(kb.lastw.keys()), writes=list(kb.lastw.keys()) + ["fence0"])
        FENCE0 = ["fence0"]
        if upto.startswith("p0"):
            kb.mute = True

        with contextlib.ExitStack() as sw:
            F0 = FENCE0
            kkc = sb(sw, "kkc", [128, 4]); kac = sb(sw, "kac", [128, 4]); omka = sb(sw, "omka", [128, 4])
            rkc = sb(sw, "rkc", [128, 4]); w0c = sb(sw, "w0c", [128, 2, 4]); a0c = sb(sw, "a0c", [128, 2, 4])
            cw = sb(sw, "cw", [128, 12, 9]); s5dc = sb(sw, "s5dc", [128, 4])
            w2pad = sb(sw, "w2pad", [128, 2, 512]); a2pad = sb(sw, "a2pad", [128, 2, 512])
            diagd = sb(sw, "diagd", [128, 4, 128], BF16)
            for t, nm in ((kkc, "kkv"), (kac, "kav"), (rkc, "rkv"), (s5dc, "s5d")):
                kb.D(t[:], din[nm].rearrange("(f p) -> p f", p=128), reads=F0, writes=["cols"], allow_slow_non_contiguous=True)
            for t, nm in ((w0c, "w0"), (a0c, "a0")):
                for d_ in range(2):
                    kb.D(t[:, d_, :], din[nm][d_].rearrange("(f p) -> p f", p=128), reads=F0, writes=["cols"], allow_slow_non_contiguous=True)
            for tp in range(9):
                kb.D(cw[:, :, tp], din["conv"][tp].rearrange("(f p) -> p f", p=128), reads=F0, writes=["cols"], allow_slow_non_contiguous=True)
            kb.I("dve", "tensor_scalar", omka[:], kac[:], -1.0, 1.0, ALU.mult, ALU.add, reads=["cols"], writes=["cols2"])
            kb.I("pool", "memset", w2pad[:], 0.0, reads=F0, writes=["w2pad"])
            kb.I("pool", "memset", a2pad[:], 0.0, reads=F0, writes=["a2pad"])
            for d_ in range(2):
                kb.D(w2pad[32 * d_:32 * d_ + 32, d_, :], din["w2"][d_], reads=["w2pad"], writes=["w2pad"])
                kb.D(a2pad[64 + 32 * d_:96 + 32 * d_, d_, :], din["a2"][d_], reads=["a2pad"], writes=["a2pad"])
            for ct in range(4):
                kb.ts1("dve", diagd[:, ct, :], ident[:], s5dc[:, ct:ct + 1], ALU.mult, reads=["cols", "ident"], writes=["diagd"])

            ST = sb(sw, "ST", [128, 4, 64])
            xw = sb(sw, "xw", [128, 2, D]); junk = sb(sw, "junk", [128, D], BF16)
            ssum = sb(sw, "ssum", [128, 2]); rstd = sb(sw, "rstd", [128, 2])
            hT = sb(sw, "hT", [128, 8, 256], BF16)
            zrkv2 = [sb(sw, f"zrkv{i}", [128, 3, 264], BF16) for i in range(2)]; zl = sb(sw, "zl", [128, 128]); zg = sb(sw, "zg", [128, 128])
            dgb = [sb(sw, f"dgb{i}", [128, 9, 128], BF16) for i in range(2)]
            rkvc2 = [sb(sw, f"rkvc{i}", [128, 3, 128]) for i in range(2)]
            W2 = {}
            for nm in ("kk", "t1", "t2", "ld", "a", "cs", "Eni", "b", "kd", "bep", "kap"):
                W2[nm] = [sb(sw, f"{nm}{i}", [128, 128]) for i in range(2)]
            zub = [sb(sw, f"zub{i}", [128, 4, 128], BF16) for i in range(2)]
            BE = [sb(sw, f"BE{i}", [128, 4, 128]) for i in range(2)]; KAP = [sb(sw, f"KAP{i}", [128, 4, 128]) for i in range(2)]
            AR = [sb(sw, f"AR{i}", [128, 4, 256]) for i in range(2)]; gC = [sb(sw, f"gC{i}", [128, 4]) for i in range(2)]
            tot = sb(sw, "tot", [128, 4])
            TMt = [sb(sw, f"TMt{i}", [128, 4, 512]) for i in range(2)]
            UT = sb(sw, "UT", [128, 512]); yT = sb(sw, "yT", [128, 512])
            flpR = sb(sw, "flpR", [128, 512]); flpS = sb(sw, "flpS", [128, 512])
            NS = 4
            NB = [sb(sw, f"NB{i}", [128, 256]) for i in range(NS)]; KA = [sb(sw, f"KA{i}", [128, 256]) for i in range(NS)]
            Pq = [sb(sw, f"Pq{i}", [128, 2, 128], BF16) for i in range(NS)]; PTq = [sb(sw, f"PTq{i}", [128, 2, 128], BF16) for i in range(NS)]
            Xq = [sb(sw, f"Xq{i}", [128, 2, 64]) for i in range(NS)]; Xb = [sb(sw, f"Xb{i}", [128, 2, 64], BF16) for i in range(NS)]
            print("SBUF bytes remaining in sweep scope:", nc.sbuf_bytes_remaining)
            for cf in range(12):
                for tp in range(9):
                    kb.ts1("dve" if tp % 2 else "pool", dgb[cf % 2][:, tp, :], ident[:], cw[:, cf, tp:tp + 1], ALU.mult,
                           reads=["cols", "ident", f"dgb{cf % 2}"], writes=[f"dgb{cf % 2}"])
                kb.D(scr["dg"][cf], dgb[cf % 2][:].rearrange("p t c -> p (t c)"), reads=[f"dgb{cf % 2}"], writes=["scr_dg"])
            kb.I("pool", "memset", xw[:], 0.0, reads=F0, writes=["xw0", "xw1"])

            def load_window(sd, is_ctx, c):
                rev = (sd == 1)
                TT = T_CTX if is_ctx else T_LAT
                src = din["ctx"] if is_ctx else din["x"]
                for half in range(2):
                    lo_r = 128 * c - 64 + 128 * half
                    if not rev:
                        lo, hi = lo_r, lo_r + 128
                    else:
                        hi, lo = TT - lo_r, TT - lo_r - 128
                    a0_, a1_ = max(lo, 0), min(hi, TT)
                    if a1_ > a0_:
                        kb.D(xw[a0_ - lo:a1_ - lo, half, :], src[a0_:a1_, :], writes=[f"xw{half}"])

            def process_chunk(sd, is_ctx, c, own, slot, nxt=None, first=False, reuse=False):
                rev = (sd == 1)
                TT = T_CTX if is_ctx else T_LAT
                src = din["ctx"] if is_ctx else din["x"]
                ab = 2 if is_ctx else 0
                tid = antiI if rev else ident
                tk = "antiI" if rev else "ident"
                dbg_here = debug and sd == 0 and (not is_ctx) and c == 0
                SL = f"@{slot}"
                streams = {"front": [], "rwkv": [], "s5": []}
                kb.rec = streams["front"]
                AR_, BE_, KAP_, gC_, TM_, zub_ = AR[slot], BE[slot], KAP[slot], gC[slot], TMt[slot], zub[slot]
                kapT = TM_[:, 0, :]; bepT = TM_[:, 1, :]; vT = TM_[:, 2, :]; bvT = TM_[:, 3, :]
                TMK = "TM" + SL
                if not reuse:
                    if first:
                        load_window(sd, is_ctx, c)
                    kb.I("pool", "memset", ssum[:], 0.0, writes=["ssum"])
                    for half in range(2):
                        kb.I("act", "activation", junk[:], xw[:, half, :], AF.Square, accum_out=ssum[:, half:half + 1],
                             reads=[f"xw{half}"], writes=["junk", "ssum"])
                    kb.I("dve", "tensor_scalar", rstd[:], ssum[:], 1.0 / D, 1e-6, ALU.mult, ALU.add, reads=["ssum"], writes=["rstd"])
                    kb.I("act", "activation", rstd[:], rstd[:], AF.Sqrt, reads=["rstd"], writes=["rstd"])
                    kb.I("dve", "reciprocal", rstd[:], rstd[:], reads=["rstd"], writes=["rstd"])
                    for half in range(2):
                        kb.ts1("dve" if half == 0 else "pool", xw[:, half, :], xw[:, half, :], rstd[:, half:half + 1], ALU.mult,
                               reads=["rstd", f"xw{half}"], writes=[f"xw{half}"])
                    for kt in range(8):
                        b_ = kt % 2
                        for half in range(2):
                            kb.I("pe", "transpose", PS[b_][:, half * 128:(half + 1) * 128], xw[:, half, kt * 128:(kt + 1) * 128], tid[:],
                                 reads=[f"xw{half}", tk], writes=[f"pX{b_}_0"])
                        if kt % 2:
                            kb.I("dve", "tensor_scalar", hT[:, kt, :], PS[b_][:, 0:256], AB[:, ab, kt:kt + 1], AB[:, ab + 1, kt:kt + 1], ALU.mult, ALU.add,
                                 reads=[f"pX{b_}_0", "AB"], writes=[f"hT{kt}"])
                        else:
                            kb.I("act", "activation", hT[:, kt, :], PS[b_][:, 0:256], AF.Identity, bias=AB[:, ab + 1, kt:kt + 1], scale=AB[:, ab, kt:kt + 1],
                                 reads=[f"pX{b_}_0", "AB"], writes=[f"hT{kt}"])
                    if nxt is not None:
                        load_window(sd, nxt[0], nxt[1])
                    others = [("zl", zl[:, :], 1536, 128), ("zg", zg[0:96, :], 1664, 96)] + [(f"zu{j}" + SL, zub_[:, j, :], 1760 + 128 * j, 128) for j in range(4)]
                    for i_, (nm, dst, c0, m) in enumerate(others):
                        b_ = i_ % 2
                        pr_ = PS[b_][0:m, 0:128]
                        pk = [f"pX{b_}_0"]
                        for kt in range(8):
                            kb.I("pe", "matmul", pr_, win[:, kt, c0:c0 + m], hT[:, kt, 64:192], start=(kt == 0), stop=(kt == 7), reads=["win", f"hT{kt}"], writes=pk)
                        if nm == "zg":
                            kb.I("act", "activation", dst, pr_, AF.Sigmoid, reads=pk, writes=[nm])
                        else:
                            kb.cp("dve" if i_ % 2 else "act", dst, pr_, reads=pk, writes=[nm])
                    if own is not None and sd == 0:
                        kb.D(scr["sg"][own * 128:own * 128 + 96, :], zg[0:96, :], reads=["zg"], writes=[f"scr_sg{own}"])
                    kb.I("act", "activation", zl[0:64, :], zl[0:64, :], AF.Tanh, reads=["zl"], writes=["zl"])
                    if sd == 0 and own is not None and not is_ctx:
                        kb.D(scr["fz"][own * 128:(own + 1) * 128, :], zl[:, :], reads=["zl"], writes=[f"scr_fz{own}"])
                        kb.D(scr["fu"][own * 128:(own + 1) * 128, :], zub_[:].rearrange("p j t -> p (j t)"), reads=[f"zu{j}" + SL for j in range(4)], writes=[f"scr_fu{own}"])
                else:
                    kb.D(xw[:, 1, 512:640], scr["fz"][own * 128:(own + 1) * 128, :], reads=[f"scr_fz{own}"], writes=["xw1"])
                    kb.I("dve", "tensor_copy", zl[:, :], xw[:, 1, 512:640][:, ::-1], reads=["xw1"], writes=["zl"])
                    kb.D(junk[:, 0:512], scr["fu"][own * 128:(own + 1) * 128, :], reads=[f"scr_fu{own}"], writes=["junk"])
                    kb.I("dve", "tensor_copy", zub_[:], junk[:, 0:512].rearrange("p (j t) -> p j t", t=128)[:, :, ::-1], reads=["junk"], writes=[f"zu{j}" + SL for j in range(4)])
                sgn = -1 if rev else 1
                nk = 4 if own is not None else 3
                for ft in range(4):
                    fp_ = ft % 2
                    zrkv = zrkv2[fp_]; rkvc = rkvc2[fp_]
                    W = {nm: W2[nm][fp_] for nm in W2}
                    if reuse:
                        stg_ = xw[:, ft % 2, 0:512]
                        kb.D(stg_, scr["fr"][(own * 4 + ft) * 128:(own * 4 + ft + 1) * 128, :], reads=[f"scr_fr{own}_{ft}"], writes=[f"xw{ft % 2}"])
                        kb.I("dve", "tensor_copy", rkvc[:, :, :], stg_[:, 0:384].rearrange("p (j t) -> p j t", t=128)[:, :, ::-1],
                             reads=[f"xw{ft % 2}"], writes=[f"rkvc{j}_{fp_}" for j in range(3)])
                        kb.I("dve", "tensor_copy", W["kk"][:, :], stg_[:, 384:512][:, ::-1], reads=[f"xw{ft % 2}"], writes=[f"W_kk{fp_}"])
                    for j3 in (range(3) if not reuse else ()):
                        b_ = j3 % 2
                        cf = (j3 * 4 + ft) * 128
                        pr_ = PS[b_][:, 0:256]
                        pk = [f"pX{b_}_0"]
                        for kt in range(8):
                            kb.I("pe", "matmul", pr_, win[:, kt, cf:cf + 128], hT[:, kt, :], start=(kt == 0), stop=(kt == 7), reads=["win", f"hT{kt}"], writes=pk)
                        if is_ctx:
                            kb.cp("act" if j3 % 2 else "dve", zrkv[:, j3, 0:256], pr_, reads=pk, writes=[f"zrkv{j3}_{fp_}"])
                        else:
                            if c == 0 and ft < 2:
                                kb.I("pool", "memset", zrkv[:, j3, :].rearrange("p (r c) -> p r c", c=66)[:, :, 0:1], 0.0, reads=[f"zrkv{j3}_{fp_}"], writes=[f"zrkv{j3}_{fp_}"])
                                kb.I("pool", "memset", zrkv[:, j3, :].rearrange("p (r c) -> p r c", c=66)[:, :, 65:66], 0.0, reads=[f"zrkv{j3}_{fp_}"], writes=[f"zrkv{j3}_{fp_}"])
                            kb.cp("act" if j3 % 2 else "dve", zrkv[:, j3, :].rearrange("p (r c) -> p r c", c=66)[:, :, 1:65],
                                  pr_.rearrange("p (r c) -> p r c", c=64), reads=pk, writes=[f"zrkv{j3}_{fp_}"])
                    if dbg_here and ft == 0:
                        kb.D(dbg["d_z"], zrkv[:].rearrange("p f t -> p (f t)"), reads=[f"zrkv{j}_{fp_}" for j in range(3)], writes=["o_d_z"])
                    for j3 in (range(3) if not reuse else ()):
                        zk = f"zrkv{j3}_{fp_}"; ok = f"rkvc{j3}_{fp_}"
                        cf = j3 * 4 + ft
                        ds = (ft * 3 + j3) % 2
                        dk = f"dgb{ds}"
                        kb.D(dgb[ds][:].rearrange("p t c -> p (t c)"), scr["dg"][cf], reads=["scr_dg"], writes=[dk])
                        b_ = (j3 + 1) % 2
                        pc = PS[b_][:, 256:384]
                        taps = [(0, 0)] + [(dy, dx) for dy in ((0,) if is_ctx else (-1, 0, 1)) for dx in (-1, 0, 1) if (dy, dx) != (0, 0)]
                        mms = []
                        for (dy, dx) in taps:
                            wi = (1 + sgn * dy) * 3 + (1 + sgn * dx)
                            if is_ctx:
                                j0, j1 = 64, 192
                                if 128 * c + dx < 0:
                                    j0 = 65
                                if 128 * c + 127 + dx >= TT:
                                    j1 = 191
                                mms.append((PS[b_][:, 256 + j0 - 64:256 + j1 - 64], wi, zrkv[:, j3, j0 + dx:j1 + dx]))
                            else:
                                i0, i1 = 0, 2
                                if 2 * c + dy < 0:
                                    i0 = 1
                                if 2 * c + 1 + dy >= 64:
                                    i1 = 1
                                o0 = 66 * i0 + 1; o1 = 66 * (i1 - 1) + 65
                                sh = 66 * (1 + dy) + dx
                                mms.append((PS[b_][:, 256 + o0:256 + o1], wi, zrkv[:, j3, o0 + sh:o1 + sh]))
                        for ti, (o_ap, wi, i_ap) in enumerate(mms):
                            kb.I("pe", "matmul", o_ap, dgb[ds][:, wi, :], i_ap, start=(ti == 0), stop=(ti == len(mms) - 1),
                                 reads=[zk, dk], writes=[f"pX{b_}_0"])
                        if is_ctx:
                            kb.cp("act" if j3 % 2 == 0 else "dve", rkvc[:, j3, :], pc, reads=[f"pX{b_}_0"], writes=[ok])
                        else:
                            kb.cp("act" if j3 % 2 == 0 else "dve", rkvc[:, j3, :].rearrange("p (r c) -> p r c", c=64),
                                  PS[b_][:, 256:388].rearrange("p (r c) -> p r c", c=66)[:, :, 1:65], reads=[f"pX{b_}_0"], writes=[ok])
                    if dbg_here and ft == 0:
                        kb.D(dbg["d_rkvc"], rkvc[:].rearrange("p f t -> p (f t)"), reads=[f"rkvc{j}_{fp_}" for j in range(3)], writes=["o_d_rkvc"])
                    rc = rkvc[:, 0, :]; kc = rkvc[:, 1, :]; vc = rkvc[:, 2, :]
                    fk = f"w{ft}" + SL
                    g_ = {nm: W[nm][:, :] for nm in W}
                    CK = ["cols", "cols2"]

                    def V(meth, *args, eng="dve", r=(), w=(), **kw):
                        kb.I(eng, meth, *args, reads=list(r) + CK, writes=list(w), **kw)
                    ARa = AR_[:, ft, 0:128]; ARr = AR_[:, ft, 128:256]
                    fa, fr, fb, fkp, fg = fk + "a", fk + "r", fk + "b", fk + "k", fk + "g"
                    if not reuse:
                        V("tensor_scalar", g_["kk"], kc, kkc[:, ft:ft + 1], 0.0, ALU.mult, ALU.add, r=[f"rkvc1_{fp_}"], w=[f"W_kk{fp_}"])
                        V("tensor_tensor", g_["t1"], g_["kk"], g_["kk"], ALU.mult, r=[f"W_kk{fp_}"], w=[f"W_t1{fp_}"])
                        kb.I("pe", "matmul", PS[0][:, 384:512], bones[:], g_["t1"], start=True, stop=True, reads=[f"W_t1{fp_}", "bones"], writes=["pX0_0"])
                        kb.I("dve", "tensor_scalar", g_["t2"], PS[0][:, 384:512], 1e-12, 0.0, ALU.max, ALU.add, reads=["pX0_0"], writes=[f"W_t2{fp_}"])
                        V("activation", g_["t2"], g_["t2"], AF.Sqrt, eng="act", r=[f"W_t2{fp_}"], w=[f"W_t2{fp_}"])
                        V("reciprocal", g_["t2"], g_["t2"], r=[f"W_t2{fp_}"], w=[f"W_t2{fp_}"])
                        V("tensor_tensor", g_["kk"], g_["kk"], g_["t2"], ALU.mult, r=[f"W_kk{fp_}", f"W_t2{fp_}"], w=[f"W_kk{fp_}"])
                        if sd == 0 and own is not None and not is_ctx:
                            r0_ = (own * 4 + ft) * 128
                            kb.D(scr["fr"][r0_:r0_ + 128, 0:384], rkvc[:, :, :].rearrange("p j t -> p (j t)"), reads=[f"rkvc{j}_{fp_}" for j in range(3)], writes=[f"scr_fr{own}_{ft}"])
                            kb.D(scr["fr"][r0_:r0_ + 128, 384:512], g_["kk"], reads=[f"W_kk{fp_}"], writes=[f"scr_fr{own}_{ft}"])
                    kb.I("pe", "matmul", PS[1][:, 256:384], w2pad[:, sd, ft * 128:(ft + 1) * 128], zl[:, :], start=True, stop=True, reads=["zl", "w2pad"], writes=["pX1_0"])
                    kb.I("pe", "matmul", PS[1][:, 384:512], a2pad[:, sd, ft * 128:(ft + 1) * 128], zl[:, :], start=True, stop=True, reads=["zl", "a2pad"], writes=["pX1_0"])
                    kb.I("act", "activation", g_["ld"], PS[1][:, 256:384], AF.Sigmoid, bias=w0c[:, sd, ft:ft + 1], scale=1.0, reads=["pX1_0", "cols"], writes=[f"W_ld{fp_}"])
                    kb.I("act", "activation", g_["a"], PS[1][:, 384:512], AF.Sigmoid, bias=a0c[:, sd, ft:ft + 1], scale=1.0, reads=["pX1_0", "cols"], writes=[f"W_a{fp_}"])
                    V("tensor_tensor_scan", g_["cs"], ones[:], g_["ld"], 0.0, ALU.mult, ALU.add, r=[f"W_ld{fp_}", "ones"], w=[f"W_cs{fp_}"])
                    V("tensor_copy", tot[:, ft:ft + 1], g_["cs"][:, 127:128], r=[f"W_cs{fp_}"], w=[f"tot{ft}"])
                    V("tensor_tensor", g_["t1"], g_["cs"], g_["ld"], ALU.subtract, r=[f"W_cs{fp_}", f"W_ld{fp_}"], w=[f"W_t1{fp_}"])
                    V("activation", ARr, g_["cs"], AF.Exp, eng="act", scale=-KDEC, r=[f"W_cs{fp_}"], w=[fr])
                    V("activation", ARa, g_["t1"], AF.Exp, eng="act", scale=-KDEC, r=[f"W_t1{fp_}"], w=[fa])
                    V("activation", g_["Eni"], g_["cs"], AF.Exp, eng="act", scale=KDEC, r=[f"W_cs{fp_}"], w=[f"W_Eni{fp_}"])
                    V("activation", gC_[:, ft:ft + 1], tot[:, ft:ft + 1], AF.Exp, eng="act", scale=-KDEC, r=[f"tot{ft}"], w=[fg])
                    V("tensor_tensor", g_["b"], g_["kk"], g_["a"], ALU.mult, r=[f"W_kk{fp_}", f"W_a{fp_}"], w=[f"W_b{fp_}"])
                    V("tensor_scalar", g_["t2"], g_["a"], kac[:, ft:ft + 1], omka[:, ft:ft + 1], ALU.mult, ALU.add, r=[f"W_a{fp_}"], w=[f"W_t2{fp_}"])
                    V("tensor_tensor", g_["kd"], kc, g_["t2"], ALU.mult, r=[f"rkvc1_{fp_}", f"W_t2{fp_}"], w=[f"W_kd{fp_}"])
                    V("tensor_tensor", ARa, ARa, g_["kk"], ALU.mult, r=[fa, f"W_kk{fp_}"], w=[fa])
                    V("tensor_tensor", ARr, ARr, rc, ALU.mult, eng="pool", r=[fr, f"rkvc0_{fp_}"], w=[fr])
                    V("tensor_tensor", BE_[:, ft, :], g_["b"], g_["Eni"], ALU.mult, r=[f"W_b{fp_}", f"W_Eni{fp_}"], w=[fb])
                    V("tensor_tensor", KAP_[:, ft, :], g_["kd"], g_["Eni"], ALU.mult, eng="pool", r=[f"W_kd{fp_}", f"W_Eni{fp_}"], w=[fkp])
                    V("tensor_scalar", g_["bep"], BE_[:, ft, :], gC_[:, ft:ft + 1], -1.0, ALU.mult, ALU.mult, r=[fb, fg], w=[f"W_bep{fp_}"])
                    V("tensor_scalar", g_["kap"], KAP_[:, ft, :], gC_[:, ft:ft + 1], 0.0, ALU.mult, ALU.add, eng="pool", r=[fkp, fg], w=[f"W_kap{fp_}"])
                    tsrc = [(g_["kap"], f"W_kap{fp_}"), (g_["bep"], f"W_bep{fp_}"), (vc, f"rkvc2_{fp_}")]
                    if own is not None:
                        V("scalar_tensor_tensor", g_["t1"], rc, rkc[:, ft:ft + 1], g_["kd"], ALU.mult, ALU.mult, r=[f"rkvc0_{fp_}", f"W_kd{fp_}"], w=[f"W_t1{fp_}"])
                        kb.I("pe", "matmul", PS[0][:, 384:512], bones[:], g_["t1"], start=True, stop=True, reads=[f"W_t1{fp_}", "bones"], writes=["pX0_0"])
                        kb.I("dve", "tensor_tensor", g_["t2"], PS[0][:, 384:512], vc, ALU.mult, reads=["pX0_0", f"rkvc2_{fp_}"], writes=[f"W_t2{fp_}"])
                        tsrc.append((g_["t2"], f"W_t2{fp_}"))
                    b_ = ft % 2
                    for i_, (s_ap, s_k) in enumerate(tsrc):
                        kb.I("pe", "transpose", PS[b_][:, i_ * 128:(i_ + 1) * 128], s_ap, ident[:], reads=[s_k, "ident"], writes=[f"pX{b_}_0"])
                    kb.cp("act" if ft % 2 else "dve", TM_[:, 0:nk, ft * 128:(ft + 1) * 128], PS[b_][:, 0:nk * 128].rearrange("p (k t) -> p k t", k=nk),
                          reads=[f"pX{b_}_0"], writes=[TMK])
                kb.rec = streams["rwkv"]
                for g0 in range(0, 8, NS):
                    heads = list(range(g0, g0 + NS))

                    def hv(h):
                        ft = h // 2; Rs = slice(64 * (h % 2), 64 * (h % 2) + 64)
                        return ft, Rs, [f"w{ft}" + SL + x_ for x_ in "arbkg"]
                    BK = [2 + s_ for s_ in range(NS)]
                    bkk = [f"pX{b_}_0" for b_ in BK]
                    for s_, h in enumerate(heads):
                        ft, Rs, fk = hv(h)
                        be = BE_[Rs, ft, :]; ar = AR_[Rs, ft, :]; al = AR_[Rs, ft, 0:128]
                        kb.I("pe", "matmul", PS[BK[s_]][:, 0:256], be, ar, start=True, stop=True, reads=fk, writes=[bkk[s_]])
                        kb.I("pe", "matmul", PS[BK[s_]][:, 256:384], al, be, start=True, stop=True, reads=fk, writes=[bkk[s_]])
                    for s_, h in enumerate(heads):
                        kb.I("dve", "tensor_tensor", NB[s_][:], PS[BK[s_]][:, 0:256], mask_b[:], ALU.mult, reads=[bkk[s_], "mask_b"], writes=[f"NB{s_}"])
                        kb.I("dve", "tensor_tensor", PTq[s_][:, 0, :], PS[BK[s_]][:, 256:384], maskT[:], ALU.mult, reads=[bkk[s_], "maskT"], writes=[f"PT{s_}_0"])
                        kb.I("dve", "tensor_tensor", Pq[s_][:, 0, :], PS[BK[s_]][:, 0:128], mask_b[:, 0:128], ALU.mult, reads=[bkk[s_], "mask_b"], writes=[f"P{s_}_0"])
                    for s_, h in enumerate(heads):
                        ft, Rs, fk = hv(h)
                        kb.I("pe", "matmul", PS[BK[s_]][:, 0:256], KAP_[Rs, ft, :], AR_[Rs, ft, :], start=True, stop=True, reads=fk, writes=[bkk[s_]])
                    for s_, h in enumerate(heads):
                        kb.I("dve", "tensor_tensor", KA[s_][:], PS[BK[s_]][:, 0:256], mask_k[:], ALU.mult, reads=[bkk[s_], "mask_k"], writes=[f"KA{s_}"])
                    for s_, h in enumerate(heads):
                        ft, Rs, fk = hv(h)
                        al = AR_[Rs, ft, 0:128]
                        kb.I("pe", "matmul", PS[BK[s_]][:, 384:448], al, ST[Rs, ft, :], start=True, stop=False, reads=fk + [f"ST{ft}"], writes=[bkk[s_]])
                        kb.I("pe", "matmul", PS[BK[s_]][:, 384:448], KA[s_][:, 0:128], vT[:, h * 64:(h + 1) * 64], start=False, stop=True, reads=[f"KA{s_}", TMK], writes=[bkk[s_]])
                    for s_, h in enumerate(heads):
                        kb.cp("act", Xq[s_][:, 0, :], PS[BK[s_]][:, 384:448], reads=[bkk[s_]], writes=[f"X{s_}_0"])
                        kb.cp("act", Xb[s_][:, 0, :], PS[BK[s_]][:, 384:448], reads=[bkk[s_]], writes=[f"Xb{s_}_0"])
                    for j in range(7):
                        cu, nx = j % 2, (j + 1) % 2
                        for s_, h in enumerate(heads):
                            Pj = Pq[s_][:, cu, :]
                            Pk = f"P{s_}_{cu}"
                            PTj = PTq[s_][:, cu, :]; PTk = f"PT{s_}_{cu}"
                            if j < 6:
                                kb.I("pe", "matmul", PS[BK[s_]][:, 0:128], PTj, Pj, start=True, stop=True, reads=[PTk, Pk], writes=[bkk[s_]])
                            if j < 5:
                                kb.I("pe", "matmul", PS[BK[s_]][:, 128:256], Pj, PTj, start=True, stop=True, reads=[PTk, Pk], writes=[bkk[s_]])
                            kb.I("pe", "matmul", PS[BK[s_]][:, 448:512], Pj, Xb[s_][:, cu, :], start=True, stop=True, reads=[Pk, f"Xb{s_}_{cu}"], writes=[bkk[s_]])
                        for s_, h in enumerate(heads):
                            if j < 6:
                                kb.cp("act", Pq[s_][:, nx, :], PS[BK[s_]][:, 0:128], reads=[bkk[s_]], writes=[f"P{s_}_{nx}"])
                            if j < 5:
                                kb.cp("act", PTq[s_][:, nx, :], PS[BK[s_]][:, 128:256], reads=[bkk[s_]], writes=[f"PT{s_}_{nx}"])
                            dst_ap = UT[:, h * 64:(h + 1) * 64] if j == 6 else Xq[s_][:, nx, :]
                            dst_k = f"UT{h}" if j == 6 else f"X{s_}_{nx}"
                            if j < 6:
                                kb.I("dve", "tensor_tensor", Xb[s_][:, nx, :], Xq[s_][:, cu, :], PS[BK[s_]][:, 448:512], ALU.subtract if j == 0 else ALU.add,
                                     reads=[bkk[s_], f"X{s_}_{cu}"], writes=[f"Xb{s_}_{nx}"])
                            kb.I("dve", "tensor_tensor", dst_ap, Xq[s_][:, cu, :], PS[BK[s_]][:, 448:512], ALU.subtract if j == 0 else ALU.add,
                                 reads=[bkk[s_], f"X{s_}_{cu}"], writes=[dst_k])
                    if own is not None:
                        for s_, h in enumerate(heads):
                            ft, Rs, fk = hv(h)
                            rho = AR_[Rs, ft, 128:256]
                            yr_ = PS[BK[s_]][:, 256:320]
                            kb.I("pe", "matmul", yr_, rho, ST[Rs, ft, :], start=True, stop=False, reads=fk + [f"ST{ft}"], writes=[bkk[s_]])
                            kb.I("pe", "matmul", yr_, NB[s_][:, 128:256], UT[:, h * 64:(h + 1) * 64], start=False, stop=False, reads=[f"NB{s_}", f"UT{h}"], writes=[bkk[s_]])
                            kb.I("pe", "matmul", yr_, KA[s_][:, 128:256], vT[:, h * 64:(h + 1) * 64], start=False, stop=True, reads=[f"KA{s_}", TMK], writes=[bkk[s_]])
                        for s_, h in enumerate(heads):
                            kb.cp("act", yT[:, h * 64:(h + 1) * 64], PS[BK[s_]][:, 256:320], reads=[bkk[s_]], writes=[f"yT{h}"])
                    for ft in range(g0 // 2, (g0 + NS) // 2):
                        h1 = 2 * ft + 1
                        bnk = BK[h1 - g0]
                        cs_ = slice(ft * 128, (ft + 1) * 128)
                        kb.I("pe", "matmul", PS[bnk][:, 0:128], bepT[:, cs_], UT[:, cs_], start=True, stop=False, reads=[TMK, f"UT{h1 - 1}", f"UT{h1}"], writes=[f"pX{bnk}_0"])
                        kb.I("pe", "matmul", PS[bnk][:, 0:128], kapT[:, cs_], vT[:, cs_], start=False, stop=True, reads=[TMK], writes=[f"pX{bnk}_0"])
                        for jj in range(2):
                            rr_ = slice(64 * jj, 64 * jj + 64)
                            kb.I("dve", "scalar_tensor_tensor", ST[rr_, ft, :], ST[rr_, ft, :], gC_[rr_, ft:ft + 1], PS[bnk][rr_, 64 * jj:64 * jj + 64], ALU.mult, ALU.add,
                                 reads=[f"pX{bnk}_0", f"ST{ft}", f"w{ft}" + SL + "g"], writes=[f"ST{ft}"])
                if dbg_here:
                    kb.D(dbg["d_yT"], yT[:], reads=[f"yT{h}" for h in range(8)], writes=["o_d_yT"])
                    kb.D(dbg["d_ST"], ST[:].rearrange("p f v -> p (f v)"), reads=[f"ST{f}" for f in range(4)], writes=["o_d_ST"])
                if own is not None:
                    for nm, tile_ap, keys in (("yr", yT[:], [f"yT{h}" for h in range(8)]), ("bv", bvT, [TMK])):
                        dst = scr[f"{nm}{sd}"][own * 128:(own + 1) * 128, :]
                        if rev:
                            kb.I("pe", "matmul", PS[5][:], antiI[:], tile_ap, start=True, stop=True, reads=keys + ["antiI"], writes=["pX5_0"])
                            kb.cp("act", flpR[:], PS[5][:], reads=["pX5_0"], writes=["flpR"])
                            kb.D(dst, flpR[:], reads=["flpR"], writes=[f"scr_{nm}{sd}_{own}"])
                        else:
                            kb.D(dst, tile_ap, reads=keys, writes=[f"scr_{nm}{sd}_{own}"])
                kb.rec = streams["s5"]
                t1_, t2_ = s5t["t1"], s5t["t2"]
                for a_ in range(4):
                    zuk = f"zu{a_}" + SL
                    for r_ in range(4):
                        q_ = a_ * 4 + r_
                        kb.I("pe", "matmul", PS[6][:, r_ * 128:(r_ + 1) * 128], Bw_re[:, q_, :], zub_[:, a_, :], start=True, stop=True, reads=["s5tab", zuk], writes=["pX6_0"])
                        kb.I("pe", "matmul", PS[7][:, r_ * 128:(r_ + 1) * 128], Bw_im[:, q_, :], zub_[:, a_, :], start=True, stop=True, reads=["s5tab", zuk], writes=["pX7_0"])
                    k3 = ["pX6_0"]; k4 = ["pX7_0"]
                    qs = slice(a_ * 4, a_ * 4 + 4)
                    Er = E_re[:, qs, :].rearrange("p q t -> p (q t)"); Ei = E_im[:, qs, :].rearrange("p q t -> p (q t)")
                    Fr = F_re[:, qs, :].rearrange("p q t -> p (q t)"); Fi = F_im[:, qs, :].rearrange("p q t -> p (q t)")
                    g1_, g2_ = s5t["Gre"], s5t["Gim"]
                    kb.I("dve", "tensor_tensor", t1_[:], PS[6][:], Er, ALU.mult, reads=k3 + ["s5tab"], writes=["s5t1"])
                    kb.I("dve", "tensor_tensor", t2_[:], PS[7][:], Ei, ALU.mult, reads=k4 + ["s5tab"], writes=["s5t2"])
                    kb.I("dve", "tensor_tensor", g1_[:], PS[7][:], Er, ALU.mult, reads=k4 + ["s5tab"], writes=["s5Gre"])
                    kb.I("dve", "tensor_tensor", g2_[:], PS[6][:], Ei, ALU.mult, reads=k3 + ["s5tab"], writes=["s5Gim"])
                    kb.I("pool", "tensor_tensor", s5t["Xre"][:], t1_[:], t2_[:], ALU.subtract, reads=["s5t1", "s5t2"], writes=["s5Xre"])
                    kb.I("dve", "tensor_tensor", s5t["Xim"][:], g1_[:], g2_[:], ALU.add, reads=["s5Gre", "s5Gim"], writes=["s5Xim"])
                    if own is None:
                        kb.I("dve", "tensor_reduce", gs_re[:, qs], s5t["Xre"][:].rearrange("p (q t) -> p q t", q=4), AXX, ALU.add, reads=["s5Xre"], writes=["gs"])
                        kb.I("dve", "tensor_reduce", gs_im[:, qs], s5t["Xim"][:].rearrange("p (q t) -> p q t", q=4), AXX, ALU.add, reads=["s5Xim"], writes=["gs"])
                        continue
                    for r_ in range(4):
                        q_ = a_ * 4 + r_
                        cs_ = slice(r_ * 128, (r_ + 1) * 128)
                        kb.I("dve", "tensor_tensor_scan", s5t["Gre"][:, cs_], ones[:], s5t["Xre"][:, cs_], car_re[:, q_:q_ + 1], ALU.mult, ALU.add,
                             reads=["s5Xre", "car", "ones"], writes=["s5Gre"])
                        kb.I("dve", "tensor_tensor_scan", s5t["Gim"][:, cs_], ones[:], s5t["Xim"][:, cs_], car_im[:, q_:q_ + 1], ALU.mult, ALU.add,
                             reads=["s5Xim", "car", "ones"], writes=["s5Gim"])
                    x1_, x2_ = s5t["Xre"], s5t["Xim"]
                    kb.I("dve", "tensor_tensor", t1_[:], s5t["Gre"][:], Fr, ALU.mult, reads=["s5Gre", "s5tab"], writes=["s5t1"])
                    kb.I("dve", "tensor_tensor", t2_[:], s5t["Gim"][:], Fi, ALU.mult, reads=["s5Gim", "s5tab"], writes=["s5t2"])
                    kb.I("pool", "tensor_tensor", x1_[:], s5t["Gim"][:], Fr, ALU.mult, reads=["s5Gim", "s5tab", "s5Xre"], writes=["s5Xre"])
                    kb.I("pool", "tensor_tensor", x2_[:], s5t["Gre"][:], Fi, ALU.mult, reads=["s5Gre", "s5tab", "s5Xim"], writes=["s5Xim"])
                    kb.I("dve", "tensor_tensor", s5t["Hre"][:], t1_[:], t2_[:], ALU.subtract, reads=["s5t1", "s5t2"], writes=["s5Hre"])
                    kb.I("pool", "tensor_tensor", s5t["Him"][:], x1_[:], x2_[:], ALU.add, reads=["s5Xre", "s5Xim"], writes=["s5Him"])
                    kb.I("dve", "tensor_copy", car_re[:, qs], s5t["Hre"][:].rearrange("p (q t) -> p q t", q=4)[:, :, 127], reads=["s5Hre"], writes=["car"])
                    kb.I("dve", "tensor_copy", car_im[:, qs], s5t["Him"][:].rearrange("p (q t) -> p q t", q=4)[:, :, 127], reads=["s5Him"], writes=["car"])
                    if own is not None:
                        for r_ in range(4):
                            q_ = a_ * 4 + r_
                            cs_ = slice(r_ * 128, (r_ + 1) * 128)
                            yb = PS[6][:, r_ * 32:(r_ + 1) * 32]
                            if sd == 0:
                                kb.I("pe", "matmul", yb, zub_[:, a_, :], diagd[:, a_, r_ * 32:(r_ + 1) * 32], start=True, stop=False, reads=[zuk, "diagd"], writes=["pX6_0"])
                            kb.I("pe", "matmul", yb, s5t["Hre"][:, cs_], Cw_re[:, q_, :], start=(sd != 0), stop=False, reads=["s5Hre", "s5tab"], writes=["pX6_0"])
                            kb.I("pe", "matmul", yb, s5t["Him"][:, cs_], Cw_imn[:, q_, :], start=False, stop=True, reads=["s5Him", "s5tab"], writes=["pX6_0"])
                        kb.cp("act", flpS[:, a_ * 128:(a_ + 1) * 128], PS[6][:, 0:128], reads=["pX6_0"], writes=["flpS"])
                if own is None:
                    F127r = F_re[:, :, 127]; F127i = F_im[:, :, 127]
                    kb.I("dve", "tensor_tensor", gs_re[:], gs_re[:], car_re[:], ALU.add, reads=["gs", "car"], writes=["gs"])
                    kb.I("dve", "tensor_tensor", gs_im[:], gs_im[:], car_im[:], ALU.add, reads=["gs", "car"], writes=["gs"])
                    kb.I("dve", "tensor_tensor", gta[:], gs_re[:], F127r, ALU.mult, reads=["gs", "s5tab"], writes=["gta"])
                    kb.I("dve", "tensor_tensor", gtb[:], gs_im[:], F127i, ALU.mult, reads=["gs", "s5tab"], writes=["gtb"])
                    kb.I("dve", "tensor_tensor", car_re[:], gta[:], gtb[:], ALU.subtract, reads=["gta", "gtb"], writes=["car"])
                    kb.I("dve", "tensor_tensor", gta[:], gs_im[:], F127r, ALU.mult, reads=["gs", "s5tab", "car"], writes=["gta"])
                    kb.I("dve", "tensor_tensor", gtb[:], gs_re[:], F127i, ALU.mult, reads=["gs", "s5tab", "car"], writes=["gtb"])
                    kb.I("dve", "tensor_tensor", car_im[:], gta[:], gtb[:], ALU.add, reads=["gta", "gtb"], writes=["car"])
                if dbg_here:
                    kb.D(dbg["d_car"][:, 0:16], car_re[:], reads=["car"], writes=["o_d_car"])
                    kb.D(dbg["d_car"][:, 16:32], car_im[:], reads=["car"], writes=["o_d_car"])
                if own is not None:
                    if dbg_here:
                        kb.D(dbg["d_ys"], flpS[:], reads=["flpS"], writes=["o_d_ys"])
                    dst = scr[f"ys{sd}"][own * 128:(own + 1) * 128, :]
                    if rev:
                        kb.I("pe", "matmul", PS[6][:], antiI[:], flpS[:], start=True, stop=True, reads=["flpS", "antiI"], writes=["pX6_0"])
                        kb.cp("dve", flpS[:], PS[6][:], reads=["pX6_0"], writes=["flpS"])
                    kb.D(dst, flpS[:], reads=["flpS"], writes=[f"scr_ys{sd}_{own}"])
                kb.rec = None
                return streams

            for sd in range(2 if not upto.startswith("p0") else 0):
                if upto == "setup":
                    break
                if sd == 1:
                    s5_setup(sd, F0)
                if upto == "s5setup":
                    break
                stk = [f"ST{f}" for f in range(4)]
                if sd == 0:
                    kb.I("pool", "memset", ST[:], 0.0, reads=stk, writes=stk)
                    kb.I("pool", "memset", car_re[:], 0.0, reads=["car"], writes=["car"])
                    kb.I("pool", "memset", car_im[:], 0.0, reads=["car"], writes=["car"])
                if sd == 0:
                    chunks = [(True, c, None) for c in range(n_ctx)] + [(False, c, c) for c in range(n_lat_chunks[0])]
                else:
                    stk = [f"ST{f}" for f in range(4)]
                    kb.D(scr["cc_in"][:, 0:256], ST[:].rearrange("p f v -> p (f v)"), reads=stk, writes=["cc_in"])
                    kb.D(scr["cc_in"][:, 256:272], car_re[:], reads=["car"], writes=["cc_in"])
                    kb.D(scr["cc_in"][:, 272:288], car_im[:], reads=["car"], writes=["cc_in"])
                    kb.I("pool", "collective_compute", "AllReduce", ALU.add, replica_groups=[[0, 1], [2, 3], [4, 5], [6, 7]],
                         ins=[scr["cc_in"]], outs=[scr["cc_out"]], reads=["cc_in"], writes=["cc_out"])
                    ccs = s5t["t1"]
                    kb.D(ccs[:, 0:288], scr["cc_out"], reads=["cc_out", "s5t1"], writes=["s5t1"])
                    kb.I("dve", "tensor_tensor", ST[:].rearrange("p f v -> p (f v)"), ccs[:, 0:256], ST[:].rearrange("p f v -> p (f v)"), ALU.subtract,
                         reads=["s5t1"] + stk, writes=stk)
                    kb.I("dve", "tensor_tensor", car_re[:], ccs[:, 256:272], car_re[:], ALU.subtract, reads=["s5t1", "car"], writes=["car"])
                    kb.I("dve", "tensor_tensor", car_im[:], ccs[:, 272:288], car_im[:], ALU.subtract, reads=["s5t1", "car"], writes=["car"])
                    chunks = [(False, c, 31 - c) for c in range(16, 16 + n_lat_chunks[1])]
                prev = None
                pend = []
                for k_, (is_ctx, c, own) in enumerate(chunks):
                    nxt = chunks[k_ + 1][:2] if k_ + 1 < len(chunks) else None
                    cur = process_chunk(sd, is_ctx, c, own, k_ % 2, nxt=(nxt if sd == 0 else None), first=(k_ == 0), reuse=(sd == 1))
                    pend.append(cur["front"])
                    if prev is not None:
                        pend += [prev["rwkv"], prev["s5"]]
                    prev = cur
                    if len(pend) >= SCHED_WINDOW_STREAMS:
                        kb.merge(*pend)
                        pend = []
                if prev is not None:
                    pend += [prev["rwkv"], prev["s5"]]
                kb.merge(*pend)
            kb.I("dve", "tensor_copy", ssum[:, 0:1], ssum[:, 0:1], reads=list(kb.lastw.keys()), writes=list(kb.lastw.keys()) + ["fence1"])
        FENCE = ["fence1"]
        wscope.close()

        if do_tail:
            with contextlib.ExitStack() as tl:
                wob = sb(tl, "wob", [128, 8, D], BF16)
                glb = sb(tl, "glb", [128, 4, 512], BF16); g2b = sb(tl, "g2b", [128, 512])
                stg = [sb(tl, f"stg{i}", [128, 1024]) for i in range(2)]
                jobs = [("w_out", wob, kt, 0, D) for kt in range(8)] + [("gluw", glb, kt, 0, 512) for kt in range(4)]
                for i, (nm, dst, kt, c0, ncol) in enumerate(jobs):
                    s_ = stg[i % 2]; sk = f"stg{i % 2}"
                    kb.D(s_[:, 0:ncol], din[nm][kt * 128:(kt + 1) * 128, c0:c0 + ncol], reads=FENCE, writes=[sk])
                    kb.cp(("dve", "act", "pool")[i % 3], dst[:, kt, c0:c0 + ncol], s_[:, 0:ncol], reads=[sk], writes=["tw"])
                kb.D(g2b[0:96, :], din["g2"], reads=FENCE, writes=["tw2"])
                rows = {}
                for nm in ("lnw", "lnb", "glub"):
                    rows[nm] = sb(tl, "row_" + nm, [128, IN_SHAPES[nm][1]])
                    kb.D(rows[nm][:], din[nm].partition_broadcast(128), reads=FENCE, writes=["rows"])
                gmix = sb(tl, "gmix", [128, D])

                def dbl(name, shape, dt=F32):
                    return [sb(tl, f"{name}_{i_}", shape, dt) for i_ in range(2)]
                x1 = dbl("x1", [128, D])
                kb.D(gmix[:], scr["mod"][:, 2 * D:3 * D], reads=FENCE + ["scr_mod"], writes=["modt"])
                tx = dbl("tx", [128, D]); yr_a = dbl("yr_a", [128, 512]); ys_a = dbl("ys_a", [128, 512]); bv_a = dbl("bv_a", [128, 512])
                yr_b = dbl("yr_b", [128, 512]); ys_b = dbl("ys_b", [128, 512]); bv_b = dbl("bv_b", [128, 512])
                sgt = dbl("sgt", [128, 128]); mix = dbl("mix", [128, D]); st8 = dbl("st8", [128, 8]); st8b = dbl("st8b", [128, 8])
                gt = dbl("gt", [128, 512]); gt2 = dbl("gt2", [128, 512]); zT = dbl("zT", [128, 4, 128], BF16); mixT = dbl("mixT", [128, 8, 128], BF16)
                TA = (tx, yr_a, ys_a, bv_a, yr_b, ys_b, bv_b, sgt, mix, st8, st8b, gt, gt2, zT, mixT, x1)
                recA = []
                kb.rec = recA
                for oc in range(NOWN):
                    sl_ = oc % 2
                    kb.ksuf = f"#{sl_}"
                    (tx, yr_a, ys_a, bv_a, yr_b, ys_b, bv_b, sgt, mix, st8, st8b, gt, gt2, zT, mixT, x1) = (t_[sl_] for t_ in TA)
                    rsl = slice(oc * 128, (oc + 1) * 128)
                    kb.D(tx[:], din["x"][rsl, :], reads=FENCE, writes=["tx"])
                    for t_, nm in ((yr_a, "yr0"), (yr_b, "yr1"), (ys_a, "ys0"), (ys_b, "ys1"), (bv_a, "bv0"), (bv_b, "bv1")):
                        kb.D(t_[:], scr[nm][rsl, :], reads=[f"scr_{nm}_{oc}"] + FENCE, writes=["t_" + nm])
                    kb.D(sgt[0:96, :], scr["sg"][oc * 128:oc * 128 + 96, :], reads=[f"scr_sg{oc}"] + FENCE, writes=["sgt"])
                    kb.I("dve", "tensor_tensor", yr_a[:], yr_a[:], yr_b[:], ALU.add, reads=["t_yr0", "t_yr1"], writes=["t_yr0"])
                    y3 = yr_a[:].rearrange("p (h n) -> p h n", n=64)
                    kb.I("dve", "tensor_reduce", st8[:], y3, AXX, ALU.add, reads=["t_yr0"], writes=["st8"])
                    kb.ts1("dve", st8[:], st8[:], 1.0 / 64, ALU.mult, reads=["st8"], writes=["st8"])
                    kb.I("dve", "tensor_tensor", y3, y3, st8[:].unsqueeze(2).to_broadcast([128, 8, 64]), ALU.subtract, reads=["st8", "t_yr0"], writes=["t_yr0"])
                    kb.I("pool", "tensor_tensor", gt2[:], yr_a[:], yr_a[:], ALU.mult, reads=["t_yr0"], writes=["gt2"])
                    kb.I("dve", "tensor_reduce", st8b[:], gt2[:].rearrange("p (h n) -> p h n", n=64), AXX, ALU.add, reads=["gt2"], writes=["st8b"])
                    kb.I("dve", "tensor_scalar", st8b[:], st8b[:], 1.0 / 64, 64e-5, ALU.mult, ALU.add, reads=["st8b"], writes=["st8b"])
                    kb.I("act", "activation", st8b[:], st8b[:], AF.Sqrt, reads=["st8b"], writes=["st8b"])
                    kb.I("dve", "reciprocal", st8b[:], st8b[:], reads=["st8b"], writes=["st8b"])
                    kb.I("dve", "tensor_tensor", y3, y3, st8b[:].unsqueeze(2).to_broadcast([128, 8, 64]), ALU.mult, reads=["st8b", "t_yr0"], writes=["t_yr0"])
                    kb.I("pool", "tensor_tensor", yr_a[:], yr_a[:], rows["lnw"][:], ALU.mult, reads=["rows", "t_yr0"], writes=["t_yr0"])
                    kb.I("pool", "tensor_tensor", yr_a[:], yr_a[:], rows["lnb"][:], ALU.add, reads=["rows", "t_yr0"], writes=["t_yr0"])
                    kb.I("dve", "tensor_tensor", bv_a[:], bv_a[:], bv_b[:], ALU.add, reads=["t_bv0", "t_bv1"], writes=["t_bv0"])
                    kb.I("dve", "tensor_tensor", yr_a[:], yr_a[:], bv_a[:], ALU.add, reads=["t_bv0", "t_yr0"], writes=["t_yr0"])
                    kb.I("pe", "matmul", PS[0][:], sgt[0:96, :], g2b[0:96, :], start=True, stop=True, reads=["sgt", "tw2"], writes=bk(0))
                    kb.I("dve", "tensor_tensor", mix[:, 0:512], yr_a[:], PS[0][:], ALU.mult, reads=bk(0) + ["t_yr0"], writes=["mixA"])
                    kb.I("dve", "tensor_tensor", ys_a[:], ys_a[:], ys_b[:], ALU.add, reads=["t_ys0", "t_ys1"], writes=["t_ys0"])
                    kb.I("pool", "tensor_tensor", gt[:], ys_a[:], ys_a[:], ALU.mult, reads=["t_ys0"], writes=["gt"])
                    kb.I("dve", "tensor_scalar", gt[:], gt[:], 0.044715, 1.0, ALU.mult, ALU.add, reads=["gt"], writes=["gt"])
                    kb.I("dve", "tensor_tensor", gt[:], gt[:], ys_a[:], ALU.mult, reads=["gt", "t_ys0"], writes=["gt"])
                    kb.I("act", "activation", gt[:], gt[:], AF.Tanh, scale=0.7978845608028654, reads=["gt"], writes=["gt"])
                    kb.I("dve", "tensor_scalar", gt[:], gt[:], 0.5, 0.5, ALU.mult, ALU.add, reads=["gt"], writes=["gt"])
                    kb.I("dve", "tensor_tensor", ys_a[:], ys_a[:], gt[:], ALU.mult, reads=["gt", "t_ys0"], writes=["t_ys0"])
                    for j in range(4):
                        kb.I("pe", "transpose", PS[1][:, j * 128:(j + 1) * 128], ys_a[:, j * 128:(j + 1) * 128], ident[:], reads=["t_ys0", "ident"], writes=[f"pX1_{j}"])
                    kb.cp("act", zT[:].rearrange("p j t -> p (j t)"), PS[1][:], reads=bk(1), writes=["zT"])
                    for j in range(4):
                        kb.I("pe", "matmul", PS[2][:], zT[:, j, :], glb[:, j, :], start=(j == 0), stop=(j == 3), reads=["zT", "tw"], writes=bk(2))
                    kb.I("dve", "tensor_tensor", gt[:], PS[2][:], rows["glub"][:], ALU.add, reads=bk(2) + ["rows", "gt"], writes=["gt"])
                    kb.I("act", "activation", gt[:], gt[:], AF.Sigmoid, reads=["gt"], writes=["gt"])
                    kb.I("dve", "tensor_tensor", mix[:, 512:1024], ys_a[:], gt[:], ALU.mult, reads=["gt", "t_ys0"], writes=["mixB"])
                    for half in range(2):
                        for j in range(4):
                            kt = half * 4 + j
                            kb.I("pe", "transpose", PS[3][:, j * 128:(j + 1) * 128], mix[:, kt * 128:(kt + 1) * 128], ident[:], reads=["mixA", "mixB", "ident"], writes=[f"pX3_{j}"])
                        kb.cp("act" if half else "dve", mixT[:, half * 4:half * 4 + 4, :].rearrange("p j t -> p (j t)"), PS[3][:], reads=bk(3), writes=["mixT"])
                    for nh in range(2):
                        ns = slice(nh * 512, (nh + 1) * 512)
                        for kt in range(8):
                            kb.I("pe", "matmul", PS[4 + nh][:], mixT[:, kt, :], wob[:, kt, ns], start=(kt == 0), stop=(kt == 7), reads=["mixT", "tw"], writes=bk(4 + nh))
                        kb.I("dve", "tensor_tensor", x1[:, ns], PS[4 + nh][:], gmix[:, ns], ALU.mult, reads=bk(4 + nh) + ["modt"], writes=["x1"])
                        kb.I("pool", "tensor_tensor", x1[:, ns], x1[:, ns], tx[:, ns], ALU.add, reads=["x1", "tx"], writes=["x1"])
                    if debug and oc == 0:
                        kb.D(dbg["d_x1"], x1[:], reads=["x1"], writes=["o_d_x1"])
                    kb.D(scr["x1"][rsl, :], x1[:], reads=["x1"], writes=[f"scr_x1_{oc}"])
                kb.rec = None
                kb.ksuf = None
                kb.merge(recA)
                st8 = TA[9][0]
                kb.I("dve", "tensor_copy", st8[:, 0:1], st8[:, 0:1], reads=list(kb.lastw.keys()), writes=list(kb.lastw.keys()) + ["fence2"])
            FENCE = ["fence2"]
            with contextlib.ExitStack() as tl:
                w1b = sb(tl, "w1b", [128, 8, DFF], BF16); w3b = sb(tl, "w3b", [128, 8, DFF], BF16)
                w2b = sb(tl, "w2b", [128, 22, D], BF16)
                stg = [sb(tl, f"stgb{i}", [128, 1024]) for i in range(2)]
                jobs = []
                for nm, dst in (("w1", w1b), ("w3", w3b)):
                    for kt in range(8):
                        for c0 in (0, 1024, 2048):
                            jobs.append((nm, dst, kt, c0, min(1024, DFF - c0)))
                jobs += [("w2f", w2b, kt, 0, D) for kt in range(22)]
                for i, (nm, dst, kt, c0, ncol) in enumerate(jobs):
                    s_ = stg[i % 2]; sk = f"stgb{i % 2}"
                    kb.D(s_[:, 0:ncol], din[nm][kt * 128:(kt + 1) * 128, c0:c0 + ncol], reads=FENCE, writes=[sk])
                    kb.cp(("dve", "act", "pool")[i % 3], dst[:, kt, c0:c0 + ncol], s_[:, 0:ncol], reads=[sk], writes=["tw"])
                rows = {"gf": sb(tl, "row_gf", [128, D])}
                kb.D(rows["gf"][:], din["gf"].partition_broadcast(128), reads=FENCE, writes=["rows"])
                A2 = sb(tl, "A2", [128, D]); sffn = sb(tl, "sffn", [128, D]); gffn = sb(tl, "gffn", [128, D])
                def dblb(name, shape, dt=F32):
                    return [sb(tl, f"{name}_{i_}", shape, dt) for i_ in range(2)]
                hh2 = dblb("hh", [128, D]); x12 = dblb("x1b", [128, D]); outt2 = dblb("outt", [128, D])
                hh = hh2[0]
                kb.D(A2[:], din["g2n"].partition_broadcast(128), reads=FENCE, writes=["A2"])
                kb.D(hh[:], scr["mod"][:, 4 * D:5 * D], reads=FENCE + ["scr_mod"], writes=["hh#0"])
                kb.D(sffn[:], scr["mod"][:, 3 * D:4 * D], reads=FENCE + ["scr_mod"], writes=["modt"])
                kb.D(gffn[:], scr["mod"][:, 5 * D:6 * D], reads=FENCE + ["scr_mod"], writes=["modt"])
                kb.I("dve", "scalar_tensor_tensor", A2[:], hh[:], 1.0, A2[:], ALU.add, ALU.mult, reads=["A2", "hh#0"], writes=["A2"])
                hhT2 = dblb("hhT", [128, 8, 128], BF16)
                actT2 = dblb("actT", [128, 22, 128], BF16); s12 = dblb("s1", [128, 256]); ss22 = dblb("ss2", [128, 2]); rs22 = dblb("rs2", [128, 2])
                print("SBUF bytes remaining in tail B scope:", nc.sbuf_bytes_remaining)
                recB = []
                kb.rec = recB
                for oc in range(NOWN):
                    sl_ = oc % 2
                    kb.ksuf = f"#{sl_}"
                    hh, x1, outt, hhT, actT, s1, ss2, rs2 = (t_[sl_] for t_ in (hh2, x12, outt2, hhT2, actT2, s12, ss22, rs22))
                    rsl = slice(oc * 128, (oc + 1) * 128)
                    kb.D(x1[:], scr["x1"][rsl, :], reads=[f"scr_x1_{oc}"], writes=["x1"])
                    kb.I("pool", "memset", ss2[:], 0.0, reads=FENCE, writes=["ss2"])
                    kb.I("act", "activation", hh[:], x1[:], AF.Square, accum_out=ss2[:, 0:1], reads=["x1", "A2"], writes=["hh", "ss2"])
                    kb.I("dve", "tensor_scalar", rs2[:, 0:1], ss2[:, 0:1], 1.0 / D, 1e-6, ALU.mult, ALU.add, reads=["ss2"], writes=["rs2"])
                    kb.I("act", "activation", rs2[:, 0:1], rs2[:, 0:1], AF.Sqrt, reads=["rs2"], writes=["rs2"])
                    kb.I("dve", "reciprocal", rs2[:, 0:1], rs2[:, 0:1], reads=["rs2"], writes=["rs2"])
                    kb.I("dve", "scalar_tensor_tensor", hh[:], x1[:], rs2[:, 0:1], A2[:], ALU.mult, ALU.mult, reads=["x1", "rs2", "A2", "hh"], writes=["hh"])
                    kb.I("pool", "tensor_tensor", hh[:], hh[:], sffn[:], ALU.add, reads=["hh", "modt"], writes=["hh"])
                    for half in range(2):
                        for j in range(4):
                            kt = half * 4 + j
                            kb.I("pe", "transpose", PS[3][:, j * 128:(j + 1) * 128], hh[:, kt * 128:(kt + 1) * 128], ident[:], reads=["hh", "ident"], writes=[f"pX3_{j}"])
                        kb.cp("act" if half else "dve", hhT[:, half * 4:half * 4 + 4, :].rearrange("p j t -> p (j t)"), PS[3][:], reads=bk(3), writes=["hhT"])
                    for ftf in range(22):
                        sl = ftf % 2
                        fs = slice(ftf * 128, (ftf + 1) * 128)
                        bq = (0, 1, 2)[ftf % 3]
                        pa = PS[bq][:, 0:128]; pb = PS[bq][:, 128:256]
                        s1_ = s1[:, (ftf % 2) * 128:(ftf % 2) * 128 + 128]
                        for kt in range(8):
                            kb.I("pe", "matmul", pa, w1b[:, kt, fs], hhT[:, kt, :], start=(kt == 0), stop=(kt == 7), reads=["hhT", "tw"], writes=[f"pX{bq}_0"])
                        for kt in range(8):
                            kb.I("pe", "matmul", pb, w3b[:, kt, fs], hhT[:, kt, :], start=(kt == 0), stop=(kt == 7), reads=["hhT", "tw"], writes=[f"pX{bq}_0"])
                        kb.I("act", "activation", s1_, pa, AF.Silu, reads=[f"pX{bq}_0"], writes=[f"s1_{ftf % 2}"])
                        kb.I("dve", "tensor_tensor", actT[:, ftf, :], s1_, pb, ALU.mult, reads=[f"pX{bq}_0", f"s1_{ftf % 2}"], writes=[f"actT{ftf}"])
                    for nh in range(2):
                        ns = slice(nh * 512, (nh + 1) * 512)
                        bd = 4 + 2 * sl_ + nh
                        for ftf in range(22):
                            kb.I("pe", "matmul", PS[bd][:], actT[:, ftf, :], w2b[:, ftf, ns], start=(ftf == 0), stop=(ftf == 21), reads=[f"actT{ftf}", "tw"], writes=bk(bd))
                        kb.I("dve", "tensor_tensor", outt[:, ns], PS[bd][:], gffn[:, ns], ALU.mult, reads=bk(bd) + ["modt"], writes=["outt"])
                        kb.I("pool", "tensor_tensor", outt[:, ns], outt[:, ns], x1[:, ns], ALU.add, reads=["outt", "x1"], writes=["outt"])
                    kb.I("act", "activation", hh[:], outt[:], AF.Square, accum_out=ss2[:, 1:2], reads=["outt", "hh"], writes=["hh", "ss2"])
                    kb.I("dve", "tensor_scalar", rs2[:, 1:2], ss2[:, 1:2], 1.0 / D, 1e-6, ALU.mult, ALU.add, reads=["ss2"], writes=["rs2"])
                    kb.I("act", "activation", rs2[:, 1:2], rs2[:, 1:2], AF.Sqrt, reads=["rs2"], writes=["rs2"])
                    kb.I("dve", "reciprocal", rs2[:, 1:2], rs2[:, 1:2], reads=["rs2"], writes=["rs2"])
                    kb.I("dve", "scalar_tensor_tensor", outt[:], outt[:], rs2[:, 1:2], rows["gf"][:], ALU.mult, ALU.mult, reads=["outt", "rs2", "rows"], writes=["outt"])
                    kb.D(out_d[rsl, :], outt[:], reads=["outt"], writes=[f"o_out{oc}"])
                kb.rec = None
                kb.ksuf = None
                kb.merge(recB)
                kb.emit(final_keys=[k for k in kb.lastw if k.startswith("o_")])
        else:
            kb.emit(final_keys=[k for k in kb.lastw if k.startswith("o_")] + FENCE)
    return nc


def make_in_maps(inp):
    f = np.float32
    ident = np.eye(128, dtype=f)
    antiI = np.ascontiguousarray(ident[::-1])
    strict = np.triu(np.ones((128, 128), f), 1)
    incl = np.triu(np.ones((128, 128), f), 0)
    consts = {
        "ident": ident, "antiI": antiI,
        "mask_b": np.concatenate([strict, -incl], axis=1), "mask_k": np.concatenate([strict, incl], axis=1),
        "maskT": np.ascontiguousarray(strict.T), "bones": np.kron(np.eye(2, dtype=f), np.ones((64, 64), f)),
    }
    maps = []
    for core in range(8):
        b, hf = core // 2, core % 2
        dsel = [1, 0] if hf else [0, 1]
        x = inp["x"][b]; ctx = inp["ctx"][b]
        conv = inp["rwkv_conv"][0]
        w_in = inp["w_in"][0]
        if hf:
            x = x[::-1]; ctx = ctx[::-1]; conv = conv[::-1, ::-1]
            perm = np.arange(2272)
            perm[1536:1568], perm[1568:1600] = np.arange(1568, 1600), np.arange(1536, 1568)
            perm[1600:1632], perm[1632:1664] = np.arange(1632, 1664), np.arange(1600, 1632)
            w_in = w_in[:, perm]
        m = {
            "x": x, "ctx": ctx, "cc": np.stack([inp["c"][b], inp["c_ctx"]]),
            "mod_w": inp["mod_w"][0], "mod_b": inp["mod_b"][0][None], "g1": inp["norm1_g"][0][None],
            "g2n": inp["norm2_g"][0][None], "gf": inp["final_g"][None], "w_in": w_in, "w_out": inp["w_out"][0],
            "conv": conv.reshape(9, 1536), "w0": inp["rwkv_w0"][0][dsel], "w2": inp["rwkv_w2"][0][dsel],
            "a0": inp["rwkv_a0"][0][dsel], "a2": inp["rwkv_a2"][0][dsel], "g2": inp["rwkv_g2"][0],
            "kkv": inp["rwkv_kk"][0], "kav": inp["rwkv_ka"][0], "rkv": inp["rwkv_rk"][0].reshape(512),
            "lnw": inp["rwkv_ln_w"][0][None], "lnb": inp["rwkv_ln_b"][0][None],
            "lam_re": inp["s5_lam_re"][0][dsel], "lam_im": inp["s5_lam_im"][0][dsel], "lstep": inp["s5_log_step"][0][dsel],
            "b_re": inp["s5_b_re"][0], "b_im": inp["s5_b_im"][0], "c_re": inp["s5_c_re"][0], "c_im": inp["s5_c_im"][0],
            "s5d": inp["s5_d"][0], "gluw": inp["s5_glu_w"][0], "glub": inp["s5_glu_b"][0][None],
            "w1": inp["ffn_w1"][0], "w3": inp["ffn_w3"][0], "w2f": inp["ffn_w2"][0],
        }
        m.update(consts)
        maps.append({k: np.ascontiguousarray(np.asarray(v, dtype=f)).reshape(IN_SHAPES[k]) for k, v in m.items()})
    return maps


def kernel(**inputs):
    inp = {k: np.asarray(v) for k, v in inputs.items()}
    nc = build_nc()
    maps = make_in_maps(inp)
    res = run_bass_kernel_spmd(nc, maps, core_ids=list(range(8)))
    out = np.zeros((4, T_LAT, D), np.float32)
    for core in range(8):
        b, hf = core // 2, core % 2
        o = np.asarray(res.results[core]["out"], dtype=np.float32)
        if hf:
            out[b, OWN:] = o[::-1]
        else:
            out[b, :OWN] = o
    return out
```

```python
import contextlib
import numpy as np
import concourse.bass as bass
import concourse.mybir as mybir
from concourse.bass_utils import run_bass_kernel_spmd

F32 = mybir.dt.float32
BF16 = mybir.dt.bfloat16
ALU = mybir.AluOpType
AF = mybir.ActivationFunctionType
AXX = mybir.AxisListType.X

SEM_CAP = 16000
N_DMA_SEM = 24
SCHED_WINDOW_STREAMS = 1
SAME_ENG_WAIT = True

T_LAT, T_CTX, D, DFF = 4096, 256, 1024, 2816
OWN = 2048
NOWN = OWN // 128
PI = float(np.pi)
KDEC = 0.6065306597126334


class KB:
    ENGS = ("pe", "dve", "act", "pool", "sp")

    def __init__(self, nc):
        self.nc = nc
        self.ops = {e: [] for e in self.ENGS}
        self.lastw = {}
        self.readers = {}
        self.ndma = 0
        self.rr = 0

    @staticmethod
    def _norm(reads, writes):
        r2 = [k for k in reads if not k.startswith("pX")]
        w2 = [k for k in writes if not k.startswith("pX")]
        banks = {"bank" + k[2:].split("_")[0] for k in list(reads) + list(writes) if k.startswith("pX")}
        return r2, w2 + sorted(banks)

    def _deps(self, me, reads, writes):
        reads, writes = self._norm(reads, writes)
        deps = set()
        for k in reads:
            w = self.lastw.get(k)
            if w is not None:
                deps.add(w)
        for k in writes:
            w = self.lastw.get(k)
            if w is not None:
                deps.add(w)
            for r in self.readers.get(k, ()):
                deps.add(r)
        deps.discard(me)
        for k in reads:
            self.readers.setdefault(k, []).append(me)
        for k in writes:
            self.lastw[k] = me
            self.readers[k] = []
        return deps

    mute = False
    rec = None
    ksuf = None
    KGLOBAL = ("pX", "scr_", "o_", "fence", "rows", "tw", "modt", "ident", "A2")

    def _sfx(self, keys):
        if not self.ksuf:
            return list(keys)
        return [k if k.startswith(self.KGLOBAL) else k + self.ksuf for k in keys]

    @staticmethod
    def _est(op):
        def nfree(ap):
            n = 1
            for d in ap.shape[1:]:
                n *= d
            return n
        if op[0] == "D":
            ap = op[1]
            nbytes = nfree(ap) * ap.shape[0] * (2 if ap.dtype == BF16 else 4)
            return "sp", 0.08, 2.2 + nbytes / 1.0e5
        eng, meth, args = op[1], op[2], op[3]
        n = nfree(args[0])
        if eng == "pe":
            f32 = args[1].dtype == F32
            d = 0.09 + n * (0.0017 if f32 else 0.00045)
        elif eng == "dve":
            d = 0.25 + n * 0.00104 * (6.0 if meth == "reciprocal" else 1.0)
        elif eng == "act":
            d = 0.2 + n * 0.00104
        else:
            d = 0.45 + n * 0.0026
        return eng, d, d

    def merge(self, *streams):
        ops = [op for st_ in streams for op in st_]
        n = len(ops)
        lastw, readers = {}, {}
        preds = [set() for _ in range(n)]
        for i, op in enumerate(ops):
            r_, w_ = (op[5], op[6]) if op[0] == "I" else (op[4], op[5])
            r_, w_ = self._norm(r_, w_)
            for k in r_:
                if k in lastw:
                    preds[i].add(lastw[k])
            for k in w_:
                if k in lastw:
                    preds[i].add(lastw[k])
                preds[i].update(readers.get(k, ()))
            preds[i].discard(i)
            for k in r_:
                readers.setdefault(k, []).append(i)
            for k in w_:
                lastw[k] = i
                readers[k] = []
        succs = [[] for _ in range(n)]
        for i in range(n):
            for p in preds[i]:
                succs[p].append(i)
        est = [self._est(op) for op in ops]
        cp = [0.0] * n
        for i in range(n - 1, -1, -1):
            cp[i] = est[i][2] + max((cp[j] for j in succs[i]), default=0.0)
        npred = [len(p) for p in preds]
        ready = [i for i in range(n) if npred[i] == 0]
        fin = [0.0] * n
        eng_free = {e: 0.0 for e in self.ENGS}
        LAT = 1.0
        while ready:
            best, bkey = None, None
            for i in ready:
                e = est[i][0]
                t = eng_free[e]
                for p in preds[i]:
                    tp = fin[p] + (0.15 if est[p][0] == e else LAT)
                    if tp > t:
                        t = tp
                key = (t - 0.05 * cp[i], i)
                if bkey is None or key < bkey:
                    best, bkey, bt = i, key, t
            i = best
            ready.remove(i)
            e = est[i][0]
            eng_free[e] = bt + est[i][1]
            fin[i] = bt + est[i][2]
            op = ops[i]
            if op[0] == "I":
                self.I(op[1], op[2], *op[3], reads=op[5], writes=op[6], **op[4])
            else:
                self.D(op[1], op[2], reads=op[4], writes=op[5], **op[3])
            for j in succs[i]:
                npred[j] -= 1
                if npred[j] == 0:
                    ready.append(j)

    def I(self, eng, meth, *args, reads=(), writes=(), **kw):
        if self.mute:
            return
        if self.rec is not None:
            self.rec.append(("I", eng, meth, args, kw, self._sfx(reads), self._sfx(writes)))
            return
        idx = len(self.ops[eng])
        deps = self._deps((eng, idx), list(reads), list(writes))
        self.ops[eng].append(((meth, args, kw), deps, None))

    def D(self, out, in_, reads=(), writes=(), **kw):
        if self.mute:
            return
        if self.rec is not None:
            self.rec.append(("D", out, in_, dict(kw), self._sfx(reads), self._sfx(writes)))
            return
        k = self.ndma
        self.ndma += 1
        deps = self._deps(("dma", k), list(reads), list(writes))
        kw = dict(kw)
        kw["out"] = out
        kw["in_"] = in_
        self.ops["sp"].append((("dma_start", (), kw), deps, k))

    def cp(self, eng, out, in_, reads=(), writes=()):
        self.I(eng, "copy" if eng == "act" else "tensor_copy", out, in_, reads=reads, writes=writes)

    def ts1(self, eng, out, in0, s, op, reads=(), writes=()):
        self.I(eng, "tensor_scalar", out, in0, s, 0.0, op, ALU.add, reads=reads, writes=writes)

    def ew(self):
        self.rr += 1
        return "dve" if (self.rr % 3) else "pool"

    def ev(self):
        self.rr += 1
        return "dve" if (self.rr % 2) else "act"

    def emit(self, final_keys=()):
        nc = self.nc
        me = ("sp", len(self.ops["sp"]))
        deps = self._deps(me, list(final_keys), [])
        self.ops["sp"].append((None, deps, None))
        nsem = {e: (len(self.ops[e]) + SEM_CAP - 1) // SEM_CAP + 1 for e in self.ENGS}
        with contextlib.ExitStack() as st:
            sems = {e: [st.enter_context(nc.semaphore(f"s_{e}{i}")) for i in range(nsem[e])]
                    for e in self.ENGS}
            dsems = [st.enter_context(nc.semaphore(f"s_dma{i}")) for i in range(N_DMA_SEM)]
            block = st.enter_context(nc.Block())

            def waitspec(p):
                if p[0] == "dma":
                    k = p[1]
                    return ("d", k % N_DMA_SEM), dsems[k % N_DMA_SEM], 16 * (k // N_DMA_SEM + 1)
                e, i = p
                return (e, i // SEM_CAP), sems[e][i // SEM_CAP], i % SEM_CAP + 1

            def run(ename, eobj):
                waited = {}
                for idx, (fn, deps, dk) in enumerate(self.ops[ename]):
                    specs = []
                    for p in deps:
                        if p[0] == ename and (ename == "pe" or not SAME_ENG_WAIT):
                            continue
                        specs.append(waitspec(p))
                    if dk is not None and dk >= N_DMA_SEM:
                        specs.append((("d", dk % N_DMA_SEM), dsems[dk % N_DMA_SEM],
                                      16 * (dk // N_DMA_SEM)))
                    for sid, sem, val in specs:
                        if waited.get(sid, 0) >= val:
                            continue
                        waited[sid] = val
                        eobj.wait_ge(sem, val)
                    if fn is None:
                        continue
                    meth, args, kw = fn
                    ins = getattr(eobj, meth)(*args, **kw)
                    if dk is not None:
                        ins.then_inc(dsems[dk % N_DMA_SEM], 16)
                    else:
                        ins.then_inc(sems[ename][idx // SEM_CAP], 1)

            @block.tensor
            def _(e):
                run("pe", e)

            @block.vector
            def _(e):
                run("dve", e)

            @block.scalar
            def _(e):
                run("act", e)

            @block.gpsimd
            def _(e):
                run("pool", e)

            @block.sync
            def _(e):
                run("sp", e)


IN_SHAPES = {
    "x": [T_LAT, D], "ctx": [T_CTX, D], "cc": [2, D], "mod_w": [D, 6 * D], "mod_b": [1, 6 * D],
    "g1": [1, D], "g2n": [1, D], "gf": [1, D], "w_in": [D, 2272], "w_out": [D, D],
    "conv": [9, 1536], "w0": [2, 512], "w2": [2, 32, 512], "a0": [2, 512], "a2": [2, 32, 512],
    "g2": [96, 512], "kkv": [512], "kav": [512], "rkv": [512], "lnw": [1, 512], "lnb": [1, 512],
    "lam_re": [2, 32, 64], "lam_im": [2, 32, 64], "lstep": [2, 32],
    "b_re": [32, 64, 16], "b_im": [32, 64, 16], "c_re": [32, 16, 64], "c_im": [32, 16, 64],
    "s5d": [512], "gluw": [512, 512], "glub": [1, 512],
    "w1": [D, DFF], "w3": [D, DFF], "w2f": [DFF, D],
    "ident": [128, 128], "antiI": [128, 128], "mask_b": [128, 256], "mask_k": [128, 256],
    "maskT": [128, 128], "bones": [128, 128],
}

DBG_SHAPES = {"d_mod": [128, 6 * D], "d_AB": [128, 32], "d_z": [128, 3 * 256], "d_zu": [128, 512],
              "d_rkvc": [128, 3 * 128], "d_yT": [128, 512], "d_ST": [128, 256], "d_ys": [128, 512],
              "d_x1": [128, D], "d_car": [128, 32], "d_F": [128, 2048], "d_Bw": [128, 2048]}


def build_nc(n_lat_chunks=(16, 16), do_tail=True, debug=False, upto="full", n_ctx=2):
    nc = bass.Bass("TRN2", target_bir_lowering=False)
    din = {k: nc.dram_tensor(k, s, F32, kind="ExternalInput").ap() for k, s in IN_SHAPES.items()}
    out_d = nc.dram_tensor("out", [OWN, D], F32, kind="ExternalOutput").ap()
    scr = {}
    for nm in ("yr0", "yr1", "ys0", "ys1", "bv0", "bv1"):
        scr[nm] = nc.dram_tensor("scr_" + nm, [OWN, 512], F32, kind="Internal").ap()
    scr["sg"] = nc.dram_tensor("scr_sg", [NOWN * 128, 128], F32, kind="Internal").ap()
    scr["x1"] = nc.dram_tensor("scr_x1", [OWN, D], F32, kind="Internal").ap()
    scr["mod"] = nc.dram_tensor("scr_mod", [128, 6 * D], F32, kind="Internal").ap()
    scr["dg"] = nc.dram_tensor("scr_dg", [12, 128, 9 * 128], BF16, kind="Internal").ap()
    scr["fr"] = nc.dram_tensor("scr_fr", [NOWN * 4 * 128, 512], F32, kind="Internal").ap()
    scr["fz"] = nc.dram_tensor("scr_fz", [NOWN * 128, 128], F32, kind="Internal").ap()
    scr["fu"] = nc.dram_tensor("scr_fu", [NOWN * 128, 512], BF16, kind="Internal").ap()
    scr["cc_in"] = nc.dram_tensor("scr_cc_in", [128, 288], F32, kind="Internal").ap()
    scr["cc_out"] = nc.dram_tensor("scr_cc_out", [128, 288], F32, kind="Internal").ap()
    dbg = {}
    if debug:
        for nm, shp in DBG_SHAPES.items():
            dbg[nm] = nc.dram_tensor(nm, shp, F32, kind="ExternalOutput").ap()
    kb = KB(nc)
    cnt = [0]

    with contextlib.ExitStack() as g:
        def sb(st, name, shape, dt=F32):
            cnt[0] += 1
            return st.enter_context(nc.sbuf_tensor(f"sb{cnt[0]}_{name}", shape, dt))

        ident = sb(g, "ident", [128, 128]); antiI = sb(g, "antiI", [128, 128])
        mask_b = sb(g, "mask_b", [128, 256]); mask_k = sb(g, "mask_k", [128, 256])
        maskT = sb(g, "maskT", [128, 128]); bones = sb(g, "bones", [128, 128])
        ones = sb(g, "ones", [128, 128])
        for nm, t in (("ident", ident), ("antiI", antiI), ("mask_b", mask_b), ("mask_k", mask_k),
                      ("maskT", maskT), ("bones", bones)):
            kb.D(t[:], din[nm], writes=[nm])
        kb.I("pool", "memset", ones[:], 1.0, writes=["ones"])
        AB = sb(g, "AB", [128, 4, 8])
        cnt[0] += 1
        PS = [g.enter_context(nc.psum_tensor(f"psbank{i}", [128, 512], F32)) for i in range(8)]

        def bk(i):
            return [f"pX{i}_{j}" for j in range(4)]

        wscope = contextlib.ExitStack()
        win = sb(wscope, "win", [128, 8, 2272], BF16)
        car_re = sb(wscope, "car_re", [128, 16]); car_im = sb(wscope, "car_im", [128, 16])
        gs_re = sb(wscope, "gs_re", [128, 16]); gs_im = sb(wscope, "gs_im", [128, 16]); gta = sb(wscope, "gta", [128, 16]); gtb = sb(wscope, "gtb", [128, 16])
        Bw_re = sb(wscope, "Bw_re", [128, 16, 128], BF16); Bw_im = sb(wscope, "Bw_im", [128, 16, 128], BF16)
        Cw_re = sb(wscope, "Cw_re", [128, 16, 32]); Cw_imn = sb(wscope, "Cw_imn", [128, 16, 32])
        E_re = sb(wscope, "E_re", [128, 16, 128]); E_im = sb(wscope, "E_im", [128, 16, 128])
        F_re = sb(wscope, "F_re", [128, 16, 128]); F_im = sb(wscope, "F_im", [128, 16, 128])
        s5t = {nm: sb(wscope, "s5" + nm, [128, 512]) for nm in ("t1", "t2", "Xre", "Xim", "Gre", "Gim", "Hre", "Him")}

        def s5_setup(sd, F0=()):
            K = "s5s"
            S5K = ["s5t1", "s5t2", "s5Xre", "s5Xim", "s5Gre", "s5Gim", "s5Hre", "s5Him", "s5tab", K]
            with contextlib.ExitStack() as t_:
                sm = sb(t_, "s5sm", [128, 20, 16])
                (lre, lim, stp, th, th2, tmpc, mag, imag, sn, csn, lbr, lbi, den, nr, qre, qim, u1, ivr, ivi) = [sm[:, i_, :] for i_ in range(19)]
                bre = s5t["Gre"][:, 0:256].rearrange("p (q h) -> p q h", h=16); bim = s5t["Gre"][:, 256:512].rearrange("p (q h) -> p q h", h=16)
                bbr = s5t["Gim"][:, 0:256].rearrange("p (q h) -> p q h", h=16); bbi = s5t["Gim"][:, 256:512].rearrange("p (q h) -> p q h", h=16)
                v1 = s5t["Hre"][:, 0:256].rearrange("p (q h) -> p q h", h=16); cst = s5t["Hre"][:, 256:512].rearrange("p (q h) -> p q h", h=16)
                bdw = s5t["Xre"][:].rearrange("p (r c) -> p r c", c=128)

                def V(meth, *args, eng="dve", **kw):
                    kb.I(eng, meth, *args, reads=S5K, writes=S5K, **kw)

                kb.D(lre, din["lam_re"][sd].rearrange("(q g) p -> (g p) q", g=2), reads=list(F0) + S5K, writes=S5K, allow_slow_non_contiguous=True)
                kb.D(lim, din["lam_im"][sd].rearrange("(q g) p -> (g p) q", g=2), reads=list(F0), writes=[K], allow_slow_non_contiguous=True)
                for g2_ in range(2):
                    kb.D(stp[64 * g2_:64 * g2_ + 64, :],
                         din["lstep"][sd:sd + 1, :].rearrange("o (q g) -> o q g", g=2)[:, :, g2_].partition_broadcast(64),
                         reads=list(F0), writes=[K], allow_slow_non_contiguous=True)
                kb.D(bre, din["b_re"].rearrange("(q g) p h -> (g p) q h", g=2), reads=list(F0), writes=[K])
                kb.D(bim, din["b_im"].rearrange("(q g) p h -> (g p) q h", g=2), reads=list(F0), writes=[K])
                V("activation", stp, stp, AF.Exp, eng="act")
                V("tensor_tensor", mag, lre, stp, ALU.mult)
                V("tensor_tensor", th, lim, stp, ALU.mult)
                V("activation", imag, mag, AF.Exp, eng="act", scale=-1.0)
                V("activation", mag, mag, AF.Exp, eng="act")
                V("tensor_copy", u1, th)
                for m in (PI, 3 * PI, 5 * PI):
                    V("tensor_scalar", tmpc, th, m, -2 * PI, ALU.is_ge, ALU.mult)
                    V("tensor_tensor", u1, u1, tmpc, ALU.add)
                V("tensor_scalar", u1, u1, 0.125, 0.0, ALU.mult, ALU.add)
                V("tensor_tensor", th2, u1, u1, ALU.mult)
                V("tensor_scalar", sn, th2, -1.0 / 5040, 1.0 / 120, ALU.mult, ALU.add)
                V("tensor_tensor", sn, sn, th2, ALU.mult)
                V("tensor_scalar", sn, sn, -1.0 / 6, 0.0, ALU.add, ALU.add)
                V("tensor_tensor", sn, sn, th2, ALU.mult)
                V("tensor_scalar", sn, sn, 1.0, 0.0, ALU.add, ALU.add)
                V("tensor_tensor", sn, sn, u1, ALU.mult)
                V("tensor_scalar", csn, th2, 1.0 / 40320, -1.0 / 720, ALU.mult, ALU.add)
                V("tensor_tensor", csn, csn, th2, ALU.mult)
                V("tensor_scalar", csn, csn, 1.0 / 24, 0.0, ALU.add, ALU.add)
                V("tensor_tensor", csn, csn, th2, ALU.mult)
                V("tensor_scalar", csn, csn, -0.5, 0.0, ALU.add, ALU.add)
                V("tensor_tensor", csn, csn, th2, ALU.mult)
                V("tensor_scalar", csn, csn, 1.0, 0.0, ALU.add, ALU.add)
                for _ in range(3):
                    V("tensor_tensor", tmpc, csn, sn, ALU.mult)
                    V("tensor_tensor", th2, sn, sn, ALU.mult)
                    V("tensor_tensor", csn, csn, csn, ALU.mult)
                    V("tensor_tensor", csn, csn, th2, ALU.subtract)
                    V("tensor_scalar", sn, tmpc, 2.0, 0.0, ALU.mult, ALU.add)
                V("tensor_tensor", lbr, mag, csn, ALU.mult)
                V("tensor_tensor", lbi, mag, sn, ALU.mult)
                V("tensor_tensor", ivr, imag, csn, ALU.mult)
                V("scalar_tensor_tensor", ivi, imag, -1.0, sn, ALU.mult, ALU.mult)
                V("tensor_tensor", den, lre, lre, ALU.mult)
                V("tensor_tensor", u1, lim, lim, ALU.mult)
                V("tensor_tensor", den, den, u1, ALU.add)
                V("reciprocal", den, den)
                V("tensor_scalar", nr, lbr, -1.0, 0.0, ALU.add, ALU.add)
                V("tensor_tensor", qre, nr, lre, ALU.mult)
                V("tensor_tensor", u1, lbi, lim, ALU.mult)
                V("tensor_tensor", qre, qre, u1, ALU.add)
                V("tensor_tensor", qre, qre, den, ALU.mult)
                V("tensor_tensor", qim, lbi, lre, ALU.mult)
                V("tensor_tensor", u1, nr, lim, ALU.mult)
                V("tensor_tensor", qim, qim, u1, ALU.subtract)
                V("tensor_tensor", qim, qim, den, ALU.mult)
                qreb = qre.unsqueeze(2).to_broadcast([128, 16, 16]); qimb = qim.unsqueeze(2).to_broadcast([128, 16, 16])
                V("tensor_tensor", bbr, bre, qreb, ALU.mult)
                V("tensor_tensor", v1, bim, qimb, ALU.mult)
                V("tensor_tensor", bbr, bbr, v1, ALU.subtract)
                V("tensor_tensor", bbi, bim, qreb, ALU.mult)
                V("tensor_tensor", v1, bre, qimb, ALU.mult)
                V("tensor_tensor", bbi, bbi, v1, ALU.add)
                for src_, dstT in ((bbr, Bw_re), (bbi, Bw_im)):
                    for qq in range(4):
                        V("memset", bdw, 0.0)
                        for r_ in range(4):
                            for g2_ in range(2):
                                c0 = 32 * r_ + 16 * g2_
                                V("tensor_copy", bdw[64 * g2_:64 * g2_ + 64, r_, c0:c0 + 16], src_[64 * g2_:64 * g2_ + 64, qq * 4 + r_, :])
                        for r_ in range(4):
                            kb.I("pe", "transpose", PS[4][:, r_ * 128:(r_ + 1) * 128], bdw[:, r_, :], ident[:],
                                 reads=S5K + ["ident"], writes=[f"pX4_{r_}"])
                        kb.I("dve", "tensor_copy", dstT[:, qq * 4:(qq + 1) * 4, :], PS[4][:].rearrange("p (r c) -> p r c", r=4),
                             reads=bk(4) + S5K, writes=S5K)
                cpad = s5t["Xim"][:].rearrange("p (j c) -> p j c", c=128)
                for nm, dstC, sc in (("c_re", Cw_re, 1.0), ("c_im", Cw_imn, -1.0)):
                    csrc = din[nm].rearrange("g h p -> (g h) p").rearrange("(j r) p -> r j p", r=128)
                    for hh_ in range(2):
                        kb.D(cpad[:, :, 64 * hh_:64 * hh_ + 64], csrc, reads=S5K, writes=S5K)
                    for j_ in range(4):
                        kb.I("pe", "transpose", PS[4][:, j_ * 128:(j_ + 1) * 128], cpad[:, j_, :], ident[:], reads=S5K + ["ident"], writes=[f"pX4_{j_}"])
                    V("memset", dstC[:], 0.0)
                    for g2_ in range(2):
                        srcv = PS[4][64 * g2_:64 * g2_ + 64, :].rearrange("p (q g h) -> p q g h", g=2, h=16)[:, :, g2_, :]
                        kb.I("dve", "tensor_scalar", dstC[64 * g2_:64 * g2_ + 64, :, 16 * g2_:16 * g2_ + 16], srcv, sc, 0.0, ALU.mult, ALU.add,
                             reads=bk(4) + S5K, writes=S5K)
                for (Tr, Ti, sr, si) in ((F_re, F_im, lbr, lbi), (E_re, E_im, ivr, ivi)):
                    V("tensor_copy", Tr[:, :, 0:1], sr.unsqueeze(2))
                    V("tensor_copy", Ti[:, :, 0:1], si.unsqueeze(2))
                    n_ = 1
                    while n_ < 128:
                        for qh in range(2):
                            qsl = slice(8 * qh, 8 * qh + 8)
                            pr = Tr[:, qsl, n_ - 1:n_].to_broadcast([128, 8, n_]); pim = Ti[:, qsl, n_ - 1:n_].to_broadcast([128, 8, n_])
                            t1_ = s5t["t1"][:, 0:8 * n_].rearrange("p (q n) -> p q n", q=8)
                            t2_ = s5t["t2"][:, 0:8 * n_].rearrange("p (q n) -> p q n", q=8)
                            V("tensor_tensor", t1_, Tr[:, qsl, 0:n_], pr, ALU.mult)
                            V("tensor_tensor", t2_, Ti[:, qsl, 0:n_], pim, ALU.mult)
                            V("tensor_tensor", Tr[:, qsl, n_:2 * n_], t1_, t2_, ALU.subtract)
                            V("tensor_tensor", t1_, Tr[:, qsl, 0:n_], pim, ALU.mult)
                            V("tensor_tensor", t2_, Ti[:, qsl, 0:n_], pr, ALU.mult)
                            V("tensor_tensor", Ti[:, qsl, n_:2 * n_], t1_, t2_, ALU.add)
                        n_ *= 2
                if debug and sd == 0:
                    kb.D(dbg["d_F"], F_re[:].rearrange("p q t -> p (q t)"), reads=S5K, writes=["o_d_F"])
                V("tensor_copy", lre[:, 0:1], lre[:, 0:1])

        with contextlib.ExitStack() as p0:
            rec0 = []
            kb.rec = rec0
            stage = [sb(p0, f"stage{i}", [128, 2272]) for i in range(2)]
            for kt in range(8):
                s_ = stage[kt % 2]; sk = f"stage{kt % 2}"
                kb.D(s_[:], din["w_in"][kt * 128:(kt + 1) * 128, :], writes=[sk])
                kb.cp("act" if kt % 2 else "pool", win[:, kt, :], s_[:], reads=[sk], writes=["win"])
            modb = sb(p0, "modb", [128, 6 * D])
            cT = sb(p0, "cT", [128, 2, 8]); cs = sb(p0, "cs", [128, 2, 8])
            lbx = sb(p0, "lbx", [128, 8, 128]); lbc = sb(p0, "lbc", [128, 8, 128])
            modc = sb(p0, "modc", [128, 2 * D]); mbias = sb(p0, "mbias", [128, 6 * D])
            g1b = sb(p0, "g1b", [128, D]); tmpA = sb(p0, "tmpA", [128, D])
            wbuf = [sb(p0, f"wbuf{i}", [128, 512]) for i in range(4)]
            for j_ in range(2):
                kb.D(cT[:, j_, :], din["cc"][j_].rearrange("(k p) -> p k", p=128), writes=["cT"], allow_slow_non_contiguous=True)
            kb.D(mbias[:], din["mod_b"].partition_broadcast(128), writes=["mbias"])
            kb.D(g1b[:], din["g1"].partition_broadcast(128), writes=["g1b"])
            kb.I("act", "activation", cs[:], cT[:], AF.Silu, reads=["cT"], writes=["cs"])
            for kt in range(8):
                kb.I("dve", "tensor_copy", lbx[:, kt, :], cs[:, 0, kt:kt + 1].to_broadcast([128, 128]), reads=["cs"], writes=["lbx"])
                kb.I("pool", "tensor_copy", lbc[:, kt, :], cs[:, 1, kt:kt + 1].to_broadcast([128, 128]), reads=["cs"], writes=["lbc"])
            i = 0
            for n in range(12 if upto != "p0a" else 0):
                ns = slice(n * 512, (n + 1) * 512)
                for kt in range(8):
                    wb = wbuf[i % 4]; wk = f"wbuf{i % 4}"; i += 1
                    kb.D(wb[:], din["mod_w"][kt * 128:(kt + 1) * 128, ns], writes=[wk])
                    kb.I("pe", "matmul", PS[0][:], lbx[:, kt, :], wb[:], start=(kt == 0), stop=(kt == 7), reads=[wk, "lbx"], writes=bk(0))
                    if n < 4:
                        kb.I("pe", "matmul", PS[1][:], lbc[:, kt, :], wb[:], start=(kt == 0), stop=(kt == 7), reads=[wk, "lbc"], writes=bk(1))
                kb.I("dve", "tensor_tensor", modb[:, ns], PS[0][:], mbias[:, ns], ALU.add, reads=bk(0) + ["mbias"], writes=["modb"])
                if n < 4:
                    kb.I("dve", "tensor_tensor", modc[:, ns], PS[1][:], mbias[:, ns], ALU.add, reads=bk(1) + ["mbias"], writes=["modc"])
            for which, src_ in ((0, modb), (1, modc)) if upto not in ("p0a", "p0b") else ():
                kb.I("dve", "scalar_tensor_tensor", tmpA[:], src_[:, D:2 * D], 1.0, g1b[:], ALU.add, ALU.mult,
                     reads=["modb", "modc", "g1b"], writes=["tmpA"])
                for half in range(2):
                    for j in range(4):
                        kt = half * 4 + j
                        kb.I("pe", "transpose", PS[2][:, j * 128:(j + 1) * 128], tmpA[:, kt * 128:(kt + 1) * 128], ident[:],
                             reads=["tmpA", "ident"], writes=[f"pX2_{j}"])
                        kb.I("pe", "transpose", PS[3][:, j * 128:(j + 1) * 128], src_[:, kt * 128:(kt + 1) * 128], ident[:],
                             reads=["modb", "modc", "ident"], writes=[f"pX3_{j}"])
                    for j in range(4):
                        kt = half * 4 + j
                        kb.I("dve", "tensor_copy", AB[:, 2 * which, kt:kt + 1], PS[2][:, j * 128:j * 128 + 1], reads=[f"pX2_{j}"], writes=["AB"])
                        kb.I("dve", "tensor_copy", AB[:, 2 * which + 1, kt:kt + 1], PS[3][:, j * 128:j * 128 + 1], reads=[f"pX3_{j}"], writes=["AB"])
            if upto not in ("p0a", "p0b", "p0c"):
                kb.D(scr["mod"], modb[:], reads=["modb"], writes=["scr_mod"])
            if debug and upto not in ("p0a", "p0b", "p0c"):
                kb.D(dbg["d_mod"], modb[:], reads=["modb"], writes=["o_d_mod"])
                kb.D(dbg["d_AB"], AB[:].rearrange("p a k -> p (a k)"), reads=["AB"], writes=["o_d_AB"])
            kb.rec = None
            recS = []
            if not upto.startswith("p0"):
                kb.rec = recS
                s5_setup(0)
                kb.rec = None
            kb.merge(rec0, recS)
            kb.I("dve", "tensor_copy", tmpA[:, 0:1], tmpA[:, 0:1],
                 reads=list(kb.lastw.keys()), writes=list(kb.lastw.keys()) + ["fence0"])
        FENCE0 = ["fence0"]
        if upto.startswith("p0"):
            kb.mute = True

        with contextlib.ExitStack() as sw:
            F0 = FENCE0
            kkc = sb(sw, "kkc", [128, 4]); kac = sb(sw, "kac", [128, 4]); omka = sb(sw, "omka", [128, 4])
            rkc = sb(sw, "rkc", [128, 4]); w0c = sb(sw, "w0c", [128, 2, 4]); a0c = sb(sw, "a0c", [128, 2, 4])
            cw = sb(sw, "cw", [128, 12, 9]); s5dc = sb(sw, "s5dc", [128, 4])
            w2pad = sb(sw, "w2pad", [128, 2, 512]); a2pad = sb(sw, "a2pad", [128, 2, 512])
            diagd = sb(sw, "diagd", [128, 4, 128], BF16)
            for t, nm in ((kkc, "kkv"), (kac, "kav"), (rkc, "rkv"), (s5dc, "s5d")):
                kb.D(t[:], din[nm].rearrange("(f p) -> p f", p=128), reads=F0, writes=["cols"], allow_slow_non_contiguous=True)
            for t, nm in ((w0c, "w0"), (a0c, "a0")):
                for d_ in range(2):
                    kb.D(t[:, d_, :], din[nm][d_].rearrange("(f p) -> p f", p=128), reads=F0, writes=["cols"], allow_slow_non_contiguous=True)
            for tp in range(9):
                kb.D(cw[:, :, tp], din["conv"][tp].rearrange("(f p) -> p f", p=128), reads=F0, writes=["cols"], allow_slow_non_contiguous=True)
            kb.I("dve", "tensor_scalar", omka[:], kac[:], -1.0, 1.0, ALU.mult, ALU.add, reads=["cols"], writes=["cols2"])
            kb.I("pool", "memset", w2pad[:], 0.0, reads=F0, writes=["w2pad"])
            kb.I("pool", "memset", a2pad[:], 0.0, reads=F0, writes=["a2pad"])
            for d_ in range(2):
                kb.D(w2pad[32 * d_:32 * d_ + 32, d_, :], din["w2"][d_], reads=["w2pad"], writes=["w2pad"])
                kb.D(a2pad[64 + 32 * d_:96 + 32 * d_, d_, :], din["a2"][d_], reads=["a2pad"], writes=["a2pad"])
            for ct in range(4):
                kb.ts1("dve", diagd[:, ct, :], ident[:], s5dc[:, ct:ct + 1], ALU.mult, reads=["cols", "ident"], writes=["diagd"])

            ST = sb(sw, "ST", [128, 4, 64])
            xw = sb(sw, "xw", [128, 2, D]); junk = sb(sw, "junk", [128, D], BF16)
            ssum = sb(sw, "ssum", [128, 2]); rstd = sb(sw, "rstd", [128, 2])
            hT = sb(sw, "hT", [128, 8, 256], BF16)
            zrkv2 = [sb(sw, f"zrkv{i}", [128, 3, 264], BF16) for i in range(2)]; zl = sb(sw, "zl", [128, 128]); zg = sb(sw, "zg", [128, 128])
            dgb = [sb(sw, f"dgb{i}", [128, 9, 128], BF16) for i in range(2)]
            rkvc2 = [sb(sw, f"rkvc{i}", [128, 3, 128]) for i in range(2)]
            W2 = {}
            for nm in ("kk", "t1", "t2", "ld", "a", "cs", "Eni", "b", "kd", "bep", "kap"):
                W2[nm] = [sb(sw, f"{nm}{i}", [128, 128]) for i in range(2)]
            zub = [sb(sw, f"zub{i}", [128, 4, 128], BF16) for i in range(2)]
            BE = [sb(sw, f"BE{i}", [128, 4, 128]) for i in range(2)]; KAP = [sb(sw, f"KAP{i}", [128, 4, 128]) for i in range(2)]
            AR = [sb(sw, f"AR{i}", [128, 4, 256]) for i in range(2)]; gC = [sb(sw, f"gC{i}", [128, 4]) for i in range(2)]
            tot = sb(sw, "tot", [128, 4])
            TMt = [sb(sw, f"TMt{i}", [128, 4, 512]) for i in range(2)]
            UT = sb(sw, "UT", [128, 512]); yT = sb(sw, "yT", [128, 512])
            flpR = sb(sw, "flpR", [128, 512]); flpS = sb(sw, "flpS", [128, 512])
            NS = 4
            NB = [sb(sw, f"NB{i}", [128, 256]) for i in range(NS)]; KA = [sb(sw, f"KA{i}", [128, 256]) for i in range(NS)]
            Pq = [sb(sw, f"Pq{i}", [128, 2, 128], BF16) for i in range(NS)]; PTq = [sb(sw, f"PTq{i}", [128, 2, 128], BF16) for i in range(NS)]
            Xq = [sb(sw, f"Xq{i}", [128, 2, 64]) for i in range(NS)]; Xb = [sb(sw, f"Xb{i}", [128, 2, 64], BF16) for i in range(NS)]
            print("SBUF bytes remaining in sweep scope:", nc.sbuf_bytes_remaining)
            for cf in range(12):
                for tp in range(9):
                    kb.ts1("dve" if tp % 2 else "pool", dgb[cf % 2][:, tp, :], ident[:], cw[:, cf, tp:tp + 1], ALU.mult,
                           reads=["cols", "ident", f"dgb{cf % 2}"], writes=[f"dgb{cf % 2}"])
                kb.D(scr["dg"][cf], dgb[cf % 2][:].rearrange("p t c -> p (t c)"), reads=[f"dgb{cf % 2}"], writes=["scr_dg"])
            kb.I("pool", "memset", xw[:], 0.0, reads=F0, writes=["xw0", "xw1"])

            def load_window(sd, is_ctx, c):
                rev = (sd == 1)
                TT = T_CTX if is_ctx else T_LAT
                src = din["ctx"] if is_ctx else din["x"]
                for half in range(2):
                    lo_r = 128 * c - 64 + 128 * half
                    if not rev:
                        lo, hi = lo_r, lo_r + 128
                    else:
                        hi, lo = TT - lo_r, TT - lo_r - 128
                    a0_, a1_ = max(lo, 0), min(hi, TT)
                    if a1_ > a0_:
                        kb.D(xw[a0_ - lo:a1_ - lo, half, :], src[a0_:a1_, :], writes=[f"xw{half}"])

            def process_chunk(sd, is_ctx, c, own, slot, nxt=None, first=False, reuse=False):
                rev = (sd == 1)
                TT = T_CTX if is_ctx else T_LAT
                src = din["ctx"] if is_ctx else din["x"]
                ab = 2 if is_ctx else 0
                tid = antiI if rev else ident
                tk = "antiI" if rev else "ident"
                dbg_here = debug and sd == 0 and (not is_ctx) and c == 0
                SL = f"@{slot}"
                streams = {"front": [], "rwkv": [], "s5": []}
                kb.rec = streams["front"]
                AR_, BE_, KAP_, gC_, TM_, zub_ = AR[slot], BE[slot], KAP[slot], gC[slot], TMt[slot], zub[slot]
                kapT = TM_[:, 0, :]; bepT = TM_[:, 1, :]; vT = TM_[:, 2, :]; bvT = TM_[:, 3, :]
                TMK = "TM" + SL
                if not reuse:
                    if first:
                        load_window(sd, is_ctx, c)
                    kb.I("pool", "memset", ssum[:], 0.0, writes=["ssum"])
                    for half in range(2):
                        kb.I("act", "activation", junk[:], xw[:, half, :], AF.Square, accum_out=ssum[:, half:half + 1],
                             reads=[f"xw{half}"], writes=["junk", "ssum"])
                    kb.I("dve", "tensor_scalar", rstd[:], ssum[:], 1.0 / D, 1e-6, ALU.mult, ALU.add, reads=["ssum"], writes=["rstd"])
                    kb.I("act", "activation", rstd[:], rstd[:], AF.Sqrt, reads=["rstd"], writes=["rstd"])
                    kb.I("dve", "reciprocal", rstd[:], rstd[:], reads=["rstd"], writes=["rstd"])
                    for half in range(2):
                        kb.ts1("dve" if half == 0 else "pool", xw[:, half, :], xw[:, half, :], rstd[:, half:half + 1], ALU.mult,
                               reads=["rstd", f"xw{half}"], writes=[f"xw{half}"])
                    for kt in range(8):
                        b_ = kt % 2
                        for half in range(2):
                            kb.I("pe", "transpose", PS[b_][:, half * 128:(half + 1) * 128], xw[:, half, kt * 128:(kt + 1) * 128], tid[:],
                                 reads=[f"xw{half}", tk], writes=[f"pX{b_}_0"])
                        if kt % 2:
                            kb.I("dve", "tensor_scalar", hT[:, kt, :], PS[b_][:, 0:256], AB[:, ab, kt:kt + 1], AB[:, ab + 1, kt:kt + 1], ALU.mult, ALU.add,
                                 reads=[f"pX{b_}_0", "AB"], writes=[f"hT{kt}"])
                        else:
                            kb.I("act", "activation", hT[:, kt, :], PS[b_][:, 0:256], AF.Identity, bias=AB[:, ab + 1, kt:kt + 1], scale=AB[:, ab, kt:kt + 1],
                                 reads=[f"pX{b_}_0", "AB"], writes=[f"hT{kt}"])
                    if nxt is not None:
                        load_window(sd, nxt[0], nxt[1])
                    others = [("zl", zl[:, :], 1536, 128), ("zg", zg[0:96, :], 1664, 96)] + [(f"zu{j}" + SL, zub_[:, j, :], 1760 + 128 * j, 128) for j in range(4)]
                    for i_, (nm, dst, c0, m) in enumerate(others):
                        b_ = i_ % 2
                        pr_ = PS[b_][0:m, 0:128]
                        pk = [f"pX{b_}_0"]
                        for kt in range(8):
                            kb.I("pe", "matmul", pr_, win[:, kt, c0:c0 + m], hT[:, kt, 64:192], start=(kt == 0), stop=(kt == 7), reads=["win", f"hT{kt}"], writes=pk)
                        if nm == "zg":
                            kb.I("act", "activation", dst, pr_, AF.Sigmoid, reads=pk, writes=[nm])
                        else:
                            kb.cp("dve" if i_ % 2 else "act", dst, pr_, reads=pk, writes=[nm])
                    if own is not None and sd == 0:
                        kb.D(scr["sg"][own * 128:own * 128 + 96, :], zg[0:96, :], reads=["zg"], writes=[f"scr_sg{own}"])
                    kb.I("act", "activation", zl[0:64, :], zl[0:64, :], AF.Tanh, reads=["zl"], writes=["zl"])
                    if sd == 0 and own is not None and not is_ctx:
                        kb.D(scr["fz"][own * 128:(own + 1) * 128, :], zl[:, :], reads=["zl"], writes=[f"scr_fz{own}"])
                        kb.D(scr["fu"][own * 128:(own + 1) * 128, :], zub_[:].rearrange("p j t -> p (j t)"), reads=[f"zu{j}" + SL for j in range(4)], writes=[f"scr_fu{own}"])
                else:
                    kb.D(xw[:, 1, 512:640], scr["fz"][own * 128:(own + 1) * 128, :], reads=[f"scr_fz{own}"], writes=["xw1"])
                    kb.I("dve", "tensor_copy", zl[:, :], xw[:, 1, 512:640][:, ::-1], reads=["xw1"], writes=["zl"])
                    kb.D(junk[:, 0:512], scr["fu"][own * 128:(own + 1) * 128, :], reads=[f"scr_fu{own}"], writes=["junk"])
                    kb.I("dve", "tensor_copy", zub_[:], junk[:, 0:512].rearrange("p (j t) -> p j t", t=128)[:, :, ::-1], reads=["junk"], writes=[f"zu{j}" + SL for j in range(4)])
                sgn = -1 if rev else 1
                nk = 4 if own is not None else 3
                for ft in range(4):
                    fp_ = ft % 2
                    zrkv = zrkv2[fp_]; rkvc = rkvc2[fp_]
                    W = {nm: W2[nm][fp_] for nm in W2}
                    if reuse:
                        stg_ = xw[:, ft % 2, 0:512]
                        kb.D(stg_, scr["fr"][(own * 4 + ft) * 128:(own * 4 + ft + 1) * 128, :], reads=[f"scr_fr{own}_{ft}"], writes=[f"xw{ft % 2}"])
                        kb.I("dve", "tensor_copy", rkvc[:, :, :], stg_[:, 0:384].rearrange("p (j t) -> p j t", t=128)[:, :, ::-1],
                             reads=[f"xw{ft % 2}"], writes=[f"rkvc{j}_{fp_}" for j in range(3)])
                        kb.I("dve", "tensor_copy", W["kk"][:, :], stg_[:, 384:512][:, ::-1], reads=[f"xw{ft % 2}"], writes=[f"W_kk{fp_}"])
                    for j3 in (range(3) if not reuse else ()):
                        b_ = j3 % 2
                        cf = (j3 * 4 + ft) * 128
                        pr_ = PS[b_][:, 0:256]
                        pk = [f"pX{b_}_0"]
                        for kt in range(8):
                            kb.I("pe", "matmul", pr_, win[:, kt, cf:cf + 128], hT[:, kt, :], start=(kt == 0), stop=(kt == 7), reads=["win", f"hT{kt}"], writes=pk)
                        if is_ctx:
                            kb.cp("act" if j3 % 2 else "dve", zrkv[:, j3, 0:256], pr_, reads=pk, writes=[f"zrkv{j3}_{fp_}"])
                        else:
                            if c == 0 and ft < 2:
                                kb.I("pool", "memset", zrkv[:, j3, :].rearrange("p (r c) -> p r c", c=66)[:, :, 0:1], 0.0, reads=[f"zrkv{j3}_{fp_}"], writes=[f"zrkv{j3}_{fp_}"])
                                kb.I("pool", "memset", zrkv[:, j3, :].rearrange("p (r c) -> p r c", c=66)[:, :, 65:66], 0.0, reads=[f"zrkv{j3}_{fp_}"], writes=[f"zrkv{j3}_{fp_}"])
                            kb.cp("act" if j3 % 2 else "dve", zrkv[:, j3, :].rearrange("p (r c) -> p r c", c=66)[:, :, 1:65],
                                  pr_.rearrange("p (r c) -> p r c", c=64), reads=pk, writes=[f"zrkv{j3}_{fp_}"])
                    if dbg_here and ft == 0:
                        kb.D(dbg["d_z"], zrkv[:].rearrange("p f t -> p (f t)"), reads=[f"zrkv{j}_{fp_}" for j in range(3)], writes=["o_d_z"])
                    for j3 in (range(3) if not reuse else ()):
                        zk = f"zrkv{j3}_{fp_}"; ok = f"rkvc{j3}_{fp_}"
                        cf = j3 * 4 + ft
                        ds = (ft * 3 + j3) % 2
                        dk = f"dgb{ds}"
                        kb.D(dgb[ds][:].rearrange("p t c -> p (t c)"), scr["dg"][cf], reads=["scr_dg"], writes=[dk])
                        b_ = (j3 + 1) % 2
                        pc = PS[b_][:, 256:384]
                        taps = [(0, 0)] + [(dy, dx) for dy in ((0,) if is_ctx else (-1, 0, 1)) for dx in (-1, 0, 1) if (dy, dx) != (0, 0)]
                        mms = []
                        for (dy, dx) in taps:
                            wi = (1 + sgn * dy) * 3 + (1 + sgn * dx)
                            if is_ctx:
                                j0, j1 = 64, 192
                                if 128 * c + dx < 0:
                                    j0 = 65
                                if 128 * c + 127 + dx >= TT:
                                    j1 = 191
                                mms.append((PS[b_][:, 256 + j0 - 64:256 + j1 - 64], wi, zrkv[:, j3, j0 + dx:j1 + dx]))
                            else:
                                i0, i1 = 0, 2
                                if 2 * c + dy < 0:
                                    i0 = 1
                                if 2 * c + 1 + dy >= 64:
                                    i1 = 1
                                o0 = 66 * i0 + 1; o1 = 66 * (i1 - 1) + 65
                                sh = 66 * (1 + dy) + dx
                                mms.append((PS[b_][:, 256 + o0:256 + o1], wi, zrkv[:, j3, o0 + sh:o1 + sh]))
                        for ti, (o_ap, wi, i_ap) in enumerate(mms):
                            kb.I("pe", "matmul", o_ap, dgb[ds][:, wi, :], i_ap, start=(ti == 0), stop=(ti == len(mms) - 1),
                                 reads=[zk, dk], writes=[f"pX{b_}_0"])
                        if is_ctx:
                            kb.cp("act" if j3 % 2 == 0 else "dve", rkvc[:, j3, :], pc, reads=[f"pX{b_}_0"], writes=[ok])
                        else:
                            kb.cp("act" if j3 % 2 == 0 else "dve", rkvc[:, j3, :].rearrange("p (r c) -> p r c", c=64),
                                  PS[b_][:, 256:388].rearrange("p (r c) -> p r c", c=66)[:, :, 1:65], reads=[f"pX{b_}_0"], writes=[ok])
                    if dbg_here and ft == 0:
                        kb.D(dbg["d_rkvc"], rkvc[:].rearrange("p f t -> p (f t)"), reads=[f"rkvc{j}_{fp_}" for j in range(3)], writes=["o_d_rkvc"])
                    rc = rkvc[:, 0, :]; kc = rkvc[:, 1, :]; vc = rkvc[:, 2, :]
                    fk = f"w{ft}" + SL
                    g_ = {nm: W[nm][:, :] for nm in W}
                    CK = ["cols", "cols2"]

                    def V(meth, *args, eng="dve", r=(), w=(), **kw):
                        kb.I(eng, meth, *args, reads=list(r) + CK, writes=list(w), **kw)
                    ARa = AR_[:, ft, 0:128]; ARr = AR_[:, ft, 128:256]
                    fa, fr, fb, fkp, fg = fk + "a", fk + "r", fk + "b", fk + "k", fk + "g"
                    if not reuse:
                        V("tensor_scalar", g_["kk"], kc, kkc[:, ft:ft + 1], 0.0, ALU.mult, ALU.add, r=[f"rkvc1_{fp_}"], w=[f"W_kk{fp_}"])
                        V("tensor_tensor", g_["t1"], g_["kk"], g_["kk"], ALU.mult, r=[f"W_kk{fp_}"], w=[f"W_t1{fp_}"])
                        kb.I("pe", "matmul", PS[0][:, 384:512], bones[:], g_["t1"], start=True, stop=True, reads=[f"W_t1{fp_}", "bones"], writes=["pX0_0"])
                        kb.I("dve", "tensor_scalar", g_["t2"], PS[0][:, 384:512], 1e-12, 0.0, ALU.max, ALU.add, reads=["pX0_0"], writes=[f"W_t2{fp_}"])
                        V("activation", g_["t2"], g_["t2"], AF.Sqrt, eng="act", r=[f"W_t2{fp_}"], w=[f"W_t2{fp_}"])
                        V("reciprocal", g_["t2"], g_["t2"], r=[f"W_t2{fp_}"], w=[f"W_t2{fp_}"])
                        V("tensor_tensor", g_["kk"], g_["kk"], g_["t2"], ALU.mult, r=[f"W_kk{fp_}", f"W_t2{fp_}"], w=[f"W_kk{fp_}"])
                        if sd == 0 and own is not None and not is_ctx:
                            r0_ = (own * 4 + ft) * 128
                            kb.D(scr["fr"][r0_:r0_ + 128, 0:384], rkvc[:, :, :].rearrange("p j t -> p (j t)"), reads=[f"rkvc{j}_{fp_}" for j in range(3)], writes=[f"scr_fr{own}_{ft}"])
                            kb.D(scr["fr"][r0_:r0_ + 128, 384:512], g_["kk"], reads=[f"W_kk{fp_}"], writes=[f"scr_fr{own}_{ft}"])
                    kb.I("pe", "matmul", PS[1][:, 256:384], w2pad[:, sd, ft * 128:(ft + 1) * 128], zl[:, :], start=True, stop=True, reads=["zl", "w2pad"], writes=["pX1_0"])
                    kb.I("pe", "matmul", PS[1][:, 384:512], a2pad[:, sd, ft * 128:(ft + 1) * 128], zl[:, :], start=True, stop=True, reads=["zl", "a2pad"], writes=["pX1_0"])
                    kb.I("act", "activation", g_["ld"], PS[1][:, 256:384], AF.Sigmoid, bias=w0c[:, sd, ft:ft + 1], scale=1.0, reads=["pX1_0", "cols"], writes=[f"W_ld{fp_}"])
                    kb.I("act", "activation", g_["a"], PS[1][:, 384:512], AF.Sigmoid, bias=a0c[:, sd, ft:ft + 1], scale=1.0, reads=["pX1_0", "cols"], writes=[f"W_a{fp_}"])
                    V("tensor_tensor_scan", g_["cs"], ones[:], g_["ld"], 0.0, ALU.mult, ALU.add, r=[f"W_ld{fp_}", "ones"], w=[f"W_cs{fp_}"])
                    V("tensor_copy", tot[:, ft:ft + 1], g_["cs"][:, 127:128], r=[f"W_cs{fp_}"], w=[f"tot{ft}"])
                    V("tensor_tensor", g_["t1"], g_["cs"], g_["ld"], ALU.subtract, r=[f"W_cs{fp_}", f"W_ld{fp_}"], w=[f"W_t1{fp_}"])
                    V("activation", ARr, g_["cs"], AF.Exp, eng="act", scale=-KDEC, r=[f"W_cs{fp_}"], w=[fr])
                    V("activation", ARa, g_["t1"], AF.Exp, eng="act", scale=-KDEC, r=[f"W_t1{fp_}"], w=[fa])
                    V("activation", g_["Eni"], g_["cs"], AF.Exp, eng="act", scale=KDEC, r=[f"W_cs{fp_}"], w=[f"W_Eni{fp_}"])
                    V("activation", gC_[:, ft:ft + 1], tot[:, ft:ft + 1], AF.Exp, eng="act", scale=-KDEC, r=[f"tot{ft}"], w=[fg])
                    V("tensor_tensor", g_["b"], g_["kk"], g_["a"], ALU.mult, r=[f"W_kk{fp_}", f"W_a{fp_}"], w=[f"W_b{fp_}"])
                    V("tensor_scalar", g_["t2"], g_["a"], kac[:, ft:ft + 1], omka[:, ft:ft + 1], ALU.mult, ALU.add, r=[f"W_a{fp_}"], w=[f"W_t2{fp_}"])
                    V("tensor_tensor", g_["kd"], kc, g_["t2"], ALU.mult, r=[f"rkvc1_{fp_}", f"W_t2{fp_}"], w=[f"W_kd{fp_}"])
                    V("tensor_tensor", ARa, ARa, g_["kk"], ALU.mult, r=[fa, f"W_kk{fp_}"], w=[fa])
                    V("tensor_tensor", ARr, ARr, rc, ALU.mult, eng="pool", r=[fr, f"rkvc0_{fp_}"], w=[fr])
                    V("tensor_tensor", BE_[:, ft, :], g_["b"], g_["Eni"], ALU.mult, r=[f"W_b{fp_}", f"W_Eni{fp_}"], w=[fb])
                    V("tensor_tensor", KAP_[:, ft, :], g_["kd"], g_["Eni"], ALU.mult, eng="pool", r=[f"W_kd{fp_}", f"W_Eni{fp_}"], w=[fkp])
                    V("tensor_scalar", g_["bep"], BE_[:, ft, :], gC_[:, ft:ft + 1], -1.0, ALU.mult, ALU.mult, r=[fb, fg], w=[f"W_bep{fp_}"])
                    V("tensor_scalar", g_["kap"], KAP_[:, ft, :], gC_[:, ft:ft + 1], 0.0, ALU.mult, ALU.add, eng="pool", r=[fkp, fg], w=[f"W_kap{fp_}"])
                    tsrc = [(g_["kap"], f"W_kap{fp_}"), (g_["bep"], f"W_bep{fp_}"), (vc, f"rkvc2_{fp_}")]
                    if own is not None:
                        V("scalar_tensor_tensor", g_["t1"], rc, rkc[:, ft:ft + 1], g_["kd"], ALU.mult, ALU.mult, r=[f"rkvc0_{fp_}", f"W_kd{fp_}"], w=[f"W_t1{fp_}"])
                        kb.I("pe", "matmul", PS[0][:, 384:512], bones[:], g_["t1"], start=True, stop=True, reads=[f"W_t1{fp_}", "bones"], writes=["pX0_0"])
                        kb.I("dve", "tensor_tensor", g_["t2"], PS[0][:, 384:512], vc, ALU.mult, reads=["pX0_0", f"rkvc2_{fp_}"], writes=[f"W_t2{fp_}"])
                        tsrc.append((g_["t2"], f"W_t2{fp_}"))
                    b_ = ft % 2
                    for i_, (s_ap, s_k) in enumerate(tsrc):
                        kb.I("pe", "transpose", PS[b_][:, i_ * 128:(i_ + 1) * 128], s_ap, ident[:], reads=[s_k, "ident"], writes=[f"pX{b_}_0"])
                    kb.cp("act" if ft % 2 else "dve", TM_[:, 0:nk, ft * 128:(ft + 1) * 128], PS[b_][:, 0:nk * 128].rearrange("p (k t) -> p k t", k=nk),
                          reads=[f"pX{b_}_0"], writes=[TMK])
                kb.rec = streams["rwkv"]
                for g0 in range(0, 8, NS):
                    heads = list(range(g0, g0 + NS))

                    def hv(h):
                        ft = h // 2; Rs = slice(64 * (h % 2), 64 * (h % 2) + 64)
                        return ft, Rs, [f"w{ft}" + SL + x_ for x_ in "arbkg"]
                    BK = [2 + s_ for s_ in range(NS)]
                    bkk = [f"pX{b_}_0" for b_ in BK]
                    for s_, h in enumerate(heads):
                        ft, Rs, fk = hv(h)
                        be = BE_[Rs, ft, :]; ar = AR_[Rs, ft, :]; al = AR_[Rs, ft, 0:128]
                        kb.I("pe", "matmul", PS[BK[s_]][:, 0:256], be, ar, start=True, stop=True, reads=fk, writes=[bkk[s_]])
                        kb.I("pe", "matmul", PS[BK[s_]][:, 256:384], al, be, start=True, stop=True, reads=fk, writes=[bkk[s_]])
                    for s_, h in enumerate(heads):
                        kb.I("dve", "tensor_tensor", NB[s_][:], PS[BK[s_]][:, 0:256], mask_b[:], ALU.mult, reads=[bkk[s_], "mask_b"], writes=[f"NB{s_}"])
                        kb.I("dve", "tensor_tensor", PTq[s_][:, 0, :], PS[BK[s_]][:, 256:384], maskT[:], ALU.mult, reads=[bkk[s_], "maskT"], writes=[f"PT{s_}_0"])
                        kb.I("dve", "tensor_tensor", Pq[s_][:, 0, :], PS[BK[s_]][:, 0:128], mask_b[:, 0:128], ALU.mult, reads=[bkk[s_], "mask_b"], writes=[f"P{s_}_0"])
                    for s_, h in enumerate(heads):
                        ft, Rs, fk = hv(h)
                        kb.I("pe", "matmul", PS[BK[s_]][:, 0:256], KAP_[Rs, ft, :], AR_[Rs, ft, :], start=True, stop=True, reads=fk, writes=[bkk[s_]])
                    for s_, h in enumerate(heads):
                        kb.I("dve", "tensor_tensor", KA[s_][:], PS[BK[s_]][:, 0:256], mask_k[:], ALU.mult, reads=[bkk[s_], "mask_k"], writes=[f"KA{s_}"])
                    for s_, h in enumerate(heads):
                        ft, Rs, fk = hv(h)
                        al = AR_[Rs, ft, 0:128]
                        kb.I("pe", "matmul", PS[BK[s_]][:, 384:448], al, ST[Rs, ft, :], start=True, stop=False, reads=fk + [f"ST{ft}"], writes=[bkk[s_]])
                        kb.I("pe", "matmul", PS[BK[s_]][:, 384:448], KA[s_][:, 0:128], vT[:, h * 64:(h + 1) * 64], start=False, stop=True, reads=[f"KA{s_}", TMK], writes=[bkk[s_]])
                    for s_, h in enumerate(heads):
                        kb.cp("act", Xq[s_][:, 0, :], PS[BK[s_]][:, 384:448], reads=[bkk[s_]], writes=[f"X{s_}_0"])
                        kb.cp("act", Xb[s_][:, 0, :], PS[BK[s_]][:, 384:448], reads=[bkk[s_]], writes=[f"Xb{s_}_0"])
                    for j in range(7):
                        cu, nx = j % 2, (j + 1) % 2
                        for s_, h in enumerate(heads):
                            Pj = Pq[s_][:, cu, :]
                            Pk = f"P{s_}_{cu}"
                            PTj = PTq[s_][:, cu, :]; PTk = f"PT{s_}_{cu}"
                            if j < 6:
                                kb.I("pe", "matmul", PS[BK[s_]][:, 0:128], PTj, Pj, start=True, stop=True, reads=[PTk, Pk], writes=[bkk[s_]])
                            if j < 5:
                                kb.I("pe", "matmul", PS[BK[s_]][:, 128:256], Pj, PTj, start=True, stop=True, reads=[PTk, Pk], writes=[bkk[s_]])
                            kb.I("pe", "matmul", PS[BK[s_]][:, 448:512], Pj, Xb[s_][:, cu, :], start=True, stop=True, reads=[Pk, f"Xb{s_}_{cu}"], writes=[bkk[s_]])
                        for s_, h in enumerate(heads):
                            if j < 6:
                                kb.cp("act", Pq[s_][:, nx, :], PS[BK[s_]][:, 0:128], reads=[bkk[s_]], writes=[f"P{s_}_{nx}"])
                            if j < 5:
                                kb.cp("act", PTq[s_][:, nx, :], PS[BK[s_]][:, 128:256], reads=[bkk[s_]], writes=[f"PT{s_}_{nx}"])
                            dst_ap = UT[:, h * 64:(h + 1) * 64] if j == 6 else Xq[s_][:, nx, :]
                            dst_k = f"UT{h}" if j == 6 else f"X{s_}_{nx}"
                            if j < 6:
                                kb.I("dve", "tensor_tensor", Xb[s_][:, nx, :], Xq[s_][:, cu, :], PS[BK[s_]][:, 448:512], ALU.subtract if j == 0 else ALU.add,
                                     reads=[bkk[s_], f"X{s_}_{cu}"], writes=[f"Xb{s_}_{nx}"])
                            kb.I("dve", "tensor_tensor", dst_ap, Xq[s_][:, cu, :], PS[BK[s_]][:, 448:512], ALU.subtract if j == 0 else ALU.add,
                                 reads=[bkk[s_], f"X{s_}_{cu}"], writes=[dst_k])
                    if own is not None:
                        for s_, h in enumerate(heads):
                            ft, Rs, fk = hv(h)
                            rho = AR_[Rs, ft, 128:256]
                            yr_ = PS[BK[s_]][:, 256:320]
                            kb.I("pe", "matmul", yr_, rho, ST[Rs, ft, :], start=True, stop=False, reads=fk + [f"ST{ft}"], writes=[bkk[s_]])
                            kb.I("pe", "matmul", yr_, NB[s_][:, 128:256], UT[:, h * 64:(h + 1) * 64], start=False, stop=False, reads=[f"NB{s_}", f"UT{h}"], writes=[bkk[s_]])
                            kb.I("pe", "matmul", yr_, KA[s_][:, 128:256], vT[:, h * 64:(h + 1) * 64], start=False, stop=True, reads=[f"KA{s_}", TMK], writes=[bkk[s_]])
                        for s_, h in enumerate(heads):
                            kb.cp("act", yT[:, h * 64:(h + 1) * 64], PS[BK[s_]][:, 256:320], reads=[bkk[s_]], writes=[f"yT{h}"])
                    for ft in range(g0 // 2, (g0 + NS) // 2):
                        h1 = 2 * ft + 1
                        bnk = BK[h1 - g0]
                        cs_ = slice(ft * 128, (ft + 1) * 128)
                        kb.I("pe", "matmul", PS[bnk][:, 0:128], bepT[:, cs_], UT[:, cs_], start=True, stop=False, reads=[TMK, f"UT{h1 - 1}", f"UT{h1}"], writes=[f"pX{bnk}_0"])
                        kb.I("pe", "matmul", PS[bnk][:, 0:128], kapT[:, cs_], vT[:, cs_], start=False, stop=True, reads=[TMK], writes=[f"pX{bnk}_0"])
                        for jj in range(2):
                            rr_ = slice(64 * jj, 64 * jj + 64)
                            kb.I("dve", "scalar_tensor_tensor", ST[rr_, ft, :], ST[rr_, ft, :], gC_[rr_, ft:ft + 1], PS[bnk][rr_, 64 * jj:64 * jj + 64], ALU.mult, ALU.add,
                                 reads=[f"pX{bnk}_0", f"ST{ft}", f"w{ft}" + SL + "g"], writes=[f"ST{ft}"])
                if dbg_here:
                    kb.D(dbg["d_yT"], yT[:], reads=[f"yT{h}" for h in range(8)], writes=["o_d_yT"])
                    kb.D(dbg["d_ST"], ST[:].rearrange("p f v -> p (f v)"), reads=[f"ST{f}" for f in range(4)], writes=["o_d_ST"])
                if own is not None:
                    for nm, tile_ap, keys in (("yr", yT[:], [f"yT{h}" for h in range(8)]), ("bv", bvT, [TMK])):
                        dst = scr[f"{nm}{sd}"][own * 128:(own + 1) * 128, :]
                        if rev:
                            kb.I("pe", "matmul", PS[5][:], antiI[:], tile_ap, start=True, stop=True, reads=keys + ["antiI"], writes=["pX5_0"])
                            kb.cp("act", flpR[:], PS[5][:], reads=["pX5_0"], writes=["flpR"])
                            kb.D(dst, flpR[:], reads=["flpR"], writes=[f"scr_{nm}{sd}_{own}"])
                        else:
                            kb.D(dst, tile_ap, reads=keys, writes=[f"scr_{nm}{sd}_{own}"])
                kb.rec = streams["s5"]
                t1_, t2_ = s5t["t1"], s5t["t2"]
                for a_ in range(4):
                    zuk = f"zu{a_}" + SL
                    for r_ in range(4):
                        q_ = a_ * 4 + r_
                        kb.I("pe", "matmul", PS[6][:, r_ * 128:(r_ + 1) * 128], Bw_re[:, q_, :], zub_[:, a_, :], start=True, stop=True, reads=["s5tab", zuk], writes=["pX6_0"])
                        kb.I("pe", "matmul", PS[7][:, r_ * 128:(r_ + 1) * 128], Bw_im[:, q_, :], zub_[:, a_, :], start=True, stop=True, reads=["s5tab", zuk], writes=["pX7_0"])
                    k3 = ["pX6_0"]; k4 = ["pX7_0"]
                    qs = slice(a_ * 4, a_ * 4 + 4)
                    Er = E_re[:, qs, :].rearrange("p q t -> p (q t)"); Ei = E_im[:, qs, :].rearrange("p q t -> p (q t)")
                    Fr = F_re[:, qs, :].rearrange("p q t -> p (q t)"); Fi = F_im[:, qs, :].rearrange("p q t -> p (q t)")
                    g1_, g2_ = s5t["Gre"], s5t["Gim"]
                    kb.I("dve", "tensor_tensor", t1_[:], PS[6][:], Er, ALU.mult, reads=k3 + ["s5tab"], writes=["s5t1"])
                    kb.I("dve", "tensor_tensor", t2_[:], PS[7][:], Ei, ALU.mult, reads=k4 + ["s5tab"], writes=["s5t2"])
                    kb.I("dve", "tensor_tensor", g1_[:], PS[7][:], Er, ALU.mult, reads=k4 + ["s5tab"], writes=["s5Gre"])
                    kb.I("dve", "tensor_tensor", g2_[:], PS[6][:], Ei, ALU.mult, reads=k3 + ["s5tab"], writes=["s5Gim"])
                    kb.I("pool", "tensor_tensor", s5t["Xre"][:], t1_[:], t2_[:], ALU.subtract, reads=["s5t1", "s5t2"], writes=["s5Xre"])
                    kb.I("dve", "tensor_tensor", s5t["Xim"][:], g1_[:], g2_[:], ALU.add, reads=["s5Gre", "s5Gim"], writes=["s5Xim"])
                    if own is None:
                        kb.I("dve", "tensor_reduce", gs_re[:, qs], s5t["Xre"][:].rearrange("p (q t) -> p q t", q=4), AXX, ALU.add, reads=["s5Xre"], writes=["gs"])
                        kb.I("dve", "tensor_reduce", gs_im[:, qs], s5t["Xim"][:].rearrange("p (q t) -> p q t", q=4), AXX, ALU.add, reads=["s5Xim"], writes=["gs"])
                        continue
                    for r_ in range(4):
                        q_ = a_ * 4 + r_
                        cs_ = slice(r_ * 128, (r_ + 1) * 128)
                        kb.I("dve", "tensor_tensor_scan", s5t["Gre"][:, cs_], ones[:], s5t["Xre"][:, cs_], car_re[:, q_:q_ + 1], ALU.mult, ALU.add,
                             reads=["s5Xre", "car", "ones"], writes=["s5Gre"])
                        kb.I("dve", "tensor_tensor_scan", s5t["Gim"][:, cs_], ones[:], s5t["Xim"][:, cs_], car_im[:, q_:q_ + 1], ALU.mult, ALU.add,
                             reads=["s5Xim", "car", "ones"], writes=["s5Gim"])
                    x1_, x2_ = s5t["Xre"], s5t["Xim"]
                    kb.I("dve", "tensor_tensor", t1_[:], s5t["Gre"][:], Fr, ALU.mult, reads=["s5Gre", "s5tab"], writes=["s5t1"])
                    kb.I("dve", "tensor_tensor", t2_[:], s5t["Gim"][:], Fi, ALU.mult, reads=["s5Gim", "s5tab"], writes=["s5t2"])
                    kb.I("pool", "tensor_tensor", x1_[:], s5t["Gim"][:], Fr, ALU.mult, reads=["s5Gim", "s5tab", "s5Xre"], writes=["s5Xre"])
                    kb.I("pool", "tensor_tensor", x2_[:], s5t["Gre"][:], Fi, ALU.mult, reads=["s5Gre", "s5tab", "s5Xim"], writes=["s5Xim"])
                    kb.I("dve", "tensor_tensor", s5t["Hre"][:], t1_[:], t2_[:], ALU.subtract, reads=["s5t1", "s5t2"], writes=["s5Hre"])
                    kb.I("pool", "tensor_tensor", s5t["Him"][:], x1_[:], x2_[:], ALU.add, reads=["s5Xre", "s5Xim"], writes=["s5Him"])
                    kb.I("dve", "tensor_copy", car_re[:, qs], s5t["Hre"][:].rearrange("p (q t) -> p q t", q=4)[:, :, 127], reads=["s5Hre"], writes=["car"])
                    kb.I("dve", "tensor_copy", car_im[:, qs], s5t["Him"][:].rearrange("p (q t) -> p q t", q=4)[:, :, 127], reads=["s5Him"], writes=["car"])
                    if own is not None:
                        for r_ in range(4):
                            q_ = a_ * 4 + r_
                            cs_ = slice(r_ * 128, (r_ + 1) * 128)
                            yb = PS[6][:, r_ * 32:(r_ + 1) * 32]
                            if sd == 0:
                                kb.I("pe", "matmul", yb, zub_[:, a_, :], diagd[:, a_, r_ * 32:(r_ + 1) * 32], start=True, stop=False, reads=[zuk, "diagd"], writes=["pX6_0"])
                            kb.I("pe", "matmul", yb, s5t["Hre"][:, cs_], Cw_re[:, q_, :], start=(sd != 0), stop=False, reads=["s5Hre", "s5tab"], writes=["pX6_0"])
                            kb.I("pe", "matmul", yb, s5t["Him"][:, cs_], Cw_imn[:, q_, :], start=False, stop=True, reads=["s5Him", "s5tab"], writes=["pX6_0"])
                        kb.cp("act", flpS[:, a_ * 128:(a_ + 1) * 128], PS[6][:, 0:128], reads=["pX6_0"], writes=["flpS"])
                if own is None:
                    F127r = F_re[:, :, 127]; F127i = F_im[:, :, 127]
                    kb.I("dve", "tensor_tensor", gs_re[:], gs_re[:], car_re[:], ALU.add, reads=["gs", "car"], writes=["gs"])
                    kb.I("dve", "tensor_tensor", gs_im[:], gs_im[:], car_im[:], ALU.add, reads=["gs", "car"], writes=["gs"])
                    kb.I("dve", "tensor_tensor", gta[:], gs_re[:], F127r, ALU.mult, reads=["gs", "s5tab"], writes=["gta"])
                    kb.I("dve", "tensor_tensor", gtb[:], gs_im[:], F127i, ALU.mult, reads=["gs", "s5tab"], writes=["gtb"])
                    kb.I("dve", "tensor_tensor", car_re[:], gta[:], gtb[:], ALU.subtract, reads=["gta", "gtb"], writes=["car"])
                    kb.I("dve", "tensor_tensor", gta[:], gs_im[:], F127r, ALU.mult, reads=["gs", "s5tab", "car"], writes=["gta"])
                    kb.I("dve", "tensor_tensor", gtb[:], gs_re[:], F127i, ALU.mult, reads=["gs", "s5tab", "car"], writes=["gtb"])
                    kb.I("dve", "tensor_tensor", car_im[:], gta[:], gtb[:], ALU.add, reads=["gta", "gtb"], writes=["car"])
                if dbg_here:
                    kb.D(dbg["d_car"][:, 0:16], car_re[:], reads=["car"], writes=["o_d_car"])
                    kb.D(dbg["d_car"][:, 16:32], car_im[:], reads=["car"], writes=["o_d_car"])
                if own is not None:
                    if dbg_here:
                        kb.D(dbg["d_ys"], flpS[:], reads=["flpS"], writes=["o_d_ys"])
                    dst = scr[f"ys{sd}"][own * 128:(own + 1) * 128, :]
                    if rev:
                        kb.I("pe", "matmul", PS[6][:], antiI[:], flpS[:], start=True, stop=True, reads=["flpS", "antiI"], writes=["pX6_0"])
                        kb.cp("dve", flpS[:], PS[6][:], reads=["pX6_0"], writes=["flpS"])
                    kb.D(dst, flpS[:], reads=["flpS"], writes=[f"scr_ys{sd}_{own}"])
                kb.rec = None
                return streams

            for sd in range(2 if not upto.startswith("p0") else 0):
                if upto == "setup":
                    break
                if sd == 1:
                    s5_setup(sd, F0)
                if upto == "s5setup":
                    break
                stk = [f"ST{f}" for f in range(4)]
                if sd == 0:
                    kb.I("pool", "memset", ST[:], 0.0, reads=stk, writes=stk)
                    kb.I("pool", "memset", car_re[:], 0.0, reads=["car"], writes=["car"])
                    kb.I("pool", "memset", car_im[:], 0.0, reads=["car"], writes=["car"])
                if sd == 0:
                    chunks = [(True, c, None) for c in range(n_ctx)] + [(False, c, c) for c in range(n_lat_chunks[0])]
                else:
                    stk = [f"ST{f}" for f in range(4)]
                    kb.D(scr["cc_in"][:, 0:256], ST[:].rearrange("p f v -> p (f v)"), reads=stk, writes=["cc_in"])
                    kb.D(scr["cc_in"][:, 256:272], car_re[:], reads=["car"], writes=["cc_in"])
                    kb.D(scr["cc_in"][:, 272:288], car_im[:], reads=["car"], writes=["cc_in"])
                    kb.I("pool", "collective_compute", "AllReduce", ALU.add, replica_groups=[[0, 1], [2, 3], [4, 5], [6, 7]],
                         ins=[scr["cc_in"]], outs=[scr["cc_out"]], reads=["cc_in"], writes=["cc_out"])
                    ccs = s5t["t1"]
                    kb.D(ccs[:, 0:288], scr["cc_out"], reads=["cc_out", "s5t1"], writes=["s5t1"])
                    kb.I("dve", "tensor_tensor", ST[:].rearrange("p f v -> p (f v)"), ccs[:, 0:256], ST[:].rearrange("p f v -> p (f v)"), ALU.subtract,
                         reads=["s5t1"] + stk, writes=stk)
                    kb.I("dve", "tensor_tensor", car_re[:], ccs[:, 256:272], car_re[:], ALU.subtract, reads=["s5t1", "car"], writes=["car"])
                    kb.I("dve", "tensor_tensor", car_im[:], ccs[:, 272:288], car_im[:], ALU.subtract, reads=["s5t1", "car"], writes=["car"])
                    chunks = [(False, c, 31 - c) for c in range(16, 16 + n_lat_chunks[1])]
                prev = None
                pend = []
                for k_, (is_ctx, c, own) in enumerate(chunks):
                    nxt = chunks[k_ + 1][:2] if k_ + 1 < len(chunks) else None
                    cur = process_chunk(sd, is_ctx, c, own, k_ % 2, nxt=(nxt if sd == 0 else None), first=(k_ == 0), reuse=(sd == 1))
                    pend.append(cur["front"])
                    if prev is not None:
                        pend += [prev["rwkv"], prev["s5"]]
                    prev = cur
                    if len(pend) >= SCHED_WINDOW_STREAMS:
                        kb.merge(*pend)
                        pend = []
                if prev is not None:
                    pend += [prev["rwkv"], prev["s5"]]
                kb.merge(*pend)
            kb.I("dve", "tensor_copy", ssum[:, 0:1], ssum[:, 0:1], reads=list(kb.lastw.keys()), writes=list(kb.lastw.keys()) + ["fence1"])
        FENCE = ["fence1"]
        wscope.close()

        if do_tail:
            with contextlib.ExitStack() as tl:
                wob = sb(tl, "wob", [128, 8, D], BF16)
                glb = sb(tl, "glb", [128, 4, 512], BF16); g2b = sb(tl, "g2b", [128, 512])
                stg = [sb(tl, f"stg{i}", [128, 1024]) for i in range(2)]
                jobs = [("w_out", wob, kt, 0, D) for kt in range(8)] + [("gluw", glb, kt, 0, 512) for kt in range(4)]
                for i, (nm, dst, kt, c0, ncol) in enumerate(jobs):
                    s_ = stg[i % 2]; sk = f"stg{i % 2}"
                    kb.D(s_[:, 0:ncol], din[nm][kt * 128:(kt + 1) * 128, c0:c0 + ncol], reads=FENCE, writes=[sk])
                    kb.cp(("dve", "act", "pool")[i % 3], dst[:, kt, c0:c0 + ncol], s_[:, 0:ncol], reads=[sk], writes=["tw"])
                kb.D(g2b[0:96, :], din["g2"], reads=FENCE, writes=["tw2"])
                rows = {}
                for nm in ("lnw", "lnb", "glub"):
                    rows[nm] = sb(tl, "row_" + nm, [128, IN_SHAPES[nm][1]])
                    kb.D(rows[nm][:], din[nm].partition_broadcast(128), reads=FENCE, writes=["rows"])
                gmix = sb(tl, "gmix", [128, D])

                def dbl(name, shape, dt=F32):
                    return [sb(tl, f"{name}_{i_}", shape, dt) for i_ in range(2)]
                x1 = dbl("x1", [128, D])
                kb.D(gmix[:], scr["mod"][:, 2 * D:3 * D], reads=FENCE + ["scr_mod"], writes=["modt"])
                tx = dbl("tx", [128, D]); yr_a = dbl("yr_a", [128, 512]); ys_a = dbl("ys_a", [128, 512]); bv_a = dbl("bv_a", [128, 512])
                yr_b = dbl("yr_b", [128, 512]); ys_b = dbl("ys_b", [128, 512]); bv_b = dbl("bv_b", [128, 512])
                sgt = dbl("sgt", [128, 128]); mix = dbl("mix", [128, D]); st8 = dbl("st8", [128, 8]); st8b = dbl("st8b", [128, 8])
                gt = dbl("gt", [128, 512]); gt2 = dbl("gt2", [128, 512]); zT = dbl("zT", [128, 4, 128], BF16); mixT = dbl("mixT", [128, 8, 128], BF16)
                TA = (tx, yr_a, ys_a, bv_a, yr_b, ys_b, bv_b, sgt, mix, st8, st8b, gt, gt2, zT, mixT, x1)
                recA = []
                kb.rec = recA
                for oc in range(NOWN):
                    sl_ = oc % 2
                    kb.ksuf = f"#{sl_}"
                    (tx, yr_a, ys_a, bv_a, yr_b, ys_b, bv_b, sgt, mix, st8, st8b, gt, gt2, zT, mixT, x1) = (t_[sl_] for t_ in TA)
                    rsl = slice(oc * 128, (oc + 1) * 128)
                    kb.D(tx[:], din["x"][rsl, :], reads=FENCE, writes=["tx"])
                    for t_, nm in ((yr_a, "yr0"), (yr_b, "yr1"), (ys_a, "ys0"), (ys_b, "ys1"), (bv_a, "bv0"), (bv_b, "bv1")):
                        kb.D(t_[:], scr[nm][rsl, :], reads=[f"scr_{nm}_{oc}"] + FENCE, writes=["t_" + nm])
                    kb.D(sgt[0:96, :], scr["sg"][oc * 128:oc * 128 + 96, :], reads=[f"scr_sg{oc}"] + FENCE, writes=["sgt"])
                    kb.I("dve", "tensor_tensor", yr_a[:], yr_a[:], yr_b[:], ALU.add, reads=["t_yr0", "t_yr1"], writes=["t_yr0"])
                    y3 = yr_a[:].rearrange("p (h n) -> p h n", n=64)
                    kb.I("dve", "tensor_reduce", st8[:], y3, AXX, ALU.add, reads=["t_yr0"], writes=["st8"])
                    kb.ts1("dve", st8[:], st8[:], 1.0 / 64, ALU.mult, reads=["st8"], writes=["st8"])
                    kb.I("dve", "tensor_tensor", y3, y3, st8[:].unsqueeze(2).to_broadcast([128, 8, 64]), ALU.subtract, reads=["st8", "t_yr0"], writes=["t_yr0"])
                    kb.I("pool", "tensor_tensor", gt2[:], yr_a[:], yr_a[:], ALU.mult, reads=["t_yr0"], writes=["gt2"])
                    kb.I("dve", "tensor_reduce", st8b[:], gt2[:].rearrange("p (h n) -> p h n", n=64), AXX, ALU.add, reads=["gt2"], writes=["st8b"])
                    kb.I("dve", "tensor_scalar", st8b[:], st8b[:], 1.0 / 64, 64e-5, ALU.mult, ALU.add, reads=["st8b"], writes=["st8b"])
                    kb.I("act", "activation", st8b[:], st8b[:], AF.Sqrt, reads=["st8b"], writes=["st8b"])
                    kb.I("dve", "reciprocal", st8b[:], st8b[:], reads=["st8b"], writes=["st8b"])
                    kb.I("dve", "tensor_tensor", y3, y3, st8b[:].unsqueeze(2).to_broadcast([128, 8, 64]), ALU.mult, reads=["st8b", "t_yr0"], writes=["t_yr0"])
                    kb.I("pool", "tensor_tensor", yr_a[:], yr_a[:], rows["lnw"][:], ALU.mult, reads=["rows", "t_yr0"], writes=["t_yr0"])
                    kb.I("pool", "tensor_tensor", yr_a[:], yr_a[:], rows["lnb"][:], ALU.add, reads=["rows", "t_yr0"], writes=["t_yr0"])
                    kb.I("dve", "tensor_tensor", bv_a[:], bv_a[:], bv_b[:], ALU.add, reads=["t_bv0", "t_bv1"], writes=["t_bv0"])
                    kb.I("dve", "tensor_tensor", yr_a[:], yr_a[:], bv_a[:], ALU.add, reads=["t_bv0", "t_yr0"], writes=["t_yr0"])
                    kb.I("pe", "matmul", PS[0][:], sgt[0:96, :], g2b[0:96, :], start=True, stop=True, reads=["sgt", "tw2"], writes=bk(0))
                    kb.I("dve", "tensor_tensor", mix[:, 0:512], yr_a[:], PS[0][:], ALU.mult, reads=bk(0) + ["t_yr0"], writes=["mixA"])
                    kb.I("dve", "tensor_tensor", ys_a[:], ys_a[:], ys_b[:], ALU.add, reads=["t_ys0", "t_ys1"], writes=["t_ys0"])
                    kb.I("pool", "tensor_tensor", gt[:], ys_a[:], ys_a[:], ALU.mult, reads=["t_ys0"], writes=["gt"])
                    kb.I("dve", "tensor_scalar", gt[:], gt[:], 0.044715, 1.0, ALU.mult, ALU.add, reads=["gt"], writes=["gt"])
                    kb.I("dve", "tensor_tensor", gt[:], gt[:], ys_a[:], ALU.mult, reads=["gt", "t_ys0"], writes=["gt"])
                    kb.I("act", "activation", gt[:], gt[:], AF.Tanh, scale=0.7978845608028654, reads=["gt"], writes=["gt"])
                    kb.I("dve", "tensor_scalar", gt[:], gt[:], 0.5, 0.5, ALU.mult, ALU.add, reads=["gt"], writes=["gt"])
                    kb.I("dve", "tensor_tensor", ys_a[:], ys_a[:], gt[:], ALU.mult, reads=["gt", "t_ys0"], writes=["t_ys0"])
                    for j in range(4):
                        kb.I("pe", "transpose", PS[1][:, j * 128:(j + 1) * 128], ys_a[:, j * 128:(j + 1) * 128], ident[:], reads=["t_ys0", "ident"], writes=[f"pX1_{j}"])
                    kb.cp("act", zT[:].rearrange("p j t -> p (j t)"), PS[1][:], reads=bk(1), writes=["zT"])
                    for j in range(4):
                        kb.I("pe", "matmul", PS[2][:], zT[:, j, :], glb[:, j, :], start=(j == 0), stop=(j == 3), reads=["zT", "tw"], writes=bk(2))
                    kb.I("dve", "tensor_tensor", gt[:], PS[2][:], rows["glub"][:], ALU.add, reads=bk(2) + ["rows", "gt"], writes=["gt"])
                    kb.I("act", "activation", gt[:], gt[:], AF.Sigmoid, reads=["gt"], writes=["gt"])
                    kb.I("dve", "tensor_tensor", mix[:, 512:1024], ys_a[:], gt[:], ALU.mult, reads=["gt", "t_ys0"], writes=["mixB"])
                    for half in range(2):
                        for j in range(4):
                            kt = half * 4 + j
                            kb.I("pe", "transpose", PS[3][:, j * 128:(j + 1) * 128], mix[:, kt * 128:(kt + 1) * 128], ident[:], reads=["mixA", "mixB", "ident"], writes=[f"pX3_{j}"])
                        kb.cp("act" if half else "dve", mixT[:, half * 4:half * 4 + 4, :].rearrange("p j t -> p (j t)"), PS[3][:], reads=bk(3), writes=["mixT"])
                    for nh in range(2):
                        ns = slice(nh * 512, (nh + 1) * 512)
                        for kt in range(8):
                            kb.I("pe", "matmul", PS[4 + nh][:], mixT[:, kt, :], wob[:, kt, ns], start=(kt == 0), stop=(kt == 7), reads=["mixT", "tw"], writes=bk(4 + nh))
                        kb.I("dve", "tensor_tensor", x1[:, ns], PS[4 + nh][:], gmix[:, ns], ALU.mult, reads=bk(4 + nh) + ["modt"], writes=["x1"])
                        kb.I("pool", "tensor_tensor", x1[:, ns], x1[:, ns], tx[:, ns], ALU.add, reads=["x1", "tx"], writes=["x1"])
                    if debug and oc == 0:
                        kb.D(dbg["d_x1"], x1[:], reads=["x1"], writes=["o_d_x1"])
                    kb.D(scr["x1"][rsl, :], x1[:], reads=["x1"], writes=[f"scr_x1_{oc}"])
                kb.rec = None
                kb.ksuf = None
                kb.merge(recA)
                st8 = TA[9][0]
                kb.I("dve", "tensor_copy", st8[:, 0:1], st8[:, 0:1], reads=list(kb.lastw.keys()), writes=list(kb.lastw.keys()) + ["fence2"])
            FENCE = ["fence2"]
            with contextlib.ExitStack() as tl:
                w1b = sb(tl, "w1b", [128, 8, DFF], BF16); w3b = sb(tl, "w3b", [128, 8, DFF], BF16)
                w2b = sb(tl, "w2b", [128, 22, D], BF16)
                stg = [sb(tl, f"stgb{i}", [128, 1024]) for i in range(2)]
                jobs = []
                for nm, dst in (("w1", w1b), ("w3", w3b)):
                    for kt in range(8):
                        for c0 in (0, 1024, 2048):
                            jobs.append((nm, dst, kt, c0, min(1024, DFF - c0)))
                jobs += [("w2f", w2b, kt, 0, D) for kt in range(22)]
                for i, (nm, dst, kt, c0, ncol) in enumerate(jobs):
                    s_ = stg[i % 2]; sk = f"stgb{i % 2}"
                    kb.D(s_[:, 0:ncol], din[nm][kt * 128:(kt + 1) * 128, c0:c0 + ncol], reads=FENCE, writes=[sk])
                    kb.cp(("dve", "act", "pool")[i % 3], dst[:, kt, c0:c0 + ncol], s_[:, 0:ncol], reads=[sk], writes=["tw"])
                rows = {"gf": sb(tl, "row_gf", [128, D])}
                kb.D(rows["gf"][:], din["gf"].partition_broadcast(128), reads=FENCE, writes=["rows"])
                A2 = sb(tl, "A2", [128, D]); sffn = sb(tl, "sffn", [128, D]); gffn = sb(tl, "gffn", [128, D])
                def dblb(name, shape, dt=F32):
                    return [sb(tl, f"{name}_{i_}", shape, dt) for i_ in range(2)]
                hh2 = dblb("hh", [128, D]); x12 = dblb("x1b", [128, D]); outt2 = dblb("outt", [128, D])
                hh = hh2[0]
                kb.D(A2[:], din["g2n"].partition_broadcast(128), reads=FENCE, writes=["A2"])
                kb.D(hh[:], scr["mod"][:, 4 * D:5 * D], reads=FENCE + ["scr_mod"], writes=["hh#0"])
                kb.D(sffn[:], scr["mod"][:, 3 * D:4 * D], reads=FENCE + ["scr_mod"], writes=["modt"])
                kb.D(gffn[:], scr["mod"][:, 5 * D:6 * D], reads=FENCE + ["scr_mod"], writes=["modt"])
                kb.I("dve", "scalar_tensor_tensor", A2[:], hh[:], 1.0, A2[:], ALU.add, ALU.mult, reads=["A2", "hh#0"], writes=["A2"])
                hhT2 = dblb("hhT", [128, 8, 128], BF16)
                actT2 = dblb("actT", [128, 22, 128], BF16); s12 = dblb("s1", [128, 256]); ss22 = dblb("ss2", [128, 2]); rs22 = dblb("rs2", [128, 2])
                print("SBUF bytes remaining in tail B scope:", nc.sbuf_bytes_remaining)
                recB = []
                kb.rec = recB
                for oc in range(NOWN):
                    sl_ = oc % 2
                    kb.ksuf = f"#{sl_}"
                    hh, x1, outt, hhT, actT, s1, ss2, rs2 = (t_[sl_] for t_ in (hh2, x12, outt2, hhT2, actT2, s12, ss22, rs22))
                    rsl = slice(oc * 128, (oc + 1) * 128)
                    kb.D(x1[:], scr["x1"][rsl, :], reads=[f"scr_x1_{oc}"], writes=["x1"])
                    kb.I("pool", "memset", ss2[:], 0.0, reads=FENCE, writes=["ss2"])
                    kb.I("act", "activation", hh[:], x1[:], AF.Square, accum_out=ss2[:, 0:1], reads=["x1", "A2"], writes=["hh", "ss2"])
                    kb.I("dve", "tensor_scalar", rs2[:, 0:1], ss2[:, 0:1], 1.0 / D, 1e-6, ALU.mult, ALU.add, reads=["ss2"], writes=["rs2"])
                    kb.I("act", "activation", rs2[:, 0:1], rs2[:, 0:1], AF.Sqrt, reads=["rs2"], writes=["rs2"])
                    kb.I("dve", "reciprocal", rs2[:, 0:1], rs2[:, 0:1], reads=["rs2"], writes=["rs2"])
                    kb.I("dve", "scalar_tensor_tensor", hh[:], x1[:], rs2[:, 0:1], A2[:], ALU.mult, ALU.mult, reads=["x1", "rs2", "A2", "hh"], writes=["hh"])
                    kb.I("pool", "tensor_tensor", hh[:], hh[:], sffn[:], ALU.add, reads=["hh", "modt"], writes=["hh"])
                    for half in range(2):
                        for j in range(4):
                            kt = half * 4 + j
                            kb.I("pe", "transpose", PS[3][:, j * 128:(j + 1) * 128], hh[:, kt * 128:(kt + 1) * 128], ident[:], reads=["hh", "ident"], writes=[f"pX3_{j}"])
                        kb.cp("act" if half else "dve", hhT[:, half * 4:half * 4 + 4, :].rearrange("p j t -> p (j t)"), PS[3][:], reads=bk(3), writes=["hhT"])
                    for ftf in range(22):
                        sl = ftf % 2
                        fs = slice(ftf * 128, (ftf + 1) * 128)
                        bq = (0, 1, 2)[ftf % 3]
                        pa = PS[bq][:, 0:128]; pb = PS[bq][:, 128:256]
                        s1_ = s1[:, (ftf % 2) * 128:(ftf % 2) * 128 + 128]
                        for kt in range(8):
                            kb.I("pe", "matmul", pa, w1b[:, kt, fs], hhT[:, kt, :], start=(kt == 0), stop=(kt == 7), reads=["hhT", "tw"], writes=[f"pX{bq}_0"])
                        for kt in range(8):
                            kb.I("pe", "matmul", pb, w3b[:, kt, fs], hhT[:, kt, :], start=(kt == 0), stop=(kt == 7), reads=["hhT", "tw"], writes=[f"pX{bq}_0"])
                        kb.I("act", "activation", s1_, pa, AF.Silu, reads=[f"pX{bq}_0"], writes=[f"s1_{ftf % 2}"])
                        kb.I("dve", "tensor_tensor", actT[:, ftf, :], s1_, pb, ALU.mult, reads=[f"pX{bq}_0", f"s1_{ftf % 2}"], writes=[f"actT{ftf}"])
                    for nh in range(2):
                        ns = slice(nh * 512, (nh + 1) * 512)
                        bd = 4 + 2 * sl_ + nh
                        for ftf in range(22):
                            kb.I("pe", "matmul", PS[bd][:], actT[:, ftf, :], w2b[:, ftf, ns], start=(ftf == 0), stop=(ftf == 21), reads=[f"actT{ftf}", "tw"], writes=bk(bd))
                        kb.I("dve", "tensor_tensor", outt[:, ns], PS[bd][:], gffn[:, ns], ALU.mult, reads=bk(bd) + ["modt"], writes=["outt"])
                        kb.I("pool", "tensor_tensor", outt[:, ns], outt[:, ns], x1[:, ns], ALU.add, reads=["outt", "x1"], writes=["outt"])
                    kb.I("act", "activation", hh[:], outt[:], AF.Square, accum_out=ss2[:, 1:2], reads=["outt", "hh"], writes=["hh", "ss2"])
                    kb.I("dve", "tensor_scalar", rs2[:, 1:2], ss2[:, 1:2], 1.0 / D, 1e-6, ALU.mult, ALU.add, reads=["ss2"], writes=["rs2"])
                    kb.I("act", "activation", rs2[:, 1:2], rs2[:, 1:2], AF.Sqrt, reads=["rs2"], writes=["rs2"])
                    kb.I("dve", "reciprocal", rs2[:, 1:2], rs2[:, 1:2], reads=["rs2"], writes=["rs2"])
                    kb.I("dve", "scalar_tensor_tensor", outt[:], outt[:], rs2[:, 1:2], rows["gf"][:], ALU.mult, ALU.mult, reads=["outt", "rs2", "rows"], writes=["outt"])
                    kb.D(out_d[rsl, :], outt[:], reads=["outt"], writes=[f"o_out{oc}"])
                kb.rec = None
                kb.ksuf = None
                kb.merge(recB)
                kb.emit(final_keys=[k for k in kb.lastw if k.startswith("o_")])
        else:
            kb.emit(final_keys=[k for k in kb.lastw if k.startswith("o_")] + FENCE)
    return nc


def make_in_maps(inp):
    f = np.float32
    ident = np.eye(128, dtype=f)
    antiI = np.ascontiguousarray(ident[::-1])
    strict = np.triu(np.ones((128, 128), f), 1)
    incl = np.triu(np.ones((128, 128), f), 0)
    consts = {
        "ident": ident, "antiI": antiI,
        "mask_b": np.concatenate([strict, -incl], axis=1), "mask_k": np.concatenate([strict, incl], axis=1),
        "maskT": np.ascontiguousarray(strict.T), "bones": np.kron(np.eye(2, dtype=f), np.ones((64, 64), f)),
    }
    maps = []
    for core in range(8):
        b, hf = core // 2, core % 2
        dsel = [1, 0] if hf else [0, 1]
        x = inp["x"][b]; ctx = inp["ctx"][b]
        conv = inp["rwkv_conv"][0]
        w_in = inp["w_in"][0]
        if hf:
            x = x[::-1]; ctx = ctx[::-1]; conv = conv[::-1, ::-1]
            perm = np.arange(2272)
            perm[1536:1568], perm[1568:1600] = np.arange(1568, 1600), np.arange(1536, 1568)
            perm[1600:1632], perm[1632:1664] = np.arange(1632, 1664), np.arange(1600, 1632)
            w_in = w_in[:, perm]
        m = {
            "x": x, "ctx": ctx, "cc": np.stack([inp["c"][b], inp["c_ctx"]]),
            "mod_w": inp["mod_w"][0], "mod_b": inp["mod_b"][0][None], "g1": inp["norm1_g"][0][None],
            "g2n": inp["norm2_g"][0][None], "gf": inp["final_g"][None], "w_in": w_in, "w_out": inp["w_out"][0],
            "conv": conv.reshape(9, 1536), "w0": inp["rwkv_w0"][0][dsel], "w2": inp["rwkv_w2"][0][dsel],
            "a0": inp["rwkv_a0"][0][dsel], "a2": inp["rwkv_a2"][0][dsel], "g2": inp["rwkv_g2"][0],
            "kkv": inp["rwkv_kk"][0], "kav": inp["rwkv_ka"][0], "rkv": inp["rwkv_rk"][0].reshape(512),
            "lnw": inp["rwkv_ln_w"][0][None], "lnb": inp["rwkv_ln_b"][0][None],
            "lam_re": inp["s5_lam_re"][0][dsel], "lam_im": inp["s5_lam_im"][0][dsel], "lstep": inp["s5_log_step"][0][dsel],
            "b_re": inp["s5_b_re"][0], "b_im": inp["s5_b_im"][0], "c_re": inp["s5_c_re"][0], "c_im": inp["s5_c_im"][0],
            "s5d": inp["s5_d"][0], "gluw": inp["s5_glu_w"][0], "glub": inp["s5_glu_b"][0][None],
            "w1": inp["ffn_w1"][0], "w3": inp["ffn_w3"][0], "w2f": inp["ffn_w2"][0],
        }
        m.update(consts)
        maps.append({k: np.ascontiguousarray(np.asarray(v, dtype=f)).reshape(IN_SHAPES[k]) for k, v in m.items()})
    return maps


def kernel(**inputs):
    inp = {k: np.asarray(v) for k, v in inputs.items()}
    nc = build_nc()
    maps = make_in_maps(inp)
    res = run_bass_kernel_spmd(nc, maps, core_ids=list(range(8)))
    out = np.zeros((4, T_LAT, D), np.float32)
    for core in range(8):
        b, hf = core // 2, core % 2
        o = np.asarray(res.results[core]["out"], dtype=np.float32)
        if hf:
            out[b, OWN:] = o[::-1]
        else:
            out[b, :OWN] = o
    return out
```

```python
import contextlib
import numpy as np
import concourse.bass as bass
import concourse.mybir as mybir
from concourse.bass_utils import run_bass_kernel_spmd

F32 = mybir.dt.float32
BF16 = mybir.dt.bfloat16
ALU = mybir.AluOpType
AF = mybir.ActivationFunctionType
AXX = mybir.AxisListType.X

SEM_CAP = 16000
N_DMA_SEM = 24
SCHED_WINDOW_STREAMS = 7
SAME_ENG_WAIT = True

T_LAT, T_CTX, D, DFF = 4096, 256, 1024, 2816
OWN = 2048
NOWN = OWN // 128
PI = float(np.pi)
KDEC = 0.6065306597126334


class KB:
    ENGS = ("pe", "dve", "act", "pool", "sp")

    def __init__(self, nc):
        self.nc = nc
        self.ops = {e: [] for e in self.ENGS}
        self.lastw = {}
        self.readers = {}
        self.ndma = 0
        self.rr = 0

    @staticmethod
    def _norm(reads, writes):
        r2 = [k for k in reads if not k.startswith("pX")]
        w2 = [k for k in writes if not k.startswith("pX")]
        banks = {"bank" + k[2:].split("_")[0] for k in list(reads) + list(writes) if k.startswith("pX")}
        return r2, w2 + sorted(banks)

    def _deps(self, me, reads, writes):
        reads, writes = self._norm(reads, writes)
        deps = set()
        for k in reads:
            w = self.lastw.get(k)
            if w is not None:
                deps.add(w)
        for k in writes:
            w = self.lastw.get(k)
            if w is not None:
                deps.add(w)
            for r in self.readers.get(k, ()):
                deps.add(r)
        deps.discard(me)
        for k in reads:
            self.readers.setdefault(k, []).append(me)
        for k in writes:
            self.lastw[k] = me
            self.readers[k] = []
        return deps

    mute = False
    rec = None
    ksuf = None
    KGLOBAL = ("pX", "scr_", "o_", "fence", "rows", "tw", "modt", "ident", "A2")

    def _sfx(self, keys):
        if not self.ksuf:
            return list(keys)
        return [k if k.startswith(self.KGLOBAL) else k + self.ksuf for k in keys]

    @staticmethod
    def _est(op):
        def nfree(ap):
            n = 1
            for d in ap.shape[1:]:
                n *= d
            return n
        if op[0] == "D":
            ap = op[1]
            nbytes = nfree(ap) * ap.shape[0] * (2 if ap.dtype == BF16 else 4)
            return "sp", 0.08, 2.2 + nbytes / 1.0e5
        eng, meth, args = op[1], op[2], op[3]
        n = nfree(args[0])
        if eng == "pe":
            f32 = args[1].dtype == F32
            d = 0.09 + n * (0.0017 if f32 else 0.00045)
        elif eng == "dve":
            d = 0.25 + n * 0.00104 * (6.0 if meth == "reciprocal" else 1.0)
        elif eng == "act":
            d = 0.2 + n * 0.00104
        else:
            d = 0.45 + n * 0.0026
        return eng, d, d

    def merge(self, *streams):
        ops = [op for st_ in streams for op in st_]
        n = len(ops)
        lastw, readers = {}, {}
        preds = [set() for _ in range(n)]
        for i, op in enumerate(ops):
            r_, w_ = (op[5], op[6]) if op[0] == "I" else (op[4], op[5])
            r_, w_ = self._norm(r_, w_)
            for k in r_:
                if k in lastw:
                    preds[i].add(lastw[k])
            for k in w_:
                if k in lastw:
                    preds[i].add(lastw[k])
                preds[i].update(readers.get(k, ()))
            preds[i].discard(i)
            for k in r_:
                readers.setdefault(k, []).append(i)
            for k in w_:
                lastw[k] = i
                readers[k] = []
        succs = [[] for _ in range(n)]
        for i in range(n):
            for p in preds[i]:
                succs[p].append(i)
        est = [self._est(op) for op in ops]
        cp = [0.0] * n
        for i in range(n - 1, -1, -1):
            cp[i] = est[i][2] + max((cp[j] for j in succs[i]), default=0.0)
        npred = [len(p) for p in preds]
        ready = [i for i in range(n) if npred[i] == 0]
        fin = [0.0] * n
        eng_free = {e: 0.0 for e in self.ENGS}
        LAT = 1.0
        while ready:
            best, bkey = None, None
            for i in ready:
                e = est[i][0]
                t = eng_free[e]
                for p in preds[i]:
                    tp = fin[p] + (0.05 if est[p][0] == e else LAT)
                    if tp > t:
                        t = tp
                key = (t - 0.02 * cp[i], i)
                if bkey is None or key < bkey:
                    best, bkey, bt = i, key, t
            i = best
            ready.remove(i)
            e = est[i][0]
            eng_free[e] = bt + est[i][1]
            fin[i] = bt + est[i][2]
            op = ops[i]
            if op[0] == "I":
                self.I(op[1], op[2], *op[3], reads=op[5], writes=op[6], **op[4])
            else:
                self.D(op[1], op[2], reads=op[4], writes=op[5], **op[3])
            for j in succs[i]:
                npred[j] -= 1
                if npred[j] == 0:
                    ready.append(j)

    def I(self, eng, meth, *args, reads=(), writes=(), **kw):
        if self.mute:
            return
        if self.rec is not None:
            self.rec.append(("I", eng, meth, args, kw, self._sfx(reads), self._sfx(writes)))
            return
        idx = len(self.ops[eng])
        deps = self._deps((eng, idx), list(reads), list(writes))
        self.ops[eng].append(((meth, args, kw), deps, None))

    def D(self, out, in_, reads=(), writes=(), **kw):
        if self.mute:
            return
        if self.rec is not None:
            self.rec.append(("D", out, in_, dict(kw), self._sfx(reads), self._sfx(writes)))
            return
        k = self.ndma
        self.ndma += 1
        deps = self._deps(("dma", k), list(reads), list(writes))
        kw = dict(kw)
        kw["out"] = out
        kw["in_"] = in_
        self.ops["sp"].append((("dma_start", (), kw), deps, k))

    def cp(self, eng, out, in_, reads=(), writes=()):
        self.I(eng, "copy" if eng == "act" else "tensor_copy", out, in_, reads=reads, writes=writes)

    def ts1(self, eng, out, in0, s, op, reads=(), writes=()):
        self.I(eng, "tensor_scalar", out, in0, s, 0.0, op, ALU.add, reads=reads, writes=writes)

    def ew(self):
        self.rr += 1
        return "dve" if (self.rr % 3) else "pool"

    def ev(self):
        self.rr += 1
        return "dve" if (self.rr % 2) else "act"

    def emit(self, final_keys=()):
        nc = self.nc
        me = ("sp", len(self.ops["sp"]))
        deps = self._deps(me, list(final_keys), [])
        self.ops["sp"].append((None, deps, None))
        nsem = {e: (len(self.ops[e]) + SEM_CAP - 1) // SEM_CAP + 1 for e in self.ENGS}
        with contextlib.ExitStack() as st:
            sems = {e: [st.enter_context(nc.semaphore(f"s_{e}{i}")) for i in range(nsem[e])]
                    for e in self.ENGS}
            dsems = [st.enter_context(nc.semaphore(f"s_dma{i}")) for i in range(N_DMA_SEM)]
            block = st.enter_context(nc.Block())

            def waitspec(p):
                if p[0] == "dma":
                    k = p[1]
                    return ("d", k % N_DMA_SEM), dsems[k % N_DMA_SEM], 16 * (k // N_DMA_SEM + 1)
                e, i = p
                return (e, i // SEM_CAP), sems[e][i // SEM_CAP], i % SEM_CAP + 1

            def run(ename, eobj):
                waited = {}
                for idx, (fn, deps, dk) in enumerate(self.ops[ename]):
                    specs = []
                    for p in deps:
                        if p[0] == ename and (ename == "pe" or not SAME_ENG_WAIT):
                            continue
                        specs.append(waitspec(p))
                    if dk is not None and dk >= N_DMA_SEM:
                        specs.append((("d", dk % N_DMA_SEM), dsems[dk % N_DMA_SEM],
                                      16 * (dk // N_DMA_SEM)))
                    for sid, sem, val in specs:
                        if waited.get(sid, 0) >= val:
                            continue
                        waited[sid] = val
                        eobj.wait_ge(sem, val)
                    if fn is None:
                        continue
                    meth, args, kw = fn
                    ins = getattr(eobj, meth)(*args, **kw)
                    if dk is not None:
                        ins.then_inc(dsems[dk % N_DMA_SEM], 16)
                    else:
                        ins.then_inc(sems[ename][idx // SEM_CAP], 1)

            @block.tensor
            def _(e):
                run("pe", e)

            @block.vector
            def _(e):
                run("dve", e)

            @block.scalar
            def _(e):
                run("act", e)

            @block.gpsimd
            def _(e):
                run("pool", e)

            @block.sync
            def _(e):
                run("sp", e)


IN_SHAPES = {
    "x": [T_LAT, D], "ctx": [T_CTX, D], "cc": [2, D], "mod_w": [D, 6 * D], "mod_b": [1, 6 * D],
    "g1": [1, D], "g2n": [1, D], "gf": [1, D], "w_in": [D, 2272], "w_out": [D, D],
    "conv": [9, 1536], "w0": [2, 512], "w2": [2, 32, 512], "a0": [2, 512], "a2": [2, 32, 512],
    "g2": [96, 512], "kkv": [512], "kav": [512], "rkv": [512], "lnw": [1, 512], "lnb": [1, 512],
    "lam_re": [2, 32, 64], "lam_im": [2, 32, 64], "lstep": [2, 32],
    "b_re": [32, 64, 16], "b_im": [32, 64, 16], "c_re": [32, 16, 64], "c_im": [32, 16, 64],
    "s5d": [512], "gluw": [512, 512], "glub": [1, 512],
    "w1": [D, DFF], "w3": [D, DFF], "w2f": [DFF, D],
    "ident": [128, 128], "antiI": [128, 128], "mask_b": [128, 256], "mask_k": [128, 256],
    "maskT": [128, 128], "bones": [128, 128],
}

DBG_SHAPES = {"d_mod": [128, 6 * D], "d_AB": [128, 32], "d_z": [128, 3 * 256], "d_zu": [128, 512],
              "d_rkvc": [128, 3 * 128], "d_yT": [128, 512], "d_ST": [128, 256], "d_ys": [128, 512],
              "d_x1": [128, D], "d_car": [128, 32], "d_F": [128, 2048], "d_Bw": [128, 2048]}


def build_nc(n_lat_chunks=(16, 16), do_tail=True, debug=False, upto="full", n_ctx=2):
    nc = bass.Bass("TRN2", target_bir_lowering=False)
    din = {k: nc.dram_tensor(k, s, F32, kind="ExternalInput").ap() for k, s in IN_SHAPES.items()}
    out_d = nc.dram_tensor("out", [OWN, D], F32, kind="ExternalOutput").ap()
    scr = {}
    for nm in ("yr0", "yr1", "ys0", "ys1", "bv0", "bv1"):
        scr[nm] = nc.dram_tensor("scr_" + nm, [OWN, 512], F32, kind="Internal").ap()
    scr["sg"] = nc.dram_tensor("scr_sg", [NOWN * 128, 128], F32, kind="Internal").ap()
    scr["x1"] = nc.dram_tensor("scr_x1", [OWN, D], F32, kind="Internal").ap()
    scr["mod"] = nc.dram_tensor("scr_mod", [128, 6 * D], F32, kind="Internal").ap()
    scr["dg"] = nc.dram_tensor("scr_dg", [12, 128, 9 * 128], BF16, kind="Internal").ap()
    scr["fr"] = nc.dram_tensor("scr_fr", [NOWN * 4 * 128, 512], F32, kind="Internal").ap()
    scr["fz"] = nc.dram_tensor("scr_fz", [NOWN * 128, 128], F32, kind="Internal").ap()
    scr["fu"] = nc.dram_tensor("scr_fu", [NOWN * 128, 512], BF16, kind="Internal").ap()
    scr["cc_in"] = nc.dram_tensor("scr_cc_in", [128, 288], F32, kind="Internal").ap()
    scr["cc_out"] = nc.dram_tensor("scr_cc_out", [128, 288], F32, kind="Internal").ap()
    dbg = {}
    if debug:
        for nm, shp in DBG_SHAPES.items():
            dbg[nm] = nc.dram_tensor(nm, shp, F32, kind="ExternalOutput").ap()
    kb = KB(nc)
    cnt = [0]

    with contextlib.ExitStack() as g:
        def sb(st, name, shape, dt=F32):
            cnt[0] += 1
            return st.enter_context(nc.sbuf_tensor(f"sb{cnt[0]}_{name}", shape, dt))

        ident = sb(g, "ident", [128, 128]); antiI = sb(g, "antiI", [128, 128])
        mask_b = sb(g, "mask_b", [128, 256]); mask_k = sb(g, "mask_k", [128, 256])
        maskT = sb(g, "maskT", [128, 128]); bones = sb(g, "bones", [128, 128])
        ones = sb(g, "ones", [128, 128])
        for nm, t in (("ident", ident), ("antiI", antiI), ("mask_b", mask_b), ("mask_k", mask_k),
                      ("maskT", maskT), ("bones", bones)):
            kb.D(t[:], din[nm], writes=[nm])
        kb.I("pool", "memset", ones[:], 1.0, writes=["ones"])
        AB = sb(g, "AB", [128, 4, 8])
        cnt[0] += 1
        PS = [g.enter_context(nc.psum_tensor(f"psbank{i}", [128, 512], F32)) for i in range(8)]

        def bk(i):
            return [f"pX{i}_{j}" for j in range(4)]

        wscope = contextlib.ExitStack()
        win = sb(wscope, "win", [128, 8, 2272], BF16)
        car_re = sb(wscope, "car_re", [128, 16]); car_im = sb(wscope, "car_im", [128, 16])
        gs_re = sb(wscope, "gs_re", [128, 16]); gs_im = sb(wscope, "gs_im", [128, 16]); gta = sb(wscope, "gta", [128, 16]); gtb = sb(wscope, "gtb", [128, 16])
        Bw_re = sb(wscope, "Bw_re", [128, 16, 128], BF16); Bw_im = sb(wscope, "Bw_im", [128, 16, 128], BF16)
        Cw_re = sb(wscope, "Cw_re", [128, 16, 32]); Cw_imn = sb(wscope, "Cw_imn", [128, 16, 32])
        E_re = sb(wscope, "E_re", [128, 16, 128]); E_im = sb(wscope, "E_im", [128, 16, 128])
        F_re = sb(wscope, "F_re", [128, 16, 128]); F_im = sb(wscope, "F_im", [128, 16, 128])
        s5t = {nm: sb(wscope, "s5" + nm, [128, 512]) for nm in ("t1", "t2", "Xre", "Xim", "Gre", "Gim", "Hre", "Him")}

        def s5_setup(sd, F0=()):
            K = "s5s"
            S5K = ["s5t1", "s5t2", "s5Xre", "s5Xim", "s5Gre", "s5Gim", "s5Hre", "s5Him", "s5tab", K]
            with contextlib.ExitStack() as t_:
                sm = sb(t_, "s5sm", [128, 20, 16])
                (lre, lim, stp, th, th2, tmpc, mag, imag, sn, csn, lbr, lbi, den, nr, qre, qim, u1, ivr, ivi) = [sm[:, i_, :] for i_ in range(19)]
                bre = s5t["Gre"][:, 0:256].rearrange("p (q h) -> p q h", h=16); bim = s5t["Gre"][:, 256:512].rearrange("p (q h) -> p q h", h=16)
                bbr = s5t["Gim"][:, 0:256].rearrange("p (q h) -> p q h", h=16); bbi = s5t["Gim"][:, 256:512].rearrange("p (q h) -> p q h", h=16)
                v1 = s5t["Hre"][:, 0:256].rearrange("p (q h) -> p q h", h=16); cst = s5t["Hre"][:, 256:512].rearrange("p (q h) -> p q h", h=16)
                bdw = s5t["Xre"][:].rearrange("p (r c) -> p r c", c=128)

                def V(meth, *args, eng="dve", **kw):
                    kb.I(eng, meth, *args, reads=S5K, writes=S5K, **kw)

                kb.D(lre, din["lam_re"][sd].rearrange("(q g) p -> (g p) q", g=2), reads=list(F0) + S5K, writes=S5K, allow_slow_non_contiguous=True)
                kb.D(lim, din["lam_im"][sd].rearrange("(q g) p -> (g p) q", g=2), reads=list(F0), writes=[K], allow_slow_non_contiguous=True)
                for g2_ in range(2):
                    kb.D(stp[64 * g2_:64 * g2_ + 64, :],
                         din["lstep"][sd:sd + 1, :].rearrange("o (q g) -> o q g", g=2)[:, :, g2_].partition_broadcast(64),
                         reads=list(F0), writes=[K], allow_slow_non_contiguous=True)
                kb.D(bre, din["b_re"].rearrange("(q g) p h -> (g p) q h", g=2), reads=list(F0), writes=[K])
                kb.D(bim, din["b_im"].rearrange("(q g) p h -> (g p) q h", g=2), reads=list(F0), writes=[K])
                V("activation", stp, stp, AF.Exp, eng="act")
                V("tensor_tensor", mag, lre, stp, ALU.mult)
                V("tensor_tensor", th, lim, stp, ALU.mult)
                V("activation", imag, mag, AF.Exp, eng="act", scale=-1.0)
                V("activation", mag, mag, AF.Exp, eng="act")
                V("tensor_copy", u1, th)
                for m in (PI, 3 * PI, 5 * PI):
                    V("tensor_scalar", tmpc, th, m, -2 * PI, ALU.is_ge, ALU.mult)
                    V("tensor_tensor", u1, u1, tmpc, ALU.add)
                V("tensor_scalar", u1, u1, 0.125, 0.0, ALU.mult, ALU.add)
                V("tensor_tensor", th2, u1, u1, ALU.mult)
                V("tensor_scalar", sn, th2, -1.0 / 5040, 1.0 / 120, ALU.mult, ALU.add)
                V("tensor_tensor", sn, sn, th2, ALU.mult)
                V("tensor_scalar", sn, sn, -1.0 / 6, 0.0, ALU.add, ALU.add)
                V("tensor_tensor", sn, sn, th2, ALU.mult)
                V("tensor_scalar", sn, sn, 1.0, 0.0, ALU.add, ALU.add)
                V("tensor_tensor", sn, sn, u1, ALU.mult)
                V("tensor_scalar", csn, th2, 1.0 / 40320, -1.0 / 720, ALU.mult, ALU.add)
                V("tensor_tensor", csn, csn, th2, ALU.mult)
                V("tensor_scalar", csn, csn, 1.0 / 24, 0.0, ALU.add, ALU.add)
                V("tensor_tensor", csn, csn, th2, ALU.mult)
                V("tensor_scalar", csn, csn, -0.5, 0.0, ALU.add, ALU.add)
                V("tensor_tensor", csn, csn, th2, ALU.mult)
                V("tensor_scalar", csn, csn, 1.0, 0.0, ALU.add, ALU.add)
                for _ in range(3):
                    V("tensor_tensor", tmpc, csn, sn, ALU.mult)
                    V("tensor_tensor", th2, sn, sn, ALU.mult)
                    V("tensor_tensor", csn, csn, csn, ALU.mult)
                    V("tensor_tensor", csn, csn, th2, ALU.subtract)
                    V("tensor_scalar", sn, tmpc, 2.0, 0.0, ALU.mult, ALU.add)
                V("tensor_tensor", lbr, mag, csn, ALU.mult)
                V("tensor_tensor", lbi, mag, sn, ALU.mult)
                V("tensor_tensor", ivr, imag, csn, ALU.mult)
                V("scalar_tensor_tensor", ivi, imag, -1.0, sn, ALU.mult, ALU.mult)
                V("tensor_tensor", den, lre, lre, ALU.mult)
                V("tensor_tensor", u1, lim, lim, ALU.mult)
                V("tensor_tensor", den, den, u1, ALU.add)
                V("reciprocal", den, den)
                V("tensor_scalar", nr, lbr, -1.0, 0.0, ALU.add, ALU.add)
                V("tensor_tensor", qre, nr, lre, ALU.mult)
                V("tensor_tensor", u1, lbi, lim, ALU.mult)
                V("tensor_tensor", qre, qre, u1, ALU.add)
                V("tensor_tensor", qre, qre, den, ALU.mult)
                V("tensor_tensor", qim, lbi, lre, ALU.mult)
                V("tensor_tensor", u1, nr, lim, ALU.mult)
                V("tensor_tensor", qim, qim, u1, ALU.subtract)
                V("tensor_tensor", qim, qim, den, ALU.mult)
                qreb = qre.unsqueeze(2).to_broadcast([128, 16, 16]); qimb = qim.unsqueeze(2).to_broadcast([128, 16, 16])
                V("tensor_tensor", bbr, bre, qreb, ALU.mult)
                V("tensor_tensor", v1, bim, qimb, ALU.mult)
                V("tensor_tensor", bbr, bbr, v1, ALU.subtract)
                V("tensor_tensor", bbi, bim, qreb, ALU.mult)
                V("tensor_tensor", v1, bre, qimb, ALU.mult)
                V("tensor_tensor", bbi, bbi, v1, ALU.add)
                for src_, dstT in ((bbr, Bw_re), (bbi, Bw_im)):
                    for qq in range(4):
                        V("memset", bdw, 0.0)
                        for r_ in range(4):
                            for g2_ in range(2):
                                c0 = 32 * r_ + 16 * g2_
                                V("tensor_copy", bdw[64 * g2_:64 * g2_ + 64, r_, c0:c0 + 16], src_[64 * g2_:64 * g2_ + 64, qq * 4 + r_, :])
                        for r_ in range(4):
                            kb.I("pe", "transpose", PS[4][:, r_ * 128:(r_ + 1) * 128], bdw[:, r_, :], ident[:],
                                 reads=S5K + ["ident"], writes=[f"pX4_{r_}"])
                        kb.I("dve", "tensor_copy", dstT[:, qq * 4:(qq + 1) * 4, :], PS[4][:].rearrange("p (r c) -> p r c", r=4),
                             reads=bk(4) + S5K, writes=S5K)
                cpad = s5t["Xim"][:].rearrange("p (j c) -> p j c", c=128)
                for nm, dstC, sc in (("c_re", Cw_re, 1.0), ("c_im", Cw_imn, -1.0)):
                    csrc = din[nm].rearrange("g h p -> (g h) p").rearrange("(j r) p -> r j p", r=128)
                    for hh_ in range(2):
                        kb.D(cpad[:, :, 64 * hh_:64 * hh_ + 64], csrc, reads=S5K, writes=S5K)
                    for j_ in range(4):
                        kb.I("pe", "transpose", PS[4][:, j_ * 128:(j_ + 1) * 128], cpad[:, j_, :], ident[:], reads=S5K + ["ident"], writes=[f"pX4_{j_}"])
                    V("memset", dstC[:], 0.0)
                    for g2_ in range(2):
                        srcv = PS[4][64 * g2_:64 * g2_ + 64, :].rearrange("p (q g h) -> p q g h", g=2, h=16)[:, :, g2_, :]
                        kb.I("dve", "tensor_scalar", dstC[64 * g2_:64 * g2_ + 64, :, 16 * g2_:16 * g2_ + 16], srcv, sc, 0.0, ALU.mult, ALU.add,
                             reads=bk(4) + S5K, writes=S5K)
                for (Tr, Ti, sr, si) in ((F_re, F_im, lbr, lbi), (E_re, E_im, ivr, ivi)):
                    V("tensor_copy", Tr[:, :, 0:1], sr.unsqueeze(2))
                    V("tensor_copy", Ti[:, :, 0:1], si.unsqueeze(2))
                    n_ = 1
                    while n_ < 128:
                        for qh in range(2):
                            qsl = slice(8 * qh, 8 * qh + 8)
                            pr = Tr[:, qsl, n_ - 1:n_].to_broadcast([128, 8, n_]); pim = Ti[:, qsl, n_ - 1:n_].to_broadcast([128, 8, n_])
                            t1_ = s5t["t1"][:, 0:8 * n_].rearrange("p (q n) -> p q n", q=8)
                            t2_ = s5t["t2"][:, 0:8 * n_].rearrange("p (q n) -> p q n", q=8)
                            V("tensor_tensor", t1_, Tr[:, qsl, 0:n_], pr, ALU.mult)
                            V("tensor_tensor", t2_, Ti[:, qsl, 0:n_], pim, ALU.mult)
                            V("tensor_tensor", Tr[:, qsl, n_:2 * n_], t1_, t2_, ALU.subtract)
                            V("tensor_tensor", t1_, Tr[:, qsl, 0:n_], pim, ALU.mult)
                            V("tensor_tensor", t2_, Ti[:, qsl, 0:n_], pr, ALU.mult)
                            V("tensor_tensor", Ti[:, qsl, n_:2 * n_], t1_, t2_, ALU.add)
                        n_ *= 2
                if debug and sd == 0:
                    kb.D(dbg["d_F"], F_re[:].rearrange("p q t -> p (q t)"), reads=S5K, writes=["o_d_F"])
                V("tensor_copy", lre[:, 0:1], lre[:, 0:1])

        with contextlib.ExitStack() as p0:
            rec0 = []
            kb.rec = rec0
            stage = [sb(p0, f"stage{i}", [128, 2272]) for i in range(2)]
            for kt in range(8):
                s_ = stage[kt % 2]; sk = f"stage{kt % 2}"
                kb.D(s_[:], din["w_in"][kt * 128:(kt + 1) * 128, :], writes=[sk])
                kb.cp("act" if kt % 2 else "pool", win[:, kt, :], s_[:], reads=[sk], writes=["win"])
            modb = sb(p0, "modb", [128, 6 * D])
            cT = sb(p0, "cT", [128, 2, 8]); cs = sb(p0, "cs", [128, 2, 8])
            lbx = sb(p0, "lbx", [128, 8, 128]); lbc = sb(p0, "lbc", [128, 8, 128])
            modc = sb(p0, "modc", [128, 2 * D]); mbias = sb(p0, "mbias", [128, 6 * D])
            g1b = sb(p0, "g1b", [128, D]); tmpA = sb(p0, "tmpA", [128, D])
            wbuf = [sb(p0, f"wbuf{i}", [128, 512]) for i in range(4)]
            for j_ in range(2):
                kb.D(cT[:, j_, :], din["cc"][j_].rearrange("(k p) -> p k", p=128), writes=["cT"], allow_slow_non_contiguous=True)
            kb.D(mbias[:], din["mod_b"].partition_broadcast(128), writes=["mbias"])
            kb.D(g1b[:], din["g1"].partition_broadcast(128), writes=["g1b"])
            kb.I("act", "activation", cs[:], cT[:], AF.Silu, reads=["cT"], writes=["cs"])
            for kt in range(8):
                kb.I("dve", "tensor_copy", lbx[:, kt, :], cs[:, 0, kt:kt + 1].to_broadcast([128, 128]), reads=["cs"], writes=["lbx"])
                kb.I("pool", "tensor_copy", lbc[:, kt, :], cs[:, 1, kt:kt + 1].to_broadcast([128, 128]), reads=["cs"], writes=["lbc"])
            i = 0
            for n in range(12 if upto != "p0a" else 0):
                ns = slice(n * 512, (n + 1) * 512)
                for kt in range(8):
                    wb = wbuf[i % 4]; wk = f"wbuf{i % 4}"; i += 1
                    kb.D(wb[:], din["mod_w"][kt * 128:(kt + 1) * 128, ns], writes=[wk])
                    kb.I("pe", "matmul", PS[0][:], lbx[:, kt, :], wb[:], start=(kt == 0), stop=(kt == 7), reads=[wk, "lbx"], writes=bk(0))
                    if n < 4:
                        kb.I("pe", "matmul", PS[1][:], lbc[:, kt, :], wb[:], start=(kt == 0), stop=(kt == 7), reads=[wk, "lbc"], writes=bk(1))
                kb.I("dve", "tensor_tensor", modb[:, ns], PS[0][:], mbias[:, ns], ALU.add, reads=bk(0) + ["mbias"], writes=["modb"])
                if n < 4:
                    kb.I("dve", "tensor_tensor", modc[:, ns], PS[1][:], mbias[:, ns], ALU.add, reads=bk(1) + ["mbias"], writes=["modc"])
            for which, src_ in ((0, modb), (1, modc)) if upto not in ("p0a", "p0b") else ():
                kb.I("dve", "scalar_tensor_tensor", tmpA[:], src_[:, D:2 * D], 1.0, g1b[:], ALU.add, ALU.mult,
                     reads=["modb", "modc", "g1b"], writes=["tmpA"])
                for half in range(2):
                    for j in range(4):
                        kt = half * 4 + j
                        kb.I("pe", "transpose", PS[2][:, j * 128:(j + 1) * 128], tmpA[:, kt * 128:(kt + 1) * 128], ident[:],
                             reads=["tmpA", "ident"], writes=[f"pX2_{j}"])
                        kb.I("pe", "transpose", PS[3][:, j * 128:(j + 1) * 128], src_[:, kt * 128:(kt + 1) * 128], ident[:],
                             reads=["modb", "modc", "ident"], writes=[f"pX3_{j}"])
                    for j in range(4):
                        kt = half * 4 + j
                        kb.I("dve", "tensor_copy", AB[:, 2 * which, kt:kt + 1], PS[2][:, j * 128:j * 128 + 1], reads=[f"pX2_{j}"], writes=["AB"])
                        kb.I("dve", "tensor_copy", AB[:, 2 * which + 1, kt:kt + 1], PS[3][:, j * 128:j * 128 + 1], reads=[f"pX3_{j}"], writes=["AB"])
            if upto not in ("p0a", "p0b", "p0c"):
                kb.D(scr["mod"], modb[:], reads=["modb"], writes=["scr_mod"])
            if debug and upto not in ("p0a", "p0b", "p0c"):
                kb.D(dbg["d_mod"], modb[:], reads=["modb"], writes=["o_d_mod"])
                kb.D(dbg["d_AB"], AB[:].rearrange("p a k -> p (a k)"), reads=["AB"], writes=["o_d_AB"])
            kb.rec = None
            recS = []
            if not upto.startswith("p0"):
                kb.rec = recS
                s5_setup(0)
                kb.rec = None
            kb.merge(rec0, recS)
            kb.I("dve", "tensor_copy", tmpA[:, 0:1], tmpA[:, 0:1],
                 reads=list(kb.lastw.keys()), writes=list(kb.lastw.keys()) + ["fence0"])
        FENCE0 = ["fence0"]
        if upto.startswith("p0"):
            kb.mute = True

        with contextlib.ExitStack() as sw:
            F0 = FENCE0
            kkc = sb(sw, "kkc", [128, 4]); kac = sb(sw, "kac", [128, 4]); omka = sb(sw, "omka", [128, 4])
            rkc = sb(sw, "rkc", [128, 4]); w0c = sb(sw, "w0c", [128, 2, 4]); a0c = sb(sw, "a0c", [128, 2, 4])
            cw = sb(sw, "cw", [128, 12, 9]); s5dc = sb(sw, "s5dc", [128, 4])
            w2pad = sb(sw, "w2pad", [128, 2, 512]); a2pad = sb(sw, "a2pad", [128, 2, 512])
            diagd = sb(sw, "diagd", [128, 4, 128], BF16)
            for t, nm in ((kkc, "kkv"), (kac, "kav"), (rkc, "rkv"), (s5dc, "s5d")):
                kb.D(t[:], din[nm].rearrange("(f p) -> p f", p=128), reads=F0, writes=["cols"], allow_slow_non_contiguous=True)
            for t, nm in ((w0c, "w0"), (a0c, "a0")):
                for d_ in range(2):
                    kb.D(t[:, d_, :], din[nm][d_].rearrange("(f p) -> p f", p=128), reads=F0, writes=["cols"], allow_slow_non_contiguous=True)
            for tp in range(9):
                kb.D(cw[:, :, tp], din["conv"][tp].rearrange("(f p) -> p f", p=128), reads=F0, writes=["cols"], allow_slow_non_contiguous=True)
            kb.I("dve", "tensor_scalar", omka[:], kac[:], -1.0, 1.0, ALU.mult, ALU.add, reads=["cols"], writes=["cols2"])
            kb.I("pool", "memset", w2pad[:], 0.0, reads=F0, writes=["w2pad"])
            kb.I("pool", "memset", a2pad[:], 0.0, reads=F0, writes=["a2pad"])
            for d_ in range(2):
                kb.D(w2pad[32 * d_:32 * d_ + 32, d_, :], din["w2"][d_], reads=["w2pad"], writes=["w2pad"])
                kb.D(a2pad[64 + 32 * d_:96 + 32 * d_, d_, :], din["a2"][d_], reads=["a2pad"], writes=["a2pad"])
            for ct in range(4):
                kb.ts1("dve", diagd[:, ct, :], ident[:], s5dc[:, ct:ct + 1], ALU.mult, reads=["cols", "ident"], writes=["diagd"])

            ST = sb(sw, "ST", [128, 4, 64])
            xw = sb(sw, "xw", [128, 2, D]); junk = sb(sw, "junk", [128, D], BF16)
            ssum = sb(sw, "ssum", [128, 2]); rstd = sb(sw, "rstd", [128, 2])
            hT = sb(sw, "hT", [128, 8, 256], BF16)
            zrkv2 = [sb(sw, f"zrkv{i}", [128, 3, 264], BF16) for i in range(2)]; zl = sb(sw, "zl", [128, 128]); zg = sb(sw, "zg", [128, 128])
            dgb = [sb(sw, f"dgb{i}", [128, 9, 128], BF16) for i in range(2)]
            rkvc2 = [sb(sw, f"rkvc{i}", [128, 3, 128]) for i in range(2)]
            W2 = {}
            for nm in ("kk", "t1", "t2", "ld", "a", "cs", "Eni", "b", "kd", "bep", "kap"):
                W2[nm] = [sb(sw, f"{nm}{i}", [128, 128]) for i in range(2)]
            zub = [sb(sw, f"zub{i}", [128, 4, 128], BF16) for i in range(2)]
            BE = [sb(sw, f"BE{i}", [128, 4, 128]) for i in range(2)]; KAP = [sb(sw, f"KAP{i}", [128, 4, 128]) for i in range(2)]
            AR = [sb(sw, f"AR{i}", [128, 4, 256]) for i in range(2)]; gC = [sb(sw, f"gC{i}", [128, 4]) for i in range(2)]
            tot = sb(sw, "tot", [128, 4])
            TMt = [sb(sw, f"TMt{i}", [128, 4, 512]) for i in range(2)]
            UT = sb(sw, "UT", [128, 512]); yT = sb(sw, "yT", [128, 512])
            flpR = sb(sw, "flpR", [128, 512]); flpS = sb(sw, "flpS", [128, 512])
            NS = 4
            NB = [sb(sw, f"NB{i}", [128, 256]) for i in range(NS)]; KA = [sb(sw, f"KA{i}", [128, 256]) for i in range(NS)]
            Pq = [sb(sw, f"Pq{i}", [128, 2, 128], BF16) for i in range(NS)]; PTq = [sb(sw, f"PTq{i}", [128, 2, 128], BF16) for i in range(NS)]
            Xq = [sb(sw, f"Xq{i}", [128, 2, 64]) for i in range(NS)]; Xb = [sb(sw, f"Xb{i}", [128, 2, 64], BF16) for i in range(NS)]
            print("SBUF bytes remaining in sweep scope:", nc.sbuf_bytes_remaining)
            for cf in range(12):
                for tp in range(9):
                    kb.ts1("dve" if tp % 2 else "pool", dgb[cf % 2][:, tp, :], ident[:], cw[:, cf, tp:tp + 1], ALU.mult,
                           reads=["cols", "ident", f"dgb{cf % 2}"], writes=[f"dgb{cf % 2}"])
                kb.D(scr["dg"][cf], dgb[cf % 2][:].rearrange("p t c -> p (t c)"), reads=[f"dgb{cf % 2}"], writes=["scr_dg"])
            kb.I("pool", "memset", xw[:], 0.0, reads=F0, writes=["xw0", "xw1"])

            def load_window(sd, is_ctx, c):
                rev = (sd == 1)
                TT = T_CTX if is_ctx else T_LAT
                src = din["ctx"] if is_ctx else din["x"]
                for half in range(2):
                    lo_r = 128 * c - 64 + 128 * half
                    if not rev:
                        lo, hi = lo_r, lo_r + 128
                    else:
                        hi, lo = TT - lo_r, TT - lo_r - 128
                    a0_, a1_ = max(lo, 0), min(hi, TT)
                    if a1_ > a0_:
                        kb.D(xw[a0_ - lo:a1_ - lo, half, :], src[a0_:a1_, :], writes=[f"xw{half}"])

            def process_chunk(sd, is_ctx, c, own, slot, nxt=None, first=False, reuse=False):
                rev = (sd == 1)
                TT = T_CTX if is_ctx else T_LAT
                src = din["ctx"] if is_ctx else din["x"]
                ab = 2 if is_ctx else 0
                tid = antiI if rev else ident
                tk = "antiI" if rev else "ident"
                dbg_here = debug and sd == 0 and (not is_ctx) and c == 0
                SL = f"@{slot}"
                streams = {"front": [], "rwkv": [], "s5": []}
                kb.rec = streams["front"]
                AR_, BE_, KAP_, gC_, TM_, zub_ = AR[slot], BE[slot], KAP[slot], gC[slot], TMt[slot], zub[slot]
                kapT = TM_[:, 0, :]; bepT = TM_[:, 1, :]; vT = TM_[:, 2, :]; bvT = TM_[:, 3, :]
                TMK = "TM" + SL
                if not reuse:
                    if first:
                        load_window(sd, is_ctx, c)
                    kb.I("pool", "memset", ssum[:], 0.0, writes=["ssum"])
                    for half in range(2):
                        kb.I("act", "activation", junk[:], xw[:, half, :], AF.Square, accum_out=ssum[:, half:half + 1],
                             reads=[f"xw{half}"], writes=["junk", "ssum"])
                    kb.I("dve", "tensor_scalar", rstd[:], ssum[:], 1.0 / D, 1e-6, ALU.mult, ALU.add, reads=["ssum"], writes=["rstd"])
                    kb.I("act", "activation", rstd[:], rstd[:], AF.Sqrt, reads=["rstd"], writes=["rstd"])
                    kb.I("dve", "reciprocal", rstd[:], rstd[:], reads=["rstd"], writes=["rstd"])
                    for half in range(2):
                        kb.ts1("dve" if half == 0 else "pool", xw[:, half, :], xw[:, half, :], rstd[:, half:half + 1], ALU.mult,
                               reads=["rstd", f"xw{half}"], writes=[f"xw{half}"])
                    for kt in range(8):
                        b_ = kt % 2
                        for half in range(2):
                            kb.I("pe", "transpose", PS[b_][:, half * 128:(half + 1) * 128], xw[:, half, kt * 128:(kt + 1) * 128], tid[:],
                                 reads=[f"xw{half}", tk], writes=[f"pX{b_}_0"])
                        if kt % 2:
                            kb.I("dve", "tensor_scalar", hT[:, kt, :], PS[b_][:, 0:256], AB[:, ab, kt:kt + 1], AB[:, ab + 1, kt:kt + 1], ALU.mult, ALU.add,
                                 reads=[f"pX{b_}_0", "AB"], writes=[f"hT{kt}"])
                        else:
                            kb.I("act", "activation", hT[:, kt, :], PS[b_][:, 0:256], AF.Identity, bias=AB[:, ab + 1, kt:kt + 1], scale=AB[:, ab, kt:kt + 1],
                                 reads=[f"pX{b_}_0", "AB"], writes=[f"hT{kt}"])
                    if nxt is not None:
                        load_window(sd, nxt[0], nxt[1])
                    others = [("zl", zl[:, :], 1536, 128), ("zg", zg[0:96, :], 1664, 96)] + [(f"zu{j}" + SL, zub_[:, j, :], 1760 + 128 * j, 128) for j in range(4)]
                    for i_, (nm, dst, c0, m) in enumerate(others):
                        b_ = i_ % 2
                        pr_ = PS[b_][0:m, 0:128]
                        pk = [f"pX{b_}_0"]
                        for kt in range(8):
                            kb.I("pe", "matmul", pr_, win[:, kt, c0:c0 + m], hT[:, kt, 64:192], start=(kt == 0), stop=(kt == 7), reads=["win", f"hT{kt}"], writes=pk)
                        if nm == "zg":
                            kb.I("act", "activation", dst, pr_, AF.Sigmoid, reads=pk, writes=[nm])
                        else:
                            kb.cp("dve" if i_ % 2 else "act", dst, pr_, reads=pk, writes=[nm])
                    if own is not None and sd == 0:
                        kb.D(scr["sg"][own * 128:own * 128 + 96, :], zg[0:96, :], reads=["zg"], writes=[f"scr_sg{own}"])
                    kb.I("act", "activation", zl[0:64, :], zl[0:64, :], AF.Tanh, reads=["zl"], writes=["zl"])
                    if sd == 0 and own is not None and not is_ctx:
                        kb.D(scr["fz"][own * 128:(own + 1) * 128, :], zl[:, :], reads=["zl"], writes=[f"scr_fz{own}"])
                        kb.D(scr["fu"][own * 128:(own + 1) * 128, :], zub_[:].rearrange("p j t -> p (j t)"), reads=[f"zu{j}" + SL for j in range(4)], writes=[f"scr_fu{own}"])
                else:
                    kb.D(xw[:, 1, 512:640], scr["fz"][own * 128:(own + 1) * 128, :], reads=[f"scr_fz{own}"], writes=["xw1"])
                    kb.I("dve", "tensor_copy", zl[:, :], xw[:, 1, 512:640][:, ::-1], reads=["xw1"], writes=["zl"])
                    kb.D(junk[:, 0:512], scr["fu"][own * 128:(own + 1) * 128, :], reads=[f"scr_fu{own}"], writes=["junk"])
                    kb.I("dve", "tensor_copy", zub_[:], junk[:, 0:512].rearrange("p (j t) -> p j t", t=128)[:, :, ::-1], reads=["junk"], writes=[f"zu{j}" + SL for j in range(4)])
                sgn = -1 if rev else 1
                nk = 4 if own is not None else 3
                for ft in range(4):
                    fp_ = ft % 2
                    zrkv = zrkv2[fp_]; rkvc = rkvc2[fp_]
                    W = {nm: W2[nm][fp_] for nm in W2}
                    if reuse:
                        stg_ = xw[:, ft % 2, 0:512]
                        kb.D(stg_, scr["fr"][(own * 4 + ft) * 128:(own * 4 + ft + 1) * 128, :], reads=[f"scr_fr{own}_{ft}"], writes=[f"xw{ft % 2}"])
                        kb.I("dve", "tensor_copy", rkvc[:, :, :], stg_[:, 0:384].rearrange("p (j t) -> p j t", t=128)[:, :, ::-1],
                             reads=[f"xw{ft % 2}"], writes=[f"rkvc{j}_{fp_}" for j in range(3)])
                        kb.I("dve", "tensor_copy", W["kk"][:, :], stg_[:, 384:512][:, ::-1], reads=[f"xw{ft % 2}"], writes=[f"W_kk{fp_}"])
                    for j3 in (range(3) if not reuse else ()):
                        b_ = j3 % 2
                        cf = (j3 * 4 + ft) * 128
                        pr_ = PS[b_][:, 0:256]
                        pk = [f"pX{b_}_0"]
                        for kt in range(8):
                            kb.I("pe", "matmul", pr_, win[:, kt, cf:cf + 128], hT[:, kt, :], start=(kt == 0), stop=(kt == 7), reads=["win", f"hT{kt}"], writes=pk)
                        if is_ctx:
                            kb.cp("act" if j3 % 2 else "dve", zrkv[:, j3, 0:256], pr_, reads=pk, writes=[f"zrkv{j3}_{fp_}"])
                        else:
                            if c == 0 and ft < 2:
                                kb.I("pool", "memset", zrkv[:, j3, :].rearrange("p (r c) -> p r c", c=66)[:, :, 0:1], 0.0, reads=[f"zrkv{j3}_{fp_}"], writes=[f"zrkv{j3}_{fp_}"])
                                kb.I("pool", "memset", zrkv[:, j3, :].rearrange("p (r c) -> p r c", c=66)[:, :, 65:66], 0.0, reads=[f"zrkv{j3}_{fp_}"], writes=[f"zrkv{j3}_{fp_}"])
                            kb.cp("act" if j3 % 2 else "dve", zrkv[:, j3, :].rearrange("p (r c) -> p r c", c=66)[:, :, 1:65],
                                  pr_.rearrange("p (r c) -> p r c", c=64), reads=pk, writes=[f"zrkv{j3}_{fp_}"])
                    if dbg_here and ft == 0:
                        kb.D(dbg["d_z"], zrkv[:].rearrange("p f t -> p (f t)"), reads=[f"zrkv{j}_{fp_}" for j in range(3)], writes=["o_d_z"])
                    for j3 in (range(3) if not reuse else ()):
                        zk = f"zrkv{j3}_{fp_}"; ok = f"rkvc{j3}_{fp_}"
                        cf = j3 * 4 + ft
                        ds = (ft * 3 + j3) % 2
                        dk = f"dgb{ds}"
                        kb.D(dgb[ds][:].rearrange("p t c -> p (t c)"), scr["dg"][cf], reads=["scr_dg"], writes=[dk])
                        b_ = (j3 + 1) % 2
                        pc = PS[b_][:, 256:384]
                        taps = [(0, 0)] + [(dy, dx) for dy in ((0,) if is_ctx else (-1, 0, 1)) for dx in (-1, 0, 1) if (dy, dx) != (0, 0)]
                        mms = []
                        for (dy, dx) in taps:
                            wi = (1 + sgn * dy) * 3 + (1 + sgn * dx)
                            if is_ctx:
                                j0, j1 = 64, 192
                                if 128 * c + dx < 0:
                                    j0 = 65
                                if 128 * c + 127 + dx >= TT:
                                    j1 = 191
                                mms.append((PS[b_][:, 256 + j0 - 64:256 + j1 - 64], wi, zrkv[:, j3, j0 + dx:j1 + dx]))
                            else:
                                i0, i1 = 0, 2
                                if 2 * c + dy < 0:
                                    i0 = 1
                                if 2 * c + 1 + dy >= 64:
                                    i1 = 1
                                o0 = 66 * i0 + 1; o1 = 66 * (i1 - 1) + 65
                                sh = 66 * (1 + dy) + dx
                                mms.append((PS[b_][:, 256 + o0:256 + o1], wi, zrkv[:, j3, o0 + sh:o1 + sh]))
                        for ti, (o_ap, wi, i_ap) in enumerate(mms):
                            kb.I("pe", "matmul", o_ap, dgb[ds][:, wi, :], i_ap, start=(ti == 0), stop=(ti == len(mms) - 1),
                                 reads=[zk, dk], writes=[f"pX{b_}_0"])
                        if is_ctx:
                            kb.cp("act" if j3 % 2 == 0 else "dve", rkvc[:, j3, :], pc, reads=[f"pX{b_}_0"], writes=[ok])
                        else:
                            kb.cp("act" if j3 % 2 == 0 else "dve", rkvc[:, j3, :].rearrange("p (r c) -> p r c", c=64),
                                  PS[b_][:, 256:388].rearrange("p (r c) -> p r c", c=66)[:, :, 1:65], reads=[f"pX{b_}_0"], writes=[ok])
                    if dbg_here and ft == 0:
                        kb.D(dbg["d_rkvc"], rkvc[:].rearrange("p f t -> p (f t)"), reads=[f"rkvc{j}_{fp_}" for j in range(3)], writes=["o_d_rkvc"])
                    rc = rkvc[:, 0, :]; kc = rkvc[:, 1, :]; vc = rkvc[:, 2, :]
                    fk = f"w{ft}" + SL
                    g_ = {nm: W[nm][:, :] for nm in W}
                    CK = ["cols", "cols2"]

                    def V(meth, *args, eng="dve", r=(), w=(), **kw):
                        kb.I(eng, meth, *args, reads=list(r) + CK, writes=list(w), **kw)
                    ARa = AR_[:, ft, 0:128]; ARr = AR_[:, ft, 128:256]
                    fa, fr, fb, fkp, fg = fk + "a", fk + "r", fk + "b", fk + "k", fk + "g"
                    if not reuse:
                        V("tensor_scalar", g_["kk"], kc, kkc[:, ft:ft + 1], 0.0, ALU.mult, ALU.add, r=[f"rkvc1_{fp_}"], w=[f"W_kk{fp_}"])
                        V("tensor_tensor", g_["t1"], g_["kk"], g_["kk"], ALU.mult, r=[f"W_kk{fp_}"], w=[f"W_t1{fp_}"])
                        kb.I("pe", "matmul", PS[0][:, 384:512], bones[:], g_["t1"], start=True, stop=True, reads=[f"W_t1{fp_}", "bones"], writes=["pX0_0"])
                        kb.I("dve", "tensor_scalar", g_["t2"], PS[0][:, 384:512], 1e-12, 0.0, ALU.max, ALU.add, reads=["pX0_0"], writes=[f"W_t2{fp_}"])
                        V("activation", g_["t2"], g_["t2"], AF.Sqrt, eng="act", r=[f"W_t2{fp_}"], w=[f"W_t2{fp_}"])
                        V("reciprocal", g_["t2"], g_["t2"], r=[f"W_t2{fp_}"], w=[f"W_t2{fp_}"])
                        V("tensor_tensor", g_["kk"], g_["kk"], g_["t2"], ALU.mult, r=[f"W_kk{fp_}", f"W_t2{fp_}"], w=[f"W_kk{fp_}"])
                        if sd == 0 and own is not None and not is_ctx:
                            r0_ = (own * 4 + ft) * 128
                            kb.D(scr["fr"][r0_:r0_ + 128, 0:384], rkvc[:, :, :].rearrange("p j t -> p (j t)"), reads=[f"rkvc{j}_{fp_}" for j in range(3)], writes=[f"scr_fr{own}_{ft}"])
                            kb.D(scr["fr"][r0_:r0_ + 128, 384:512], g_["kk"], reads=[f"W_kk{fp_}"], writes=[f"scr_fr{own}_{ft}"])
                    kb.I("pe", "matmul", PS[1][:, 256:384], w2pad[:, sd, ft * 128:(ft + 1) * 128], zl[:, :], start=True, stop=True, reads=["zl", "w2pad"], writes=["pX1_0"])
                    kb.I("pe", "matmul", PS[1][:, 384:512], a2pad[:, sd, ft * 128:(ft + 1) * 128], zl[:, :], start=True, stop=True, reads=["zl", "a2pad"], writes=["pX1_0"])
                    kb.I("act", "activation", g_["ld"], PS[1][:, 256:384], AF.Sigmoid, bias=w0c[:, sd, ft:ft + 1], scale=1.0, reads=["pX1_0", "cols"], writes=[f"W_ld{fp_}"])
                    kb.I("act", "activation", g_["a"], PS[1][:, 384:512], AF.Sigmoid, bias=a0c[:, sd, ft:ft + 1], scale=1.0, reads=["pX1_0", "cols"], writes=[f"W_a{fp_}"])
                    V("tensor_tensor_scan", g_["cs"], ones[:], g_["ld"], 0.0, ALU.mult, ALU.add, r=[f"W_ld{fp_}", "ones"], w=[f"W_cs{fp_}"])
                    V("tensor_copy", tot[:, ft:ft + 1], g_["cs"][:, 127:128], r=[f"W_cs{fp_}"], w=[f"tot{ft}"])
                    V("tensor_tensor", g_["t1"], g_["cs"], g_["ld"], ALU.subtract, r=[f"W_cs{fp_}", f"W_ld{fp_}"], w=[f"W_t1{fp_}"])
                    V("activation", ARr, g_["cs"], AF.Exp, eng="act", scale=-KDEC, r=[f"W_cs{fp_}"], w=[fr])
                    V("activation", ARa, g_["t1"], AF.Exp, eng="act", scale=-KDEC, r=[f"W_t1{fp_}"], w=[fa])
                    V("activation", g_["Eni"], g_["cs"], AF.Exp, eng="act", scale=KDEC, r=[f"W_cs{fp_}"], w=[f"W_Eni{fp_}"])
                    V("activation", gC_[:, ft:ft + 1], tot[:, ft:ft + 1], AF.Exp, eng="act", scale=-KDEC, r=[f"tot{ft}"], w=[fg])
                    V("tensor_tensor", g_["b"], g_["kk"], g_["a"], ALU.mult, r=[f"W_kk{fp_}", f"W_a{fp_}"], w=[f"W_b{fp_}"])
                    V("tensor_scalar", g_["t2"], g_["a"], kac[:, ft:ft + 1], omka[:, ft:ft + 1], ALU.mult, ALU.add, r=[f"W_a{fp_}"], w=[f"W_t2{fp_}"])
                    V("tensor_tensor", g_["kd"], kc, g_["t2"], ALU.mult, r=[f"rkvc1_{fp_}", f"W_t2{fp_}"], w=[f"W_kd{fp_}"])
                    V("tensor_tensor", ARa, ARa, g_["kk"], ALU.mult, r=[fa, f"W_kk{fp_}"], w=[fa])
                    V("tensor_tensor", ARr, ARr, rc, ALU.mult, eng="pool", r=[fr, f"rkvc0_{fp_}"], w=[fr])
                    V("tensor_tensor", BE_[:, ft, :], g_["b"], g_["Eni"], ALU.mult, r=[f"W_b{fp_}", f"W_Eni{fp_}"], w=[fb])
                    V("tensor_tensor", KAP_[:, ft, :], g_["kd"], g_["Eni"], ALU.mult, eng="pool", r=[f"W_kd{fp_}", f"W_Eni{fp_}"], w=[fkp])
                    V("tensor_scalar", g_["bep"], BE_[:, ft, :], gC_[:, ft:ft + 1], -1.0, ALU.mult, ALU.mult, r=[fb, fg], w=[f"W_bep{fp_}"])
                    V("tensor_scalar", g_["kap"], KAP_[:, ft, :], gC_[:, ft:ft + 1], 0.0, ALU.mult, ALU.add, eng="pool", r=[fkp, fg], w=[f"W_kap{fp_}"])
                    tsrc = [(g_["kap"], f"W_kap{fp_}"), (g_["bep"], f"W_bep{fp_}"), (vc, f"rkvc2_{fp_}")]
                    if own is not None:
                        V("scalar_tensor_tensor", g_["t1"], rc, rkc[:, ft:ft + 1], g_["kd"], ALU.mult, ALU.mult, r=[f"rkvc0_{fp_}", f"W_kd{fp_}"], w=[f"W_t1{fp_}"])
                        kb.I("pe", "matmul", PS[0][:, 384:512], bones[:], g_["t1"], start=True, stop=True, reads=[f"W_t1{fp_}", "bones"], writes=["pX0_0"])
                        kb.I("dve", "tensor_tensor", g_["t2"], PS[0][:, 384:512], vc, ALU.mult, reads=["pX0_0", f"rkvc2_{fp_}"], writes=[f"W_t2{fp_}"])
                        tsrc.append((g_["t2"], f"W_t2{fp_}"))
                    b_ = ft % 2
                    for i_, (s_ap, s_k) in enumerate(tsrc):
                        kb.I("pe", "transpose", PS[b_][:, i_ * 128:(i_ + 1) * 128], s_ap, ident[:], reads=[s_k, "ident"], writes=[f"pX{b_}_0"])
                    kb.cp("act" if ft % 2 else "dve", TM_[:, 0:nk, ft * 128:(ft + 1) * 128], PS[b_][:, 0:nk * 128].rearrange("p (k t) -> p k t", k=nk),
                          reads=[f"pX{b_}_0"], writes=[TMK])
                kb.rec = streams["rwkv"]
                for g0 in range(0, 8, NS):
                    heads = list(range(g0, g0 + NS))

                    def hv(h):
                        ft = h // 2; Rs = slice(64 * (h % 2), 64 * (h % 2) + 64)
                        return ft, Rs, [f"w{ft}" + SL + x_ for x_ in "arbkg"]
                    BK = [2 + s_ for s_ in range(NS)]
                    bkk = [f"pX{b_}_0" for b_ in BK]
                    for s_, h in enumerate(heads):
                        ft, Rs, fk = hv(h)
                        be = BE_[Rs, ft, :]; ar = AR_[Rs, ft, :]; al = AR_[Rs, ft, 0:128]
                        kb.I("pe", "matmul", PS[BK[s_]][:, 0:256], be, ar, start=True, stop=True, reads=fk, writes=[bkk[s_]])
                        kb.I("pe", "matmul", PS[BK[s_]][:, 256:384], al, be, start=True, stop=True, reads=fk, writes=[bkk[s_]])
                    for s_, h in enumerate(heads):
                        kb.I("dve", "tensor_tensor", NB[s_][:], PS[BK[s_]][:, 0:256], mask_b[:], ALU.mult, reads=[bkk[s_], "mask_b"], writes=[f"NB{s_}"])
                        kb.I("dve", "tensor_tensor", PTq[s_][:, 0, :], PS[BK[s_]][:, 256:384], maskT[:], ALU.mult, reads=[bkk[s_], "maskT"], writes=[f"PT{s_}_0"])
                        kb.I("dve", "tensor_tensor", Pq[s_][:, 0, :], PS[BK[s_]][:, 0:128], mask_b[:, 0:128], ALU.mult, reads=[bkk[s_], "mask_b"], writes=[f"P{s_}_0"])
                    for s_, h in enumerate(heads):
                        ft, Rs, fk = hv(h)
                        kb.I("pe", "matmul", PS[BK[s_]][:, 0:256], KAP_[Rs, ft, :], AR_[Rs, ft, :], start=True, stop=True, reads=fk, writes=[bkk[s_]])
                    for s_, h in enumerate(heads):
                        kb.I("dve", "tensor_tensor", KA[s_][:], PS[BK[s_]][:, 0:256], mask_k[:], ALU.mult, reads=[bkk[s_], "mask_k"], writes=[f"KA{s_}"])
                    for s_, h in enumerate(heads):
                        ft, Rs, fk = hv(h)
                        al = AR_[Rs, ft, 0:128]
                        kb.I("pe", "matmul", PS[BK[s_]][:, 384:448], al, ST[Rs, ft, :], start=True, stop=False, reads=fk + [f"ST{ft}"], writes=[bkk[s_]])
                        kb.I("pe", "matmul", PS[BK[s_]][:, 384:448], KA[s_][:, 0:128], vT[:, h * 64:(h + 1) * 64], start=False, stop=True, reads=[f"KA{s_}", TMK], writes=[bkk[s_]])
                    for s_, h in enumerate(heads):
                        kb.cp("act", Xq[s_][:, 0, :], PS[BK[s_]][:, 384:448], reads=[bkk[s_]], writes=[f"X{s_}_0"])
                        kb.cp("act", Xb[s_][:, 0, :], PS[BK[s_]][:, 384:448], reads=[bkk[s_]], writes=[f"Xb{s_}_0"])
                    for j in range(7):
                        cu, nx = j % 2, (j + 1) % 2
                        for s_, h in enumerate(heads):
                            Pj = Pq[s_][:, cu, :]
                            Pk = f"P{s_}_{cu}"
                            PTj = PTq[s_][:, cu, :]; PTk = f"PT{s_}_{cu}"
                            if j < 6:
                                kb.I("pe", "matmul", PS[BK[s_]][:, 0:128], PTj, Pj, start=True, stop=True, reads=[PTk, Pk], writes=[bkk[s_]])
                            if j < 5:
                                kb.I("pe", "matmul", PS[BK[s_]][:, 128:256], Pj, PTj, start=True, stop=True, reads=[PTk, Pk], writes=[bkk[s_]])
                            kb.I("pe", "matmul", PS[BK[s_]][:, 448:512], Pj, Xb[s_][:, cu, :], start=True, stop=True, reads=[Pk, f"Xb{s_}_{cu}"], writes=[bkk[s_]])
                        for s_, h in enumerate(heads):
                            if j < 6:
                                kb.cp("act", Pq[s_][:, nx, :], PS[BK[s_]][:, 0:128], reads=[bkk[s_]], writes=[f"P{s_}_{nx}"])
                            if j < 5:
                                kb.cp("act", PTq[s_][:, nx, :], PS[BK[s_]][:, 128:256], reads=[bkk[s_]], writes=[f"PT{s_}_{nx}"])
                            dst_ap = UT[:, h * 64:(h + 1) * 64] if j == 6 else Xq[s_][:, nx, :]
                            dst_k = f"UT{h}" if j == 6 else f"X{s_}_{nx}"
                            if j < 6:
                                kb.I("dve", "tensor_tensor", Xb[s_][:, nx, :], Xq[s_][:, cu, :], PS[BK[s_]][:, 448:512], ALU.subtract if j == 0 else ALU.add,
                                     reads=[bkk[s_], f"X{s_}_{cu}"], writes=[f"Xb{s_}_{nx}"])
                            kb.I("dve", "tensor_tensor", dst_ap, Xq[s_][:, cu, :], PS[BK[s_]][:, 448:512], ALU.subtract if j == 0 else ALU.add,
                                 reads=[bkk[s_], f"X{s_}_{cu}"], writes=[dst_k])
                    if own is not None:
                        for s_, h in enumerate(heads):
                            ft, Rs, fk = hv(h)
                            rho = AR_[Rs, ft, 128:256]
                            yr_ = PS[BK[s_]][:, 256:320]
                            kb.I("pe", "matmul", yr_, rho, ST[Rs, ft, :], start=True, stop=False, reads=fk + [f"ST{ft}"], writes=[bkk[s_]])
                            kb.I("pe", "matmul", yr_, NB[s_][:, 128:256], UT[:, h * 64:(h + 1) * 64], start=False, stop=False, reads=[f"NB{s_}", f"UT{h}"], writes=[bkk[s_]])
                            kb.I("pe", "matmul", yr_, KA[s_][:, 128:256], vT[:, h * 64:(h + 1) * 64], start=False, stop=True, reads=[f"KA{s_}", TMK], writes=[bkk[s_]])
                        for s_, h in enumerate(heads):
                            kb.cp("act", yT[:, h * 64:(h + 1) * 64], PS[BK[s_]][:, 256:320], reads=[bkk[s_]], writes=[f"yT{h}"])
                    for ft in range(g0 // 2, (g0 + NS) // 2):
                        h1 = 2 * ft + 1
                        bnk = BK[h1 - g0]
                        cs_ = slice(ft * 128, (ft + 1) * 128)
                        kb.I("pe", "matmul", PS[bnk][:, 0:128], bepT[:, cs_], UT[:, cs_], start=True, stop=False, reads=[TMK, f"UT{h1 - 1}", f"UT{h1}"], writes=[f"pX{bnk}_0"])
                        kb.I("pe", "matmul", PS[bnk][:, 0:128], kapT[:, cs_], vT[:, cs_], start=False, stop=True, reads=[TMK], writes=[f"pX{bnk}_0"])
                        for jj in range(2):
                            rr_ = slice(64 * jj, 64 * jj + 64)
                            kb.I("dve", "scalar_tensor_tensor", ST[rr_, ft, :], ST[rr_, ft, :], gC_[rr_, ft:ft + 1], PS[bnk][rr_, 64 * jj:64 * jj + 64], ALU.mult, ALU.add,
                                 reads=[f"pX{bnk}_0", f"ST{ft}", f"w{ft}" + SL + "g"], writes=[f"ST{ft}"])
                if dbg_here:
                    kb.D(dbg["d_yT"], yT[:], reads=[f"yT{h}" for h in range(8)], writes=["o_d_yT"])
                    kb.D(dbg["d_ST"], ST[:].rearrange("p f v -> p (f v)"), reads=[f"ST{f}" for f in range(4)], writes=["o_d_ST"])
                if own is not None:
                    for nm, tile_ap, keys in (("yr", yT[:], [f"yT{h}" for h in range(8)]), ("bv", bvT, [TMK])):
                        dst = scr[f"{nm}{sd}"][own * 128:(own + 1) * 128, :]
                        if rev:
                            kb.I("pe", "matmul", PS[5][:], antiI[:], tile_ap, start=True, stop=True, reads=keys + ["antiI"], writes=["pX5_0"])
                            kb.cp("act", flpR[:], PS[5][:], reads=["pX5_0"], writes=["flpR"])
                            kb.D(dst, flpR[:], reads=["flpR"], writes=[f"scr_{nm}{sd}_{own}"])
                        else:
                            kb.D(dst, tile_ap, reads=keys, writes=[f"scr_{nm}{sd}_{own}"])
                kb.rec = streams["s5"]
                t1_, t2_ = s5t["t1"], s5t["t2"]
                for a_ in range(4):
                    zuk = f"zu{a_}" + SL
                    for r_ in range(4):
                        q_ = a_ * 4 + r_
                        kb.I("pe", "matmul", PS[6][:, r_ * 128:(r_ + 1) * 128], Bw_re[:, q_, :], zub_[:, a_, :], start=True, stop=True, reads=["s5tab", zuk], writes=["pX6_0"])
                        kb.I("pe", "matmul", PS[7][:, r_ * 128:(r_ + 1) * 128], Bw_im[:, q_, :], zub_[:, a_, :], start=True, stop=True, reads=["s5tab", zuk], writes=["pX7_0"])
                    k3 = ["pX6_0"]; k4 = ["pX7_0"]
                    qs = slice(a_ * 4, a_ * 4 + 4)
                    Er = E_re[:, qs, :].rearrange("p q t -> p (q t)"); Ei = E_im[:, qs, :].rearrange("p q t -> p (q t)")
                    Fr = F_re[:, qs, :].rearrange("p q t -> p (q t)"); Fi = F_im[:, qs, :].rearrange("p q t -> p (q t)")
                    g1_, g2_ = s5t["Gre"], s5t["Gim"]
                    kb.I("dve", "tensor_tensor", t1_[:], PS[6][:], Er, ALU.mult, reads=k3 + ["s5tab"], writes=["s5t1"])
                    kb.I("dve", "tensor_tensor", t2_[:], PS[7][:], Ei, ALU.mult, reads=k4 + ["s5tab"], writes=["s5t2"])
                    kb.I("dve", "tensor_tensor", g1_[:], PS[7][:], Er, ALU.mult, reads=k4 + ["s5tab"], writes=["s5Gre"])
                    kb.I("dve", "tensor_tensor", g2_[:], PS[6][:], Ei, ALU.mult, reads=k3 + ["s5tab"], writes=["s5Gim"])
                    kb.I("pool", "tensor_tensor", s5t["Xre"][:], t1_[:], t2_[:], ALU.subtract, reads=["s5t1", "s5t2"], writes=["s5Xre"])
                    kb.I("dve", "tensor_tensor", s5t["Xim"][:], g1_[:], g2_[:], ALU.add, reads=["s5Gre", "s5Gim"], writes=["s5Xim"])
                    if own is None:
                        kb.I("dve", "tensor_reduce", gs_re[:, qs], s5t["Xre"][:].rearrange("p (q t) -> p q t", q=4), AXX, ALU.add, reads=["s5Xre"], writes=["gs"])
                        kb.I("dve", "tensor_reduce", gs_im[:, qs], s5t["Xim"][:].rearrange("p (q t) -> p q t", q=4), AXX, ALU.add, reads=["s5Xim"], writes=["gs"])
                        continue
                    for r_ in range(4):
                        q_ = a_ * 4 + r_
                        cs_ = slice(r_ * 128, (r_ + 1) * 128)
                        kb.I("dve", "tensor_tensor_scan", s5t["Gre"][:, cs_], ones[:], s5t["Xre"][:, cs_], car_re[:, q_:q_ + 1], ALU.mult, ALU.add,
                             reads=["s5Xre", "car", "ones"], writes=["s5Gre"])
                        kb.I("dve", "tensor_tensor_scan", s5t["Gim"][:, cs_], ones[:], s5t["Xim"][:, cs_], car_im[:, q_:q_ + 1], ALU.mult, ALU.add,
                             reads=["s5Xim", "car", "ones"], writes=["s5Gim"])
                    x1_, x2_ = s5t["Xre"], s5t["Xim"]
                    kb.I("dve", "tensor_tensor", t1_[:], s5t["Gre"][:], Fr, ALU.mult, reads=["s5Gre", "s5tab"], writes=["s5t1"])
                    kb.I("dve", "tensor_tensor", t2_[:], s5t["Gim"][:], Fi, ALU.mult, reads=["s5Gim", "s5tab"], writes=["s5t2"])
                    kb.I("pool", "tensor_tensor", x1_[:], s5t["Gim"][:], Fr, ALU.mult, reads=["s5Gim", "s5tab", "s5Xre"], writes=["s5Xre"])
                    kb.I("pool", "tensor_tensor", x2_[:], s5t["Gre"][:], Fi, ALU.mult, reads=["s5Gre", "s5tab", "s5Xim"], writes=["s5Xim"])
                    kb.I("dve", "tensor_tensor", s5t["Hre"][:], t1_[:], t2_[:], ALU.subtract, reads=["s5t1", "s5t2"], writes=["s5Hre"])
                    kb.I("pool", "tensor_tensor", s5t["Him"][:], x1_[:], x2_[:], ALU.add, reads=["s5Xre", "s5Xim"], writes=["s5Him"])
                    kb.I("dve", "tensor_copy", car_re[:, qs], s5t["Hre"][:].rearrange("p (q t) -> p q t", q=4)[:, :, 127], reads=["s5Hre"], writes=["car"])
                    kb.I("dve", "tensor_copy", car_im[:, qs], s5t["Him"][:].rearrange("p (q t) -> p q t", q=4)[:, :, 127], reads=["s5Him"], writes=["car"])
                    if own is not None:
                        for r_ in range(4):
                            q_ = a_ * 4 + r_
                            cs_ = slice(r_ * 128, (r_ + 1) * 128)
                            yb = PS[6][:, r_ * 32:(r_ + 1) * 32]
                            if sd == 0:
                                kb.I("pe", "matmul", yb, zub_[:, a_, :], diagd[:, a_, r_ * 32:(r_ + 1) * 32], start=True, stop=False, reads=[zuk, "diagd"], writes=["pX6_0"])
                            kb.I("pe", "matmul", yb, s5t["Hre"][:, cs_], Cw_re[:, q_, :], start=(sd != 0), stop=False, reads=["s5Hre", "s5tab"], writes=["pX6_0"])
                            kb.I("pe", "matmul", yb, s5t["Him"][:, cs_], Cw_imn[:, q_, :], start=False, stop=True, reads=["s5Him", "s5tab"], writes=["pX6_0"])
                        kb.cp("act", flpS[:, a_ * 128:(a_ + 1) * 128], PS[6][:, 0:128], reads=["pX6_0"], writes=["flpS"])
                if own is None:
                    F127r = F_re[:, :, 127]; F127i = F_im[:, :, 127]
                    kb.I("dve", "tensor_tensor", gs_re[:], gs_re[:], car_re[:], ALU.add, reads=["gs", "car"], writes=["gs"])
                    kb.I("dve", "tensor_tensor", gs_im[:], gs_im[:], car_im[:], ALU.add, reads=["gs", "car"], writes=["gs"])
                    kb.I("dve", "tensor_tensor", gta[:], gs_re[:], F127r, ALU.mult, reads=["gs", "s5tab"], writes=["gta"])
                    kb.I("dve", "tensor_tensor", gtb[:], gs_im[:], F127i, ALU.mult, reads=["gs", "s5tab"], writes=["gtb"])
                    kb.I("dve", "tensor_tensor", car_re[:], gta[:], gtb[:], ALU.subtract, reads=["gta", "gtb"], writes=["car"])
                    kb.I("dve", "tensor_tensor", gta[:], gs_im[:], F127r, ALU.mult, reads=["gs", "s5tab", "car"], writes=["gta"])
                    kb.I("dve", "tensor_tensor", gtb[:], gs_re[:], F127i, ALU.mult, reads=["gs", "s5tab", "car"], writes=["gtb"])
                    kb.I("dve", "tensor_tensor", car_im[:], gta[:], gtb[:], ALU.add, reads=["gta", "gtb"], writes=["car"])
                if dbg_here:
                    kb.D(dbg["d_car"][:, 0:16], car_re[:], reads=["car"], writes=["o_d_car"])
                    kb.D(dbg["d_car"][:, 16:32], car_im[:], reads=["car"], writes=["o_d_car"])
                if own is not None:
                    if dbg_here:
                        kb.D(dbg["d_ys"], flpS[:], reads=["flpS"], writes=["o_d_ys"])
                    dst = scr[f"ys{sd}"][own * 128:(own + 1) * 128, :]
                    if rev:
                        kb.I("pe", "matmul", PS[6][:], antiI[:], flpS[:], start=True, stop=True, reads=["flpS", "antiI"], writes=["pX6_0"])
                        kb.cp("dve", flpS[:], PS[6][:], reads=["pX6_0"], writes=["flpS"])
                    kb.D(dst, flpS[:], reads=["flpS"], writes=[f"scr_ys{sd}_{own}"])
                kb.rec = None
                return streams

            for sd in range(2 if not upto.startswith("p0") else 0):
                if upto == "setup":
                    break
                if sd == 1:
                    s5_setup(sd, F0)
                if upto == "s5setup":
                    break
                stk = [f"ST{f}" for f in range(4)]
                if sd == 0:
                    kb.I("pool", "memset", ST[:], 0.0, reads=stk, writes=stk)
                    kb.I("pool", "memset", car_re[:], 0.0, reads=["car"], writes=["car"])
                    kb.I("pool", "memset", car_im[:], 0.0, reads=["car"], writes=["car"])
                if sd == 0:
                    chunks = [(True, c, None) for c in range(n_ctx)] + [(False, c, c) for c in range(n_lat_chunks[0])]
                else:
                    stk = [f"ST{f}" for f in range(4)]
                    kb.D(scr["cc_in"][:, 0:256], ST[:].rearrange("p f v -> p (f v)"), reads=stk, writes=["cc_in"])
                    kb.D(scr["cc_in"][:, 256:272], car_re[:], reads=["car"], writes=["cc_in"])
                    kb.D(scr["cc_in"][:, 272:288], car_im[:], reads=["car"], writes=["cc_in"])
                    kb.I("pool", "collective_compute", "AllReduce", ALU.add, replica_groups=[[0, 1], [2, 3], [4, 5], [6, 7]],
                         ins=[scr["cc_in"]], outs=[scr["cc_out"]], reads=["cc_in"], writes=["cc_out"])
                    ccs = s5t["t1"]
                    kb.D(ccs[:, 0:288], scr["cc_out"], reads=["cc_out", "s5t1"], writes=["s5t1"])
                    kb.I("dve", "tensor_tensor", ST[:].rearrange("p f v -> p (f v)"), ccs[:, 0:256], ST[:].rearrange("p f v -> p (f v)"), ALU.subtract,
                         reads=["s5t1"] + stk, writes=stk)
                    kb.I("dve", "tensor_tensor", car_re[:], ccs[:, 256:272], car_re[:], ALU.subtract, reads=["s5t1", "car"], writes=["car"])
                    kb.I("dve", "tensor_tensor", car_im[:], ccs[:, 272:288], car_im[:], ALU.subtract, reads=["s5t1", "car"], writes=["car"])
                    chunks = [(False, c, 31 - c) for c in range(16, 16 + n_lat_chunks[1])]
                prev = None
                pend = []
                for k_, (is_ctx, c, own) in enumerate(chunks):
                    nxt = chunks[k_ + 1][:2] if k_ + 1 < len(chunks) else None
                    cur = process_chunk(sd, is_ctx, c, own, k_ % 2, nxt=(nxt if sd == 0 else None), first=(k_ == 0), reuse=(sd == 1))
                    pend.append(cur["front"])
                    if prev is not None:
                        pend += [prev["rwkv"], prev["s5"]]
                    prev = cur
                    if len(pend) >= SCHED_WINDOW_STREAMS:
                        kb.merge(*pend)
                        pend = []
                if prev is not None:
                    pend += [prev["rwkv"], prev["s5"]]
                kb.merge(*pend)
            kb.I("dve", "tensor_copy", ssum[:, 0:1], ssum[:, 0:1], reads=list(kb.lastw.keys()), writes=list(kb.lastw.keys()) + ["fence1"])
        FENCE = ["fence1"]
        wscope.close()

        if do_tail:
            with contextlib.ExitStack() as tl:
                wob = sb(tl, "wob", [128, 8, D], BF16)
                glb = sb(tl, "glb", [128, 4, 512], BF16); g2b = sb(tl, "g2b", [128, 512])
                stg = [sb(tl, f"stg{i}", [128, 1024]) for i in range(2)]
                jobs = [("w_out", wob, kt, 0, D) for kt in range(8)] + [("gluw", glb, kt, 0, 512) for kt in range(4)]
                for i, (nm, dst, kt, c0, ncol) in enumerate(jobs):
                    s_ = stg[i % 2]; sk = f"stg{i % 2}"
                    kb.D(s_[:, 0:ncol], din[nm][kt * 128:(kt + 1) * 128, c0:c0 + ncol], reads=FENCE, writes=[sk])
                    kb.cp(("dve", "act", "pool")[i % 3], dst[:, kt, c0:c0 + ncol], s_[:, 0:ncol], reads=[sk], writes=["tw"])
                kb.D(g2b[0:96, :], din["g2"], reads=FENCE, writes=["tw2"])
                rows = {}
                for nm in ("lnw", "lnb", "glub"):
                    rows[nm] = sb(tl, "row_" + nm, [128, IN_SHAPES[nm][1]])
                    kb.D(rows[nm][:], din[nm].partition_broadcast(128), reads=FENCE, writes=["rows"])
                gmix = sb(tl, "gmix", [128, D])

                def dbl(name, shape, dt=F32):
                    return [sb(tl, f"{name}_{i_}", shape, dt) for i_ in range(2)]
                x1 = dbl("x1", [128, D])
                kb.D(gmix[:], scr["mod"][:, 2 * D:3 * D], reads=FENCE + ["scr_mod"], writes=["modt"])
                tx = dbl("tx", [128, D]); yr_a = dbl("yr_a", [128, 512]); ys_a = dbl("ys_a", [128, 512]); bv_a = dbl("bv_a", [128, 512])
                yr_b = dbl("yr_b", [128, 512]); ys_b = dbl("ys_b", [128, 512]); bv_b = dbl("bv_b", [128, 512])
                sgt = dbl("sgt", [128, 128]); mix = dbl("mix", [128, D]); st8 = dbl("st8", [128, 8]); st8b = dbl("st8b", [128, 8])
                gt = dbl("gt", [128, 512]); gt2 = dbl("gt2", [128, 512]); zT = dbl("zT", [128, 4, 128], BF16); mixT = dbl("mixT", [128, 8, 128], BF16)
                TA = (tx, yr_a, ys_a, bv_a, yr_b, ys_b, bv_b, sgt, mix, st8, st8b, gt, gt2, zT, mixT, x1)
                recA = []
                kb.rec = recA
                for oc in range(NOWN):
                    sl_ = oc % 2
                    kb.ksuf = f"#{sl_}"
                    (tx, yr_a, ys_a, bv_a, yr_b, ys_b, bv_b, sgt, mix, st8, st8b, gt, gt2, zT, mixT, x1) = (t_[sl_] for t_ in TA)
                    rsl = slice(oc * 128, (oc + 1) * 128)
                    kb.D(tx[:], din["x"][rsl, :], reads=FENCE, writes=["tx"])
                    for t_, nm in ((yr_a, "yr0"), (yr_b, "yr1"), (ys_a, "ys0"), (ys_b, "ys1"), (bv_a, "bv0"), (bv_b, "bv1")):
                        kb.D(t_[:], scr[nm][rsl, :], reads=[f"scr_{nm}_{oc}"] + FENCE, writes=["t_" + nm])
                    kb.D(sgt[0:96, :], scr["sg"][oc * 128:oc * 128 + 96, :], reads=[f"scr_sg{oc}"] + FENCE, writes=["sgt"])
                    kb.I("dve", "tensor_tensor", yr_a[:], yr_a[:], yr_b[:], ALU.add, reads=["t_yr0", "t_yr1"], writes=["t_yr0"])
                    y3 = yr_a[:].rearrange("p (h n) -> p h n", n=64)
                    kb.I("dve", "tensor_reduce", st8[:], y3, AXX, ALU.add, reads=["t_yr0"], writes=["st8"])
                    kb.ts1("dve", st8[:], st8[:], 1.0 / 64, ALU.mult, reads=["st8"], writes=["st8"])
                    kb.I("dve", "tensor_tensor", y3, y3, st8[:].unsqueeze(2).to_broadcast([128, 8, 64]), ALU.subtract, reads=["st8", "t_yr0"], writes=["t_yr0"])
                    kb.I("pool", "tensor_tensor", gt2[:], yr_a[:], yr_a[:], ALU.mult, reads=["t_yr0"], writes=["gt2"])
                    kb.I("dve", "tensor_reduce", st8b[:], gt2[:].rearrange("p (h n) -> p h n", n=64), AXX, ALU.add, reads=["gt2"], writes=["st8b"])
                    kb.I("dve", "tensor_scalar", st8b[:], st8b[:], 1.0 / 64, 64e-5, ALU.mult, ALU.add, reads=["st8b"], writes=["st8b"])
                    kb.I("act", "activation", st8b[:], st8b[:], AF.Sqrt, reads=["st8b"], writes=["st8b"])
                    kb.I("dve", "reciprocal", st8b[:], st8b[:], reads=["st8b"], writes=["st8b"])
                    kb.I("dve", "tensor_tensor", y3, y3, st8b[:].unsqueeze(2).to_broadcast([128, 8, 64]), ALU.mult, reads=["st8b", "t_yr0"], writes=["t_yr0"])
                    kb.I("pool", "tensor_tensor", yr_a[:], yr_a[:], rows["lnw"][:], ALU.mult, reads=["rows", "t_yr0"], writes=["t_yr0"])
                    kb.I("pool", "tensor_tensor", yr_a[:], yr_a[:], rows["lnb"][:], ALU.add, reads=["rows", "t_yr0"], writes=["t_yr0"])
                    kb.I("dve", "tensor_tensor", bv_a[:], bv_a[:], bv_b[:], ALU.add, reads=["t_bv0", "t_bv1"], writes=["t_bv0"])
                    kb.I("dve", "tensor_tensor", yr_a[:], yr_a[:], bv_a[:], ALU.add, reads=["t_bv0", "t_yr0"], writes=["t_yr0"])
                    kb.I("pe", "matmul", PS[0][:], sgt[0:96, :], g2b[0:96, :], start=True, stop=True, reads=["sgt", "tw2"], writes=bk(0))
                    kb.I("dve", "tensor_tensor", mix[:, 0:512], yr_a[:], PS[0][:], ALU.mult, reads=bk(0) + ["t_yr0"], writes=["mixA"])
                    kb.I("dve", "tensor_tensor", ys_a[:], ys_a[:], ys_b[:], ALU.add, reads=["t_ys0", "t_ys1"], writes=["t_ys0"])
                    kb.I("pool", "tensor_tensor", gt[:], ys_a[:], ys_a[:], ALU.mult, reads=["t_ys0"], writes=["gt"])
                    kb.I("dve", "tensor_scalar", gt[:], gt[:], 0.044715, 1.0, ALU.mult, ALU.add, reads=["gt"], writes=["gt"])
                    kb.I("dve", "tensor_tensor", gt[:], gt[:], ys_a[:], ALU.mult, reads=["gt", "t_ys0"], writes=["gt"])
                    kb.I("act", "activation", gt[:], gt[:], AF.Tanh, scale=0.7978845608028654, reads=["gt"], writes=["gt"])
                    kb.I("dve", "tensor_scalar", gt[:], gt[:], 0.5, 0.5, ALU.mult, ALU.add, reads=["gt"], writes=["gt"])
                    kb.I("dve", "tensor_tensor", ys_a[:], ys_a[:], gt[:], ALU.mult, reads=["gt", "t_ys0"], writes=["t_ys0"])
                    for j in range(4):
                        kb.I("pe", "transpose", PS[1][:, j * 128:(j + 1) * 128], ys_a[:, j * 128:(j + 1) * 128], ident[:], reads=["t_ys0", "ident"], writes=[f"pX1_{j}"])
                    kb.cp("act", zT[:].rearrange("p j t -> p (j t)"), PS[1][:], reads=bk(1), writes=["zT"])
                    for j in range(4):
                        kb.I("pe", "matmul", PS[2][:], zT[:, j, :], glb[:, j, :], start=(j == 0), stop=(j == 3), reads=["zT", "tw"], writes=bk(2))
                    kb.I("dve", "tensor_tensor", gt[:], PS[2][:], rows["glub"][:], ALU.add, reads=bk(2) + ["rows", "gt"], writes=["gt"])
                    kb.I("act", "activation", gt[:], gt[:], AF.Sigmoid, reads=["gt"], writes=["gt"])
                    kb.I("dve", "tensor_tensor", mix[:, 512:1024], ys_a[:], gt[:], ALU.mult, reads=["gt", "t_ys0"], writes=["mixB"])
                    for half in range(2):
                        for j in range(4):
                            kt = half * 4 + j
                            kb.I("pe", "transpose", PS[3][:, j * 128:(j + 1) * 128], mix[:, kt * 128:(kt + 1) * 128], ident[:], reads=["mixA", "mixB", "ident"], writes=[f"pX3_{j}"])
                        kb.cp("act" if half else "dve", mixT[:, half * 4:half * 4 + 4, :].rearrange("p j t -> p (j t)"), PS[3][:], reads=bk(3), writes=["mixT"])
                    for nh in range(2):
                        ns = slice(nh * 512, (nh + 1) * 512)
                        for kt in range(8):
                            kb.I("pe", "matmul", PS[4 + nh][:], mixT[:, kt, :], wob[:, kt, ns], start=(kt == 0), stop=(kt == 7), reads=["mixT", "tw"], writes=bk(4 + nh))
                        kb.I("dve", "tensor_tensor", x1[:, ns], PS[4 + nh][:], gmix[:, ns], ALU.mult, reads=bk(4 + nh) + ["modt"], writes=["x1"])
                        kb.I("pool", "tensor_tensor", x1[:, ns], x1[:, ns], tx[:, ns], ALU.add, reads=["x1", "tx"], writes=["x1"])
                    if debug and oc == 0:
                        kb.D(dbg["d_x1"], x1[:], reads=["x1"], writes=["o_d_x1"])
                    kb.D(scr["x1"][rsl, :], x1[:], reads=["x1"], writes=[f"scr_x1_{oc}"])
                kb.rec = None
                kb.ksuf = None
                kb.merge(recA)
                st8 = TA[9][0]
                kb.I("dve", "tensor_copy", st8[:, 0:1], st8[:, 0:1], reads=list(kb.lastw.keys()), writes=list(kb.lastw.keys()) + ["fence2"])
            FENCE = ["fence2"]
            with contextlib.ExitStack() as tl:
                w1b = sb(tl, "w1b", [128, 8, DFF], BF16); w3b = sb(tl, "w3b", [128, 8, DFF], BF16)
                w2b = sb(tl, "w2b", [128, 22, D], BF16)
                stg = [sb(tl, f"stgb{i}", [128, 1024]) for i in range(2)]
                jobs = []
                for nm, dst in (("w1", w1b), ("w3", w3b)):
                    for kt in range(8):
                        for c0 in (0, 1024, 2048):
                            jobs.append((nm, dst, kt, c0, min(1024, DFF - c0)))
                jobs += [("w2f", w2b, kt, 0, D) for kt in range(22)]
                for i, (nm, dst, kt, c0, ncol) in enumerate(jobs):
                    s_ = stg[i % 2]; sk = f"stgb{i % 2}"
                    kb.D(s_[:, 0:ncol], din[nm][kt * 128:(kt + 1) * 128, c0:c0 + ncol], reads=FENCE, writes=[sk])
                    kb.cp(("dve", "act", "pool")[i % 3], dst[:, kt, c0:c0 + ncol], s_[:, 0:ncol], reads=[sk], writes=["tw"])
                rows = {"gf": sb(tl, "row_gf", [128, D])}
                kb.D(rows["gf"][:], din["gf"].partition_broadcast(128), reads=FENCE, writes=["rows"])
                A2 = sb(tl, "A2", [128, D]); sffn = sb(tl, "sffn", [128, D]); gffn = sb(tl, "gffn", [128, D])
                def dblb(name, shape, dt=F32):
                    return [sb(tl, f"{name}_{i_}", shape, dt) for i_ in range(2)]
                hh2 = dblb("hh", [128, D]); x12 = dblb("x1b", [128, D]); outt2 = dblb("outt", [128, D])
                hh = hh2[0]
                kb.D(A2[:], din["g2n"].partition_broadcast(128), reads=FENCE, writes=["A2"])
                kb.D(hh[:], scr["mod"][:, 4 * D:5 * D], reads=FENCE + ["scr_mod"], writes=["hh#0"])
                kb.D(sffn[:], scr["mod"][:, 3 * D:4 * D], reads=FENCE + ["scr_mod"], writes=["modt"])
                kb.D(gffn[:], scr["mod"][:, 5 * D:6 * D], reads=FENCE + ["scr_mod"], writes=["modt"])
                kb.I("dve", "scalar_tensor_tensor", A2[:], hh[:], 1.0, A2[:], ALU.add, ALU.mult, reads=["A2", "hh#0"], writes=["A2"])
                hhT2 = dblb("hhT", [128, 8, 128], BF16)
                actT2 = dblb("actT", [128, 22, 128], BF16); s12 = dblb("s1", [128, 256]); ss22 = dblb("ss2", [128, 2]); rs22 = dblb("rs2", [128, 2])
                print("SBUF bytes remaining in tail B scope:", nc.sbuf_bytes_remaining)
                recB = []
                kb.rec = recB
                for oc in range(NOWN):
                    sl_ = oc % 2
                    kb.ksuf = f"#{sl_}"
                    hh, x1, outt, hhT, actT, s1, ss2, rs2 = (t_[sl_] for t_ in (hh2, x12, outt2, hhT2, actT2, s12, ss22, rs22))
                    rsl = slice(oc * 128, (oc + 1) * 128)
                    kb.D(x1[:], scr["x1"][rsl, :], reads=[f"scr_x1_{oc}"], writes=["x1"])
                    kb.I("pool", "memset", ss2[:], 0.0, reads=FENCE, writes=["ss2"])
                    kb.I("act", "activation", hh[:], x1[:], AF.Square, accum_out=ss2[:, 0:1], reads=["x1", "A2"], writes=["hh", "ss2"])
                    kb.I("dve", "tensor_scalar", rs2[:, 0:1], ss2[:, 0:1], 1.0 / D, 1e-6, ALU.mult, ALU.add, reads=["ss2"], writes=["rs2"])
                    kb.I("act", "activation", rs2[:, 0:1], rs2[:, 0:1], AF.Sqrt, reads=["rs2"], writes=["rs2"])
                    kb.I("dve", "reciprocal", rs2[:, 0:1], rs2[:, 0:1], reads=["rs2"], writes=["rs2"])
                    kb.I("dve", "scalar_tensor_tensor", hh[:], x1[:], rs2[:, 0:1], A2[:], ALU.mult, ALU.mult, reads=["x1", "rs2", "A2", "hh"], writes=["hh"])
                    kb.I("pool", "tensor_tensor", hh[:], hh[:], sffn[:], ALU.add, reads=["hh", "modt"], writes=["hh"])
                    for half in range(2):
                        for j in range(4):
                            kt = half * 4 + j
                            kb.I("pe", "transpose", PS[3][:, j * 128:(j + 1) * 128], hh[:, kt * 128:(kt + 1) * 128], ident[:], reads=["hh", "ident"], writes=[f"pX3_{j}"])
                        kb.cp("act" if half else "dve", hhT[:, half * 4:half * 4 + 4, :].rearrange("p j t -> p (j t)"), PS[3][:], reads=bk(3), writes=["hhT"])
                    for ftf in range(22):
                        sl = ftf % 2
                        fs = slice(ftf * 128, (ftf + 1) * 128)
                        bq = (0, 1, 2)[ftf % 3]
                        pa = PS[bq][:, 0:128]; pb = PS[bq][:, 128:256]
                        s1_ = s1[:, (ftf % 2) * 128:(ftf % 2) * 128 + 128]
                        for kt in range(8):
                            kb.I("pe", "matmul", pa, w1b[:, kt, fs], hhT[:, kt, :], start=(kt == 0), stop=(kt == 7), reads=["hhT", "tw"], writes=[f"pX{bq}_0"])
                        for kt in range(8):
                            kb.I("pe", "matmul", pb, w3b[:, kt, fs], hhT[:, kt, :], start=(kt == 0), stop=(kt == 7), reads=["hhT", "tw"], writes=[f"pX{bq}_0"])
                        kb.I("act", "activation", s1_, pa, AF.Silu, reads=[f"pX{bq}_0"], writes=[f"s1_{ftf % 2}"])
                        kb.I("dve", "tensor_tensor", actT[:, ftf, :], s1_, pb, ALU.mult, reads=[f"pX{bq}_0", f"s1_{ftf % 2}"], writes=[f"actT{ftf}"])
                    for nh in range(2):
                        ns = slice(nh * 512, (nh + 1) * 512)
                        bd = 4 + 2 * sl_ + nh
                        for ftf in range(22):
                            kb.I("pe", "matmul", PS[bd][:], actT[:, ftf, :], w2b[:, ftf, ns], start=(ftf == 0), stop=(ftf == 21), reads=[f"actT{ftf}", "tw"], writes=bk(bd))
                        kb.I("dve", "tensor_tensor", outt[:, ns], PS[bd][:], gffn[:, ns], ALU.mult, reads=bk(bd) + ["modt"], writes=["outt"])
                        kb.I("pool", "tensor_tensor", outt[:, ns], outt[:, ns], x1[:, ns], ALU.add, reads=["outt", "x1"], writes=["outt"])
                    kb.I("act", "activation", hh[:], outt[:], AF.Square, accum_out=ss2[:, 1:2], reads=["outt", "hh"], writes=["hh", "ss2"])
                    kb.I("dve", "tensor_scalar", rs2[:, 1:2], ss2[:, 1:2], 1.0 / D, 1e-6, ALU.mult, ALU.add, reads=["ss2"], writes=["rs2"])
                    kb.I("act", "activation", rs2[:, 1:2], rs2[:, 1:2], AF.Sqrt, reads=["rs2"], writes=["rs2"])
                    kb.I("dve", "reciprocal", rs2[:, 1:2], rs2[:, 1:2], reads=["rs2"], writes=["rs2"])
                    kb.I("dve", "scalar_tensor_tensor", outt[:], outt[:], rs2[:, 1:2], rows["gf"][:], ALU.mult, ALU.mult, reads=["outt", "rs2", "rows"], writes=["outt"])
                    kb.D(out_d[rsl, :], outt[:], reads=["outt"], writes=[f"o_out{oc}"])
                kb.rec = None
                kb.ksuf = None
                kb.merge(recB)
                kb.emit(final_keys=[k for k in kb.lastw if k.startswith("o_")])
        else:
            kb.emit(final_keys=[k for k in kb.lastw if k.startswith("o_")] + FENCE)
    return nc


def make_in_maps(inp):
    f = np.float32
    ident = np.eye(128, dtype=f)
    antiI = np.ascontiguousarray(ident[::-1])
    strict = np.triu(np.ones((128, 128), f), 1)
    incl = np.triu(np.ones((128, 128), f), 0)
    consts = {
        "ident": ident, "antiI": antiI,
        "mask_b": np.concatenate([strict, -incl], axis=1), "mask_k": np.concatenate([strict, incl], axis=1),
        "maskT": np.ascontiguousarray(strict.T), "bones": np.kron(np.eye(2, dtype=f), np.ones((64, 64), f)),
    }
    maps = []
    for core in range(8):
        b, hf = core // 2, core % 2
        dsel = [1, 0] if hf else [0, 1]
        x = inp["x"][b]; ctx = inp["ctx"][b]
        conv = inp["rwkv_conv"][0]
        w_in = inp["w_in"][0]
        if hf:
            x = x[::-1]; ctx = ctx[::-1]; conv = conv[::-1, ::-1]
            perm = np.arange(2272)
            perm[1536:1568], perm[1568:1600] = np.arange(1568, 1600), np.arange(1536, 1568)
            perm[1600:1632], perm[1632:1664] = np.arange(1632, 1664), np.arange(1600, 1632)
            w_in = w_in[:, perm]
        m = {
            "x": x, "ctx": ctx, "cc": np.stack([inp["c"][b], inp["c_ctx"]]),
            "mod_w": inp["mod_w"][0], "mod_b": inp["mod_b"][0][None], "g1": inp["norm1_g"][0][None],
            "g2n": inp["norm2_g"][0][None], "gf": inp["final_g"][None], "w_in": w_in, "w_out": inp["w_out"][0],
            "conv": conv.reshape(9, 1536), "w0": inp["rwkv_w0"][0][dsel], "w2": inp["rwkv_w2"][0][dsel],
            "a0": inp["rwkv_a0"][0][dsel], "a2": inp["rwkv_a2"][0][dsel], "g2": inp["rwkv_g2"][0],
            "kkv": inp["rwkv_kk"][0], "kav": inp["rwkv_ka"][0], "rkv": inp["rwkv_rk"][0].reshape(512),
            "lnw": inp["rwkv_ln_w"][0][None], "lnb": inp["rwkv_ln_b"][0][None],
            "lam_re": inp["s5_lam_re"][0][dsel], "lam_im": inp["s5_lam_im"][0][dsel], "lstep": inp["s5_log_step"][0][dsel],
            "b_re": inp["s5_b_re"][0], "b_im": inp["s5_b_im"][0], "c_re": inp["s5_c_re"][0], "c_im": inp["s5_c_im"][0],
            "s5d": inp["s5_d"][0], "gluw": inp["s5_glu_w"][0], "glub": inp["s5_glu_b"][0][None],
            "w1": inp["ffn_w1"][0], "w3": inp["ffn_w3"][0], "w2f": inp["ffn_w2"][0],
        }
        m.update(consts)
        maps.append({k: np.ascontiguousarray(np.asarray(v, dtype=f)).reshape(IN_SHAPES[k]) for k, v in m.items()})
    return maps


def kernel(**inputs):
    inp = {k: np.asarray(v) for k, v in inputs.items()}
    nc = build_nc()
    maps = make_in_maps(inp)
    res = run_bass_kernel_spmd(nc, maps, core_ids=list(range(8)))
    out = np.zeros((4, T_LAT, D), np.float32)
    for core in range(8):
        b, hf = core // 2, core % 2
        o = np.asarray(res.results[core]["out"], dtype=np.float32)
        if hf:
            out[b, OWN:] = o[::-1]
        else:
            out[b, :OWN] = o
    return out
```

```python
import contextlib
import numpy as np
import concourse.bass as bass
import concourse.mybir as mybir
from concourse.bass_utils import run_bass_kernel_spmd

F32 = mybir.dt.float32
BF16 = mybir.dt.bfloat16
ALU = mybir.AluOpType
AF = mybir.ActivationFunctionType
AXX = mybir.AxisListType.X

SEM_CAP = 16000
N_DMA_SEM = 24
SCHED_WINDOW_STREAMS = 1000
SAME_ENG_WAIT = True

T_LAT, T_CTX, D, DFF = 4096, 256, 1024, 2816
OWN = 2048
NOWN = OWN // 128
PI = float(np.pi)
KDEC = 0.6065306597126334


class KB:
    ENGS = ("pe", "dve", "act", "pool", "sp")

    def __init__(self, nc):
        self.nc = nc
        self.ops = {e: [] for e in self.ENGS}
        self.lastw = {}
        self.readers = {}
        self.ndma = 0
        self.rr = 0

    @staticmethod
    def _norm(reads, writes):
        r2 = [k for k in reads if not k.startswith("pX")]
        w2 = [k for k in writes if not k.startswith("pX")]
        banks = {"bank" + k[2:].split("_")[0] for k in list(reads) + list(writes) if k.startswith("pX")}
        return r2, w2 + sorted(banks)

    def _deps(self, me, reads, writes):
        reads, writes = self._norm(reads, writes)
        deps = set()
        for k in reads:
            w = self.lastw.get(k)
            if w is not None:
                deps.add(w)
        for k in writes:
            w = self.lastw.get(k)
            if w is not None:
                deps.add(w)
            for r in self.readers.get(k, ()):
                deps.add(r)
        deps.discard(me)
        for k in reads:
            self.readers.setdefault(k, []).append(me)
        for k in writes:
            self.lastw[k] = me
            self.readers[k] = []
        return deps

    mute = False
    rec = None
    ksuf = None
    KGLOBAL = ("pX", "scr_", "o_", "fence", "rows", "tw", "modt", "ident", "A2")

    def _sfx(self, keys):
        if not self.ksuf:
            return list(keys)
        return [k if k.startswith(self.KGLOBAL) else k + self.ksuf for k in keys]

    @staticmethod
    def _est(op):
        def nfree(ap):
            n = 1
            for d in ap.shape[1:]:
                n *= d
            return n
        if op[0] == "D":
            ap = op[1]
            nbytes = nfree(ap) * ap.shape[0] * (2 if ap.dtype == BF16 else 4)
            return "sp", 0.08, 2.2 + nbytes / 1.0e5
        eng, meth, args = op[1], op[2], op[3]
        n = nfree(args[0])
        if eng == "pe":
            f32 = args[1].dtype == F32
            d = 0.09 + n * (0.0017 if f32 else 0.00045)
        elif eng == "dve":
            d = 0.25 + n * 0.00104 * (6.0 if meth == "reciprocal" else 1.0)
        elif eng == "act":
            d = 0.2 + n * 0.00104
        else:
            d = 0.45 + n * 0.0026
        return eng, d, d

    def merge(self, *streams):
        ops = [op for st_ in streams for op in st_]
        n = len(ops)
        lastw, readers = {}, {}
        preds = [set() for _ in range(n)]
        for i, op in enumerate(ops):
            r_, w_ = (op[5], op[6]) if op[0] == "I" else (op[4], op[5])
            r_, w_ = self._norm(r_, w_)
            for k in r_:
                if k in lastw:
                    preds[i].add(lastw[k])
            for k in w_:
                if k in lastw:
                    preds[i].add(lastw[k])
                preds[i].update(readers.get(k, ()))
            preds[i].discard(i)
            for k in r_:
                readers.setdefault(k, []).append(i)
            for k in w_:
                lastw[k] = i
                readers[k] = []
        succs = [[] for _ in range(n)]
        for i in range(n):
            for p in preds[i]:
                succs[p].append(i)
        est = [self._est(op) for op in ops]
        cp = [0.0] * n
        for i in range(n - 1, -1, -1):
            cp[i] = est[i][2] + max((cp[j] for j in succs[i]), default=0.0)
        npred = [len(p) for p in preds]
        ready = [i for i in range(n) if npred[i] == 0]
        fin = [0.0] * n
        eng_free = {e: 0.0 for e in self.ENGS}
        LAT = 1.0
        while ready:
            best, bkey = None, None
            for i in ready:
                e = est[i][0]
                t = eng_free[e]
                for p in preds[i]:
                    tp = fin[p] + (0.05 if est[p][0] == e else LAT)
                    if tp > t:
                        t = tp
                key = (t - 0.02 * cp[i], i)
                if bkey is None or key < bkey:
                    best, bkey, bt = i, key, t
            i = best
            ready.remove(i)
            e = est[i][0]
            eng_free[e] = bt + est[i][1]
            fin[i] = bt + est[i][2]
            op = ops[i]
            if op[0] == "I":
                self.I(op[1], op[2], *op[3], reads=op[5], writes=op[6], **op[4])
            else:
                self.D(op[1], op[2], reads=op[4], writes=op[5], **op[3])
            for j in succs[i]:
                npred[j] -= 1
                if npred[j] == 0:
                    ready.append(j)

    def I(self, eng, meth, *args, reads=(), writes=(), **kw):
        if self.mute:
            return
        if self.rec is not None:
            self.rec.append(("I", eng, meth, args, kw, self._sfx(reads), self._sfx(writes)))
            return
        idx = len(self.ops[eng])
        deps = self._deps((eng, idx), list(reads), list(writes))
        self.ops[eng].append(((meth, args, kw), deps, None))

    def D(self, out, in_, reads=(), writes=(), **kw):
        if self.mute:
            return
        if self.rec is not None:
            self.rec.append(("D", out, in_, dict(kw), self._sfx(reads), self._sfx(writes)))
            return
        k = self.ndma
        self.ndma += 1
        deps = self._deps(("dma", k), list(reads), list(writes))
        kw = dict(kw)
        kw["out"] = out
        kw["in_"] = in_
        self.ops["sp"].append((("dma_start", (), kw), deps, k))

    def cp(self, eng, out, in_, reads=(), writes=()):
        self.I(eng, "copy" if eng == "act" else "tensor_copy", out, in_, reads=reads, writes=writes)

    def ts1(self, eng, out, in0, s, op, reads=(), writes=()):
        self.I(eng, "tensor_scalar", out, in0, s, 0.0, op, ALU.add, reads=reads, writes=writes)

    def ew(self):
        self.rr += 1
        return "dve" if (self.rr % 3) else "pool"

    def ev(self):
        self.rr += 1
        return "dve" if (self.rr % 2) else "act"

    def emit(self, final_keys=()):
        nc = self.nc
        me = ("sp", len(self.ops["sp"]))
        deps = self._deps(me, list(final_keys), [])
        self.ops["sp"].append((None, deps, None))
        nsem = {e: (len(self.ops[e]) + SEM_CAP - 1) // SEM_CAP + 1 for e in self.ENGS}
        with contextlib.ExitStack() as st:
            sems = {e: [st.enter_context(nc.semaphore(f"s_{e}{i}")) for i in range(nsem[e])]
                    for e in self.ENGS}
            dsems = [st.enter_context(nc.semaphore(f"s_dma{i}")) for i in range(N_DMA_SEM)]
            block = st.enter_context(nc.Block())

            def waitspec(p):
                if p[0] == "dma":
                    k = p[1]
                    return ("d", k % N_DMA_SEM), dsems[k % N_DMA_SEM], 16 * (k // N_DMA_SEM + 1)
                e, i = p
                return (e, i // SEM_CAP), sems[e][i // SEM_CAP], i % SEM_CAP + 1

            def run(ename, eobj):
                waited = {}
                for idx, (fn, deps, dk) in enumerate(self.ops[ename]):
                    specs = []
                    for p in deps:
                        if p[0] == ename and (ename == "pe" or not SAME_ENG_WAIT):
                            continue
                        specs.append(waitspec(p))
                    if dk is not None and dk >= N_DMA_SEM:
                        specs.append((("d", dk % N_DMA_SEM), dsems[dk % N_DMA_SEM],
                                      16 * (dk // N_DMA_SEM)))
                    for sid, sem, val in specs:
                        if waited.get(sid, 0) >= val:
                            continue
                        waited[sid] = val
                        eobj.wait_ge(sem, val)
                    if fn is None:
                        continue
                    meth, args, kw = fn
                    ins = getattr(eobj, meth)(*args, **kw)
                    if dk is not None:
                        ins.then_inc(dsems[dk % N_DMA_SEM], 16)
                    else:
                        ins.then_inc(sems[ename][idx // SEM_CAP], 1)

            @block.tensor
            def _(e):
                run("pe", e)

            @block.vector
            def _(e):
                run("dve", e)

            @block.scalar
            def _(e):
                run("act", e)

            @block.gpsimd
            def _(e):
                run("pool", e)

            @block.sync
            def _(e):
                run("sp", e)


IN_SHAPES = {
    "x": [T_LAT, D], "ctx": [T_CTX, D], "cc": [2, D], "mod_w": [D, 6 * D], "mod_b": [1, 6 * D],
    "g1": [1, D], "g2n": [1, D], "gf": [1, D], "w_in": [D, 2272], "w_out": [D, D],
    "conv": [9, 1536], "w0": [2, 512], "w2": [2, 32, 512], "a0": [2, 512], "a2": [2, 32, 512],
    "g2": [96, 512], "kkv": [512], "kav": [512], "rkv": [512], "lnw": [1, 512], "lnb": [1, 512],
    "lam_re": [2, 32, 64], "lam_im": [2, 32, 64], "lstep": [2, 32],
    "b_re": [32, 64, 16], "b_im": [32, 64, 16], "c_re": [32, 16, 64], "c_im": [32, 16, 64],
    "s5d": [512], "gluw": [512, 512], "glub": [1, 512],
    "w1": [D, DFF], "w3": [D, DFF], "w2f": [DFF, D],
    "ident": [128, 128], "antiI": [128, 128], "mask_b": [128, 256], "mask_k": [128, 256],
    "maskT": [128, 128], "bones": [128, 128],
}

DBG_SHAPES = {"d_mod": [128, 6 * D], "d_AB": [128, 32], "d_z": [128, 3 * 256], "d_zu": [128, 512],
              "d_rkvc": [128, 3 * 128], "d_yT": [128, 512], "d_ST": [128, 256], "d_ys": [128, 512],
              "d_x1": [128, D], "d_car": [128, 32], "d_F": [128, 2048], "d_Bw": [128, 2048]}


def build_nc(n_lat_chunks=(16, 16), do_tail=True, debug=False, upto="full", n_ctx=2):
    nc = bass.Bass("TRN2", target_bir_lowering=False)
    din = {k: nc.dram_tensor(k, s, F32, kind="ExternalInput").ap() for k, s in IN_SHAPES.items()}
    out_d = nc.dram_tensor("out", [OWN, D], F32, kind="ExternalOutput").ap()
    scr = {}
    for nm in ("yr0", "yr1", "ys0", "ys1", "bv0", "bv1"):
        scr[nm] = nc.dram_tensor("scr_" + nm, [OWN, 512], F32, kind="Internal").ap()
    scr["sg"] = nc.dram_tensor("scr_sg", [NOWN * 128, 128], F32, kind="Internal").ap()
    scr["x1"] = nc.dram_tensor("scr_x1", [OWN, D], F32, kind="Internal").ap()
    scr["mod"] = nc.dram_tensor("scr_mod", [128, 6 * D], F32, kind="Internal").ap()
    scr["dg"] = nc.dram_tensor("scr_dg", [12, 128, 9 * 128], BF16, kind="Internal").ap()
    scr["fr"] = nc.dram_tensor("scr_fr", [NOWN * 4 * 128, 512], F32, kind="Internal").ap()
    scr["fz"] = nc.dram_tensor("scr_fz", [NOWN * 128, 128], F32, kind="Internal").ap()
    scr["fu"] = nc.dram_tensor("scr_fu", [NOWN * 128, 512], BF16, kind="Internal").ap()
    scr["cc_in"] = nc.dram_tensor("scr_cc_in", [128, 288], F32, kind="Internal").ap()
    scr["cc_out"] = nc.dram_tensor("scr_cc_out", [128, 288], F32, kind="Internal").ap()
    dbg = {}
    if debug:
        for nm, shp in DBG_SHAPES.items():
            dbg[nm] = nc.dram_tensor(nm, shp, F32, kind="ExternalOutput").ap()
    kb = KB(nc)
    cnt = [0]

    with contextlib.ExitStack() as g:
        def sb(st, name, shape, dt=F32):
            cnt[0] += 1
            return st.enter_context(nc.sbuf_tensor(f"sb{cnt[0]}_{name}", shape, dt))

        ident = sb(g, "ident", [128, 128]); antiI = sb(g, "antiI", [128, 128])
        mask_b = sb(g, "mask_b", [128, 256]); mask_k = sb(g, "mask_k", [128, 256])
        maskT = sb(g, "maskT", [128, 128]); bones = sb(g, "bones", [128, 128])
        ones = sb(g, "ones", [128, 128])
        for nm, t in (("ident", ident), ("antiI", antiI), ("mask_b", mask_b), ("mask_k", mask_k),
                      ("maskT", maskT), ("bones", bones)):
            kb.D(t[:], din[nm], writes=[nm])
        kb.I("pool", "memset", ones[:], 1.0, writes=["ones"])
        AB = sb(g, "AB", [128, 4, 8])
        cnt[0] += 1
        PS = [g.enter_context(nc.psum_tensor(f"psbank{i}", [128, 512], F32)) for i in range(8)]

        def bk(i):
            return [f"pX{i}_{j}" for j in range(4)]

        wscope = contextlib.ExitStack()
        win = sb(wscope, "win", [128, 8, 2272], BF16)
        car_re = sb(wscope, "car_re", [128, 16]); car_im = sb(wscope, "car_im", [128, 16])
        gs_re = sb(wscope, "gs_re", [128, 16]); gs_im = sb(wscope, "gs_im", [128, 16]); gta = sb(wscope, "gta", [128, 16]); gtb = sb(wscope, "gtb", [128, 16])
        Bw_re = sb(wscope, "Bw_re", [128, 16, 128], BF16); Bw_im = sb(wscope, "Bw_im", [128, 16, 128], BF16)
        Cw_re = sb(wscope, "Cw_re", [128, 16, 32]); Cw_imn = sb(wscope, "Cw_imn", [128, 16, 32])
        E_re = sb(wscope, "E_re", [128, 16, 128]); E_im = sb(wscope, "E_im", [128, 16, 128])
        F_re = sb(wscope, "F_re", [128, 16, 128]); F_im = sb(wscope, "F_im", [128, 16, 128])
        s5t = {nm: sb(wscope, "s5" + nm, [128, 512]) for nm in ("t1", "t2", "Xre", "Xim", "Gre", "Gim", "Hre", "Him")}

        def s5_setup(sd, F0=()):
            K = "s5s"
            S5K = ["s5t1", "s5t2", "s5Xre", "s5Xim", "s5Gre", "s5Gim", "s5Hre", "s5Him", "s5tab", K]
            with contextlib.ExitStack() as t_:
                sm = sb(t_, "s5sm", [128, 20, 16])
                (lre, lim, stp, th, th2, tmpc, mag, imag, sn, csn, lbr, lbi, den, nr, qre, qim, u1, ivr, ivi) = [sm[:, i_, :] for i_ in range(19)]
                bre = s5t["Gre"][:, 0:256].rearrange("p (q h) -> p q h", h=16); bim = s5t["Gre"][:, 256:512].rearrange("p (q h) -> p q h", h=16)
                bbr = s5t["Gim"][:, 0:256].rearrange("p (q h) -> p q h", h=16); bbi = s5t["Gim"][:, 256:512].rearrange("p (q h) -> p q h", h=16)
                v1 = s5t["Hre"][:, 0:256].rearrange("p (q h) -> p q h", h=16); cst = s5t["Hre"][:, 256:512].rearrange("p (q h) -> p q h", h=16)
                bdw = s5t["Xre"][:].rearrange("p (r c) -> p r c", c=128)

                def V(meth, *args, eng="dve", **kw):
                    kb.I(eng, meth, *args, reads=S5K, writes=S5K, **kw)

                kb.D(lre, din["lam_re"][sd].rearrange("(q g) p -> (g p) q", g=2), reads=list(F0) + S5K, writes=S5K, allow_slow_non_contiguous=True)
                kb.D(lim, din["lam_im"][sd].rearrange("(q g) p -> (g p) q", g=2), reads=list(F0), writes=[K], allow_slow_non_contiguous=True)
                for g2_ in range(2):
                    kb.D(stp[64 * g2_:64 * g2_ + 64, :],
                         din["lstep"][sd:sd + 1, :].rearrange("o (q g) -> o q g", g=2)[:, :, g2_].partition_broadcast(64),
                         reads=list(F0), writes=[K], allow_slow_non_contiguous=True)
                kb.D(bre, din["b_re"].rearrange("(q g) p h -> (g p) q h", g=2), reads=list(F0), writes=[K])
                kb.D(bim, din["b_im"].rearrange("(q g) p h -> (g p) q h", g=2), reads=list(F0), writes=[K])
                V("activation", stp, stp, AF.Exp, eng="act")
                V("tensor_tensor", mag, lre, stp, ALU.mult)
                V("tensor_tensor", th, lim, stp, ALU.mult)
                V("activation", imag, mag, AF.Exp, eng="act", scale=-1.0)
                V("activation", mag, mag, AF.Exp, eng="act")
                V("tensor_copy", u1, th)
                for m in (PI, 3 * PI, 5 * PI):
                    V("tensor_scalar", tmpc, th, m, -2 * PI, ALU.is_ge, ALU.mult)
                    V("tensor_tensor", u1, u1, tmpc, ALU.add)
                V("tensor_scalar", u1, u1, 0.125, 0.0, ALU.mult, ALU.add)
                V("tensor_tensor", th2, u1, u1, ALU.mult)
                V("tensor_scalar", sn, th2, -1.0 / 5040, 1.0 / 120, ALU.mult, ALU.add)
                V("tensor_tensor", sn, sn, th2, ALU.mult)
                V("tensor_scalar", sn, sn, -1.0 / 6, 0.0, ALU.add, ALU.add)
                V("tensor_tensor", sn, sn, th2, ALU.mult)
                V("tensor_scalar", sn, sn, 1.0, 0.0, ALU.add, ALU.add)
                V("tensor_tensor", sn, sn, u1, ALU.mult)
                V("tensor_scalar", csn, th2, 1.0 / 40320, -1.0 / 720, ALU.mult, ALU.add)
                V("tensor_tensor", csn, csn, th2, ALU.mult)
                V("tensor_scalar", csn, csn, 1.0 / 24, 0.0, ALU.add, ALU.add)
                V("tensor_tensor", csn, csn, th2, ALU.mult)
                V("tensor_scalar", csn, csn, -0.5, 0.0, ALU.add, ALU.add)
                V("tensor_tensor", csn, csn, th2, ALU.mult)
                V("tensor_scalar", csn, csn, 1.0, 0.0, ALU.add, ALU.add)
                for _ in range(3):
                    V("tensor_tensor", tmpc, csn, sn, ALU.mult)
                    V("tensor_tensor", th2, sn, sn, ALU.mult)
                    V("tensor_tensor", csn, csn, csn, ALU.mult)
                    V("tensor_tensor", csn, csn, th2, ALU.subtract)
                    V("tensor_scalar", sn, tmpc, 2.0, 0.0, ALU.mult, ALU.add)
                V("tensor_tensor", lbr, mag, csn, ALU.mult)
                V("tensor_tensor", lbi, mag, sn, ALU.mult)
                V("tensor_tensor", ivr, imag, csn, ALU.mult)
                V("scalar_tensor_tensor", ivi, imag, -1.0, sn, ALU.mult, ALU.mult)
                V("tensor_tensor", den, lre, lre, ALU.mult)
                V("tensor_tensor", u1, lim, lim, ALU.mult)
                V("tensor_tensor", den, den, u1, ALU.add)
                V("reciprocal", den, den)
                V("tensor_scalar", nr, lbr, -1.0, 0.0, ALU.add, ALU.add)
                V("tensor_tensor", qre, nr, lre, ALU.mult)
                V("tensor_tensor", u1, lbi, lim, ALU.mult)
                V("tensor_tensor", qre, qre, u1, ALU.add)
                V("tensor_tensor", qre, qre, den, ALU.mult)
                V("tensor_tensor", qim, lbi, lre, ALU.mult)
                V("tensor_tensor", u1, nr, lim, ALU.mult)
                V("tensor_tensor", qim, qim, u1, ALU.subtract)
                V("tensor_tensor", qim, qim, den, ALU.mult)
                qreb = qre.unsqueeze(2).to_broadcast([128, 16, 16]); qimb = qim.unsqueeze(2).to_broadcast([128, 16, 16])
                V("tensor_tensor", bbr, bre, qreb, ALU.mult)
                V("tensor_tensor", v1, bim, qimb, ALU.mult)
                V("tensor_tensor", bbr, bbr, v1, ALU.subtract)
                V("tensor_tensor", bbi, bim, qreb, ALU.mult)
                V("tensor_tensor", v1, bre, qimb, ALU.mult)
                V("tensor_tensor", bbi, bbi, v1, ALU.add)
                for src_, dstT in ((bbr, Bw_re), (bbi, Bw_im)):
                    for qq in range(4):
                        V("memset", bdw, 0.0)
                        for r_ in range(4):
                            for g2_ in range(2):
                                c0 = 32 * r_ + 16 * g2_
                                V("tensor_copy", bdw[64 * g2_:64 * g2_ + 64, r_, c0:c0 + 16], src_[64 * g2_:64 * g2_ + 64, qq * 4 + r_, :])
                        for r_ in range(4):
                            kb.I("pe", "transpose", PS[4][:, r_ * 128:(r_ + 1) * 128], bdw[:, r_, :], ident[:],
                                 reads=S5K + ["ident"], writes=[f"pX4_{r_}"])
                        kb.I("dve", "tensor_copy", dstT[:, qq * 4:(qq + 1) * 4, :], PS[4][:].rearrange("p (r c) -> p r c", r=4),
                             reads=bk(4) + S5K, writes=S5K)
                cpad = s5t["Xim"][:].rearrange("p (j c) -> p j c", c=128)
                for nm, dstC, sc in (("c_re", Cw_re, 1.0), ("c_im", Cw_imn, -1.0)):
                    csrc = din[nm].rearrange("g h p -> (g h) p").rearrange("(j r) p -> r j p", r=128)
                    for hh_ in range(2):
                        kb.D(cpad[:, :, 64 * hh_:64 * hh_ + 64], csrc, reads=S5K, writes=S5K)
                    for j_ in range(4):
                        kb.I("pe", "transpose", PS[4][:, j_ * 128:(j_ + 1) * 128], cpad[:, j_, :], ident[:], reads=S5K + ["ident"], writes=[f"pX4_{j_}"])
                    V("memset", dstC[:], 0.0)
                    for g2_ in range(2):
                        srcv = PS[4][64 * g2_:64 * g2_ + 64, :].rearrange("p (q g h) -> p q g h", g=2, h=16)[:, :, g2_, :]
                        kb.I("dve", "tensor_scalar", dstC[64 * g2_:64 * g2_ + 64, :, 16 * g2_:16 * g2_ + 16], srcv, sc, 0.0, ALU.mult, ALU.add,
                             reads=bk(4) + S5K, writes=S5K)
                for (Tr, Ti, sr, si) in ((F_re, F_im, lbr, lbi), (E_re, E_im, ivr, ivi)):
                    V("tensor_copy", Tr[:, :, 0:1], sr.unsqueeze(2))
                    V("tensor_copy", Ti[:, :, 0:1], si.unsqueeze(2))
                    n_ = 1
                    while n_ < 128:
                        for qh in range(2):
                            qsl = slice(8 * qh, 8 * qh + 8)
                            pr = Tr[:, qsl, n_ - 1:n_].to_broadcast([128, 8, n_]); pim = Ti[:, qsl, n_ - 1:n_].to_broadcast([128, 8, n_])
                            t1_ = s5t["t1"][:, 0:8 * n_].rearrange("p (q n) -> p q n", q=8)
                            t2_ = s5t["t2"][:, 0:8 * n_].rearrange("p (q n) -> p q n", q=8)
                            V("tensor_tensor", t1_, Tr[:, qsl, 0:n_], pr, ALU.mult)
                            V("tensor_tensor", t2_, Ti[:, qsl, 0:n_], pim, ALU.mult)
                            V("tensor_tensor", Tr[:, qsl, n_:2 * n_], t1_, t2_, ALU.subtract)
                            V("tensor_tensor", t1_, Tr[:, qsl, 0:n_], pim, ALU.mult)
                            V("tensor_tensor", t2_, Ti[:, qsl, 0:n_], pr, ALU.mult)
                            V("tensor_tensor", Ti[:, qsl, n_:2 * n_], t1_, t2_, ALU.add)
                        n_ *= 2
                if debug and sd == 0:
                    kb.D(dbg["d_F"], F_re[:].rearrange("p q t -> p (q t)"), reads=S5K, writes=["o_d_F"])
                V("tensor_copy", lre[:, 0:1], lre[:, 0:1])

        with contextlib.ExitStack() as p0:
            rec0 = []
            kb.rec = rec0
            stage = [sb(p0, f"stage{i}", [128, 2272]) for i in range(2)]
            for kt in range(8):
                s_ = stage[kt % 2]; sk = f"stage{kt % 2}"
                kb.D(s_[:], din["w_in"][kt * 128:(kt + 1) * 128, :], writes=[sk])
                kb.cp("act" if kt % 2 else "pool", win[:, kt, :], s_[:], reads=[sk], writes=["win"])
            modb = sb(p0, "modb", [128, 6 * D])
            cT = sb(p0, "cT", [128, 2, 8]); cs = sb(p0, "cs", [128, 2, 8])
            lbx = sb(p0, "lbx", [128, 8, 128]); lbc = sb(p0, "lbc", [128, 8, 128])
            modc = sb(p0, "modc", [128, 2 * D]); mbias = sb(p0, "mbias", [128, 6 * D])
            g1b = sb(p0, "g1b", [128, D]); tmpA = sb(p0, "tmpA", [128, D])
            wbuf = [sb(p0, f"wbuf{i}", [128, 512]) for i in range(4)]
            for j_ in range(2):
                kb.D(cT[:, j_, :], din["cc"][j_].rearrange("(k p) -> p k", p=128), writes=["cT"], allow_slow_non_contiguous=True)
            kb.D(mbias[:], din["mod_b"].partition_broadcast(128), writes=["mbias"])
            kb.D(g1b[:], din["g1"].partition_broadcast(128), writes=["g1b"])
            kb.I("act", "activation", cs[:], cT[:], AF.Silu, reads=["cT"], writes=["cs"])
            for kt in range(8):
                kb.I("dve", "tensor_copy", lbx[:, kt, :], cs[:, 0, kt:kt + 1].to_broadcast([128, 128]), reads=["cs"], writes=["lbx"])
                kb.I("pool", "tensor_copy", lbc[:, kt, :], cs[:, 1, kt:kt + 1].to_broadcast([128, 128]), reads=["cs"], writes=["lbc"])
            i = 0
            for n in range(12 if upto != "p0a" else 0):
                ns = slice(n * 512, (n + 1) * 512)
                for kt in range(8):
                    wb = wbuf[i % 4]; wk = f"wbuf{i % 4}"; i += 1
                    kb.D(wb[:], din["mod_w"][kt * 128:(kt + 1) * 128, ns], writes=[wk])
                    kb.I("pe", "matmul", PS[0][:], lbx[:, kt, :], wb[:], start=(kt == 0), stop=(kt == 7), reads=[wk, "lbx"], writes=bk(0))
                    if n < 4:
                        kb.I("pe", "matmul", PS[1][:], lbc[:, kt, :], wb[:], start=(kt == 0), stop=(kt == 7), reads=[wk, "lbc"], writes=bk(1))
                kb.I("dve", "tensor_tensor", modb[:, ns], PS[0][:], mbias[:, ns], ALU.add, reads=bk(0) + ["mbias"], writes=["modb"])
                if n < 4:
                    kb.I("dve", "tensor_tensor", modc[:, ns], PS[1][:], mbias[:, ns], ALU.add, reads=bk(1) + ["mbias"], writes=["modc"])
            for which, src_ in ((0, modb), (1, modc)) if upto not in ("p0a", "p0b") else ():
                kb.I("dve", "scalar_tensor_tensor", tmpA[:], src_[:, D:2 * D], 1.0, g1b[:], ALU.add, ALU.mult,
                     reads=["modb", "modc", "g1b"], writes=["tmpA"])
                for half in range(2):
                    for j in range(4):
                        kt = half * 4 + j
                        kb.I("pe", "transpose", PS[2][:, j * 128:(j + 1) * 128], tmpA[:, kt * 128:(kt + 1) * 128], ident[:],
                             reads=["tmpA", "ident"], writes=[f"pX2_{j}"])
                        kb.I("pe", "transpose", PS[3][:, j * 128:(j + 1) * 128], src_[:, kt * 128:(kt + 1) * 128], ident[:],
                             reads=["modb", "modc", "ident"], writes=[f"pX3_{j}"])
                    for j in range(4):
                        kt = half * 4 + j
                        kb.I("dve", "tensor_copy", AB[:, 2 * which, kt:kt + 1], PS[2][:, j * 128:j * 128 + 1], reads=[f"pX2_{j}"], writes=["AB"])
                        kb.I("dve", "tensor_copy", AB[:, 2 * which + 1, kt:kt + 1], PS[3][:, j * 128:j * 128 + 1], reads=[f"pX3_{j}"], writes=["AB"])
            if upto not in ("p0a", "p0b", "p0c"):
                kb.D(scr["mod"], modb[:], reads=["modb"], writes=["scr_mod"])
            if debug and upto not in ("p0a", "p0b", "p0c"):
                kb.D(dbg["d_mod"], modb[:], reads=["modb"], writes=["o_d_mod"])
                kb.D(dbg["d_AB"], AB[:].rearrange("p a k -> p (a k)"), reads=["AB"], writes=["o_d_AB"])
            kb.rec = None
            recS = []
            if not upto.startswith("p0"):
                kb.rec = recS
                s5_setup(0)
                kb.rec = None
            kb.merge(rec0, recS)
            kb.I("dve", "tensor_copy", tmpA[:, 0:1], tmpA[:, 0:1],
                 reads=list(kb.lastw.keys()), writes=list(kb.lastw.keys()) + ["fence0"])
        FENCE0 = ["fence0"]
        if upto.startswith("p0"):
            kb.mute = True

        with contextlib.ExitStack() as sw:
            F0 = FENCE0
            kkc = sb(sw, "kkc", [128, 4]); kac = sb(sw, "kac", [128, 4]); omka = sb(sw, "omka", [128, 4])
            rkc = sb(sw, "rkc", [128, 4]); w0c = sb(sw, "w0c", [128, 2, 4]); a0c = sb(sw, "a0c", [128, 2, 4])
            cw = sb(sw, "cw", [128, 12, 9]); s5dc = sb(sw, "s5dc", [128, 4])
            w2pad = sb(sw, "w2pad", [128, 2, 512]); a2pad = sb(sw, "a2pad", [128, 2, 512])
            diagd = sb(sw, "diagd", [128, 4, 128], BF16)
            for t, nm in ((kkc, "kkv"), (kac, "kav"), (rkc, "rkv"), (s5dc, "s5d")):
                kb.D(t[:], din[nm].rearrange("(f p) -> p f", p=128), reads=F0, writes=["cols"], allow_slow_non_contiguous=True)
            for t, nm in ((w0c, "w0"), (a0c, "a0")):
                for d_ in range(2):
                    kb.D(t[:, d_, :], din[nm][d_].rearrange("(f p) -> p f", p=128), reads=F0, writes=["cols"], allow_slow_non_contiguous=True)
            for tp in range(9):
                kb.D(cw[:, :, tp], din["conv"][tp].rearrange("(f p) -> p f", p=128), reads=F0, writes=["cols"], allow_slow_non_contiguous=True)
            kb.I("dve", "tensor_scalar", omka[:], kac[:], -1.0, 1.0, ALU.mult, ALU.add, reads=["cols"], writes=["cols2"])
            kb.I("pool", "memset", w2pad[:], 0.0, reads=F0, writes=["w2pad"])
            kb.I("pool", "memset", a2pad[:], 0.0, reads=F0, writes=["a2pad"])
            for d_ in range(2):
                kb.D(w2pad[32 * d_:32 * d_ + 32, d_, :], din["w2"][d_], reads=["w2pad"], writes=["w2pad"])
                kb.D(a2pad[64 + 32 * d_:96 + 32 * d_, d_, :], din["a2"][d_], reads=["a2pad"], writes=["a2pad"])
            for ct in range(4):
                kb.ts1("dve", diagd[:, ct, :], ident[:], s5dc[:, ct:ct + 1], ALU.mult, reads=["cols", "ident"], writes=["diagd"])

            ST = sb(sw, "ST", [128, 4, 64])
            xw = sb(sw, "xw", [128, 2, D]); junk = sb(sw, "junk", [128, D], BF16)
            ssum = sb(sw, "ssum", [128, 2]); rstd = sb(sw, "rstd", [128, 2])
            hT = sb(sw, "hT", [128, 8, 256], BF16)
            zrkv2 = [sb(sw, f"zrkv{i}", [128, 3, 264], BF16) for i in range(2)]; zl = sb(sw, "zl", [128, 128]); zg = sb(sw, "zg", [128, 128])
            dgb = [sb(sw, f"dgb{i}", [128, 9, 128], BF16) for i in range(2)]
            rkvc2 = [sb(sw, f"rkvc{i}", [128, 3, 128]) for i in range(2)]
            W2 = {}
            for nm in ("kk", "t1", "t2", "ld", "a", "cs", "Eni", "b", "kd", "bep", "kap"):
                W2[nm] = [sb(sw, f"{nm}{i}", [128, 128]) for i in range(2)]
            zub = [sb(sw, f"zub{i}", [128, 4, 128], BF16) for i in range(2)]
            BE = [sb(sw, f"BE{i}", [128, 4, 128]) for i in range(2)]; KAP = [sb(sw, f"KAP{i}", [128, 4, 128]) for i in range(2)]
            AR = [sb(sw, f"AR{i}", [128, 4, 256]) for i in range(2)]; gC = [sb(sw, f"gC{i}", [128, 4]) for i in range(2)]
            tot = sb(sw, "tot", [128, 4])
            TMt = [sb(sw, f"TMt{i}", [128, 4, 512]) for i in range(2)]
            UT = sb(sw, "UT", [128, 512]); yT = sb(sw, "yT", [128, 512])
            flpR = sb(sw, "flpR", [128, 512]); flpS = sb(sw, "flpS", [128, 512])
            NS = 4
            NB = [sb(sw, f"NB{i}", [128, 256]) for i in range(NS)]; KA = [sb(sw, f"KA{i}", [128, 256]) for i in range(NS)]
            Pq = [sb(sw, f"Pq{i}", [128, 2, 128], BF16) for i in range(NS)]; PTq = [sb(sw, f"PTq{i}", [128, 2, 128], BF16) for i in range(NS)]
            Xq = [sb(sw, f"Xq{i}", [128, 2, 64]) for i in range(NS)]; Xb = [sb(sw, f"Xb{i}", [128, 2, 64], BF16) for i in range(NS)]
            print("SBUF bytes remaining in sweep scope:", nc.sbuf_bytes_remaining)
            for cf in range(12):
                for tp in range(9):
                    kb.ts1("dve" if tp % 2 else "pool", dgb[cf % 2][:, tp, :], ident[:], cw[:, cf, tp:tp + 1], ALU.mult,
                           reads=["cols", "ident", f"dgb{cf % 2}"], writes=[f"dgb{cf % 2}"])
                kb.D(scr["dg"][cf], dgb[cf % 2][:].rearrange("p t c -> p (t c)"), reads=[f"dgb{cf % 2}"], writes=["scr_dg"])
            kb.I("pool", "memset", xw[:], 0.0, reads=F0, writes=["xw0", "xw1"])

            def load_window(sd, is_ctx, c):
                rev = (sd == 1)
                TT = T_CTX if is_ctx else T_LAT
                src = din["ctx"] if is_ctx else din["x"]
                for half in range(2):
                    lo_r = 128 * c - 64 + 128 * half
                    if not rev:
                        lo, hi = lo_r, lo_r + 128
                    else:
                        hi, lo = TT - lo_r, TT - lo_r - 128
                    a0_, a1_ = max(lo, 0), min(hi, TT)
                    if a1_ > a0_:
                        kb.D(xw[a0_ - lo:a1_ - lo, half, :], src[a0_:a1_, :], writes=[f"xw{half}"])

            def process_chunk(sd, is_ctx, c, own, slot, nxt=None, first=False, reuse=False):
                rev = (sd == 1)
                TT = T_CTX if is_ctx else T_LAT
                src = din["ctx"] if is_ctx else din["x"]
                ab = 2 if is_ctx else 0
                tid = antiI if rev else ident
                tk = "antiI" if rev else "ident"
                dbg_here = debug and sd == 0 and (not is_ctx) and c == 0
                SL = f"@{slot}"
                streams = {"front": [], "rwkv": [], "s5": []}
                kb.rec = streams["front"]
                AR_, BE_, KAP_, gC_, TM_, zub_ = AR[slot], BE[slot], KAP[slot], gC[slot], TMt[slot], zub[slot]
                kapT = TM_[:, 0, :]; bepT = TM_[:, 1, :]; vT = TM_[:, 2, :]; bvT = TM_[:, 3, :]
                TMK = "TM" + SL
                if not reuse:
                    if first:
                        load_window(sd, is_ctx, c)
                    kb.I("pool", "memset", ssum[:], 0.0, writes=["ssum"])
                    for half in range(2):
                        kb.I("act", "activation", junk[:], xw[:, half, :], AF.Square, accum_out=ssum[:, half:half + 1],
                             reads=[f"xw{half}"], writes=["junk", "ssum"])
                    kb.I("dve", "tensor_scalar", rstd[:], ssum[:], 1.0 / D, 1e-6, ALU.mult, ALU.add, reads=["ssum"], writes=["rstd"])
                    kb.I("act", "activation", rstd[:], rstd[:], AF.Sqrt, reads=["rstd"], writes=["rstd"])
                    kb.I("dve", "reciprocal", rstd[:], rstd[:], reads=["rstd"], writes=["rstd"])
                    for half in range(2):
                        kb.ts1("dve" if half == 0 else "pool", xw[:, half, :], xw[:, half, :], rstd[:, half:half + 1], ALU.mult,
                               reads=["rstd", f"xw{half}"], writes=[f"xw{half}"])
                    for kt in range(8):
                        b_ = kt % 2
                        for half in range(2):
                            kb.I("pe", "transpose", PS[b_][:, half * 128:(half + 1) * 128], xw[:, half, kt * 128:(kt + 1) * 128], tid[:],
                                 reads=[f"xw{half}", tk], writes=[f"pX{b_}_0"])
                        if kt % 2:
                            kb.I("dve", "tensor_scalar", hT[:, kt, :], PS[b_][:, 0:256], AB[:, ab, kt:kt + 1], AB[:, ab + 1, kt:kt + 1], ALU.mult, ALU.add,
                                 reads=[f"pX{b_}_0", "AB"], writes=[f"hT{kt}"])
                        else:
                            kb.I("act", "activation", hT[:, kt, :], PS[b_][:, 0:256], AF.Identity, bias=AB[:, ab + 1, kt:kt + 1], scale=AB[:, ab, kt:kt + 1],
                                 reads=[f"pX{b_}_0", "AB"], writes=[f"hT{kt}"])
                    if nxt is not None:
                        load_window(sd, nxt[0], nxt[1])
                    others = [("zl", zl[:, :], 1536, 128), ("zg", zg[0:96, :], 1664, 96)] + [(f"zu{j}" + SL, zub_[:, j, :], 1760 + 128 * j, 128) for j in range(4)]
                    for i_, (nm, dst, c0, m) in enumerate(others):
                        b_ = i_ % 2
                        pr_ = PS[b_][0:m, 0:128]
                        pk = [f"pX{b_}_0"]
                        for kt in range(8):
                            kb.I("pe", "matmul", pr_, win[:, kt, c0:c0 + m], hT[:, kt, 64:192], start=(kt == 0), stop=(kt == 7), reads=["win", f"hT{kt}"], writes=pk)
                        if nm == "zg":
                            kb.I("act", "activation", dst, pr_, AF.Sigmoid, reads=pk, writes=[nm])
                        else:
                            kb.cp("dve" if i_ % 2 else "act", dst, pr_, reads=pk, writes=[nm])
                    if own is not None and sd == 0:
                        kb.D(scr["sg"][own * 128:own * 128 + 96, :], zg[0:96, :], reads=["zg"], writes=[f"scr_sg{own}"])
                    kb.I("act", "activation", zl[0:64, :], zl[0:64, :], AF.Tanh, reads=["zl"], writes=["zl"])
                    if sd == 0 and own is not None and not is_ctx:
                        kb.D(scr["fz"][own * 128:(own + 1) * 128, :], zl[:, :], reads=["zl"], writes=[f"scr_fz{own}"])
                        kb.D(scr["fu"][own * 128:(own + 1) * 128, :], zub_[:].rearrange("p j t -> p (j t)"), reads=[f"zu{j}" + SL for j in range(4)], writes=[f"scr_fu{own}"])
                else:
                    kb.D(xw[:, 1, 512:640], scr["fz"][own * 128:(own + 1) * 128, :], reads=[f"scr_fz{own}"], writes=["xw1"])
                    kb.I("dve", "tensor_copy", zl[:, :], xw[:, 1, 512:640][:, ::-1], reads=["xw1"], writes=["zl"])
                    kb.D(junk[:, 0:512], scr["fu"][own * 128:(own + 1) * 128, :], reads=[f"scr_fu{own}"], writes=["junk"])
                    kb.I("dve", "tensor_copy", zub_[:], junk[:, 0:512].rearrange("p (j t) -> p j t", t=128)[:, :, ::-1], reads=["junk"], writes=[f"zu{j}" + SL for j in range(4)])
                sgn = -1 if rev else 1
                nk = 4 if own is not None else 3
                for ft in range(4):
                    fp_ = ft % 2
                    zrkv = zrkv2[fp_]; rkvc = rkvc2[fp_]
                    W = {nm: W2[nm][fp_] for nm in W2}
                    if reuse:
                        stg_ = xw[:, ft % 2, 0:512]
                        kb.D(stg_, scr["fr"][(own * 4 + ft) * 128:(own * 4 + ft + 1) * 128, :], reads=[f"scr_fr{own}_{ft}"], writes=[f"xw{ft % 2}"])
                        kb.I("dve", "tensor_copy", rkvc[:, :, :], stg_[:, 0:384].rearrange("p (j t) -> p j t", t=128)[:, :, ::-1],
                             reads=[f"xw{ft % 2}"], writes=[f"rkvc{j}_{fp_}" for j in range(3)])
                        kb.I("dve", "tensor_copy", W["kk"][:, :], stg_[:, 384:512][:, ::-1], reads=[f"xw{ft % 2}"], writes=[f"W_kk{fp_}"])
                    for j3 in (range(3) if not reuse else ()):
                        b_ = j3 % 2
                        cf = (j3 * 4 + ft) * 128
                        pr_ = PS[b_][:, 0:256]
                        pk = [f"pX{b_}_0"]
                        for kt in range(8):
                            kb.I("pe", "matmul", pr_, win[:, kt, cf:cf + 128], hT[:, kt, :], start=(kt == 0), stop=(kt == 7), reads=["win", f"hT{kt}"], writes=pk)
                        if is_ctx:
                            kb.cp("act" if j3 % 2 else "dve", zrkv[:, j3, 0:256], pr_, reads=pk, writes=[f"zrkv{j3}_{fp_}"])
                        else:
                            if c == 0 and ft < 2:
                                kb.I("pool", "memset", zrkv[:, j3, :].rearrange("p (r c) -> p r c", c=66)[:, :, 0:1], 0.0, reads=[f"zrkv{j3}_{fp_}"], writes=[f"zrkv{j3}_{fp_}"])
                                kb.I("pool", "memset", zrkv[:, j3, :].rearrange("p (r c) -> p r c", c=66)[:, :, 65:66], 0.0, reads=[f"zrkv{j3}_{fp_}"], writes=[f"zrkv{j3}_{fp_}"])
                            kb.cp("act" if j3 % 2 else "dve", zrkv[:, j3, :].rearrange("p (r c) -> p r c", c=66)[:, :, 1:65],
                                  pr_.rearrange("p (r c) -> p r c", c=64), reads=pk, writes=[f"zrkv{j3}_{fp_}"])
                    if dbg_here and ft == 0:
                        kb.D(dbg["d_z"], zrkv[:].rearrange("p f t -> p (f t)"), reads=[f"zrkv{j}_{fp_}" for j in range(3)], writes=["o_d_z"])
                    for j3 in (range(3) if not reuse else ()):
                        zk = f"zrkv{j3}_{fp_}"; ok = f"rkvc{j3}_{fp_}"
                        cf = j3 * 4 + ft
                        ds = (ft * 3 + j3) % 2
                        dk = f"dgb{ds}"
                        kb.D(dgb[ds][:].rearrange("p t c -> p (t c)"), scr["dg"][cf], reads=["scr_dg"], writes=[dk])
                        b_ = (j3 + 1) % 2
                        pc = PS[b_][:, 256:384]
                        taps = [(0, 0)] + [(dy, dx) for dy in ((0,) if is_ctx else (-1, 0, 1)) for dx in (-1, 0, 1) if (dy, dx) != (0, 0)]
                        mms = []
                        for (dy, dx) in taps:
                            wi = (1 + sgn * dy) * 3 + (1 + sgn * dx)
                            if is_ctx:
                                j0, j1 = 64, 192
                                if 128 * c + dx < 0:
                                    j0 = 65
                                if 128 * c + 127 + dx >= TT:
                                    j1 = 191
                                mms.append((PS[b_][:, 256 + j0 - 64:256 + j1 - 64], wi, zrkv[:, j3, j0 + dx:j1 + dx]))
                            else:
                                i0, i1 = 0, 2
                                if 2 * c + dy < 0:
                                    i0 = 1
                                if 2 * c + 1 + dy >= 64:
                                    i1 = 1
                                o0 = 66 * i0 + 1; o1 = 66 * (i1 - 1) + 65
                                sh = 66 * (1 + dy) + dx
                                mms.append((PS[b_][:, 256 + o0:256 + o1], wi, zrkv[:, j3, o0 + sh:o1 + sh]))
                        for ti, (o_ap, wi, i_ap) in enumerate(mms):
                            kb.I("pe", "matmul", o_ap, dgb[ds][:, wi, :], i_ap, start=(ti == 0), stop=(ti == len(mms) - 1),
                                 reads=[zk, dk], writes=[f"pX{b_}_0"])
                        if is_ctx:
                            kb.cp("act" if j3 % 2 == 0 else "dve", rkvc[:, j3, :], pc, reads=[f"pX{b_}_0"], writes=[ok])
                        else:
                            kb.cp("act" if j3 % 2 == 0 else "dve", rkvc[:, j3, :].rearrange("p (r c) -> p r c", c=64),
                                  PS[b_][:, 256:388].rearrange("p (r c) -> p r c", c=66)[:, :, 1:65], reads=[f"pX{b_}_0"], writes=[ok])
                    if dbg_here and ft == 0:
                        kb.D(dbg["d_rkvc"], rkvc[:].rearrange("p f t -> p (f t)"), reads=[f"rkvc{j}_{fp_}" for j in range(3)], writes=["o_d_rkvc"])
                    rc = rkvc[:, 0, :]; kc = rkvc[:, 1, :]; vc = rkvc[:, 2, :]
                    fk = f"w{ft}" + SL
                    g_ = {nm: W[nm][:, :] for nm in W}
                    CK = ["cols", "cols2"]

                    def V(meth, *args, eng="dve", r=(), w=(), **kw):
                        kb.I(eng, meth, *args, reads=list(r) + CK, writes=list(w), **kw)
                    ARa = AR_[:, ft, 0:128]; ARr = AR_[:, ft, 128:256]
                    fa, fr, fb, fkp, fg = fk + "a", fk + "r", fk + "b", fk + "k", fk + "g"
                    if not reuse:
                        V("tensor_scalar", g_["kk"], kc, kkc[:, ft:ft + 1], 0.0, ALU.mult, ALU.add, r=[f"rkvc1_{fp_}"], w=[f"W_kk{fp_}"])
                        V("tensor_tensor", g_["t1"], g_["kk"], g_["kk"], ALU.mult, r=[f"W_kk{fp_}"], w=[f"W_t1{fp_}"])
                        kb.I("pe", "matmul", PS[0][:, 384:512], bones[:], g_["t1"], start=True, stop=True, reads=[f"W_t1{fp_}", "bones"], writes=["pX0_0"])
                        kb.I("dve", "tensor_scalar", g_["t2"], PS[0][:, 384:512], 1e-12, 0.0, ALU.max, ALU.add, reads=["pX0_0"], writes=[f"W_t2{fp_}"])
                        V("activation", g_["t2"], g_["t2"], AF.Sqrt, eng="act", r=[f"W_t2{fp_}"], w=[f"W_t2{fp_}"])
                        V("reciprocal", g_["t2"], g_["t2"], r=[f"W_t2{fp_}"], w=[f"W_t2{fp_}"])
                        V("tensor_tensor", g_["kk"], g_["kk"], g_["t2"], ALU.mult, r=[f"W_kk{fp_}", f"W_t2{fp_}"], w=[f"W_kk{fp_}"])
                        if sd == 0 and own is not None and not is_ctx:
                            r0_ = (own * 4 + ft) * 128
                            kb.D(scr["fr"][r0_:r0_ + 128, 0:384], rkvc[:, :, :].rearrange("p j t -> p (j t)"), reads=[f"rkvc{j}_{fp_}" for j in range(3)], writes=[f"scr_fr{own}_{ft}"])
                            kb.D(scr["fr"][r0_:r0_ + 128, 384:512], g_["kk"], reads=[f"W_kk{fp_}"], writes=[f"scr_fr{own}_{ft}"])
                    kb.I("pe", "matmul", PS[1][:, 256:384], w2pad[:, sd, ft * 128:(ft + 1) * 128], zl[:, :], start=True, stop=True, reads=["zl", "w2pad"], writes=["pX1_0"])
                    kb.I("pe", "matmul", PS[1][:, 384:512], a2pad[:, sd, ft * 128:(ft + 1) * 128], zl[:, :], start=True, stop=True, reads=["zl", "a2pad"], writes=["pX1_0"])
                    kb.I("act", "activation", g_["ld"], PS[1][:, 256:384], AF.Sigmoid, bias=w0c[:, sd, ft:ft + 1], scale=1.0, reads=["pX1_0", "cols"], writes=[f"W_ld{fp_}"])
                    kb.I("act", "activation", g_["a"], PS[1][:, 384:512], AF.Sigmoid, bias=a0c[:, sd, ft:ft + 1], scale=1.0, reads=["pX1_0", "cols"], writes=[f"W_a{fp_}"])
                    V("tensor_tensor_scan", g_["cs"], ones[:], g_["ld"], 0.0, ALU.mult, ALU.add, r=[f"W_ld{fp_}", "ones"], w=[f"W_cs{fp_}"])
                    V("tensor_copy", tot[:, ft:ft + 1], g_["cs"][:, 127:128], r=[f"W_cs{fp_}"], w=[f"tot{ft}"])
                    V("tensor_tensor", g_["t1"], g_["cs"], g_["ld"], ALU.subtract, r=[f"W_cs{fp_}", f"W_ld{fp_}"], w=[f"W_t1{fp_}"])
                    V("activation", ARr, g_["cs"], AF.Exp, eng="act", scale=-KDEC, r=[f"W_cs{fp_}"], w=[fr])
                    V("activation", ARa, g_["t1"], AF.Exp, eng="act", scale=-KDEC, r=[f"W_t1{fp_}"], w=[fa])
                    V("activation", g_["Eni"], g_["cs"], AF.Exp, eng="act", scale=KDEC, r=[f"W_cs{fp_}"], w=[f"W_Eni{fp_}"])
                    V("activation", gC_[:, ft:ft + 1], tot[:, ft:ft + 1], AF.Exp, eng="act", scale=-KDEC, r=[f"tot{ft}"], w=[fg])
                    V("tensor_tensor", g_["b"], g_["kk"], g_["a"], ALU.mult, r=[f"W_kk{fp_}", f"W_a{fp_}"], w=[f"W_b{fp_}"])
                    V("tensor_scalar", g_["t2"], g_["a"], kac[:, ft:ft + 1], omka[:, ft:ft + 1], ALU.mult, ALU.add, r=[f"W_a{fp_}"], w=[f"W_t2{fp_}"])
                    V("tensor_tensor", g_["kd"], kc, g_["t2"], ALU.mult, r=[f"rkvc1_{fp_}", f"W_t2{fp_}"], w=[f"W_kd{fp_}"])
                    V("tensor_tensor", ARa, ARa, g_["kk"], ALU.mult, r=[fa, f"W_kk{fp_}"], w=[fa])
                    V("tensor_tensor", ARr, ARr, rc, ALU.mult, eng="pool", r=[fr, f"rkvc0_{fp_}"], w=[fr])
                    V("tensor_tensor", BE_[:, ft, :], g_["b"], g_["Eni"], ALU.mult, r=[f"W_b{fp_}", f"W_Eni{fp_}"], w=[fb])
                    V("tensor_tensor", KAP_[:, ft, :], g_["kd"], g_["Eni"], ALU.mult, eng="pool", r=[f"W_kd{fp_}", f"W_Eni{fp_}"], w=[fkp])
                    V("tensor_scalar", g_["bep"], BE_[:, ft, :], gC_[:, ft:ft + 1], -1.0, ALU.mult, ALU.mult, r=[fb, fg], w=[f"W_bep{fp_}"])
                    V("tensor_scalar", g_["kap"], KAP_[:, ft, :], gC_[:, ft:ft + 1], 0.0, ALU.mult, ALU.add, eng="pool", r=[fkp, fg], w=[f"W_kap{fp_}"])
                    tsrc = [(g_["kap"], f"W_kap{fp_}"), (g_["bep"], f"W_bep{fp_}"), (vc, f"rkvc2_{fp_}")]
                    if own is not None:
                        V("scalar_tensor_tensor", g_["t1"], rc, rkc[:, ft:ft + 1], g_["kd"], ALU.mult, ALU.mult, r=[f"rkvc0_{fp_}", f"W_kd{fp_}"], w=[f"W_t1{fp_}"])
                        kb.I("pe", "matmul", PS[0][:, 384:512], bones[:], g_["t1"], start=True, stop=True, reads=[f"W_t1{fp_}", "bones"], writes=["pX0_0"])
                        kb.I("dve", "tensor_tensor", g_["t2"], PS[0][:, 384:512], vc, ALU.mult, reads=["pX0_0", f"rkvc2_{fp_}"], writes=[f"W_t2{fp_}"])
                        tsrc.append((g_["t2"], f"W_t2{fp_}"))
                    b_ = ft % 2
                    for i_, (s_ap, s_k) in enumerate(tsrc):
                        kb.I("pe", "transpose", PS[b_][:, i_ * 128:(i_ + 1) * 128], s_ap, ident[:], reads=[s_k, "ident"], writes=[f"pX{b_}_0"])
                    kb.cp("act" if ft % 2 else "dve", TM_[:, 0:nk, ft * 128:(ft + 1) * 128], PS[b_][:, 0:nk * 128].rearrange("p (k t) -> p k t", k=nk),
                          reads=[f"pX{b_}_0"], writes=[TMK])
                kb.rec = streams["rwkv"]
                for g0 in range(0, 8, NS):
                    heads = list(range(g0, g0 + NS))

                    def hv(h):
                        ft = h // 2; Rs = slice(64 * (h % 2), 64 * (h % 2) + 64)
                        return ft, Rs, [f"w{ft}" + SL + x_ for x_ in "arbkg"]
                    BK = [2 + s_ for s_ in range(NS)]
                    bkk = [f"pX{b_}_0" for b_ in BK]
                    for s_, h in enumerate(heads):
                        ft, Rs, fk = hv(h)
                        be = BE_[Rs, ft, :]; ar = AR_[Rs, ft, :]; al = AR_[Rs, ft, 0:128]
                        kb.I("pe", "matmul", PS[BK[s_]][:, 0:256], be, ar, start=True, stop=True, reads=fk, writes=[bkk[s_]])
                        kb.I("pe", "matmul", PS[BK[s_]][:, 256:384], al, be, start=True, stop=True, reads=fk, writes=[bkk[s_]])
                    for s_, h in enumerate(heads):
                        kb.I("dve", "tensor_tensor", NB[s_][:], PS[BK[s_]][:, 0:256], mask_b[:], ALU.mult, reads=[bkk[s_], "mask_b"], writes=[f"NB{s_}"])
                        kb.I("dve", "tensor_tensor", PTq[s_][:, 0, :], PS[BK[s_]][:, 256:384], maskT[:], ALU.mult, reads=[bkk[s_], "maskT"], writes=[f"PT{s_}_0"])
                        kb.I("dve", "tensor_tensor", Pq[s_][:, 0, :], PS[BK[s_]][:, 0:128], mask_b[:, 0:128], ALU.mult, reads=[bkk[s_], "mask_b"], writes=[f"P{s_}_0"])
                    for s_, h in enumerate(heads):
                        ft, Rs, fk = hv(h)
                        kb.I("pe", "matmul", PS[BK[s_]][:, 0:256], KAP_[Rs, ft, :], AR_[Rs, ft, :], start=True, stop=True, reads=fk, writes=[bkk[s_]])
                    for s_, h in enumerate(heads):
                        kb.I("dve", "tensor_tensor", KA[s_][:], PS[BK[s_]][:, 0:256], mask_k[:], ALU.mult, reads=[bkk[s_], "mask_k"], writes=[f"KA{s_}"])
                    for s_, h in enumerate(heads):
                        ft, Rs, fk = hv(h)
                        al = AR_[Rs, ft, 0:128]
                        kb.I("pe", "matmul", PS[BK[s_]][:, 384:448], al, ST[Rs, ft, :], start=True, stop=False, reads=fk + [f"ST{ft}"], writes=[bkk[s_]])
                        kb.I("pe", "matmul", PS[BK[s_]][:, 384:448], KA[s_][:, 0:128], vT[:, h * 64:(h + 1) * 64], start=False, stop=True, reads=[f"KA{s_}", TMK], writes=[bkk[s_]])
                    for s_, h in enumerate(heads):
                        kb.cp("act", Xq[s_][:, 0, :], PS[BK[s_]][:, 384:448], reads=[bkk[s_]], writes=[f"X{s_}_0"])
                        kb.cp("act", Xb[s_][:, 0, :], PS[BK[s_]][:, 384:448], reads=[bkk[s_]], writes=[f"Xb{s_}_0"])
                    for j in range(7):
                        cu, nx = j % 2, (j + 1) % 2
                        for s_, h in enumerate(heads):
                            Pj = Pq[s_][:, cu, :]
                            Pk = f"P{s_}_{cu}"
                            PTj = PTq[s_][:, cu, :]; PTk = f"PT{s_}_{cu}"
                            if j < 6:
                                kb.I("pe", "matmul", PS[BK[s_]][:, 0:128], PTj, Pj, start=True, stop=True, reads=[PTk, Pk], writes=[bkk[s_]])
                            if j < 5:
                                kb.I("pe", "matmul", PS[BK[s_]][:, 128:256], Pj, PTj, start=True, stop=True, reads=[PTk, Pk], writes=[bkk[s_]])
                            kb.I("pe", "matmul", PS[BK[s_]][:, 448:512], Pj, Xb[s_][:, cu, :], start=True, stop=True, reads=[Pk, f"Xb{s_}_{cu}"], writes=[bkk[s_]])
                        for s_, h in enumerate(heads):
                            if j < 6:
                                kb.cp("act", Pq[s_][:, nx, :], PS[BK[s_]][:, 0:128], reads=[bkk[s_]], writes=[f"P{s_}_{nx}"])
                            if j < 5:
                                kb.cp("act", PTq[s_][:, nx, :], PS[BK[s_]][:, 128:256], reads=[bkk[s_]], writes=[f"PT{s_}_{nx}"])
                            dst_ap = UT[:, h * 64:(h + 1) * 64] if j == 6 else Xq[s_][:, nx, :]
                            dst_k = f"UT{h}" if j == 6 else f"X{s_}_{nx}"
                            if j < 6:
                                kb.I("dve", "tensor_tensor", Xb[s_][:, nx, :], Xq[s_][:, cu, :], PS[BK[s_]][:, 448:512], ALU.subtract if j == 0 else ALU.add,
                                     reads=[bkk[s_], f"X{s_}_{cu}"], writes=[f"Xb{s_}_{nx}"])
                            kb.I("dve", "tensor_tensor", dst_ap, Xq[s_][:, cu, :], PS[BK[s_]][:, 448:512], ALU.subtract if j == 0 else ALU.add,
                                 reads=[bkk[s_], f"X{s_}_{cu}"], writes=[dst_k])
                    if own is not None:
                        for s_, h in enumerate(heads):
                            ft, Rs, fk = hv(h)
                            rho = AR_[Rs, ft, 128:256]
                            yr_ = PS[BK[s_]][:, 256:320]
                            kb.I("pe", "matmul", yr_, rho, ST[Rs, ft, :], start=True, stop=False, reads=fk + [f"ST{ft}"], writes=[bkk[s_]])
                            kb.I("pe", "matmul", yr_, NB[s_][:, 128:256], UT[:, h * 64:(h + 1) * 64], start=False, stop=False, reads=[f"NB{s_}", f"UT{h}"], writes=[bkk[s_]])
                            kb.I("pe", "matmul", yr_, KA[s_][:, 128:256], vT[:, h * 64:(h + 1) * 64], start=False, stop=True, reads=[f"KA{s_}", TMK], writes=[bkk[s_]])
                        for s_, h in enumerate(heads):
                            kb.cp("act", yT[:, h * 64:(h + 1) * 64], PS[BK[s_]][:, 256:320], reads=[bkk[s_]], writes=[f"yT{h}"])
                    for ft in range(g0 // 2, (g0 + NS) // 2):
                        h1 = 2 * ft + 1
                        bnk = BK[h1 - g0]
                        cs_ = slice(ft * 128, (ft + 1) * 128)
                        kb.I("pe", "matmul", PS[bnk][:, 0:128], bepT[:, cs_], UT[:, cs_], start=True, stop=False, reads=[TMK, f"UT{h1 - 1}", f"UT{h1}"], writes=[f"pX{bnk}_0"])
                        kb.I("pe", "matmul", PS[bnk][:, 0:128], kapT[:, cs_], vT[:, cs_], start=False, stop=True, reads=[TMK], writes=[f"pX{bnk}_0"])
                        for jj in range(2):
                            rr_ = slice(64 * jj, 64 * jj + 64)
                            kb.I("dve", "scalar_tensor_tensor", ST[rr_, ft, :], ST[rr_, ft, :], gC_[rr_, ft:ft + 1], PS[bnk][rr_, 64 * jj:64 * jj + 64], ALU.mult, ALU.add,
                                 reads=[f"pX{bnk}_0", f"ST{ft}", f"w{ft}" + SL + "g"], writes=[f"ST{ft}"])
                if dbg_here:
                    kb.D(dbg["d_yT"], yT[:], reads=[f"yT{h}" for h in range(8)], writes=["o_d_yT"])
                    kb.D(dbg["d_ST"], ST[:].rearrange("p f v -> p (f v)"), reads=[f"ST{f}" for f in range(4)], writes=["o_d_ST"])
                if own is not None:
                    for nm, tile_ap, keys in (("yr", yT[:], [f"yT{h}" for h in range(8)]), ("bv", bvT, [TMK])):
                        dst = scr[f"{nm}{sd}"][own * 128:(own + 1) * 128, :]
                        if rev:
                            kb.I("pe", "matmul", PS[5][:], antiI[:], tile_ap, start=True, stop=True, reads=keys + ["antiI"], writes=["pX5_0"])
                            kb.cp("act", flpR[:], PS[5][:], reads=["pX5_0"], writes=["flpR"])
                            kb.D(dst, flpR[:], reads=["flpR"], writes=[f"scr_{nm}{sd}_{own}"])
                        else:
                            kb.D(dst, tile_ap, reads=keys, writes=[f"scr_{nm}{sd}_{own}"])
                kb.rec = streams["s5"]
                t1_, t2_ = s5t["t1"], s5t["t2"]
                for a_ in range(4):
                    zuk = f"zu{a_}" + SL
                    for r_ in range(4):
                        q_ = a_ * 4 + r_
                        kb.I("pe", "matmul", PS[6][:, r_ * 128:(r_ + 1) * 128], Bw_re[:, q_, :], zub_[:, a_, :], start=True, stop=True, reads=["s5tab", zuk], writes=["pX6_0"])
                        kb.I("pe", "matmul", PS[7][:, r_ * 128:(r_ + 1) * 128], Bw_im[:, q_, :], zub_[:, a_, :], start=True, stop=True, reads=["s5tab", zuk], writes=["pX7_0"])
                    k3 = ["pX6_0"]; k4 = ["pX7_0"]
                    qs = slice(a_ * 4, a_ * 4 + 4)
                    Er = E_re[:, qs, :].rearrange("p q t -> p (q t)"); Ei = E_im[:, qs, :].rearrange("p q t -> p (q t)")
                    Fr = F_re[:, qs, :].rearrange("p q t -> p (q t)"); Fi = F_im[:, qs, :].rearrange("p q t -> p (q t)")
                    g1_, g2_ = s5t["Gre"], s5t["Gim"]
                    kb.I("dve", "tensor_tensor", t1_[:], PS[6][:], Er, ALU.mult, reads=k3 + ["s5tab"], writes=["s5t1"])
                    kb.I("dve", "tensor_tensor", t2_[:], PS[7][:], Ei, ALU.mult, reads=k4 + ["s5tab"], writes=["s5t2"])
                    kb.I("dve", "tensor_tensor", g1_[:], PS[7][:], Er, ALU.mult, reads=k4 + ["s5tab"], writes=["s5Gre"])
                    kb.I("dve", "tensor_tensor", g2_[:], PS[6][:], Ei, ALU.mult, reads=k3 + ["s5tab"], writes=["s5Gim"])
                    kb.I("pool", "tensor_tensor", s5t["Xre"][:], t1_[:], t2_[:], ALU.subtract, reads=["s5t1", "s5t2"], writes=["s5Xre"])
                    kb.I("dve", "tensor_tensor", s5t["Xim"][:], g1_[:], g2_[:], ALU.add, reads=["s5Gre", "s5Gim"], writes=["s5Xim"])
                    if own is None:
                        kb.I("dve", "tensor_reduce", gs_re[:, qs], s5t["Xre"][:].rearrange("p (q t) -> p q t", q=4), AXX, ALU.add, reads=["s5Xre"], writes=["gs"])
                        kb.I("dve", "tensor_reduce", gs_im[:, qs], s5t["Xim"][:].rearrange("p (q t) -> p q t", q=4), AXX, ALU.add, reads=["s5Xim"], writes=["gs"])
                        continue
                    for r_ in range(4):
                        q_ = a_ * 4 + r_
                        cs_ = slice(r_ * 128, (r_ + 1) * 128)
                        kb.I("dve", "tensor_tensor_scan", s5t["Gre"][:, cs_], ones[:], s5t["Xre"][:, cs_], car_re[:, q_:q_ + 1], ALU.mult, ALU.add,
                             reads=["s5Xre", "car", "ones"], writes=["s5Gre"])
                        kb.I("dve", "tensor_tensor_scan", s5t["Gim"][:, cs_], ones[:], s5t["Xim"][:, cs_], car_im[:, q_:q_ + 1], ALU.mult, ALU.add,
                             reads=["s5Xim", "car", "ones"], writes=["s5Gim"])
                    x1_, x2_ = s5t["Xre"], s5t["Xim"]
                    kb.I("dve", "tensor_tensor", t1_[:], s5t["Gre"][:], Fr, ALU.mult, reads=["s5Gre", "s5tab"], writes=["s5t1"])
                    kb.I("dve", "tensor_tensor", t2_[:], s5t["Gim"][:], Fi, ALU.mult, reads=["s5Gim", "s5tab"], writes=["s5t2"])
                    kb.I("pool", "tensor_tensor", x1_[:], s5t["Gim"][:], Fr, ALU.mult, reads=["s5Gim", "s5tab", "s5Xre"], writes=["s5Xre"])
                    kb.I("pool", "tensor_tensor", x2_[:], s5t["Gre"][:], Fi, ALU.mult, reads=["s5Gre", "s5tab", "s5Xim"], writes=["s5Xim"])
                    kb.I("dve", "tensor_tensor", s5t["Hre"][:], t1_[:], t2_[:], ALU.subtract, reads=["s5t1", "s5t2"], writes=["s5Hre"])
                    kb.I("pool", "tensor_tensor", s5t["Him"][:], x1_[:], x2_[:], ALU.add, reads=["s5Xre", "s5Xim"], writes=["s5Him"])
                    kb.I("dve", "tensor_copy", car_re[:, qs], s5t["Hre"][:].rearrange("p (q t) -> p q t", q=4)[:, :, 127], reads=["s5Hre"], writes=["car"])
                    kb.I("dve", "tensor_copy", car_im[:, qs], s5t["Him"][:].rearrange("p (q t) -> p q t", q=4)[:, :, 127], reads=["s5Him"], writes=["car"])
                    if own is not None:
                        for r_ in range(4):
                            q_ = a_ * 4 + r_
                            cs_ = slice(r_ * 128, (r_ + 1) * 128)
                            yb = PS[6][:, r_ * 32:(r_ + 1) * 32]
                            if sd == 0:
                                kb.I("pe", "matmul", yb, zub_[:, a_, :], diagd[:, a_, r_ * 32:(r_ + 1) * 32], start=True, stop=False, reads=[zuk, "diagd"], writes=["pX6_0"])
                            kb.I("pe", "matmul", yb, s5t["Hre"][:, cs_], Cw_re[:, q_, :], start=(sd != 0), stop=False, reads=["s5Hre", "s5tab"], writes=["pX6_0"])
                            kb.I("pe", "matmul", yb, s5t["Him"][:, cs_], Cw_imn[:, q_, :], start=False, stop=True, reads=["s5Him", "s5tab"], writes=["pX6_0"])
                        kb.cp("act", flpS[:, a_ * 128:(a_ + 1) * 128], PS[6][:, 0:128], reads=["pX6_0"], writes=["flpS"])
                if own is None:
                    F127r = F_re[:, :, 127]; F127i = F_im[:, :, 127]
                    kb.I("dve", "tensor_tensor", gs_re[:], gs_re[:], car_re[:], ALU.add, reads=["gs", "car"], writes=["gs"])
                    kb.I("dve", "tensor_tensor", gs_im[:], gs_im[:], car_im[:], ALU.add, reads=["gs", "car"], writes=["gs"])
                    kb.I("dve", "tensor_tensor", gta[:], gs_re[:], F127r, ALU.mult, reads=["gs", "s5tab"], writes=["gta"])
                    kb.I("dve", "tensor_tensor", gtb[:], gs_im[:], F127i, ALU.mult, reads=["gs", "s5tab"], writes=["gtb"])
                    kb.I("dve", "tensor_tensor", car_re[:], gta[:], gtb[:], ALU.subtract, reads=["gta", "gtb"], writes=["car"])
                    kb.I("dve", "tensor_tensor", gta[:], gs_im[:], F127r, ALU.mult, reads=["gs", "s5tab", "car"], writes=["gta"])
                    kb.I("dve", "tensor_tensor", gtb[:], gs_re[:], F127i, ALU.mult, reads=["gs", "s5tab", "car"], writes=["gtb"])
                    kb.I("dve", "tensor_tensor", car_im[:], gta[:], gtb[:], ALU.add, reads=["gta", "gtb"], writes=["car"])
                if dbg_here:
                    kb.D(dbg["d_car"][:, 0:16], car_re[:], reads=["car"], writes=["o_d_car"])
                    kb.D(dbg["d_car"][:, 16:32], car_im[:], reads=["car"], writes=["o_d_car"])
                if own is not None:
                    if dbg_here:
                        kb.D(dbg["d_ys"], flpS[:], reads=["flpS"], writes=["o_d_ys"])
                    dst = scr[f"ys{sd}"][own * 128:(own + 1) * 128, :]
                    if rev:
                        kb.I("pe", "matmul", PS[6][:], antiI[:], flpS[:], start=True, stop=True, reads=["flpS", "antiI"], writes=["pX6_0"])
                        kb.cp("dve", flpS[:], PS[6][:], reads=["pX6_0"], writes=["flpS"])
                    kb.D(dst, flpS[:], reads=["flpS"], writes=[f"scr_ys{sd}_{own}"])
                kb.rec = None
                return streams

            for sd in range(2 if not upto.startswith("p0") else 0):
                if upto == "setup":
                    break
                if sd == 1:
                    s5_setup(sd, F0)
                if upto == "s5setup":
                    break
                stk = [f"ST{f}" for f in range(4)]
                if sd == 0:
                    kb.I("pool", "memset", ST[:], 0.0, reads=stk, writes=stk)
                    kb.I("pool", "memset", car_re[:], 0.0, reads=["car"], writes=["car"])
                    kb.I("pool", "memset", car_im[:], 0.0, reads=["car"], writes=["car"])
                if sd == 0:
                    chunks = [(True, c, None) for c in range(n_ctx)] + [(False, c, c) for c in range(n_lat_chunks[0])]
                else:
                    stk = [f"ST{f}" for f in range(4)]
                    kb.D(scr["cc_in"][:, 0:256], ST[:].rearrange("p f v -> p (f v)"), reads=stk, writes=["cc_in"])
                    kb.D(scr["cc_in"][:, 256:272], car_re[:], reads=["car"], writes=["cc_in"])
                    kb.D(scr["cc_in"][:, 272:288], car_im[:], reads=["car"], writes=["cc_in"])
                    kb.I("pool", "collective_compute", "AllReduce", ALU.add, replica_groups=[[0, 1], [2, 3], [4, 5], [6, 7]],
                         ins=[scr["cc_in"]], outs=[scr["cc_out"]], reads=["cc_in"], writes=["cc_out"])
                    ccs = s5t["t1"]
                    kb.D(ccs[:, 0:288], scr["cc_out"], reads=["cc_out", "s5t1"], writes=["s5t1"])
                    kb.I("dve", "tensor_tensor", ST[:].rearrange("p f v -> p (f v)"), ccs[:, 0:256], ST[:].rearrange("p f v -> p (f v)"), ALU.subtract,
                         reads=["s5t1"] + stk, writes=stk)
                    kb.I("dve", "tensor_tensor", car_re[:], ccs[:, 256:272], car_re[:], ALU.subtract, reads=["s5t1", "car"], writes=["car"])
                    kb.I("dve", "tensor_tensor", car_im[:], ccs[:, 272:288], car_im[:], ALU.subtract, reads=["s5t1", "car"], writes=["car"])
                    chunks = [(False, c, 31 - c) for c in range(16, 16 + n_lat_chunks[1])]
                prev = None
                pend = []
                for k_, (is_ctx, c, own) in enumerate(chunks):
                    nxt = chunks[k_ + 1][:2] if k_ + 1 < len(chunks) else None
                    cur = process_chunk(sd, is_ctx, c, own, k_ % 2, nxt=(nxt if sd == 0 else None), first=(k_ == 0), reuse=(sd == 1))
                    pend.append(cur["front"])
                    if prev is not None:
                        pend += [prev["rwkv"], prev["s5"]]
                    prev = cur
                    if len(pend) >= SCHED_WINDOW_STREAMS:
                        kb.merge(*pend)
                        pend = []
                if prev is not None:
                    pend += [prev["rwkv"], prev["s5"]]
                kb.merge(*pend)
            kb.I("dve", "tensor_copy", ssum[:, 0:1], ssum[:, 0:1], reads=list(kb.lastw.keys()), writes=list(kb.lastw.keys()) + ["fence1"])
        FENCE = ["fence1"]
        wscope.close()

        if do_tail:
            with contextlib.ExitStack() as tl:
                wob = sb(tl, "wob", [128, 8, D], BF16)
                glb = sb(tl, "glb", [128, 4, 512], BF16); g2b = sb(tl, "g2b", [128, 512])
                stg = [sb(tl, f"stg{i}", [128, 1024]) for i in range(2)]
                jobs = [("w_out", wob, kt, 0, D) for kt in range(8)] + [("gluw", glb, kt, 0, 512) for kt in range(4)]
                for i, (nm, dst, kt, c0, ncol) in enumerate(jobs):
                    s_ = stg[i % 2]; sk = f"stg{i % 2}"
                    kb.D(s_[:, 0:ncol], din[nm][kt * 128:(kt + 1) * 128, c0:c0 + ncol], reads=FENCE, writes=[sk])
                    kb.cp(("dve", "act", "pool")[i % 3], dst[:, kt, c0:c0 + ncol], s_[:, 0:ncol], reads=[sk], writes=["tw"])
                kb.D(g2b[0:96, :], din["g2"], reads=FENCE, writes=["tw2"])
                rows = {}
                for nm in ("lnw", "lnb", "glub"):
                    rows[nm] = sb(tl, "row_" + nm, [128, IN_SHAPES[nm][1]])
                    kb.D(rows[nm][:], din[nm].partition_broadcast(128), reads=FENCE, writes=["rows"])
                gmix = sb(tl, "gmix", [128, D])

                def dbl(name, shape, dt=F32):
                    return [sb(tl, f"{name}_{i_}", shape, dt) for i_ in range(2)]
                x1 = dbl("x1", [128, D])
                kb.D(gmix[:], scr["mod"][:, 2 * D:3 * D], reads=FENCE + ["scr_mod"], writes=["modt"])
                tx = dbl("tx", [128, D]); yr_a = dbl("yr_a", [128, 512]); ys_a = dbl("ys_a", [128, 512]); bv_a = dbl("bv_a", [128, 512])
                yr_b = dbl("yr_b", [128, 512]); ys_b = dbl("ys_b", [128, 512]); bv_b = dbl("bv_b", [128, 512])
                sgt = dbl("sgt", [128, 128]); mix = dbl("mix", [128, D]); st8 = dbl("st8", [128, 8]); st8b = dbl("st8b", [128, 8])
                gt = dbl("gt", [128, 512]); gt2 = dbl("gt2", [128, 512]); zT = dbl("zT", [128, 4, 128], BF16); mixT = dbl("mixT", [128, 8, 128], BF16)
                TA = (tx, yr_a, ys_a, bv_a, yr_b, ys_b, bv_b, sgt, mix, st8, st8b, gt, gt2, zT, mixT, x1)
                recA = []
                kb.rec = recA
                for oc in range(NOWN):
                    sl_ = oc % 2
                    kb.ksuf = f"#{sl_}"
                    (tx, yr_a, ys_a, bv_a, yr_b, ys_b, bv_b, sgt, mix, st8, st8b, gt, gt2, zT, mixT, x1) = (t_[sl_] for t_ in TA)
                    rsl = slice(oc * 128, (oc + 1) * 128)
                    kb.D(tx[:], din["x"][rsl, :], reads=FENCE, writes=["tx"])
                    for t_, nm in ((yr_a, "yr0"), (yr_b, "yr1"), (ys_a, "ys0"), (ys_b, "ys1"), (bv_a, "bv0"), (bv_b, "bv1")):
                        kb.D(t_[:], scr[nm][rsl, :], reads=[f"scr_{nm}_{oc}"] + FENCE, writes=["t_" + nm])
                    kb.D(sgt[0:96, :], scr["sg"][oc * 128:oc * 128 + 96, :], reads=[f"scr_sg{oc}"] + FENCE, writes=["sgt"])
                    kb.I("dve", "tensor_tensor", yr_a[:], yr_a[:], yr_b[:], ALU.add, reads=["t_yr0", "t_yr1"], writes=["t_yr0"])
                    y3 = yr_a[:].rearrange("p (h n) -> p h n", n=64)
                    kb.I("dve", "tensor_reduce", st8[:], y3, AXX, ALU.add, reads=["t_yr0"], writes=["st8"])
                    kb.ts1("dve", st8[:], st8[:], 1.0 / 64, ALU.mult, reads=["st8"], writes=["st8"])
                    kb.I("dve", "tensor_tensor", y3, y3, st8[:].unsqueeze(2).to_broadcast([128, 8, 64]), ALU.subtract, reads=["st8", "t_yr0"], writes=["t_yr0"])
                    kb.I("pool", "tensor_tensor", gt2[:], yr_a[:], yr_a[:], ALU.mult, reads=["t_yr0"], writes=["gt2"])
                    kb.I("dve", "tensor_reduce", st8b[:], gt2[:].rearrange("p (h n) -> p h n", n=64), AXX, ALU.add, reads=["gt2"], writes=["st8b"])
                    kb.I("dve", "tensor_scalar", st8b[:], st8b[:], 1.0 / 64, 64e-5, ALU.mult, ALU.add, reads=["st8b"], writes=["st8b"])
                    kb.I("act", "activation", st8b[:], st8b[:], AF.Sqrt, reads=["st8b"], writes=["st8b"])
                    kb.I("dve", "reciprocal", st8b[:], st8b[:], reads=["st8b"], writes=["st8b"])
                    kb.I("dve", "tensor_tensor", y3, y3, st8b[:].unsqueeze(2).to_broadcast([128, 8, 64]), ALU.mult, reads=["st8b", "t_yr0"], writes=["t_yr0"])
                    kb.I("pool", "tensor_tensor", yr_a[:], yr_a[:], rows["lnw"][:], ALU.mult, reads=["rows", "t_yr0"], writes=["t_yr0"])
                    kb.I("pool", "tensor_tensor", yr_a[:], yr_a[:], rows["lnb"][:], ALU.add, reads=["rows", "t_yr0"], writes=["t_yr0"])
                    kb.I("dve", "tensor_tensor", bv_a[:], bv_a[:], bv_b[:], ALU.add, reads=["t_bv0", "t_bv1"], writes=["t_bv0"])
                    kb.I("dve", "tensor_tensor", yr_a[:], yr_a[:], bv_a[:], ALU.add, reads=["t_bv0", "t_yr0"], writes=["t_yr0"])
                    kb.I("pe", "matmul", PS[0][:], sgt[0:96, :], g2b[0:96, :], start=True, stop=True, reads=["sgt", "tw2"], writes=bk(0))
                    kb.I("dve", "tensor_tensor", mix[:, 0:512], yr_a[:], PS[0][:], ALU.mult, reads=bk(0) + ["t_yr0"], writes=["mixA"])
                    kb.I("dve", "tensor_tensor", ys_a[:], ys_a[:], ys_b[:], ALU.add, reads=["t_ys0", "t_ys1"], writes=["t_ys0"])
                    kb.I("pool", "tensor_tensor", gt[:], ys_a[:], ys_a[:], ALU.mult, reads=["t_ys0"], writes=["gt"])
                    kb.I("dve", "tensor_scalar", gt[:], gt[:], 0.044715, 1.0, ALU.mult, ALU.add, reads=["gt"], writes=["gt"])
                    kb.I("dve", "tensor_tensor", gt[:], gt[:], ys_a[:], ALU.mult, reads=["gt", "t_ys0"], writes=["gt"])
                    kb.I("act", "activation", gt[:], gt[:], AF.Tanh, scale=0.7978845608028654, reads=["gt"], writes=["gt"])
                    kb.I("dve", "tensor_scalar", gt[:], gt[:], 0.5, 0.5, ALU.mult, ALU.add, reads=["gt"], writes=["gt"])
                    kb.I("dve", "tensor_tensor", ys_a[:], ys_a[:], gt[:], ALU.mult, reads=["gt", "t_ys0"], writes=["t_ys0"])
                    for j in range(4):
                        kb.I("pe", "transpose", PS[1][:, j * 128:(j + 1) * 128], ys_a[:, j * 128:(j + 1) * 128], ident[:], reads=["t_ys0", "ident"], writes=[f"pX1_{j}"])
                    kb.cp("act", zT[:].rearrange("p j t -> p (j t)"), PS[1][:], reads=bk(1), writes=["zT"])
                    for j in range(4):
                        kb.I("pe", "matmul", PS[2][:], zT[:, j, :], glb[:, j, :], start=(j == 0), stop=(j == 3), reads=["zT", "tw"], writes=bk(2))
                    kb.I("dve", "tensor_tensor", gt[:], PS[2][:], rows["glub"][:], ALU.add, reads=bk(2) + ["rows", "gt"], writes=["gt"])
                    kb.I("act", "activation", gt[:], gt[:], AF.Sigmoid, reads=["gt"], writes=["gt"])
                    kb.I("dve", "tensor_tensor", mix[:, 512:1024], ys_a[:], gt[:], ALU.mult, reads=["gt", "t_ys0"], writes=["mixB"])
                    for half in range(2):
                        for j in range(4):
                            kt = half * 4 + j
                            kb.I("pe", "transpose", PS[3][:, j * 128:(j + 1) * 128], mix[:, kt * 128:(kt + 1) * 128], ident[:], reads=["mixA", "mixB", "ident"], writes=[f"pX3_{j}"])
                        kb.cp("act" if half else "dve", mixT[:, half * 4:half * 4 + 4, :].rearrange("p j t -> p (j t)"), PS[3][:], reads=bk(3), writes=["mixT"])
                    for nh in range(2):
                        ns = slice(nh * 512, (nh + 1) * 512)
                        for kt in range(8):
                            kb.I("pe", "matmul", PS[4 + nh][:], mixT[:, kt, :], wob[:, kt, ns], start=(kt == 0), stop=(kt == 7), reads=["mixT", "tw"], writes=bk(4 + nh))
                        kb.I("dve", "tensor_tensor", x1[:, ns], PS[4 + nh][:], gmix[:, ns], ALU.mult, reads=bk(4 + nh) + ["modt"], writes=["x1"])
                        kb.I("pool", "tensor_tensor", x1[:, ns], x1[:, ns], tx[:, ns], ALU.add, reads=["x1", "tx"], writes=["x1"])
                    if debug and oc == 0:
                        kb.D(dbg["d_x1"], x1[:], reads=["x1"], writes=["o_d_x1"])
                    kb.D(scr["x1"][rsl, :], x1[:], reads=["x1"], writes=[f"scr_x1_{oc}"])
                kb.rec = None
                kb.ksuf = None
                kb.merge(recA)
                st8 = TA[9][0]
                kb.I("dve", "tensor_copy", st8[:, 0:1], st8[:, 0:1], reads=list(kb.lastw.keys()), writes=list(kb.lastw.keys()) + ["fence2"])
            FENCE = ["fence2"]
            with contextlib.ExitStack() as tl:
                w1b = sb(tl, "w1b", [128, 8, DFF], BF16); w3b = sb(tl, "w3b", [128, 8, DFF], BF16)
                w2b = sb(tl, "w2b", [128, 22, D], BF16)
                stg = [sb(tl, f"stgb{i}", [128, 1024]) for i in range(2)]
                jobs = []
                for nm, dst in (("w1", w1b), ("w3", w3b)):
                    for kt in range(8):
                        for c0 in (0, 1024, 2048):
                            jobs.append((nm, dst, kt, c0, min(1024, DFF - c0)))
                jobs += [("w2f", w2b, kt, 0, D) for kt in range(22)]
                for i, (nm, dst, kt, c0, ncol) in enumerate(jobs):
                    s_ = stg[i % 2]; sk = f"stgb{i % 2}"
                    kb.D(s_[:, 0:ncol], din[nm][kt * 128:(kt + 1) * 128, c0:c0 + ncol], reads=FENCE, writes=[sk])
                    kb.cp(("dve", "act", "pool")[i % 3], dst[:, kt, c0:c0 + ncol], s_[:, 0:ncol], reads=[sk], writes=["tw"])
                rows = {"gf": sb(tl, "row_gf", [128, D])}
                kb.D(rows["gf"][:], din["gf"].partition_broadcast(128), reads=FENCE, writes=["rows"])
                A2 = sb(tl, "A2", [128, D]); sffn = sb(tl, "sffn", [128, D]); gffn = sb(tl, "gffn", [128, D])
                def dblb(name, shape, dt=F32):
                    return [sb(tl, f"{name}_{i_}", shape, dt) for i_ in range(2)]
                hh2 = dblb("hh", [128, D]); x12 = dblb("x1b", [128, D]); outt2 = dblb("outt", [128, D])
                hh = hh2[0]
                kb.D(A2[:], din["g2n"].partition_broadcast(128), reads=FENCE, writes=["A2"])
                kb.D(hh[:], scr["mod"][:, 4 * D:5 * D], reads=FENCE + ["scr_mod"], writes=["hh#0"])
                kb.D(sffn[:], scr["mod"][:, 3 * D:4 * D], reads=FENCE + ["scr_mod"], writes=["modt"])
                kb.D(gffn[:], scr["mod"][:, 5 * D:6 * D], reads=FENCE + ["scr_mod"], writes=["modt"])
                kb.I("dve", "scalar_tensor_tensor", A2[:], hh[:], 1.0, A2[:], ALU.add, ALU.mult, reads=["A2", "hh#0"], writes=["A2"])
                hhT2 = dblb("hhT", [128, 8, 128], BF16)
                actT2 = dblb("actT", [128, 22, 128], BF16); s12 = dblb("s1", [128, 256]); ss22 = dblb("ss2", [128, 2]); rs22 = dblb("rs2", [128, 2])
                print("SBUF bytes remaining in tail B scope:", nc.sbuf_bytes_remaining)
                recB = []
                kb.rec = recB
                for oc in range(NOWN):
                    sl_ = oc % 2
                    kb.ksuf = f"#{sl_}"
                    hh, x1, outt, hhT, actT, s1, ss2, rs2 = (t_[sl_] for t_ in (hh2, x12, outt2, hhT2, actT2, s12, ss22, rs22))
                    rsl = slice(oc * 128, (oc + 1) * 128)
                    kb.D(x1[:], scr["x1"][rsl, :], reads=[f"scr_x1_{oc}"], writes=["x1"])
                    kb.I("pool", "memset", ss2[:], 0.0, reads=FENCE, writes=["ss2"])
                    kb.I("act", "activation", hh[:], x1[:], AF.Square, accum_out=ss2[:, 0:1], reads=["x1", "A2"], writes=["hh", "ss2"])
                    kb.I("dve", "tensor_scalar", rs2[:, 0:1], ss2[:, 0:1], 1.0 / D, 1e-6, ALU.mult, ALU.add, reads=["ss2"], writes=["rs2"])
                    kb.I("act", "activation", rs2[:, 0:1], rs2[:, 0:1], AF.Sqrt, reads=["rs2"], writes=["rs2"])
                    kb.I("dve", "reciprocal", rs2[:, 0:1], rs2[:, 0:1], reads=["rs2"], writes=["rs2"])
                    kb.I("dve", "scalar_tensor_tensor", hh[:], x1[:], rs2[:, 0:1], A2[:], ALU.mult, ALU.mult, reads=["x1", "rs2", "A2", "hh"], writes=["hh"])
                    kb.I("pool", "tensor_tensor", hh[:], hh[:], sffn[:], ALU.add, reads=["hh", "modt"], writes=["hh"])
                    for half in range(2):
                        for j in range(4):
                            kt = half * 4 + j
                            kb.I("pe", "transpose", PS[3][:, j * 128:(j + 1) * 128], hh[:, kt * 128:(kt + 1) * 128], ident[:], reads=["hh", "ident"], writes=[f"pX3_{j}"])
                        kb.cp("act" if half else "dve", hhT[:, half * 4:half * 4 + 4, :].rearrange("p j t -> p (j t)"), PS[3][:], reads=bk(3), writes=["hhT"])
                    for ftf in range(22):
                        sl = ftf % 2
                        fs = slice(ftf * 128, (ftf + 1) * 128)
                        bq = (0, 1, 2)[ftf % 3]
                        pa = PS[bq][:, 0:128]; pb = PS[bq][:, 128:256]
                        s1_ = s1[:, (ftf % 2) * 128:(ftf % 2) * 128 + 128]
                        for kt in range(8):
                            kb.I("pe", "matmul", pa, w1b[:, kt, fs], hhT[:, kt, :], start=(kt == 0), stop=(kt == 7), reads=["hhT", "tw"], writes=[f"pX{bq}_0"])
                        for kt in range(8):
                            kb.I("pe", "matmul", pb, w3b[:, kt, fs], hhT[:, kt, :], start=(kt == 0), stop=(kt == 7), reads=["hhT", "tw"], writes=[f"pX{bq}_0"])
                        kb.I("act", "activation", s1_, pa, AF.Silu, reads=[f"pX{bq}_0"], writes=[f"s1_{ftf % 2}"])
                        kb.I("dve", "tensor_tensor", actT[:, ftf, :], s1_, pb, ALU.mult, reads=[f"pX{bq}_0", f"s1_{ftf % 2}"], writes=[f"actT{ftf}"])
                    for nh in range(2):
                        ns = slice(nh * 512, (nh + 1) * 512)
                        bd = 4 + 2 * sl_ + nh
                        for ftf in range(22):
                            kb.I("pe", "matmul", PS[bd][:], actT[:, ftf, :], w2b[:, ftf, ns], start=(ftf == 0), stop=(ftf == 21), reads=[f"actT{ftf}", "tw"], writes=bk(bd))
                        kb.I("dve", "tensor_tensor", outt[:, ns], PS[bd][:], gffn[:, ns], ALU.mult, reads=bk(bd) + ["modt"], writes=["outt"])
                        kb.I("pool", "tensor_tensor", outt[:, ns], outt[:, ns], x1[:, ns], ALU.add, reads=["outt", "x1"], writes=["outt"])
                    kb.I("act", "activation", hh[:], outt[:], AF.Square, accum_out=ss2[:, 1:2], reads=["outt", "hh"], writes=["hh", "ss2"])
                    kb.I("dve", "tensor_scalar", rs2[:, 1:2], ss2[:, 1:2], 1.0 / D, 1e-6, ALU.mult, ALU.add, reads=["ss2"], writes=["rs2"])
                    kb.I("act", "activation", rs2[:, 1:2], rs2[:, 1:2], AF.Sqrt, reads=["rs2"], writes=["rs2"])
                    kb.I("dve", "reciprocal", rs2[:, 1:2], rs2[:, 1:2], reads=["rs2"], writes=["rs2"])
                    kb.I("dve", "scalar_tensor_tensor", outt[:], outt[:], rs2[:, 1:2], rows["gf"][:], ALU.mult, ALU.mult, reads=["outt", "rs2", "rows"], writes=["outt"])
                    kb.D(out_d[rsl, :], outt[:], reads=["outt"], writes=[f"o_out{oc}"])
                kb.rec = None
                kb.ksuf = None
                kb.merge(recB)
                kb.emit(final_keys=[k for k in kb.lastw if k.startswith("o_")])
        else:
            kb.emit(final_keys=[k for k in kb.lastw if k.startswith("o_")] + FENCE)
    return nc


def make_in_maps(inp):
    f = np.float32
    ident = np.eye(128, dtype=f)
    antiI = np.ascontiguousarray(ident[::-1])
    strict = np.triu(np.ones((128, 128), f), 1)
    incl = np.triu(np.ones((128, 128), f), 0)
    consts = {
        "ident": ident, "antiI": antiI,
        "mask_b": np.concatenate([strict, -incl], axis=1), "mask_k": np.concatenate([strict, incl], axis=1),
        "maskT": np.ascontiguousarray(strict.T), "bones": np.kron(np.eye(2, dtype=f), np.ones((64, 64), f)),
    }
    maps = []
    for core in range(8):
        b, hf = core // 2, core % 2
        dsel = [1, 0] if hf else [0, 1]
        x = inp["x"][b]; ctx = inp["ctx"][b]
        conv = inp["rwkv_conv"][0]
        w_in = inp["w_in"][0]
        if hf:
            x = x[::-1]; ctx = ctx[::-1]; conv = conv[::-1, ::-1]
            perm = np.arange(2272)
            perm[1536:1568], perm[1568:1600] = np.arange(1568, 1600), np.arange(1536, 1568)
            perm[1600:1632], perm[1632:1664] = np.arange(1632, 1664), np.arange(1600, 1632)
            w_in = w_in[:, perm]
        m = {
            "x": x, "ctx": ctx, "cc": np.stack([inp["c"][b], inp["c_ctx"]]),
            "mod_w": inp["mod_w"][0], "mod_b": inp["mod_b"][0][None], "g1": inp["norm1_g"][0][None],
            "g2n": inp["norm2_g"][0][None], "gf": inp["final_g"][None], "w_in": w_in, "w_out": inp["w_out"][0],
            "conv": conv.reshape(9, 1536), "w0": inp["rwkv_w0"][0][dsel], "w2": inp["rwkv_w2"][0][dsel],
            "a0": inp["rwkv_a0"][0][dsel], "a2": inp["rwkv_a2"][0][dsel], "g2": inp["rwkv_g2"][0],
            "kkv": inp["rwkv_kk"][0], "kav": inp["rwkv_ka"][0], "rkv": inp["rwkv_rk"][0].reshape(512),
            "lnw": inp["rwkv_ln_w"][0][None], "lnb": inp["rwkv_ln_b"][0][None],
            "lam_re": inp["s5_lam_re"][0][dsel], "lam_im": inp["s5_lam_im"][0][dsel], "lstep": inp["s5_log_step"][0][dsel],
            "b_re": inp["s5_b_re"][0], "b_im": inp["s5_b_im"][0], "c_re": inp["s5_c_re"][0], "c_im": inp["s5_c_im"][0],
            "s5d": inp["s5_d"][0], "gluw": inp["s5_glu_w"][0], "glub": inp["s5_glu_b"][0][None],
            "w1": inp["ffn_w1"][0], "w3": inp["ffn_w3"][0], "w2f": inp["ffn_w2"][0],
        }
        m.update(consts)
        maps.append({k: np.ascontiguousarray(np.asarray(v, dtype=f)).reshape(IN_SHAPES[k]) for k, v in m.items()})
    return maps


def kernel(**inputs):
    inp = {k: np.asarray(v) for k, v in inputs.items()}
    nc = build_nc()
    maps = make_in_maps(inp)
    res = run_bass_kernel_spmd(nc, maps, core_ids=list(range(8)))
    out = np.zeros((4, T_LAT, D), np.float32)
    for core in range(8):
        b, hf = core // 2, core % 2
        o = np.asarray(res.results[core]["out"], dtype=np.float32)
        if hf:
            out[b, OWN:] = o[::-1]
        else:
            out[b, :OWN] = o
    return out
```

```python
import contextlib
import numpy as np
import concourse.bass as bass
import concourse.mybir as mybir
from concourse.bass_utils import run_bass_kernel_spmd

F32 = mybir.dt.float32
BF16 = mybir.dt.bfloat16
ALU = mybir.AluOpType
AF = mybir.ActivationFunctionType
AXX = mybir.AxisListType.X

SEM_CAP = 16000
N_DMA_SEM = 24
SCHED_WINDOW_STREAMS = 1000
SAME_ENG_WAIT = True

T_LAT, T_CTX, D, DFF = 4096, 256, 1024, 2816
OWN = 2048
NOWN = OWN // 128
PI = float(np.pi)
KDEC = 0.6065306597126334


class KB:
    ENGS = ("pe", "dve", "act", "pool", "sp")

    def __init__(self, nc):
        self.nc = nc
        self.ops = {e: [] for e in self.ENGS}
        self.lastw = {}
        self.readers = {}
        self.ndma = 0
        self.rr = 0

    @staticmethod
    def _norm(reads, writes):
        r2 = [k for k in reads if not k.startswith("pX")]
        w2 = [k for k in writes if not k.startswith("pX")]
        banks = {"bank" + k[2:].split("_")[0] for k in list(reads) + list(writes) if k.startswith("pX")}
        return r2, w2 + sorted(banks)

    def _deps(self, me, reads, writes):
        reads, writes = self._norm(reads, writes)
        deps = set()
        for k in reads:
            w = self.lastw.get(k)
            if w is not None:
                deps.add(w)
        for k in writes:
            w = self.lastw.get(k)
            if w is not None:
                deps.add(w)
            for r in self.readers.get(k, ()):
                deps.add(r)
        deps.discard(me)
        for k in reads:
            self.readers.setdefault(k, []).append(me)
        for k in writes:
            self.lastw[k] = me
            self.readers[k] = []
        return deps

    mute = False
    rec = None
    ksuf = None
    KGLOBAL = ("pX", "scr_", "o_", "fence", "rows", "tw", "modt", "ident", "A2")

    def _sfx(self, keys):
        if not self.ksuf:
            return list(keys)
        return [k if k.startswith(self.KGLOBAL) else k + self.ksuf for k in keys]

    @staticmethod
    def _est(op):
        def nfree(ap):
            n = 1
            for d in ap.shape[1:]:
                n *= d
            return n
        if op[0] == "D":
            ap = op[1]
            nbytes = nfree(ap) * ap.shape[0] * (2 if ap.dtype == BF16 else 4)
            return "sp", 0.08, 2.2 + nbytes / 1.0e5
        eng, meth, args = op[1], op[2], op[3]
        if meth == "collective_compute":
            return eng, 0.5, 30.0
        n = nfree(args[0])
        if eng == "pe":
            f32 = args[1].dtype == F32
            d = 0.09 + n * (0.0017 if f32 else 0.00045)
        elif eng == "dve":
            d = 0.25 + n * 0.00104 * (6.0 if meth == "reciprocal" else 1.0)
        elif eng == "act":
            d = 0.2 + n * 0.00104
        else:
            d = 0.45 + n * 0.0026
        return eng, d, d

    def merge(self, *streams):
        ops = [op for st_ in streams for op in st_]
        n = len(ops)
        lastw, readers = {}, {}
        preds = [set() for _ in range(n)]
        for i, op in enumerate(ops):
            r_, w_ = (op[5], op[6]) if op[0] == "I" else (op[4], op[5])
            r_, w_ = self._norm(r_, w_)
            for k in r_:
                if k in lastw:
                    preds[i].add(lastw[k])
            for k in w_:
                if k in lastw:
                    preds[i].add(lastw[k])
                preds[i].update(readers.get(k, ()))
            preds[i].discard(i)
            for k in r_:
                readers.setdefault(k, []).append(i)
            for k in w_:
                lastw[k] = i
                readers[k] = []
        succs = [[] for _ in range(n)]
        for i in range(n):
            for p in preds[i]:
                succs[p].append(i)
        est = [self._est(op) for op in ops]
        cp = [0.0] * n
        for i in range(n - 1, -1, -1):
            cp[i] = est[i][2] + max((cp[j] for j in succs[i]), default=0.0)
        npred = [len(p) for p in preds]
        ready = [i for i in range(n) if npred[i] == 0]
        fin = [0.0] * n
        eng_free = {e: 0.0 for e in self.ENGS}
        LAT = 1.0
        while ready:
            best, bkey = None, None
            for i in ready:
                e = est[i][0]
                t = eng_free[e]
                for p in preds[i]:
                    tp = fin[p] + (0.05 if est[p][0] == e else LAT)
                    if tp > t:
                        t = tp
                key = (t - 0.02 * cp[i], i)
                if bkey is None or key < bkey:
                    best, bkey, bt = i, key, t
            i = best
            ready.remove(i)
            e = est[i][0]
            eng_free[e] = bt + est[i][1]
            fin[i] = bt + est[i][2]
            op = ops[i]
            if op[0] == "I":
                self.I(op[1], op[2], *op[3], reads=op[5], writes=op[6], **op[4])
            else:
                self.D(op[1], op[2], reads=op[4], writes=op[5], **op[3])
            for j in succs[i]:
                npred[j] -= 1
                if npred[j] == 0:
                    ready.append(j)

    def I(self, eng, meth, *args, reads=(), writes=(), **kw):
        if self.mute:
            return
        if self.rec is not None:
            self.rec.append(("I", eng, meth, args, kw, self._sfx(reads), self._sfx(writes)))
            return
        idx = len(self.ops[eng])
        deps = self._deps((eng, idx), list(reads), list(writes))
        self.ops[eng].append(((meth, args, kw), deps, None))

    def D(self, out, in_, reads=(), writes=(), **kw):
        if self.mute:
            return
        if self.rec is not None:
            self.rec.append(("D", out, in_, dict(kw), self._sfx(reads), self._sfx(writes)))
            return
        k = self.ndma
        self.ndma += 1
        deps = self._deps(("dma", k), list(reads), list(writes))
        kw = dict(kw)
        kw["out"] = out
        kw["in_"] = in_
        self.ops["sp"].append((("dma_start", (), kw), deps, k))

    def cp(self, eng, out, in_, reads=(), writes=()):
        self.I(eng, "copy" if eng == "act" else "tensor_copy", out, in_, reads=reads, writes=writes)

    def ts1(self, eng, out, in0, s, op, reads=(), writes=()):
        self.I(eng, "tensor_scalar", out, in0, s, 0.0, op, ALU.add, reads=reads, writes=writes)

    def ew(self):
        self.rr += 1
        return "dve" if (self.rr % 3) else "pool"

    def ev(self):
        self.rr += 1
        return "dve" if (self.rr % 2) else "act"

    def emit(self, final_keys=()):
        nc = self.nc
        me = ("sp", len(self.ops["sp"]))
        deps = self._deps(me, list(final_keys), [])
        self.ops["sp"].append((None, deps, None))
        nsem = {e: (len(self.ops[e]) + SEM_CAP - 1) // SEM_CAP + 1 for e in self.ENGS}
        with contextlib.ExitStack() as st:
            sems = {e: [st.enter_context(nc.semaphore(f"s_{e}{i}")) for i in range(nsem[e])]
                    for e in self.ENGS}
            dsems = [st.enter_context(nc.semaphore(f"s_dma{i}")) for i in range(N_DMA_SEM)]
            block = st.enter_context(nc.Block())

            def waitspec(p):
                if p[0] == "dma":
                    k = p[1]
                    return ("d", k % N_DMA_SEM), dsems[k % N_DMA_SEM], 16 * (k // N_DMA_SEM + 1)
                e, i = p
                return (e, i // SEM_CAP), sems[e][i // SEM_CAP], i % SEM_CAP + 1

            def run(ename, eobj):
                waited = {}
                for idx, (fn, deps, dk) in enumerate(self.ops[ename]):
                    specs = []
                    for p in deps:
                        if p[0] == ename and (ename == "pe" or not SAME_ENG_WAIT):
                            continue
                        specs.append(waitspec(p))
                    if dk is not None and dk >= N_DMA_SEM:
                        specs.append((("d", dk % N_DMA_SEM), dsems[dk % N_DMA_SEM],
                                      16 * (dk // N_DMA_SEM)))
                    for sid, sem, val in specs:
                        if waited.get(sid, 0) >= val:
                            continue
                        waited[sid] = val
                        eobj.wait_ge(sem, val)
                    if fn is None:
                        continue
                    meth, args, kw = fn
                    ins = getattr(eobj, meth)(*args, **kw)
                    if dk is not None:
                        ins.then_inc(dsems[dk % N_DMA_SEM], 16)
                    else:
                        ins.then_inc(sems[ename][idx // SEM_CAP], 1)

            @block.tensor
            def _(e):
                run("pe", e)

            @block.vector
            def _(e):
                run("dve", e)

            @block.scalar
            def _(e):
                run("act", e)

            @block.gpsimd
            def _(e):
                run("pool", e)

            @block.sync
            def _(e):
                run("sp", e)


IN_SHAPES = {
    "x": [T_LAT, D], "ctx": [T_CTX, D], "cc": [2, D], "mod_w": [D, 6 * D], "mod_b": [1, 6 * D],
    "g1": [1, D], "g2n": [1, D], "gf": [1, D], "w_in": [D, 2272], "w_out": [D, D],
    "conv": [9, 1536], "w0": [2, 512], "w2": [2, 32, 512], "a0": [2, 512], "a2": [2, 32, 512],
    "g2": [96, 512], "kkv": [512], "kav": [512], "rkv": [512], "lnw": [1, 512], "lnb": [1, 512],
    "lam_re": [2, 32, 64], "lam_im": [2, 32, 64], "lstep": [2, 32],
    "b_re": [32, 64, 16], "b_im": [32, 64, 16], "c_re": [32, 16, 64], "c_im": [32, 16, 64],
    "s5d": [512], "gluw": [512, 512], "glub": [1, 512],
    "w1": [D, DFF], "w3": [D, DFF], "w2f": [DFF, D],
    "ident": [128, 128], "antiI": [128, 128], "mask_b": [128, 256], "mask_k": [128, 256],
    "maskT": [128, 128], "bones": [128, 128],
}

DBG_SHAPES = {"d_mod": [128, 6 * D], "d_AB": [128, 32], "d_z": [128, 3 * 256], "d_zu": [128, 512],
              "d_rkvc": [128, 3 * 128], "d_yT": [128, 512], "d_ST": [128, 256], "d_ys": [128, 512],
              "d_x1": [128, D], "d_car": [128, 32], "d_F": [128, 2048], "d_Bw": [128, 2048]}


def build_nc(n_lat_chunks=(16, 16), do_tail=True, debug=False, upto="full", n_ctx=2):
    nc = bass.Bass("TRN2", target_bir_lowering=False)
    din = {k: nc.dram_tensor(k, s, F32, kind="ExternalInput").ap() for k, s in IN_SHAPES.items()}
    out_d = nc.dram_tensor("out", [OWN, D], F32, kind="ExternalOutput").ap()
    scr = {}
    for nm in ("yr0", "yr1", "ys0", "ys1", "bv0", "bv1"):
        scr[nm] = nc.dram_tensor("scr_" + nm, [OWN, 512], F32, kind="Internal").ap()
    scr["sg"] = nc.dram_tensor("scr_sg", [NOWN * 128, 128], F32, kind="Internal").ap()
    scr["x1"] = nc.dram_tensor("scr_x1", [OWN, D], F32, kind="Internal").ap()
    scr["mod"] = nc.dram_tensor("scr_mod", [128, 6 * D], F32, kind="Internal").ap()
    scr["dg"] = nc.dram_tensor("scr_dg", [12, 128, 9 * 128], BF16, kind="Internal").ap()
    scr["fr"] = nc.dram_tensor("scr_fr", [NOWN * 4 * 128, 512], F32, kind="Internal").ap()
    scr["fz"] = nc.dram_tensor("scr_fz", [NOWN * 128, 128], F32, kind="Internal").ap()
    scr["fu"] = nc.dram_tensor("scr_fu", [NOWN * 128, 512], BF16, kind="Internal").ap()
    scr["cc_in"] = nc.dram_tensor("scr_cc_in", [128, 288], F32, kind="Internal").ap()
    scr["cc_out"] = nc.dram_tensor("scr_cc_out", [128, 288], F32, kind="Internal").ap()
    dbg = {}
    if debug:
        for nm, shp in DBG_SHAPES.items():
            dbg[nm] = nc.dram_tensor(nm, shp, F32, kind="ExternalOutput").ap()
    kb = KB(nc)
    cnt = [0]

    with contextlib.ExitStack() as g:
        def sb(st, name, shape, dt=F32):
            cnt[0] += 1
            return st.enter_context(nc.sbuf_tensor(f"sb{cnt[0]}_{name}", shape, dt))

        ident = sb(g, "ident", [128, 128]); antiI = sb(g, "antiI", [128, 128])
        mask_b = sb(g, "mask_b", [128, 256]); mask_k = sb(g, "mask_k", [128, 256])
        maskT = sb(g, "maskT", [128, 128]); bones = sb(g, "bones", [128, 128])
        ones = sb(g, "ones", [128, 128])
        for nm, t in (("ident", ident), ("antiI", antiI), ("mask_b", mask_b), ("mask_k", mask_k),
                      ("maskT", maskT), ("bones", bones)):
            kb.D(t[:], din[nm], writes=[nm])
        kb.I("pool", "memset", ones[:], 1.0, writes=["ones"])
        AB = sb(g, "AB", [128, 4, 8])
        cnt[0] += 1
        PS = [g.enter_context(nc.psum_tensor(f"psbank{i}", [128, 512], F32)) for i in range(8)]

        def bk(i):
            return [f"pX{i}_{j}" for j in range(4)]

        wscope = contextlib.ExitStack()
        win = sb(wscope, "win", [128, 8, 2272], BF16)
        car_re = sb(wscope, "car_re", [128, 16]); car_im = sb(wscope, "car_im", [128, 16])
        gs_re = sb(wscope, "gs_re", [128, 16]); gs_im = sb(wscope, "gs_im", [128, 16]); gta = sb(wscope, "gta", [128, 16]); gtb = sb(wscope, "gtb", [128, 16])
        Bw_re = sb(wscope, "Bw_re", [128, 16, 128], BF16); Bw_im = sb(wscope, "Bw_im", [128, 16, 128], BF16)
        Cw_re = sb(wscope, "Cw_re", [128, 16, 32]); Cw_imn = sb(wscope, "Cw_imn", [128, 16, 32])
        E_re = sb(wscope, "E_re", [128, 16, 128]); E_im = sb(wscope, "E_im", [128, 16, 128])
        F_re = sb(wscope, "F_re", [128, 16, 128]); F_im = sb(wscope, "F_im", [128, 16, 128])
        s5t = {nm: sb(wscope, "s5" + nm, [128, 512]) for nm in ("t1", "t2", "Xre", "Xim", "Gre", "Gim", "Hre", "Him")}

        def s5_setup(sd, F0=()):
            K = "s5s"
            S5K = ["s5t1", "s5t2", "s5Xre", "s5Xim", "s5Gre", "s5Gim", "s5Hre", "s5Him", "s5tab", K]
            with contextlib.ExitStack() as t_:
                sm = sb(t_, "s5sm", [128, 20, 16])
                (lre, lim, stp, th, th2, tmpc, mag, imag, sn, csn, lbr, lbi, den, nr, qre, qim, u1, ivr, ivi) = [sm[:, i_, :] for i_ in range(19)]
                bre = s5t["Gre"][:, 0:256].rearrange("p (q h) -> p q h", h=16); bim = s5t["Gre"][:, 256:512].rearrange("p (q h) -> p q h", h=16)
                bbr = s5t["Gim"][:, 0:256].rearrange("p (q h) -> p q h", h=16); bbi = s5t["Gim"][:, 256:512].rearrange("p (q h) -> p q h", h=16)
                v1 = s5t["Hre"][:, 0:256].rearrange("p (q h) -> p q h", h=16); cst = s5t["Hre"][:, 256:512].rearrange("p (q h) -> p q h", h=16)
                bdw = s5t["Xre"][:].rearrange("p (r c) -> p r c", c=128)

                def V(meth, *args, eng="dve", **kw):
                    kb.I(eng, meth, *args, reads=S5K, writes=S5K, **kw)

                kb.D(lre, din["lam_re"][sd].rearrange("(q g) p -> (g p) q", g=2), reads=list(F0) + S5K, writes=S5K, allow_slow_non_contiguous=True)
                kb.D(lim, din["lam_im"][sd].rearrange("(q g) p -> (g p) q", g=2), reads=list(F0), writes=[K], allow_slow_non_contiguous=True)
                for g2_ in range(2):
                    kb.D(stp[64 * g2_:64 * g2_ + 64, :],
                         din["lstep"][sd:sd + 1, :].rearrange("o (q g) -> o q g", g=2)[:, :, g2_].partition_broadcast(64),
                         reads=list(F0), writes=[K], allow_slow_non_contiguous=True)
                kb.D(bre, din["b_re"].rearrange("(q g) p h -> (g p) q h", g=2), reads=list(F0), writes=[K])
                kb.D(bim, din["b_im"].rearrange("(q g) p h -> (g p) q h", g=2), reads=list(F0), writes=[K])
                V("activation", stp, stp, AF.Exp, eng="act")
                V("tensor_tensor", mag, lre, stp, ALU.mult)
                V("tensor_tensor", th, lim, stp, ALU.mult)
                V("activation", imag, mag, AF.Exp, eng="act", scale=-1.0)
                V("activation", mag, mag, AF.Exp, eng="act")
                V("tensor_copy", u1, th)
                for m in (PI, 3 * PI, 5 * PI):
                    V("tensor_scalar", tmpc, th, m, -2 * PI, ALU.is_ge, ALU.mult)
                    V("tensor_tensor", u1, u1, tmpc, ALU.add)
                V("tensor_scalar", u1, u1, 0.125, 0.0, ALU.mult, ALU.add)
                V("tensor_tensor", th2, u1, u1, ALU.mult)
                V("tensor_scalar", sn, th2, -1.0 / 5040, 1.0 / 120, ALU.mult, ALU.add)
                V("tensor_tensor", sn, sn, th2, ALU.mult)
                V("tensor_scalar", sn, sn, -1.0 / 6, 0.0, ALU.add, ALU.add)
                V("tensor_tensor", sn, sn, th2, ALU.mult)
                V("tensor_scalar", sn, sn, 1.0, 0.0, ALU.add, ALU.add)
                V("tensor_tensor", sn, sn, u1, ALU.mult)
                V("tensor_scalar", csn, th2, 1.0 / 40320, -1.0 / 720, ALU.mult, ALU.add)
                V("tensor_tensor", csn, csn, th2, ALU.mult)
                V("tensor_scalar", csn, csn, 1.0 / 24, 0.0, ALU.add, ALU.add)
                V("tensor_tensor", csn, csn, th2, ALU.mult)
                V("tensor_scalar", csn, csn, -0.5, 0.0, ALU.add, ALU.add)
                V("tensor_tensor", csn, csn, th2, ALU.mult)
                V("tensor_scalar", csn, csn, 1.0, 0.0, ALU.add, ALU.add)
                for _ in range(3):
                    V("tensor_tensor", tmpc, csn, sn, ALU.mult)
                    V("tensor_tensor", th2, sn, sn, ALU.mult)
                    V("tensor_tensor", csn, csn, csn, ALU.mult)
                    V("tensor_tensor", csn, csn, th2, ALU.subtract)
                    V("tensor_scalar", sn, tmpc, 2.0, 0.0, ALU.mult, ALU.add)
                V("tensor_tensor", lbr, mag, csn, ALU.mult)
                V("tensor_tensor", lbi, mag, sn, ALU.mult)
                V("tensor_tensor", ivr, imag, csn, ALU.mult)
                V("scalar_tensor_tensor", ivi, imag, -1.0, sn, ALU.mult, ALU.mult)
                V("tensor_tensor", den, lre, lre, ALU.mult)
                V("tensor_tensor", u1, lim, lim, ALU.mult)
                V("tensor_tensor", den, den, u1, ALU.add)
                V("reciprocal", den, den)
                V("tensor_scalar", nr, lbr, -1.0, 0.0, ALU.add, ALU.add)
                V("tensor_tensor", qre, nr, lre, ALU.mult)
                V("tensor_tensor", u1, lbi, lim, ALU.mult)
                V("tensor_tensor", qre, qre, u1, ALU.add)
                V("tensor_tensor", qre, qre, den, ALU.mult)
                V("tensor_tensor", qim, lbi, lre, ALU.mult)
                V("tensor_tensor", u1, nr, lim, ALU.mult)
                V("tensor_tensor", qim, qim, u1, ALU.subtract)
                V("tensor_tensor", qim, qim, den, ALU.mult)
                qreb = qre.unsqueeze(2).to_broadcast([128, 16, 16]); qimb = qim.unsqueeze(2).to_broadcast([128, 16, 16])
                V("tensor_tensor", bbr, bre, qreb, ALU.mult)
                V("tensor_tensor", v1, bim, qimb, ALU.mult)
                V("tensor_tensor", bbr, bbr, v1, ALU.subtract)
                V("tensor_tensor", bbi, bim, qreb, ALU.mult)
                V("tensor_tensor", v1, bre, qimb, ALU.mult)
                V("tensor_tensor", bbi, bbi, v1, ALU.add)
                for src_, dstT in ((bbr, Bw_re), (bbi, Bw_im)):
                    for qq in range(4):
                        V("memset", bdw, 0.0)
                        for r_ in range(4):
                            for g2_ in range(2):
                                c0 = 32 * r_ + 16 * g2_
                                V("tensor_copy", bdw[64 * g2_:64 * g2_ + 64, r_, c0:c0 + 16], src_[64 * g2_:64 * g2_ + 64, qq * 4 + r_, :])
                        for r_ in range(4):
                            kb.I("pe", "transpose", PS[4][:, r_ * 128:(r_ + 1) * 128], bdw[:, r_, :], ident[:],
                                 reads=S5K + ["ident"], writes=[f"pX4_{r_}"])
                        kb.I("dve", "tensor_copy", dstT[:, qq * 4:(qq + 1) * 4, :], PS[4][:].rearrange("p (r c) -> p r c", r=4),
                             reads=bk(4) + S5K, writes=S5K)
                cpad = s5t["Xim"][:].rearrange("p (j c) -> p j c", c=128)
                for nm, dstC, sc in (("c_re", Cw_re, 1.0), ("c_im", Cw_imn, -1.0)):
                    csrc = din[nm].rearrange("g h p -> (g h) p").rearrange("(j r) p -> r j p", r=128)
                    for hh_ in range(2):
                        kb.D(cpad[:, :, 64 * hh_:64 * hh_ + 64], csrc, reads=S5K, writes=S5K)
                    for j_ in range(4):
                        kb.I("pe", "transpose", PS[4][:, j_ * 128:(j_ + 1) * 128], cpad[:, j_, :], ident[:], reads=S5K + ["ident"], writes=[f"pX4_{j_}"])
                    V("memset", dstC[:], 0.0)
                    for g2_ in range(2):
                        srcv = PS[4][64 * g2_:64 * g2_ + 64, :].rearrange("p (q g h) -> p q g h", g=2, h=16)[:, :, g2_, :]
                        kb.I("dve", "tensor_scalar", dstC[64 * g2_:64 * g2_ + 64, :, 16 * g2_:16 * g2_ + 16], srcv, sc, 0.0, ALU.mult, ALU.add,
                             reads=bk(4) + S5K, writes=S5K)
                for (Tr, Ti, sr, si) in ((F_re, F_im, lbr, lbi), (E_re, E_im, ivr, ivi)):
                    V("tensor_copy", Tr[:, :, 0:1], sr.unsqueeze(2))
                    V("tensor_copy", Ti[:, :, 0:1], si.unsqueeze(2))
                    n_ = 1
                    while n_ < 128:
                        for qh in range(2):
                            qsl = slice(8 * qh, 8 * qh + 8)
                            pr = Tr[:, qsl, n_ - 1:n_].to_broadcast([128, 8, n_]); pim = Ti[:, qsl, n_ - 1:n_].to_broadcast([128, 8, n_])
                            t1_ = s5t["t1"][:, 0:8 * n_].rearrange("p (q n) -> p q n", q=8)
                            t2_ = s5t["t2"][:, 0:8 * n_].rearrange("p (q n) -> p q n", q=8)
                            V("tensor_tensor", t1_, Tr[:, qsl, 0:n_], pr, ALU.mult)
                            V("tensor_tensor", t2_, Ti[:, qsl, 0:n_], pim, ALU.mult)
                            V("tensor_tensor", Tr[:, qsl, n_:2 * n_], t1_, t2_, ALU.subtract)
                            V("tensor_tensor", t1_, Tr[:, qsl, 0:n_], pim, ALU.mult)
                            V("tensor_tensor", t2_, Ti[:, qsl, 0:n_], pr, ALU.mult)
                            V("tensor_tensor", Ti[:, qsl, n_:2 * n_], t1_, t2_, ALU.add)
                        n_ *= 2
                if debug and sd == 0:
                    kb.D(dbg["d_F"], F_re[:].rearrange("p q t -> p (q t)"), reads=S5K, writes=["o_d_F"])
                V("tensor_copy", lre[:, 0:1], lre[:, 0:1])

        with contextlib.ExitStack() as p0:
            rec0 = []
            kb.rec = rec0
            stage = [sb(p0, f"stage{i}", [128, 2272]) for i in range(2)]
            for kt in range(8):
                s_ = stage[kt % 2]; sk = f"stage{kt % 2}"
                kb.D(s_[:], din["w_in"][kt * 128:(kt + 1) * 128, :], writes=[sk])
                kb.cp("act" if kt % 2 else "pool", win[:, kt, :], s_[:], reads=[sk], writes=["win"])
            modb = sb(p0, "modb", [128, 6 * D])
            cT = sb(p0, "cT", [128, 2, 8]); cs = sb(p0, "cs", [128, 2, 8])
            lbx = sb(p0, "lbx", [128, 8, 128]); lbc = sb(p0, "lbc", [128, 8, 128])
            modc = sb(p0, "modc", [128, 2 * D]); mbias = sb(p0, "mbias", [128, 6 * D])
            g1b = sb(p0, "g1b", [128, D]); tmpA = sb(p0, "tmpA", [128, D])
            wbuf = [sb(p0, f"wbuf{i}", [128, 512]) for i in range(4)]
            for j_ in range(2):
                kb.D(cT[:, j_, :], din["cc"][j_].rearrange("(k p) -> p k", p=128), writes=["cT"], allow_slow_non_contiguous=True)
            kb.D(mbias[:], din["mod_b"].partition_broadcast(128), writes=["mbias"])
            kb.D(g1b[:], din["g1"].partition_broadcast(128), writes=["g1b"])
            kb.I("act", "activation", cs[:], cT[:], AF.Silu, reads=["cT"], writes=["cs"])
            for kt in range(8):
                kb.I("dve", "tensor_copy", lbx[:, kt, :], cs[:, 0, kt:kt + 1].to_broadcast([128, 128]), reads=["cs"], writes=["lbx"])
                kb.I("pool", "tensor_copy", lbc[:, kt, :], cs[:, 1, kt:kt + 1].to_broadcast([128, 128]), reads=["cs"], writes=["lbc"])
            i = 0
            for n in range(12 if upto != "p0a" else 0):
                ns = slice(n * 512, (n + 1) * 512)
                for kt in range(8):
                    wb = wbuf[i % 4]; wk = f"wbuf{i % 4}"; i += 1
                    kb.D(wb[:], din["mod_w"][kt * 128:(kt + 1) * 128, ns], writes=[wk])
                    kb.I("pe", "matmul", PS[0][:], lbx[:, kt, :], wb[:], start=(kt == 0), stop=(kt == 7), reads=[wk, "lbx"], writes=bk(0))
                    if n < 4:
                        kb.I("pe", "matmul", PS[1][:], lbc[:, kt, :], wb[:], start=(kt == 0), stop=(kt == 7), reads=[wk, "lbc"], writes=bk(1))
                kb.I("dve", "tensor_tensor", modb[:, ns], PS[0][:], mbias[:, ns], ALU.add, reads=bk(0) + ["mbias"], writes=["modb"])
                if n < 4:
                    kb.I("dve", "tensor_tensor", modc[:, ns], PS[1][:], mbias[:, ns], ALU.add, reads=bk(1) + ["mbias"], writes=["modc"])
            for which, src_ in ((0, modb), (1, modc)) if upto not in ("p0a", "p0b") else ():
                kb.I("dve", "scalar_tensor_tensor", tmpA[:], src_[:, D:2 * D], 1.0, g1b[:], ALU.add, ALU.mult,
                     reads=["modb", "modc", "g1b"], writes=["tmpA"])
                for half in range(2):
                    for j in range(4):
                        kt = half * 4 + j
                        kb.I("pe", "transpose", PS[2][:, j * 128:(j + 1) * 128], tmpA[:, kt * 128:(kt + 1) * 128], ident[:],
                             reads=["tmpA", "ident"], writes=[f"pX2_{j}"])
                        kb.I("pe", "transpose", PS[3][:, j * 128:(j + 1) * 128], src_[:, kt * 128:(kt + 1) * 128], ident[:],
                             reads=["modb", "modc", "ident"], writes=[f"pX3_{j}"])
                    for j in range(4):
                        kt = half * 4 + j
                        kb.I("dve", "tensor_copy", AB[:, 2 * which, kt:kt + 1], PS[2][:, j * 128:j * 128 + 1], reads=[f"pX2_{j}"], writes=["AB"])
                        kb.I("dve", "tensor_copy", AB[:, 2 * which + 1, kt:kt + 1], PS[3][:, j * 128:j * 128 + 1], reads=[f"pX3_{j}"], writes=["AB"])
            if upto not in ("p0a", "p0b", "p0c"):
                kb.D(scr["mod"], modb[:], reads=["modb"], writes=["scr_mod"])
            if debug and upto not in ("p0a", "p0b", "p0c"):
                kb.D(dbg["d_mod"], modb[:], reads=["modb"], writes=["o_d_mod"])
                kb.D(dbg["d_AB"], AB[:].rearrange("p a k -> p (a k)"), reads=["AB"], writes=["o_d_AB"])
            kb.rec = None
            recS = []
            if not upto.startswith("p0"):
                kb.rec = recS
                s5_setup(0)
                kb.rec = None
            kb.merge(rec0, recS)
            kb.I("dve", "tensor_copy", tmpA[:, 0:1], tmpA[:, 0:1],
                 reads=list(kb.lastw.keys()), writes=list(kb.lastw.keys()) + ["fence0"])
        FENCE0 = ["fence0"]
        if upto.startswith("p0"):
            kb.mute = True

        with contextlib.ExitStack() as sw:
            F0 = FENCE0
            kkc = sb(sw, "kkc", [128, 4]); kac = sb(sw, "kac", [128, 4]); omka = sb(sw, "omka", [128, 4])
            rkc = sb(sw, "rkc", [128, 4]); w0c = sb(sw, "w0c", [128, 2, 4]); a0c = sb(sw, "a0c", [128, 2, 4])
            cw = sb(sw, "cw", [128, 12, 9]); s5dc = sb(sw, "s5dc", [128, 4])
            w2pad = sb(sw, "w2pad", [128, 2, 512]); a2pad = sb(sw, "a2pad", [128, 2, 512])
            diagd = sb(sw, "diagd", [128, 4, 128], BF16)
            for t, nm in ((kkc, "kkv"), (kac, "kav"), (rkc, "rkv"), (s5dc, "s5d")):
                kb.D(t[:], din[nm].rearrange("(f p) -> p f", p=128), reads=F0, writes=["cols"], allow_slow_non_contiguous=True)
            for t, nm in ((w0c, "w0"), (a0c, "a0")):
                for d_ in range(2):
                    kb.D(t[:, d_, :], din[nm][d_].rearrange("(f p) -> p f", p=128), reads=F0, writes=["cols"], allow_slow_non_contiguous=True)
            for tp in range(9):
                kb.D(cw[:, :, tp], din["conv"][tp].rearrange("(f p) -> p f", p=128), reads=F0, writes=["cols"], allow_slow_non_contiguous=True)
            kb.I("dve", "tensor_scalar", omka[:], kac[:], -1.0, 1.0, ALU.mult, ALU.add, reads=["cols"], writes=["cols2"])
            kb.I("pool", "memset", w2pad[:], 0.0, reads=F0, writes=["w2pad"])
            kb.I("pool", "memset", a2pad[:], 0.0, reads=F0, writes=["a2pad"])
            for d_ in range(2):
                kb.D(w2pad[32 * d_:32 * d_ + 32, d_, :], din["w2"][d_], reads=["w2pad"], writes=["w2pad"])
                kb.D(a2pad[64 + 32 * d_:96 + 32 * d_, d_, :], din["a2"][d_], reads=["a2pad"], writes=["a2pad"])
            for ct in range(4):
                kb.ts1("dve", diagd[:, ct, :], ident[:], s5dc[:, ct:ct + 1], ALU.mult, reads=["cols", "ident"], writes=["diagd"])

            ST = sb(sw, "ST", [128, 4, 64])
            xw = sb(sw, "xw", [128, 2, D]); junk = sb(sw, "junk", [128, D], BF16)
            ssum = sb(sw, "ssum", [128, 2]); rstd = sb(sw, "rstd", [128, 2])
            hT = sb(sw, "hT", [128, 8, 256], BF16)
            zrkv2 = [sb(sw, f"zrkv{i}", [128, 3, 264], BF16) for i in range(2)]; zl = sb(sw, "zl", [128, 128]); zg = sb(sw, "zg", [128, 128])
            dgb = [sb(sw, f"dgb{i}", [128, 9, 128], BF16) for i in range(2)]
            rkvc2 = [sb(sw, f"rkvc{i}", [128, 3, 128]) for i in range(2)]
            W2 = {}
            for nm in ("kk", "t1", "t2", "ld", "a", "cs", "Eni", "b", "kd", "bep", "kap"):
                W2[nm] = [sb(sw, f"{nm}{i}", [128, 128]) for i in range(2)]
            zub = [sb(sw, f"zub{i}", [128, 4, 128], BF16) for i in range(2)]
            BE = [sb(sw, f"BE{i}", [128, 4, 128]) for i in range(2)]; KAP = [sb(sw, f"KAP{i}", [128, 4, 128]) for i in range(2)]
            AR = [sb(sw, f"AR{i}", [128, 4, 256]) for i in range(2)]; gC = [sb(sw, f"gC{i}", [128, 4]) for i in range(2)]
            tot = sb(sw, "tot", [128, 4])
            TMt = [sb(sw, f"TMt{i}", [128, 4, 512]) for i in range(2)]
            UT = sb(sw, "UT", [128, 512]); yT = sb(sw, "yT", [128, 512])
            flpR = sb(sw, "flpR", [128, 512]); flpS = sb(sw, "flpS", [128, 512])
            NS = 4
            NB = [sb(sw, f"NB{i}", [128, 256]) for i in range(NS)]; KA = [sb(sw, f"KA{i}", [128, 256]) for i in range(NS)]
            Pq = [sb(sw, f"Pq{i}", [128, 2, 128], BF16) for i in range(NS)]; PTq = [sb(sw, f"PTq{i}", [128, 2, 128], BF16) for i in range(NS)]
            Xq = [sb(sw, f"Xq{i}", [128, 2, 64]) for i in range(NS)]; Xb = [sb(sw, f"Xb{i}", [128, 2, 64], BF16) for i in range(NS)]
            print("SBUF bytes remaining in sweep scope:", nc.sbuf_bytes_remaining)
            for cf in range(12):
                for tp in range(9):
                    kb.ts1("dve" if tp % 2 else "pool", dgb[cf % 2][:, tp, :], ident[:], cw[:, cf, tp:tp + 1], ALU.mult,
                           reads=["cols", "ident", f"dgb{cf % 2}"], writes=[f"dgb{cf % 2}"])
                kb.D(scr["dg"][cf], dgb[cf % 2][:].rearrange("p t c -> p (t c)"), reads=[f"dgb{cf % 2}"], writes=["scr_dg"])
            kb.I("pool", "memset", xw[:], 0.0, reads=F0, writes=["xw0", "xw1"])

            def load_window(sd, is_ctx, c):
                rev = (sd == 1)
                TT = T_CTX if is_ctx else T_LAT
                src = din["ctx"] if is_ctx else din["x"]
                for half in range(2):
                    lo_r = 128 * c - 64 + 128 * half
                    if not rev:
                        lo, hi = lo_r, lo_r + 128
                    else:
                        hi, lo = TT - lo_r, TT - lo_r - 128
                    a0_, a1_ = max(lo, 0), min(hi, TT)
                    if a1_ > a0_:
                        kb.D(xw[a0_ - lo:a1_ - lo, half, :], src[a0_:a1_, :], writes=[f"xw{half}"])

            def process_chunk(sd, is_ctx, c, own, slot, nxt=None, first=False, reuse=False):
                rev = (sd == 1)
                TT = T_CTX if is_ctx else T_LAT
                src = din["ctx"] if is_ctx else din["x"]
                ab = 2 if is_ctx else 0
                tid = antiI if rev else ident
                tk = "antiI" if rev else "ident"
                dbg_here = debug and sd == 0 and (not is_ctx) and c == 0
                SL = f"@{slot}"
                streams = {"front": [], "rwkv": [], "s5": []}
                kb.rec = streams["front"]
                AR_, BE_, KAP_, gC_, TM_, zub_ = AR[slot], BE[slot], KAP[slot], gC[slot], TMt[slot], zub[slot]
                kapT = TM_[:, 0, :]; bepT = TM_[:, 1, :]; vT = TM_[:, 2, :]; bvT = TM_[:, 3, :]
                TMK = "TM" + SL
                if not reuse:
                    if first:
                        load_window(sd, is_ctx, c)
                    kb.I("pool", "memset", ssum[:], 0.0, writes=["ssum"])
                    for half in range(2):
                        kb.I("act", "activation", junk[:], xw[:, half, :], AF.Square, accum_out=ssum[:, half:half + 1],
                             reads=[f"xw{half}"], writes=["junk", "ssum"])
                    kb.I("dve", "tensor_scalar", rstd[:], ssum[:], 1.0 / D, 1e-6, ALU.mult, ALU.add, reads=["ssum"], writes=["rstd"])
                    kb.I("act", "activation", rstd[:], rstd[:], AF.Sqrt, reads=["rstd"], writes=["rstd"])
                    kb.I("dve", "reciprocal", rstd[:], rstd[:], reads=["rstd"], writes=["rstd"])
                    for half in range(2):
                        kb.ts1("dve" if half == 0 else "pool", xw[:, half, :], xw[:, half, :], rstd[:, half:half + 1], ALU.mult,
                               reads=["rstd", f"xw{half}"], writes=[f"xw{half}"])
                    for kt in range(8):
                        b_ = kt % 2
                        for half in range(2):
                            kb.I("pe", "transpose", PS[b_][:, half * 128:(half + 1) * 128], xw[:, half, kt * 128:(kt + 1) * 128], tid[:],
                                 reads=[f"xw{half}", tk], writes=[f"pX{b_}_0"])
                        if kt % 2:
                            kb.I("dve", "tensor_scalar", hT[:, kt, :], PS[b_][:, 0:256], AB[:, ab, kt:kt + 1], AB[:, ab + 1, kt:kt + 1], ALU.mult, ALU.add,
                                 reads=[f"pX{b_}_0", "AB"], writes=[f"hT{kt}"])
                        else:
                            kb.I("act", "activation", hT[:, kt, :], PS[b_][:, 0:256], AF.Identity, bias=AB[:, ab + 1, kt:kt + 1], scale=AB[:, ab, kt:kt + 1],
                                 reads=[f"pX{b_}_0", "AB"], writes=[f"hT{kt}"])
                    if nxt is not None:
                        load_window(sd, nxt[0], nxt[1])
                    others = [("zl", zl[:, :], 1536, 128), ("zg", zg[0:96, :], 1664, 96)] + [(f"zu{j}" + SL, zub_[:, j, :], 1760 + 128 * j, 128) for j in range(4)]
                    for i_, (nm, dst, c0, m) in enumerate(others):
                        b_ = i_ % 2
                        pr_ = PS[b_][0:m, 0:128]
                        pk = [f"pX{b_}_0"]
                        for kt in range(8):
                            kb.I("pe", "matmul", pr_, win[:, kt, c0:c0 + m], hT[:, kt, 64:192], start=(kt == 0), stop=(kt == 7), reads=["win", f"hT{kt}"], writes=pk)
                        if nm == "zg":
                            kb.I("act", "activation", dst, pr_, AF.Sigmoid, reads=pk, writes=[nm])
                        else:
                            kb.cp("dve" if i_ % 2 else "act", dst, pr_, reads=pk, writes=[nm])
                    if own is not None and sd == 0:
                        kb.D(scr["sg"][own * 128:own * 128 + 96, :], zg[0:96, :], reads=["zg"], writes=[f"scr_sg{own}"])
                    kb.I("act", "activation", zl[0:64, :], zl[0:64, :], AF.Tanh, reads=["zl"], writes=["zl"])
                    if sd == 0 and own is not None and not is_ctx:
                        kb.D(scr["fz"][own * 128:(own + 1) * 128, :], zl[:, :], reads=["zl"], writes=[f"scr_fz{own}"])
                        kb.D(scr["fu"][own * 128:(own + 1) * 128, :], zub_[:].rearrange("p j t -> p (j t)"), reads=[f"zu{j}" + SL for j in range(4)], writes=[f"scr_fu{own}"])
                else:
                    kb.D(xw[:, 1, 512:640], scr["fz"][own * 128:(own + 1) * 128, :], reads=[f"scr_fz{own}"], writes=["xw1"])
                    kb.I("dve", "tensor_copy", zl[:, :], xw[:, 1, 512:640][:, ::-1], reads=["xw1"], writes=["zl"])
                    kb.D(junk[:, 0:512], scr["fu"][own * 128:(own + 1) * 128, :], reads=[f"scr_fu{own}"], writes=["junk"])
                    kb.I("dve", "tensor_copy", zub_[:], junk[:, 0:512].rearrange("p (j t) -> p j t", t=128)[:, :, ::-1], reads=["junk"], writes=[f"zu{j}" + SL for j in range(4)])
                sgn = -1 if rev else 1
                nk = 4 if own is not None else 3
                for ft in range(4):
                    fp_ = ft % 2
                    zrkv = zrkv2[fp_]; rkvc = rkvc2[fp_]
                    W = {nm: W2[nm][fp_] for nm in W2}
                    if reuse:
                        stg_ = xw[:, ft % 2, 0:512]
                        kb.D(stg_, scr["fr"][(own * 4 + ft) * 128:(own * 4 + ft + 1) * 128, :], reads=[f"scr_fr{own}_{ft}"], writes=[f"xw{ft % 2}"])
                        kb.I("dve", "tensor_copy", rkvc[:, :, :], stg_[:, 0:384].rearrange("p (j t) -> p j t", t=128)[:, :, ::-1],
                             reads=[f"xw{ft % 2}"], writes=[f"rkvc{j}_{fp_}" for j in range(3)])
                        kb.I("dve", "tensor_copy", W["kk"][:, :], stg_[:, 384:512][:, ::-1], reads=[f"xw{ft % 2}"], writes=[f"W_kk{fp_}"])
                    for j3 in (range(3) if not reuse else ()):
                        b_ = j3 % 2
                        cf = (j3 * 4 + ft) * 128
                        pr_ = PS[b_][:, 0:256]
                        pk = [f"pX{b_}_0"]
                        for kt in range(8):
                            kb.I("pe", "matmul", pr_, win[:, kt, cf:cf + 128], hT[:, kt, :], start=(kt == 0), stop=(kt == 7), reads=["win", f"hT{kt}"], writes=pk)
                        if is_ctx:
                            kb.cp("act" if j3 % 2 else "dve", zrkv[:, j3, 0:256], pr_, reads=pk, writes=[f"zrkv{j3}_{fp_}"])
                        else:
                            if c == 0 and ft < 2:
                                kb.I("pool", "memset", zrkv[:, j3, :].rearrange("p (r c) -> p r c", c=66)[:, :, 0:1], 0.0, reads=[f"zrkv{j3}_{fp_}"], writes=[f"zrkv{j3}_{fp_}"])
                                kb.I("pool", "memset", zrkv[:, j3, :].rearrange("p (r c) -> p r c", c=66)[:, :, 65:66], 0.0, reads=[f"zrkv{j3}_{fp_}"], writes=[f"zrkv{j3}_{fp_}"])
                            kb.cp("act" if j3 % 2 else "dve", zrkv[:, j3, :].rearrange("p (r c) -> p r c", c=66)[:, :, 1:65],
                                  pr_.rearrange("p (r c) -> p r c", c=64), reads=pk, writes=[f"zrkv{j3}_{fp_}"])
                    if dbg_here and ft == 0:
                        kb.D(dbg["d_z"], zrkv[:].rearrange("p f t -> p (f t)"), reads=[f"zrkv{j}_{fp_}" for j in range(3)], writes=["o_d_z"])
                    for j3 in (range(3) if not reuse else ()):
                        zk = f"zrkv{j3}_{fp_}"; ok = f"rkvc{j3}_{fp_}"
                        cf = j3 * 4 + ft
                        ds = (ft * 3 + j3) % 2
                        dk = f"dgb{ds}"
                        kb.D(dgb[ds][:].rearrange("p t c -> p (t c)"), scr["dg"][cf], reads=["scr_dg"], writes=[dk])
                        b_ = (j3 + 1) % 2
                        pc = PS[b_][:, 256:384]
                        taps = [(0, 0)] + [(dy, dx) for dy in ((0,) if is_ctx else (-1, 0, 1)) for dx in (-1, 0, 1) if (dy, dx) != (0, 0)]
                        mms = []
                        for (dy, dx) in taps:
                            wi = (1 + sgn * dy) * 3 + (1 + sgn * dx)
                            if is_ctx:
                                j0, j1 = 64, 192
                                if 128 * c + dx < 0:
                                    j0 = 65
                                if 128 * c + 127 + dx >= TT:
                                    j1 = 191
                                mms.append((PS[b_][:, 256 + j0 - 64:256 + j1 - 64], wi, zrkv[:, j3, j0 + dx:j1 + dx]))
                            else:
                                i0, i1 = 0, 2
                                if 2 * c + dy < 0:
                                    i0 = 1
                                if 2 * c + 1 + dy >= 64:
                                    i1 = 1
                                o0 = 66 * i0 + 1; o1 = 66 * (i1 - 1) + 65
                                sh = 66 * (1 + dy) + dx
                                mms.append((PS[b_][:, 256 + o0:256 + o1], wi, zrkv[:, j3, o0 + sh:o1 + sh]))
                        for ti, (o_ap, wi, i_ap) in enumerate(mms):
                            kb.I("pe", "matmul", o_ap, dgb[ds][:, wi, :], i_ap, start=(ti == 0), stop=(ti == len(mms) - 1),
                                 reads=[zk, dk], writes=[f"pX{b_}_0"])
                        if is_ctx:
                            kb.cp("act" if j3 % 2 == 0 else "dve", rkvc[:, j3, :], pc, reads=[f"pX{b_}_0"], writes=[ok])
                        else:
                            kb.cp("act" if j3 % 2 == 0 else "dve", rkvc[:, j3, :].rearrange("p (r c) -> p r c", c=64),
                                  PS[b_][:, 256:388].rearrange("p (r c) -> p r c", c=66)[:, :, 1:65], reads=[f"pX{b_}_0"], writes=[ok])
                    if dbg_here and ft == 0:
                        kb.D(dbg["d_rkvc"], rkvc[:].rearrange("p f t -> p (f t)"), reads=[f"rkvc{j}_{fp_}" for j in range(3)], writes=["o_d_rkvc"])
                    rc = rkvc[:, 0, :]; kc = rkvc[:, 1, :]; vc = rkvc[:, 2, :]
                    fk = f"w{ft}" + SL
                    g_ = {nm: W[nm][:, :] for nm in W}
                    CK = ["cols", "cols2"]

                    def V(meth, *args, eng="dve", r=(), w=(), **kw):
                        kb.I(eng, meth, *args, reads=list(r) + CK, writes=list(w), **kw)
                    ARa = AR_[:, ft, 0:128]; ARr = AR_[:, ft, 128:256]
                    fa, fr, fb, fkp, fg = fk + "a", fk + "r", fk + "b", fk + "k", fk + "g"
                    if not reuse:
                        V("tensor_scalar", g_["kk"], kc, kkc[:, ft:ft + 1], 0.0, ALU.mult, ALU.add, r=[f"rkvc1_{fp_}"], w=[f"W_kk{fp_}"])
                        V("tensor_tensor", g_["t1"], g_["kk"], g_["kk"], ALU.mult, r=[f"W_kk{fp_}"], w=[f"W_t1{fp_}"])
                        kb.I("pe", "matmul", PS[0][:, 384:512], bones[:], g_["t1"], start=True, stop=True, reads=[f"W_t1{fp_}", "bones"], writes=["pX0_0"])
                        kb.I("dve", "tensor_scalar", g_["t2"], PS[0][:, 384:512], 1e-12, 0.0, ALU.max, ALU.add, reads=["pX0_0"], writes=[f"W_t2{fp_}"])
                        V("activation", g_["t2"], g_["t2"], AF.Sqrt, eng="act", r=[f"W_t2{fp_}"], w=[f"W_t2{fp_}"])
                        V("reciprocal", g_["t2"], g_["t2"], r=[f"W_t2{fp_}"], w=[f"W_t2{fp_}"])
                        V("tensor_tensor", g_["kk"], g_["kk"], g_["t2"], ALU.mult, r=[f"W_kk{fp_}", f"W_t2{fp_}"], w=[f"W_kk{fp_}"])
                        if sd == 0 and own is not None and not is_ctx:
                            r0_ = (own * 4 + ft) * 128
                            kb.D(scr["fr"][r0_:r0_ + 128, 0:384], rkvc[:, :, :].rearrange("p j t -> p (j t)"), reads=[f"rkvc{j}_{fp_}" for j in range(3)], writes=[f"scr_fr{own}_{ft}"])
                            kb.D(scr["fr"][r0_:r0_ + 128, 384:512], g_["kk"], reads=[f"W_kk{fp_}"], writes=[f"scr_fr{own}_{ft}"])
                    kb.I("pe", "matmul", PS[1][:, 256:384], w2pad[:, sd, ft * 128:(ft + 1) * 128], zl[:, :], start=True, stop=True, reads=["zl", "w2pad"], writes=["pX1_0"])
                    kb.I("pe", "matmul", PS[1][:, 384:512], a2pad[:, sd, ft * 128:(ft + 1) * 128], zl[:, :], start=True, stop=True, reads=["zl", "a2pad"], writes=["pX1_0"])
                    kb.I("act", "activation", g_["ld"], PS[1][:, 256:384], AF.Sigmoid, bias=w0c[:, sd, ft:ft + 1], scale=1.0, reads=["pX1_0", "cols"], writes=[f"W_ld{fp_}"])
                    kb.I("act", "activation", g_["a"], PS[1][:, 384:512], AF.Sigmoid, bias=a0c[:, sd, ft:ft + 1], scale=1.0, reads=["pX1_0", "cols"], writes=[f"W_a{fp_}"])
                    V("tensor_tensor_scan", g_["cs"], ones[:], g_["ld"], 0.0, ALU.mult, ALU.add, r=[f"W_ld{fp_}", "ones"], w=[f"W_cs{fp_}"])
                    V("tensor_copy", tot[:, ft:ft + 1], g_["cs"][:, 127:128], r=[f"W_cs{fp_}"], w=[f"tot{ft}"])
                    V("tensor_tensor", g_["t1"], g_["cs"], g_["ld"], ALU.subtract, r=[f"W_cs{fp_}", f"W_ld{fp_}"], w=[f"W_t1{fp_}"])
                    V("activation", ARr, g_["cs"], AF.Exp, eng="act", scale=-KDEC, r=[f"W_cs{fp_}"], w=[fr])
                    V("activation", ARa, g_["t1"], AF.Exp, eng="act", scale=-KDEC, r=[f"W_t1{fp_}"], w=[fa])
                    V("activation", g_["Eni"], g_["cs"], AF.Exp, eng="act", scale=KDEC, r=[f"W_cs{fp_}"], w=[f"W_Eni{fp_}"])
                    V("activation", gC_[:, ft:ft + 1], tot[:, ft:ft + 1], AF.Exp, eng="act", scale=-KDEC, r=[f"tot{ft}"], w=[fg])
                    V("tensor_tensor", g_["b"], g_["kk"], g_["a"], ALU.mult, r=[f"W_kk{fp_}", f"W_a{fp_}"], w=[f"W_b{fp_}"])
                    V("tensor_scalar", g_["t2"], g_["a"], kac[:, ft:ft + 1], omka[:, ft:ft + 1], ALU.mult, ALU.add, r=[f"W_a{fp_}"], w=[f"W_t2{fp_}"])
                    V("tensor_tensor", g_["kd"], kc, g_["t2"], ALU.mult, r=[f"rkvc1_{fp_}", f"W_t2{fp_}"], w=[f"W_kd{fp_}"])
                    V("tensor_tensor", ARa, ARa, g_["kk"], ALU.mult, r=[fa, f"W_kk{fp_}"], w=[fa])
                    V("tensor_tensor", ARr, ARr, rc, ALU.mult, eng="pool", r=[fr, f"rkvc0_{fp_}"], w=[fr])
                    V("tensor_tensor", BE_[:, ft, :], g_["b"], g_["Eni"], ALU.mult, r=[f"W_b{fp_}", f"W_Eni{fp_}"], w=[fb])
                    V("tensor_tensor", KAP_[:, ft, :], g_["kd"], g_["Eni"], ALU.mult, eng="pool", r=[f"W_kd{fp_}", f"W_Eni{fp_}"], w=[fkp])
                    V("tensor_scalar", g_["bep"], BE_[:, ft, :], gC_[:, ft:ft + 1], -1.0, ALU.mult, ALU.mult, r=[fb, fg], w=[f"W_bep{fp_}"])
                    V("tensor_scalar", g_["kap"], KAP_[:, ft, :], gC_[:, ft:ft + 1], 0.0, ALU.mult, ALU.add, eng="pool", r=[fkp, fg], w=[f"W_kap{fp_}"])
                    tsrc = [(g_["kap"], f"W_kap{fp_}"), (g_["bep"], f"W_bep{fp_}"), (vc, f"rkvc2_{fp_}")]
                    if own is not None:
                        V("scalar_tensor_tensor", g_["t1"], rc, rkc[:, ft:ft + 1], g_["kd"], ALU.mult, ALU.mult, r=[f"rkvc0_{fp_}", f"W_kd{fp_}"], w=[f"W_t1{fp_}"])
                        kb.I("pe", "matmul", PS[0][:, 384:512], bones[:], g_["t1"], start=True, stop=True, reads=[f"W_t1{fp_}", "bones"], writes=["pX0_0"])
                        kb.I("dve", "tensor_tensor", g_["t2"], PS[0][:, 384:512], vc, ALU.mult, reads=["pX0_0", f"rkvc2_{fp_}"], writes=[f"W_t2{fp_}"])
                        tsrc.append((g_["t2"], f"W_t2{fp_}"))
                    b_ = ft % 2
                    for i_, (s_ap, s_k) in enumerate(tsrc):
                        kb.I("pe", "transpose", PS[b_][:, i_ * 128:(i_ + 1) * 128], s_ap, ident[:], reads=[s_k, "ident"], writes=[f"pX{b_}_0"])
                    kb.cp("act" if ft % 2 else "dve", TM_[:, 0:nk, ft * 128:(ft + 1) * 128], PS[b_][:, 0:nk * 128].rearrange("p (k t) -> p k t", k=nk),
                          reads=[f"pX{b_}_0"], writes=[TMK])
                kb.rec = streams["rwkv"]
                for g0 in range(0, 8, NS):
                    heads = list(range(g0, g0 + NS))

                    def hv(h):
                        ft = h // 2; Rs = slice(64 * (h % 2), 64 * (h % 2) + 64)
                        return ft, Rs, [f"w{ft}" + SL + x_ for x_ in "arbkg"]
                    BK = [2 + s_ for s_ in range(NS)]
                    bkk = [f"pX{b_}_0" for b_ in BK]
                    for s_, h in enumerate(heads):
                        ft, Rs, fk = hv(h)
                        be = BE_[Rs, ft, :]; ar = AR_[Rs, ft, :]; al = AR_[Rs, ft, 0:128]
                        kb.I("pe", "matmul", PS[BK[s_]][:, 0:256], be, ar, start=True, stop=True, reads=fk, writes=[bkk[s_]])
                        kb.I("pe", "matmul", PS[BK[s_]][:, 256:384], al, be, start=True, stop=True, reads=fk, writes=[bkk[s_]])
                    for s_, h in enumerate(heads):
                        kb.I("dve", "tensor_tensor", NB[s_][:], PS[BK[s_]][:, 0:256], mask_b[:], ALU.mult, reads=[bkk[s_], "mask_b"], writes=[f"NB{s_}"])
                        kb.I("dve", "tensor_tensor", PTq[s_][:, 0, :], PS[BK[s_]][:, 256:384], maskT[:], ALU.mult, reads=[bkk[s_], "maskT"], writes=[f"PT{s_}_0"])
                        kb.I("dve", "tensor_tensor", Pq[s_][:, 0, :], PS[BK[s_]][:, 0:128], mask_b[:, 0:128], ALU.mult, reads=[bkk[s_], "mask_b"], writes=[f"P{s_}_0"])
                    for s_, h in enumerate(heads):
                        ft, Rs, fk = hv(h)
                        kb.I("pe", "matmul", PS[BK[s_]][:, 0:256], KAP_[Rs, ft, :], AR_[Rs, ft, :], start=True, stop=True, reads=fk, writes=[bkk[s_]])
                    for s_, h in enumerate(heads):
                        kb.I("dve", "tensor_tensor", KA[s_][:], PS[BK[s_]][:, 0:256], mask_k[:], ALU.mult, reads=[bkk[s_], "mask_k"], writes=[f"KA{s_}"])
                    for s_, h in enumerate(heads):
                        ft, Rs, fk = hv(h)
                        al = AR_[Rs, ft, 0:128]
                        kb.I("pe", "matmul", PS[BK[s_]][:, 384:448], al, ST[Rs, ft, :], start=True, stop=False, reads=fk + [f"ST{ft}"], writes=[bkk[s_]])
                        kb.I("pe", "matmul", PS[BK[s_]][:, 384:448], KA[s_][:, 0:128], vT[:, h * 64:(h + 1) * 64], start=False, stop=True, reads=[f"KA{s_}", TMK], writes=[bkk[s_]])
                    for s_, h in enumerate(heads):
                        kb.cp("act", Xq[s_][:, 0, :], PS[BK[s_]][:, 384:448], reads=[bkk[s_]], writes=[f"X{s_}_0"])
                        kb.cp("act", Xb[s_][:, 0, :], PS[BK[s_]][:, 384:448], reads=[bkk[s_]], writes=[f"Xb{s_}_0"])
                    for j in range(7):
                        cu, nx = j % 2, (j + 1) % 2
                        for s_, h in enumerate(heads):
                            Pj = Pq[s_][:, cu, :]
                            Pk = f"P{s_}_{cu}"
                            PTj = PTq[s_][:, cu, :]; PTk = f"PT{s_}_{cu}"
                            if j < 6:
                                kb.I("pe", "matmul", PS[BK[s_]][:, 0:128], PTj, Pj, start=True, stop=True, reads=[PTk, Pk], writes=[bkk[s_]])
                            if j < 5:
                                kb.I("pe", "matmul", PS[BK[s_]][:, 128:256], Pj, PTj, start=True, stop=True, reads=[PTk, Pk], writes=[bkk[s_]])
                            kb.I("pe", "matmul", PS[BK[s_]][:, 448:512], Pj, Xb[s_][:, cu, :], start=True, stop=True, reads=[Pk, f"Xb{s_}_{cu}"], writes=[bkk[s_]])
                        for s_, h in enumerate(heads):
                            if j < 6:
                                kb.cp("act", Pq[s_][:, nx, :], PS[BK[s_]][:, 0:128], reads=[bkk[s_]], writes=[f"P{s_}_{nx}"])
                            if j < 5:
                                kb.cp("act", PTq[s_][:, nx, :], PS[BK[s_]][:, 128:256], reads=[bkk[s_]], writes=[f"PT{s_}_{nx}"])
                            dst_ap = UT[:, h * 64:(h + 1) * 64] if j == 6 else Xq[s_][:, nx, :]
                            dst_k = f"UT{h}" if j == 6 else f"X{s_}_{nx}"
                            if j < 6:
                                kb.I("dve", "tensor_tensor", Xb[s_][:, nx, :], Xq[s_][:, cu, :], PS[BK[s_]][:, 448:512], ALU.subtract if j == 0 else ALU.add,
                                     reads=[bkk[s_], f"X{s_}_{cu}"], writes=[f"Xb{s_}_{nx}"])
                            kb.I("dve", "tensor_tensor", dst_ap, Xq[s_][:, cu, :], PS[BK[s_]][:, 448:512], ALU.subtract if j == 0 else ALU.add,
                                 reads=[bkk[s_], f"X{s_}_{cu}"], writes=[dst_k])
                    if own is not None:
                        for s_, h in enumerate(heads):
                            ft, Rs, fk = hv(h)
                            rho = AR_[Rs, ft, 128:256]
                            yr_ = PS[BK[s_]][:, 256:320]
                            kb.I("pe", "matmul", yr_, rho, ST[Rs, ft, :], start=True, stop=False, reads=fk + [f"ST{ft}"], writes=[bkk[s_]])
                            kb.I("pe", "matmul", yr_, NB[s_][:, 128:256], UT[:, h * 64:(h + 1) * 64], start=False, stop=False, reads=[f"NB{s_}", f"UT{h}"], writes=[bkk[s_]])
                            kb.I("pe", "matmul", yr_, KA[s_][:, 128:256], vT[:, h * 64:(h + 1) * 64], start=False, stop=True, reads=[f"KA{s_}", TMK], writes=[bkk[s_]])
                        for s_, h in enumerate(heads):
                            kb.cp("act", yT[:, h * 64:(h + 1) * 64], PS[BK[s_]][:, 256:320], reads=[bkk[s_]], writes=[f"yT{h}"])
                    for ft in range(g0 // 2, (g0 + NS) // 2):
                        h1 = 2 * ft + 1
                        bnk = BK[h1 - g0]
                        cs_ = slice(ft * 128, (ft + 1) * 128)
                        kb.I("pe", "matmul", PS[bnk][:, 0:128], bepT[:, cs_], UT[:, cs_], start=True, stop=False, reads=[TMK, f"UT{h1 - 1}", f"UT{h1}"], writes=[f"pX{bnk}_0"])
                        kb.I("pe", "matmul", PS[bnk][:, 0:128], kapT[:, cs_], vT[:, cs_], start=False, stop=True, reads=[TMK], writes=[f"pX{bnk}_0"])
                        for jj in range(2):
                            rr_ = slice(64 * jj, 64 * jj + 64)
                            kb.I("dve", "scalar_tensor_tensor", ST[rr_, ft, :], ST[rr_, ft, :], gC_[rr_, ft:ft + 1], PS[bnk][rr_, 64 * jj:64 * jj + 64], ALU.mult, ALU.add,
                                 reads=[f"pX{bnk}_0", f"ST{ft}", f"w{ft}" + SL + "g"], writes=[f"ST{ft}"])
                if dbg_here:
                    kb.D(dbg["d_yT"], yT[:], reads=[f"yT{h}" for h in range(8)], writes=["o_d_yT"])
                    kb.D(dbg["d_ST"], ST[:].rearrange("p f v -> p (f v)"), reads=[f"ST{f}" for f in range(4)], writes=["o_d_ST"])
                if own is not None:
                    for nm, tile_ap, keys in (("yr", yT[:], [f"yT{h}" for h in range(8)]), ("bv", bvT, [TMK])):
                        dst = scr[f"{nm}{sd}"][own * 128:(own + 1) * 128, :]
                        if rev:
                            kb.I("pe", "matmul", PS[5][:], antiI[:], tile_ap, start=True, stop=True, reads=keys + ["antiI"], writes=["pX5_0"])
                            kb.cp("act", flpR[:], PS[5][:], reads=["pX5_0"], writes=["flpR"])
                            kb.D(dst, flpR[:], reads=["flpR"], writes=[f"scr_{nm}{sd}_{own}"])
                        else:
                            kb.D(dst, tile_ap, reads=keys, writes=[f"scr_{nm}{sd}_{own}"])
                kb.rec = streams["s5"]
                t1_, t2_ = s5t["t1"], s5t["t2"]
                for a_ in range(4):
                    zuk = f"zu{a_}" + SL
                    for r_ in range(4):
                        q_ = a_ * 4 + r_
                        kb.I("pe", "matmul", PS[6][:, r_ * 128:(r_ + 1) * 128], Bw_re[:, q_, :], zub_[:, a_, :], start=True, stop=True, reads=["s5tab", zuk], writes=["pX6_0"])
                        kb.I("pe", "matmul", PS[7][:, r_ * 128:(r_ + 1) * 128], Bw_im[:, q_, :], zub_[:, a_, :], start=True, stop=True, reads=["s5tab", zuk], writes=["pX7_0"])
                    k3 = ["pX6_0"]; k4 = ["pX7_0"]
                    qs = slice(a_ * 4, a_ * 4 + 4)
                    Er = E_re[:, qs, :].rearrange("p q t -> p (q t)"); Ei = E_im[:, qs, :].rearrange("p q t -> p (q t)")
                    Fr = F_re[:, qs, :].rearrange("p q t -> p (q t)"); Fi = F_im[:, qs, :].rearrange("p q t -> p (q t)")
                    g1_, g2_ = s5t["Gre"], s5t["Gim"]
                    kb.I("dve", "tensor_tensor", t1_[:], PS[6][:], Er, ALU.mult, reads=k3 + ["s5tab"], writes=["s5t1"])
                    kb.I("dve", "tensor_tensor", t2_[:], PS[7][:], Ei, ALU.mult, reads=k4 + ["s5tab"], writes=["s5t2"])
                    kb.I("dve", "tensor_tensor", g1_[:], PS[7][:], Er, ALU.mult, reads=k4 + ["s5tab"], writes=["s5Gre"])
                    kb.I("dve", "tensor_tensor", g2_[:], PS[6][:], Ei, ALU.mult, reads=k3 + ["s5tab"], writes=["s5Gim"])
                    kb.I("pool", "tensor_tensor", s5t["Xre"][:], t1_[:], t2_[:], ALU.subtract, reads=["s5t1", "s5t2"], writes=["s5Xre"])
                    kb.I("dve", "tensor_tensor", s5t["Xim"][:], g1_[:], g2_[:], ALU.add, reads=["s5Gre", "s5Gim"], writes=["s5Xim"])
                    if own is None:
                        kb.I("dve", "tensor_reduce", gs_re[:, qs], s5t["Xre"][:].rearrange("p (q t) -> p q t", q=4), AXX, ALU.add, reads=["s5Xre"], writes=["gs"])
                        kb.I("dve", "tensor_reduce", gs_im[:, qs], s5t["Xim"][:].rearrange("p (q t) -> p q t", q=4), AXX, ALU.add, reads=["s5Xim"], writes=["gs"])
                        continue
                    for r_ in range(4):
                        q_ = a_ * 4 + r_
                        cs_ = slice(r_ * 128, (r_ + 1) * 128)
                        kb.I("dve", "tensor_tensor_scan", s5t["Gre"][:, cs_], ones[:], s5t["Xre"][:, cs_], car_re[:, q_:q_ + 1], ALU.mult, ALU.add,
                             reads=["s5Xre", "car", "ones"], writes=["s5Gre"])
                        kb.I("dve", "tensor_tensor_scan", s5t["Gim"][:, cs_], ones[:], s5t["Xim"][:, cs_], car_im[:, q_:q_ + 1], ALU.mult, ALU.add,
                             reads=["s5Xim", "car", "ones"], writes=["s5Gim"])
                    x1_, x2_ = s5t["Xre"], s5t["Xim"]
                    kb.I("dve", "tensor_tensor", t1_[:], s5t["Gre"][:], Fr, ALU.mult, reads=["s5Gre", "s5tab"], writes=["s5t1"])
                    kb.I("dve", "tensor_tensor", t2_[:], s5t["Gim"][:], Fi, ALU.mult, reads=["s5Gim", "s5tab"], writes=["s5t2"])
                    kb.I("pool", "tensor_tensor", x1_[:], s5t["Gim"][:], Fr, ALU.mult, reads=["s5Gim", "s5tab", "s5Xre"], writes=["s5Xre"])
                    kb.I("pool", "tensor_tensor", x2_[:], s5t["Gre"][:], Fi, ALU.mult, reads=["s5Gre", "s5tab", "s5Xim"], writes=["s5Xim"])
                    kb.I("dve", "tensor_tensor", s5t["Hre"][:], t1_[:], t2_[:], ALU.subtract, reads=["s5t1", "s5t2"], writes=["s5Hre"])
                    kb.I("pool", "tensor_tensor", s5t["Him"][:], x1_[:], x2_[:], ALU.add, reads=["s5Xre", "s5Xim"], writes=["s5Him"])
                    kb.I("dve", "tensor_copy", car_re[:, qs], s5t["Hre"][:].rearrange("p (q t) -> p q t", q=4)[:, :, 127], reads=["s5Hre"], writes=["car"])
                    kb.I("dve", "tensor_copy", car_im[:, qs], s5t["Him"][:].rearrange("p (q t) -> p q t", q=4)[:, :, 127], reads=["s5Him"], writes=["car"])
                    if own is not None:
                        for r_ in range(4):
                            q_ = a_ * 4 + r_
                            cs_ = slice(r_ * 128, (r_ + 1) * 128)
                            yb = PS[6][:, r_ * 32:(r_ + 1) * 32]
                            if sd == 0:
                                kb.I("pe", "matmul", yb, zub_[:, a_, :], diagd[:, a_, r_ * 32:(r_ + 1) * 32], start=True, stop=False, reads=[zuk, "diagd"], writes=["pX6_0"])
                            kb.I("pe", "matmul", yb, s5t["Hre"][:, cs_], Cw_re[:, q_, :], start=(sd != 0), stop=False, reads=["s5Hre", "s5tab"], writes=["pX6_0"])
                            kb.I("pe", "matmul", yb, s5t["Him"][:, cs_], Cw_imn[:, q_, :], start=False, stop=True, reads=["s5Him", "s5tab"], writes=["pX6_0"])
                        kb.cp("act", flpS[:, a_ * 128:(a_ + 1) * 128], PS[6][:, 0:128], reads=["pX6_0"], writes=["flpS"])
                if own is None:
                    F127r = F_re[:, :, 127]; F127i = F_im[:, :, 127]
                    kb.I("dve", "tensor_tensor", gs_re[:], gs_re[:], car_re[:], ALU.add, reads=["gs", "car"], writes=["gs"])
                    kb.I("dve", "tensor_tensor", gs_im[:], gs_im[:], car_im[:], ALU.add, reads=["gs", "car"], writes=["gs"])
                    kb.I("dve", "tensor_tensor", gta[:], gs_re[:], F127r, ALU.mult, reads=["gs", "s5tab"], writes=["gta"])
                    kb.I("dve", "tensor_tensor", gtb[:], gs_im[:], F127i, ALU.mult, reads=["gs", "s5tab"], writes=["gtb"])
                    kb.I("dve", "tensor_tensor", car_re[:], gta[:], gtb[:], ALU.subtract, reads=["gta", "gtb"], writes=["car"])
                    kb.I("dve", "tensor_tensor", gta[:], gs_im[:], F127r, ALU.mult, reads=["gs", "s5tab", "car"], writes=["gta"])
                    kb.I("dve", "tensor_tensor", gtb[:], gs_re[:], F127i, ALU.mult, reads=["gs", "s5tab", "car"], writes=["gtb"])
                    kb.I("dve", "tensor_tensor", car_im[:], gta[:], gtb[:], ALU.add, reads=["gta", "gtb"], writes=["car"])
                if dbg_here:
                    kb.D(dbg["d_car"][:, 0:16], car_re[:], reads=["car"], writes=["o_d_car"])
                    kb.D(dbg["d_car"][:, 16:32], car_im[:], reads=["car"], writes=["o_d_car"])
                if own is not None:
                    if dbg_here:
                        kb.D(dbg["d_ys"], flpS[:], reads=["flpS"], writes=["o_d_ys"])
                    dst = scr[f"ys{sd}"][own * 128:(own + 1) * 128, :]
                    if rev:
                        kb.I("pe", "matmul", PS[6][:], antiI[:], flpS[:], start=True, stop=True, reads=["flpS", "antiI"], writes=["pX6_0"])
                        kb.cp("dve", flpS[:], PS[6][:], reads=["pX6_0"], writes=["flpS"])
                    kb.D(dst, flpS[:], reads=["flpS"], writes=[f"scr_ys{sd}_{own}"])
                kb.rec = None
                return streams

            for sd in range(2 if not upto.startswith("p0") else 0):
                if upto == "setup":
                    break
                pre = []
                if sd == 1:
                    kb.rec = pre
                    s5_setup(sd, F0)
                if upto == "s5setup":
                    break
                stk = [f"ST{f}" for f in range(4)]
                if sd == 0:
                    kb.I("pool", "memset", ST[:], 0.0, reads=stk, writes=stk)
                    kb.I("pool", "memset", car_re[:], 0.0, reads=["car"], writes=["car"])
                    kb.I("pool", "memset", car_im[:], 0.0, reads=["car"], writes=["car"])
                if sd == 0:
                    chunks = [(True, c, None) for c in range(n_ctx)] + [(False, c, c) for c in range(n_lat_chunks[0])]
                else:
                    stk = [f"ST{f}" for f in range(4)]
                    kb.D(scr["cc_in"][:, 0:256], ST[:].rearrange("p f v -> p (f v)"), reads=stk, writes=["cc_in"])
                    kb.D(scr["cc_in"][:, 256:272], car_re[:], reads=["car"], writes=["cc_in"])
                    kb.D(scr["cc_in"][:, 272:288], car_im[:], reads=["car"], writes=["cc_in"])
                    kb.I("pool", "collective_compute", "AllReduce", ALU.add, replica_groups=[[0, 1], [2, 3], [4, 5], [6, 7]],
                         ins=[scr["cc_in"]], outs=[scr["cc_out"]], reads=["cc_in"], writes=["cc_out"])
                    ccs = s5t["t1"]
                    kb.D(ccs[:, 0:288], scr["cc_out"], reads=["cc_out", "s5t1"], writes=["s5t1"])
                    kb.I("dve", "tensor_tensor", ST[:].rearrange("p f v -> p (f v)"), ccs[:, 0:256], ST[:].rearrange("p f v -> p (f v)"), ALU.subtract,
                         reads=["s5t1"] + stk, writes=stk)
                    kb.I("dve", "tensor_tensor", car_re[:], ccs[:, 256:272], car_re[:], ALU.subtract, reads=["s5t1", "car"], writes=["car"])
                    kb.I("dve", "tensor_tensor", car_im[:], ccs[:, 272:288], car_im[:], ALU.subtract, reads=["s5t1", "car"], writes=["car"])
                    chunks = [(False, c, 31 - c) for c in range(16, 16 + n_lat_chunks[1])]
                    kb.rec = None
                prev = None
                pend = [pre] if pre else []
                for k_, (is_ctx, c, own) in enumerate(chunks):
                    nxt = chunks[k_ + 1][:2] if k_ + 1 < len(chunks) else None
                    cur = process_chunk(sd, is_ctx, c, own, k_ % 2, nxt=(nxt if sd == 0 else None), first=(k_ == 0), reuse=(sd == 1))
                    pend.append(cur["front"])
                    if prev is not None:
                        pend += [prev["rwkv"], prev["s5"]]
                    prev = cur
                    if len(pend) >= SCHED_WINDOW_STREAMS:
                        kb.merge(*pend)
                        pend = []
                if prev is not None:
                    pend += [prev["rwkv"], prev["s5"]]
                kb.merge(*pend)
            kb.I("dve", "tensor_copy", ssum[:, 0:1], ssum[:, 0:1], reads=list(kb.lastw.keys()), writes=list(kb.lastw.keys()) + ["fence1"])
        FENCE = ["fence1"]
        wscope.close()

        if do_tail:
            with contextlib.ExitStack() as tl:
                wob = sb(tl, "wob", [128, 8, D], BF16)
                glb = sb(tl, "glb", [128, 4, 512], BF16); g2b = sb(tl, "g2b", [128, 512])
                stg = [sb(tl, f"stg{i}", [128, 1024]) for i in range(2)]
                jobs = [("w_out", wob, kt, 0, D) for kt in range(8)] + [("gluw", glb, kt, 0, 512) for kt in range(4)]
                for i, (nm, dst, kt, c0, ncol) in enumerate(jobs):
                    s_ = stg[i % 2]; sk = f"stg{i % 2}"
                    kb.D(s_[:, 0:ncol], din[nm][kt * 128:(kt + 1) * 128, c0:c0 + ncol], reads=FENCE, writes=[sk])
                    kb.cp(("dve", "act", "pool")[i % 3], dst[:, kt, c0:c0 + ncol], s_[:, 0:ncol], reads=[sk], writes=["tw"])
                kb.D(g2b[0:96, :], din["g2"], reads=FENCE, writes=["tw2"])
                rows = {}
                for nm in ("lnw", "lnb", "glub"):
                    rows[nm] = sb(tl, "row_" + nm, [128, IN_SHAPES[nm][1]])
                    kb.D(rows[nm][:], din[nm].partition_broadcast(128), reads=FENCE, writes=["rows"])
                gmix = sb(tl, "gmix", [128, D])

                def dbl(name, shape, dt=F32):
                    return [sb(tl, f"{name}_{i_}", shape, dt) for i_ in range(2)]
                x1 = dbl("x1", [128, D])
                kb.D(gmix[:], scr["mod"][:, 2 * D:3 * D], reads=FENCE + ["scr_mod"], writes=["modt"])
                tx = dbl("tx", [128, D]); yr_a = dbl("yr_a", [128, 512]); ys_a = dbl("ys_a", [128, 512]); bv_a = dbl("bv_a", [128, 512])
                yr_b = dbl("yr_b", [128, 512]); ys_b = dbl("ys_b", [128, 512]); bv_b = dbl("bv_b", [128, 512])
                sgt = dbl("sgt", [128, 128]); mix = dbl("mix", [128, D]); st8 = dbl("st8", [128, 8]); st8b = dbl("st8b", [128, 8])
                gt = dbl("gt", [128, 512]); gt2 = dbl("gt2", [128, 512]); zT = dbl("zT", [128, 4, 128], BF16); mixT = dbl("mixT", [128, 8, 128], BF16)
                TA = (tx, yr_a, ys_a, bv_a, yr_b, ys_b, bv_b, sgt, mix, st8, st8b, gt, gt2, zT, mixT, x1)
                recA = []
                kb.rec = recA
                for oc in range(NOWN):
                    sl_ = oc % 2
                    kb.ksuf = f"#{sl_}"
                    (tx, yr_a, ys_a, bv_a, yr_b, ys_b, bv_b, sgt, mix, st8, st8b, gt, gt2, zT, mixT, x1) = (t_[sl_] for t_ in TA)
                    rsl = slice(oc * 128, (oc + 1) * 128)
                    kb.D(tx[:], din["x"][rsl, :], reads=FENCE, writes=["tx"])
                    for t_, nm in ((yr_a, "yr0"), (yr_b, "yr1"), (ys_a, "ys0"), (ys_b, "ys1"), (bv_a, "bv0"), (bv_b, "bv1")):
                        kb.D(t_[:], scr[nm][rsl, :], reads=[f"scr_{nm}_{oc}"] + FENCE, writes=["t_" + nm])
                    kb.D(sgt[0:96, :], scr["sg"][oc * 128:oc * 128 + 96, :], reads=[f"scr_sg{oc}"] + FENCE, writes=["sgt"])
                    kb.I("dve", "tensor_tensor", yr_a[:], yr_a[:], yr_b[:], ALU.add, reads=["t_yr0", "t_yr1"], writes=["t_yr0"])
                    y3 = yr_a[:].rearrange("p (h n) -> p h n", n=64)
                    kb.I("dve", "tensor_reduce", st8[:], y3, AXX, ALU.add, reads=["t_yr0"], writes=["st8"])
                    kb.ts1("dve", st8[:], st8[:], 1.0 / 64, ALU.mult, reads=["st8"], writes=["st8"])
                    kb.I("dve", "tensor_tensor", y3, y3, st8[:].unsqueeze(2).to_broadcast([128, 8, 64]), ALU.subtract, reads=["st8", "t_yr0"], writes=["t_yr0"])
                    kb.I("pool", "tensor_tensor", gt2[:], yr_a[:], yr_a[:], ALU.mult, reads=["t_yr0"], writes=["gt2"])
                    kb.I("dve", "tensor_reduce", st8b[:], gt2[:].rearrange("p (h n) -> p h n", n=64), AXX, ALU.add, reads=["gt2"], writes=["st8b"])
                    kb.I("dve", "tensor_scalar", st8b[:], st8b[:], 1.0 / 64, 64e-5, ALU.mult, ALU.add, reads=["st8b"], writes=["st8b"])
                    kb.I("act", "activation", st8b[:], st8b[:], AF.Sqrt, reads=["st8b"], writes=["st8b"])
                    kb.I("dve", "reciprocal", st8b[:], st8b[:], reads=["st8b"], writes=["st8b"])
                    kb.I("dve", "tensor_tensor", y3, y3, st8b[:].unsqueeze(2).to_broadcast([128, 8, 64]), ALU.mult, reads=["st8b", "t_yr0"], writes=["t_yr0"])
                    kb.I("pool", "tensor_tensor", yr_a[:], yr_a[:], rows["lnw"][:], ALU.mult, reads=["rows", "t_yr0"], writes=["t_yr0"])
                    kb.I("pool", "tensor_tensor", yr_a[:], yr_a[:], rows["lnb"][:], ALU.add, reads=["rows", "t_yr0"], writes=["t_yr0"])
                    kb.I("dve", "tensor_tensor", bv_a[:], bv_a[:], bv_b[:], ALU.add, reads=["t_bv0", "t_bv1"], writes=["t_bv0"])
                    kb.I("dve", "tensor_tensor", yr_a[:], yr_a[:], bv_a[:], ALU.add, reads=["t_bv0", "t_yr0"], writes=["t_yr0"])
                    kb.I("pe", "matmul", PS[0][:], sgt[0:96, :], g2b[0:96, :], start=True, stop=True, reads=["sgt", "tw2"], writes=bk(0))
                    kb.I("dve", "tensor_tensor", mix[:, 0:512], yr_a[:], PS[0][:], ALU.mult, reads=bk(0) + ["t_yr0"], writes=["mixA"])
                    kb.I("dve", "tensor_tensor", ys_a[:], ys_a[:], ys_b[:], ALU.add, reads=["t_ys0", "t_ys1"], writes=["t_ys0"])
                    kb.I("pool", "tensor_tensor", gt[:], ys_a[:], ys_a[:], ALU.mult, reads=["t_ys0"], writes=["gt"])
                    kb.I("dve", "tensor_scalar", gt[:], gt[:], 0.044715, 1.0, ALU.mult, ALU.add, reads=["gt"], writes=["gt"])
                    kb.I("dve", "tensor_tensor", gt[:], gt[:], ys_a[:], ALU.mult, reads=["gt", "t_ys0"], writes=["gt"])
                    kb.I("act", "activation", gt[:], gt[:], AF.Tanh, scale=0.7978845608028654, reads=["gt"], writes=["gt"])
                    kb.I("dve", "tensor_scalar", gt[:], gt[:], 0.5, 0.5, ALU.mult, ALU.add, reads=["gt"], writes=["gt"])
                    kb.I("dve", "tensor_tensor", ys_a[:], ys_a[:], gt[:], ALU.mult, reads=["gt", "t_ys0"], writes=["t_ys0"])
                    for j in range(4):
                        kb.I("pe", "transpose", PS[1][:, j * 128:(j + 1) * 128], ys_a[:, j * 128:(j + 1) * 128], ident[:], reads=["t_ys0", "ident"], writes=[f"pX1_{j}"])
                    kb.cp("act", zT[:].rearrange("p j t -> p (j t)"), PS[1][:], reads=bk(1), writes=["zT"])
                    for j in range(4):
                        kb.I("pe", "matmul", PS[2][:], zT[:, j, :], glb[:, j, :], start=(j == 0), stop=(j == 3), reads=["zT", "tw"], writes=bk(2))
                    kb.I("dve", "tensor_tensor", gt[:], PS[2][:], rows["glub"][:], ALU.add, reads=bk(2) + ["rows", "gt"], writes=["gt"])
                    kb.I("act", "activation", gt[:], gt[:], AF.Sigmoid, reads=["gt"], writes=["gt"])
                    kb.I("dve", "tensor_tensor", mix[:, 512:1024], ys_a[:], gt[:], ALU.mult, reads=["gt", "t_ys0"], writes=["mixB"])
                    for half in range(2):
                        for j in range(4):
                            kt = half * 4 + j
                            kb.I("pe", "transpose", PS[3][:, j * 128:(j + 1) * 128], mix[:, kt * 128:(kt + 1) * 128], ident[:], reads=["mixA", "mixB", "ident"], writes=[f"pX3_{j}"])
                        kb.cp("act" if half else "dve", mixT[:, half * 4:half * 4 + 4, :].rearrange("p j t -> p (j t)"), PS[3][:], reads=bk(3), writes=["mixT"])
                    for nh in range(2):
                        ns = slice(nh * 512, (nh + 1) * 512)
                        for kt in range(8):
                            kb.I("pe", "matmul", PS[4 + nh][:], mixT[:, kt, :], wob[:, kt, ns], start=(kt == 0), stop=(kt == 7), reads=["mixT", "tw"], writes=bk(4 + nh))
                        kb.I("dve", "tensor_tensor", x1[:, ns], PS[4 + nh][:], gmix[:, ns], ALU.mult, reads=bk(4 + nh) + ["modt"], writes=["x1"])
                        kb.I("pool", "tensor_tensor", x1[:, ns], x1[:, ns], tx[:, ns], ALU.add, reads=["x1", "tx"], writes=["x1"])
                    if debug and oc == 0:
                        kb.D(dbg["d_x1"], x1[:], reads=["x1"], writes=["o_d_x1"])
                    kb.D(scr["x1"][rsl, :], x1[:], reads=["x1"], writes=[f"scr_x1_{oc}"])
                kb.rec = None
                kb.ksuf = None
                kb.merge(recA)
                st8 = TA[9][0]
                kb.I("dve", "tensor_copy", st8[:, 0:1], st8[:, 0:1], reads=list(kb.lastw.keys()), writes=list(kb.lastw.keys()) + ["fence2"])
            FENCE = ["fence2"]
            with contextlib.ExitStack() as tl:
                w1b = sb(tl, "w1b", [128, 8, DFF], BF16); w3b = sb(tl, "w3b", [128, 8, DFF], BF16)
                w2b = sb(tl, "w2b", [128, 22, D], BF16)
                stg = [sb(tl, f"stgb{i}", [128, 1024]) for i in range(2)]
                jobs = []
                for nm, dst in (("w1", w1b), ("w3", w3b)):
                    for kt in range(8):
                        for c0 in (0, 1024, 2048):
                            jobs.append((nm, dst, kt, c0, min(1024, DFF - c0)))
                jobs += [("w2f", w2b, kt, 0, D) for kt in range(22)]
                for i, (nm, dst, kt, c0, ncol) in enumerate(jobs):
                    s_ = stg[i % 2]; sk = f"stgb{i % 2}"
                    kb.D(s_[:, 0:ncol], din[nm][kt * 128:(kt + 1) * 128, c0:c0 + ncol], reads=FENCE, writes=[sk])
                    kb.cp(("dve", "act", "pool")[i % 3], dst[:, kt, c0:c0 + ncol], s_[:, 0:ncol], reads=[sk], writes=["tw"])
                rows = {"gf": sb(tl, "row_gf", [128, D])}
                kb.D(rows["gf"][:], din["gf"].partition_broadcast(128), reads=FENCE, writes=["rows"])
                A2 = sb(tl, "A2", [128, D]); sffn = sb(tl, "sffn", [128, D]); gffn = sb(tl, "gffn", [128, D])
                def dblb(name, shape, dt=F32):
                    return [sb(tl, f"{name}_{i_}", shape, dt) for i_ in range(2)]
                hh2 = dblb("hh", [128, D]); x12 = dblb("x1b", [128, D]); outt2 = dblb("outt", [128, D])
                hh = hh2[0]
                kb.D(A2[:], din["g2n"].partition_broadcast(128), reads=FENCE, writes=["A2"])
                kb.D(hh[:], scr["mod"][:, 4 * D:5 * D], reads=FENCE + ["scr_mod"], writes=["hh#0"])
                kb.D(sffn[:], scr["mod"][:, 3 * D:4 * D], reads=FENCE + ["scr_mod"], writes=["modt"])
                kb.D(gffn[:], scr["mod"][:, 5 * D:6 * D], reads=FENCE + ["scr_mod"], writes=["modt"])
                kb.I("dve", "scalar_tensor_tensor", A2[:], hh[:], 1.0, A2[:], ALU.add, ALU.mult, reads=["A2", "hh#0"], writes=["A2"])
                hhT2 = dblb("hhT", [128, 8, 128], BF16)
                actT2 = dblb("actT", [128, 22, 128], BF16); s12 = dblb("s1", [128, 256]); ss22 = dblb("ss2", [128, 2]); rs22 = dblb("rs2", [128, 2])
                print("SBUF bytes remaining in tail B scope:", nc.sbuf_bytes_remaining)
                recB = []
                kb.rec = recB
                for oc in range(NOWN):
                    sl_ = oc % 2
                    kb.ksuf = f"#{sl_}"
                    hh, x1, outt, hhT, actT, s1, ss2, rs2 = (t_[sl_] for t_ in (hh2, x12, outt2, hhT2, actT2, s12, ss22, rs22))
                    rsl = slice(oc * 128, (oc + 1) * 128)
                    kb.D(x1[:], scr["x1"][rsl, :], reads=[f"scr_x1_{oc}"], writes=["x1"])
                    kb.I("pool", "memset", ss2[:], 0.0, reads=FENCE, writes=["ss2"])
                    kb.I("act", "activation", hh[:], x1[:], AF.Square, accum_out=ss2[:, 0:1], reads=["x1", "A2"], writes=["hh", "ss2"])
                    kb.I("dve", "tensor_scalar", rs2[:, 0:1], ss2[:, 0:1], 1.0 / D, 1e-6, ALU.mult, ALU.add, reads=["ss2"], writes=["rs2"])
                    kb.I("act", "activation", rs2[:, 0:1], rs2[:, 0:1], AF.Sqrt, reads=["rs2"], writes=["rs2"])
                    kb.I("dve", "reciprocal", rs2[:, 0:1], rs2[:, 0:1], reads=["rs2"], writes=["rs2"])
                    kb.I("dve", "scalar_tensor_tensor", hh[:], x1[:], rs2[:, 0:1], A2[:], ALU.mult, ALU.mult, reads=["x1", "rs2", "A2", "hh"], writes=["hh"])
                    kb.I("pool", "tensor_tensor", hh[:], hh[:], sffn[:], ALU.add, reads=["hh", "modt"], writes=["hh"])
                    for half in range(2):
                        for j in range(4):
                            kt = half * 4 + j
                            kb.I("pe", "transpose", PS[3][:, j * 128:(j + 1) * 128], hh[:, kt * 128:(kt + 1) * 128], ident[:], reads=["hh", "ident"], writes=[f"pX3_{j}"])
                        kb.cp("act" if half else "dve", hhT[:, half * 4:half * 4 + 4, :].rearrange("p j t -> p (j t)"), PS[3][:], reads=bk(3), writes=["hhT"])
                    for ftf in range(22):
                        sl = ftf % 2
                        fs = slice(ftf * 128, (ftf + 1) * 128)
                        bq = (0, 1, 2)[ftf % 3]
                        pa = PS[bq][:, 0:128]; pb = PS[bq][:, 128:256]
                        s1_ = s1[:, (ftf % 2) * 128:(ftf % 2) * 128 + 128]
                        for kt in range(8):
                            kb.I("pe", "matmul", pa, w1b[:, kt, fs], hhT[:, kt, :], start=(kt == 0), stop=(kt == 7), reads=["hhT", "tw"], writes=[f"pX{bq}_0"])
                        for kt in range(8):
                            kb.I("pe", "matmul", pb, w3b[:, kt, fs], hhT[:, kt, :], start=(kt == 0), stop=(kt == 7), reads=["hhT", "tw"], writes=[f"pX{bq}_0"])
                        kb.I("act", "activation", s1_, pa, AF.Silu, reads=[f"pX{bq}_0"], writes=[f"s1_{ftf % 2}"])
                        kb.I("dve", "tensor_tensor", actT[:, ftf, :], s1_, pb, ALU.mult, reads=[f"pX{bq}_0", f"s1_{ftf % 2}"], writes=[f"actT{ftf}"])
                    for nh in range(2):
                        ns = slice(nh * 512, (nh + 1) * 512)
                        bd = 4 + 2 * sl_ + nh
                        for ftf in range(22):
                            kb.I("pe", "matmul", PS[bd][:], actT[:, ftf, :], w2b[:, ftf, ns], start=(ftf == 0), stop=(ftf == 21), reads=[f"actT{ftf}", "tw"], writes=bk(bd))
                        kb.I("dve", "tensor_tensor", outt[:, ns], PS[bd][:], gffn[:, ns], ALU.mult, reads=bk(bd) + ["modt"], writes=["outt"])
                        kb.I("pool", "tensor_tensor", outt[:, ns], outt[:, ns], x1[:, ns], ALU.add, reads=["outt", "x1"], writes=["outt"])
                    kb.I("act", "activation", hh[:], outt[:], AF.Square, accum_out=ss2[:, 1:2], reads=["outt", "hh"], writes=["hh", "ss2"])
                    kb.I("dve", "tensor_scalar", rs2[:, 1:2], ss2[:, 1:2], 1.0 / D, 1e-6, ALU.mult, ALU.add, reads=["ss2"], writes=["rs2"])
                    kb.I("act", "activation", rs2[:, 1:2], rs2[:, 1:2], AF.Sqrt, reads=["rs2"], writes=["rs2"])
                    kb.I("dve", "reciprocal", rs2[:, 1:2], rs2[:, 1:2], reads=["rs2"], writes=["rs2"])
                    kb.I("dve", "scalar_tensor_tensor", outt[:], outt[:], rs2[:, 1:2], rows["gf"][:], ALU.mult, ALU.mult, reads=["outt", "rs2", "rows"], writes=["outt"])
                    kb.D(out_d[rsl, :], outt[:], reads=["outt"], writes=[f"o_out{oc}"])
                kb.rec = None
                kb.ksuf = None
                kb.merge(recB)
                kb.emit(final_keys=[k for k in kb.lastw if k.startswith("o_")])
        else:
            kb.emit(final_keys=[k for k in kb.lastw if k.startswith("o_")] + FENCE)
    return nc


def make_in_maps(inp):
    f = np.float32
    ident = np.eye(128, dtype=f)
    antiI = np.ascontiguousarray(ident[::-1])
    strict = np.triu(np.ones((128, 128), f), 1)
    incl = np.triu(np.ones((128, 128), f), 0)
    consts = {
        "ident": ident, "antiI": antiI,
        "mask_b": np.concatenate([strict, -incl], axis=1), "mask_k": np.concatenate([strict, incl], axis=1),
        "maskT": np.ascontiguousarray(strict.T), "bones": np.kron(np.eye(2, dtype=f), np.ones((64, 64), f)),
    }
    maps = []
    for core in range(8):
        b, hf = core // 2, core % 2
        dsel = [1, 0] if hf else [0, 1]
        x = inp["x"][b]; ctx = inp["ctx"][b]
        conv = inp["rwkv_conv"][0]
        w_in = inp["w_in"][0]
        if hf:
            x = x[::-1]; ctx = ctx[::-1]; conv = conv[::-1, ::-1]
            perm = np.arange(2272)
            perm[1536:1568], perm[1568:1600] = np.arange(1568, 1600), np.arange(1536, 1568)
            perm[1600:1632], perm[1632:1664] = np.arange(1632, 1664), np.arange(1600, 1632)
            w_in = w_in[:, perm]
        m = {
            "x": x, "ctx": ctx, "cc": np.stack([inp["c"][b], inp["c_ctx"]]),
            "mod_w": inp["mod_w"][0], "mod_b": inp["mod_b"][0][None], "g1": inp["norm1_g"][0][None],
            "g2n": inp["norm2_g"][0][None], "gf": inp["final_g"][None], "w_in": w_in, "w_out": inp["w_out"][0],
            "conv": conv.reshape(9, 1536), "w0": inp["rwkv_w0"][0][dsel], "w2": inp["rwkv_w2"][0][dsel],
            "a0": inp["rwkv_a0"][0][dsel], "a2": inp["rwkv_a2"][0][dsel], "g2": inp["rwkv_g2"][0],
            "kkv": inp["rwkv_kk"][0], "kav": inp["rwkv_ka"][0], "rkv": inp["rwkv_rk"][0].reshape(512),
            "lnw": inp["rwkv_ln_w"][0][None], "lnb": inp["rwkv_ln_b"][0][None],
            "lam_re": inp["s5_lam_re"][0][dsel], "lam_im": inp["s5_lam_im"][0][dsel], "lstep": inp["s5_log_step"][0][dsel],
            "b_re": inp["s5_b_re"][0], "b_im": inp["s5_b_im"][0], "c_re": inp["s5_c_re"][0], "c_im": inp["s5_c_im"][0],
            "s5d": inp["s5_d"][0], "gluw": inp["s5_glu_w"][0], "glub": inp["s5_glu_b"][0][None],
            "w1": inp["ffn_w1"][0], "w3": inp["ffn_w3"][0], "w2f": inp["ffn_w2"][0],
        }
        m.update(consts)
        maps.append({k: np.ascontiguousarray(np.asarray(v, dtype=f)).reshape(IN_SHAPES[k]) for k, v in m.items()})
    return maps


def kernel(**inputs):
    inp = {k: np.asarray(v) for k, v in inputs.items()}
    nc = build_nc()
    maps = make_in_maps(inp)
    res = run_bass_kernel_spmd(nc, maps, core_ids=list(range(8)))
    out = np.zeros((4, T_LAT, D), np.float32)
    for core in range(8):
        b, hf = core // 2, core % 2
        o = np.asarray(res.results[core]["out"], dtype=np.float32)
        if hf:
            out[b, OWN:] = o[::-1]
        else:
            out[b, :OWN] = o
    return out
```

```python
import contextlib
import numpy as np
import concourse.bass as bass
import concourse.mybir as mybir
from concourse.bass_utils import run_bass_kernel_spmd

F32 = mybir.dt.float32
BF16 = mybir.dt.bfloat16
ALU = mybir.AluOpType
AF = mybir.ActivationFunctionType
AXX = mybir.AxisListType.X

SEM_CAP = 16000
N_DMA_SEM = 24
SCHED_WINDOW_STREAMS = 1000
SAME_ENG_WAIT = True

T_LAT, T_CTX, D, DFF = 4096, 256, 1024, 2816
OWN = 2048
NOWN = OWN // 128
PI = float(np.pi)
KDEC = 0.6065306597126334


class KB:
    ENGS = ("pe", "dve", "act", "pool", "sp")

    def __init__(self, nc):
        self.nc = nc
        self.ops = {e: [] for e in self.ENGS}
        self.lastw = {}
        self.readers = {}
        self.ndma = 0
        self.rr = 0

    @staticmethod
    def _norm(reads, writes):
        r2 = [k for k in reads if not k.startswith("pX")]
        w2 = [k for k in writes if not k.startswith("pX")]
        banks = {"bank" + k[2:].split("_")[0] for k in list(reads) + list(writes) if k.startswith("pX")}
        return r2, w2 + sorted(banks)

    def _deps(self, me, reads, writes):
        reads, writes = self._norm(reads, writes)
        deps = set()
        for k in reads:
            w = self.lastw.get(k)
            if w is not None:
                deps.add(w)
        for k in writes:
            w = self.lastw.get(k)
            if w is not None:
                deps.add(w)
            for r in self.readers.get(k, ()):
                deps.add(r)
        deps.discard(me)
        for k in reads:
            self.readers.setdefault(k, []).append(me)
        for k in writes:
            self.lastw[k] = me
            self.readers[k] = []
        return deps

    mute = False
    rec = None
    ksuf = None
    KGLOBAL = ("pX", "scr_", "o_", "fence", "rows", "tw", "modt", "ident", "A2")

    def _sfx(self, keys):
        if not self.ksuf:
            return list(keys)
        return [k if k.startswith(self.KGLOBAL) else k + self.ksuf for k in keys]

    @staticmethod
    def _est(op):
        def nfree(ap):
            n = 1
            for d in ap.shape[1:]:
                n *= d
            return n
        if op[0] == "D":
            ap = op[1]
            nbytes = nfree(ap) * ap.shape[0] * (2 if ap.dtype == BF16 else 4)
            return "sp", 0.08, 2.2 + nbytes / 1.0e5
        eng, meth, args = op[1], op[2], op[3]
        if meth == "collective_compute":
            return eng, 0.5, 30.0
        n = nfree(args[0])
        if eng == "pe":
            f32 = args[1].dtype == F32
            d = 0.09 + n * (0.0017 if f32 else 0.00045)
        elif eng == "dve":
            d = 0.25 + n * 0.00104 * (6.0 if meth == "reciprocal" else 1.0)
        elif eng == "act":
            d = 0.2 + n * 0.00104
        else:
            d = 0.45 + n * 0.0026
        return eng, d, d

    def merge(self, *streams):
        ops = [op for st_ in streams for op in st_]
        n = len(ops)
        lastw, readers = {}, {}
        preds = [set() for _ in range(n)]
        for i, op in enumerate(ops):
            r_, w_ = (op[5], op[6]) if op[0] == "I" else (op[4], op[5])
            r_, w_ = self._norm(r_, w_)
            for k in r_:
                if k in lastw:
                    preds[i].add(lastw[k])
            for k in w_:
                if k in lastw:
                    preds[i].add(lastw[k])
                preds[i].update(readers.get(k, ()))
            preds[i].discard(i)
            for k in r_:
                readers.setdefault(k, []).append(i)
            for k in w_:
                lastw[k] = i
                readers[k] = []
        succs = [[] for _ in range(n)]
        for i in range(n):
            for p in preds[i]:
                succs[p].append(i)
        est = [self._est(op) for op in ops]
        cp = [0.0] * n
        for i in range(n - 1, -1, -1):
            cp[i] = est[i][2] + max((cp[j] for j in succs[i]), default=0.0)
        npred = [len(p) for p in preds]
        ready = [i for i in range(n) if npred[i] == 0]
        fin = [0.0] * n
        eng_free = {e: 0.0 for e in self.ENGS}
        LAT = 1.0
        while ready:
            best, bkey = None, None
            for i in ready:
                e = est[i][0]
                t = eng_free[e]
                for p in preds[i]:
                    tp = fin[p] + (0.05 if est[p][0] == e else LAT)
                    if tp > t:
                        t = tp
                key = (t - 0.02 * cp[i], i)
                if bkey is None or key < bkey:
                    best, bkey, bt = i, key, t
            i = best
            ready.remove(i)
            e = est[i][0]
            eng_free[e] = bt + est[i][1]
            fin[i] = bt + est[i][2]
            op = ops[i]
            if op[0] == "I":
                self.I(op[1], op[2], *op[3], reads=op[5], writes=op[6], **op[4])
            else:
                self.D(op[1], op[2], reads=op[4], writes=op[5], **op[3])
            for j in succs[i]:
                npred[j] -= 1
                if npred[j] == 0:
                    ready.append(j)

    def I(self, eng, meth, *args, reads=(), writes=(), **kw):
        if self.mute:
            return
        if self.rec is not None:
            self.rec.append(("I", eng, meth, args, kw, self._sfx(reads), self._sfx(writes)))
            return
        idx = len(self.ops[eng])
        deps = self._deps((eng, idx), list(reads), list(writes))
        self.ops[eng].append(((meth, args, kw), deps, None))

    def D(self, out, in_, reads=(), writes=(), **kw):
        if self.mute:
            return
        if self.rec is not None:
            self.rec.append(("D", out, in_, dict(kw), self._sfx(reads), self._sfx(writes)))
            return
        k = self.ndma
        self.ndma += 1
        deps = self._deps(("dma", k), list(reads), list(writes))
        kw = dict(kw)
        kw["out"] = out
        kw["in_"] = in_
        self.ops["sp"].append((("dma_start", (), kw), deps, k))

    def cp(self, eng, out, in_, reads=(), writes=()):
        self.I(eng, "copy" if eng == "act" else "tensor_copy", out, in_, reads=reads, writes=writes)

    def ts1(self, eng, out, in0, s, op, reads=(), writes=()):
        self.I(eng, "tensor_scalar", out, in0, s, 0.0, op, ALU.add, reads=reads, writes=writes)

    def ew(self):
        self.rr += 1
        return "dve" if (self.rr % 3) else "pool"

    def ev(self):
        self.rr += 1
        return "dve" if (self.rr % 2) else "act"

    def emit(self, final_keys=()):
        nc = self.nc
        me = ("sp", len(self.ops["sp"]))
        deps = self._deps(me, list(final_keys), [])
        self.ops["sp"].append((None, deps, None))
        nsem = {e: (len(self.ops[e]) + SEM_CAP - 1) // SEM_CAP + 1 for e in self.ENGS}
        with contextlib.ExitStack() as st:
            sems = {e: [st.enter_context(nc.semaphore(f"s_{e}{i}")) for i in range(nsem[e])]
                    for e in self.ENGS}
            dsems = [st.enter_context(nc.semaphore(f"s_dma{i}")) for i in range(N_DMA_SEM)]
            block = st.enter_context(nc.Block())

            def waitspec(p):
                if p[0] == "dma":
                    k = p[1]
                    return ("d", k % N_DMA_SEM), dsems[k % N_DMA_SEM], 16 * (k // N_DMA_SEM + 1)
                e, i = p
                return (e, i // SEM_CAP), sems[e][i // SEM_CAP], i % SEM_CAP + 1

            def run(ename, eobj):
                waited = {}
                for idx, (fn, deps, dk) in enumerate(self.ops[ename]):
                    specs = []
                    for p in deps:
                        if p[0] == ename and (ename == "pe" or not SAME_ENG_WAIT):
                            continue
                        specs.append(waitspec(p))
                    if dk is not None and dk >= N_DMA_SEM:
                        specs.append((("d", dk % N_DMA_SEM), dsems[dk % N_DMA_SEM],
                                      16 * (dk // N_DMA_SEM)))
                    for sid, sem, val in specs:
                        if waited.get(sid, 0) >= val:
                            continue
                        waited[sid] = val
                        eobj.wait_ge(sem, val)
                    if fn is None:
                        continue
                    meth, args, kw = fn
                    ins = getattr(eobj, meth)(*args, **kw)
                    if dk is not None:
                        ins.then_inc(dsems[dk % N_DMA_SEM], 16)
                    else:
                        ins.then_inc(sems[ename][idx // SEM_CAP], 1)

            @block.tensor
            def _(e):
                run("pe", e)

            @block.vector
            def _(e):
                run("dve", e)

            @block.scalar
            def _(e):
                run("act", e)

            @block.gpsimd
            def _(e):
                run("pool", e)

            @block.sync
            def _(e):
                run("sp", e)


IN_SHAPES = {
    "x": [T_LAT, D], "ctx": [T_CTX, D], "cc": [2, D], "mod_w": [D, 6 * D], "mod_b": [1, 6 * D],
    "g1": [1, D], "g2n": [1, D], "gf": [1, D], "w_in": [D, 2272], "w_out": [D, D],
    "conv": [9, 1536], "w0": [2, 512], "w2": [2, 32, 512], "a0": [2, 512], "a2": [2, 32, 512],
    "g2": [96, 512], "kkv": [512], "kav": [512], "rkv": [512], "lnw": [1, 512], "lnb": [1, 512],
    "lam_re": [2, 32, 64], "lam_im": [2, 32, 64], "lstep": [2, 32],
    "b_re": [32, 64, 16], "b_im": [32, 64, 16], "c_re": [32, 16, 64], "c_im": [32, 16, 64],
    "s5d": [512], "gluw": [512, 512], "glub": [1, 512],
    "w1": [D, DFF], "w3": [D, DFF], "w2f": [DFF, D],
    "ident": [128, 128], "antiI": [128, 128], "mask_b": [128, 256], "mask_k": [128, 256],
    "maskT": [128, 128], "bones": [128, 128],
}

DBG_SHAPES = {"d_mod": [128, 6 * D], "d_AB": [128, 32], "d_z": [128, 3 * 256], "d_zu": [128, 512],
              "d_rkvc": [128, 3 * 128], "d_yT": [128, 512], "d_ST": [128, 256], "d_ys": [128, 512],
              "d_x1": [128, D], "d_car": [128, 32], "d_F": [128, 2048], "d_Bw": [128, 2048]}


def build_nc(n_lat_chunks=(16, 16), do_tail=True, debug=False, upto="full", n_ctx=2):
    nc = bass.Bass("TRN2", target_bir_lowering=False)
    din = {k: nc.dram_tensor(k, s, F32, kind="ExternalInput").ap() for k, s in IN_SHAPES.items()}
    out_d = nc.dram_tensor("out", [OWN, D], F32, kind="ExternalOutput").ap()
    scr = {}
    for nm in ("yr0", "yr1", "ys0", "ys1", "bv0", "bv1"):
        scr[nm] = nc.dram_tensor("scr_" + nm, [OWN, 512], F32, kind="Internal").ap()
    scr["sg"] = nc.dram_tensor("scr_sg", [NOWN * 128, 128], F32, kind="Internal").ap()
    scr["x1"] = nc.dram_tensor("scr_x1", [OWN, D], F32, kind="Internal").ap()
    scr["mod"] = nc.dram_tensor("scr_mod", [128, 6 * D], F32, kind="Internal").ap()
    scr["dg"] = nc.dram_tensor("scr_dg", [12, 128, 9 * 128], BF16, kind="Internal").ap()
    scr["fr"] = nc.dram_tensor("scr_fr", [NOWN * 4 * 128, 512], F32, kind="Internal").ap()
    scr["fz"] = nc.dram_tensor("scr_fz", [NOWN * 128, 128], F32, kind="Internal").ap()
    scr["fu"] = nc.dram_tensor("scr_fu", [NOWN * 128, 512], BF16, kind="Internal").ap()
    scr["cc_in"] = nc.dram_tensor("scr_cc_in", [128, 288], F32, kind="Internal").ap()
    scr["cc_out"] = nc.dram_tensor("scr_cc_out", [128, 288], F32, kind="Internal").ap()
    dbg = {}
    if debug:
        for nm, shp in DBG_SHAPES.items():
            dbg[nm] = nc.dram_tensor(nm, shp, F32, kind="ExternalOutput").ap()
    kb = KB(nc)
    cnt = [0]

    with contextlib.ExitStack() as g:
        def sb(st, name, shape, dt=F32):
            cnt[0] += 1
            return st.enter_context(nc.sbuf_tensor(f"sb{cnt[0]}_{name}", shape, dt))

        ident = sb(g, "ident", [128, 128]); antiI = sb(g, "antiI", [128, 128])
        mask_b = sb(g, "mask_b", [128, 256]); mask_k = sb(g, "mask_k", [128, 256])
        maskT = sb(g, "maskT", [128, 128]); bones = sb(g, "bones", [128, 128])
        ones = sb(g, "ones", [128, 128])
        for nm, t in (("ident", ident), ("antiI", antiI), ("mask_b", mask_b), ("mask_k", mask_k),
                      ("maskT", maskT), ("bones", bones)):
            kb.D(t[:], din[nm], writes=[nm])
        kb.I("pool", "memset", ones[:], 1.0, writes=["ones"])
        AB = sb(g, "AB", [128, 4, 8])
        cnt[0] += 1
        PS = [g.enter_context(nc.psum_tensor(f"psbank{i}", [128, 512], F32)) for i in range(8)]

        def bk(i):
            return [f"pX{i}_{j}" for j in range(4)]

        wscope = contextlib.ExitStack()
        win = sb(wscope, "win", [128, 8, 2272], BF16)
        car_re = sb(wscope, "car_re", [128, 16]); car_im = sb(wscope, "car_im", [128, 16])
        gs_re = sb(wscope, "gs_re", [128, 16]); gs_im = sb(wscope, "gs_im", [128, 16]); gta = sb(wscope, "gta", [128, 16]); gtb = sb(wscope, "gtb", [128, 16])
        Bw_re = sb(wscope, "Bw_re", [128, 16, 128], BF16); Bw_im = sb(wscope, "Bw_im", [128, 16, 128], BF16)
        Cw_re = sb(wscope, "Cw_re", [128, 16, 32]); Cw_imn = sb(wscope, "Cw_imn", [128, 16, 32])
        E_re = sb(wscope, "E_re", [128, 16, 128]); E_im = sb(wscope, "E_im", [128, 16, 128])
        F_re = sb(wscope, "F_re", [128, 16, 128]); F_im = sb(wscope, "F_im", [128, 16, 128])
        s5t = {nm: sb(wscope, "s5" + nm, [128, 512]) for nm in ("t1", "t2", "Xre", "Xim", "Gre", "Gim", "Hre", "Him")}

        def s5_setup(sd, F0=()):
            K = "s5s"
            S5K = ["s5t1", "s5t2", "s5Xre", "s5Xim", "s5Gre", "s5Gim", "s5Hre", "s5Him", "s5tab", K]
            with contextlib.ExitStack() as t_:
                sm = sb(t_, "s5sm", [128, 20, 16])
                (lre, lim, stp, th, th2, tmpc, mag, imag, sn, csn, lbr, lbi, den, nr, qre, qim, u1, ivr, ivi) = [sm[:, i_, :] for i_ in range(19)]
                bre = s5t["Gre"][:, 0:256].rearrange("p (q h) -> p q h", h=16); bim = s5t["Gre"][:, 256:512].rearrange("p (q h) -> p q h", h=16)
                bbr = s5t["Gim"][:, 0:256].rearrange("p (q h) -> p q h", h=16); bbi = s5t["Gim"][:, 256:512].rearrange("p (q h) -> p q h", h=16)
                v1 = s5t["Hre"][:, 0:256].rearrange("p (q h) -> p q h", h=16); cst = s5t["Hre"][:, 256:512].rearrange("p (q h) -> p q h", h=16)
                bdw = s5t["Xre"][:].rearrange("p (r c) -> p r c", c=128)

                def V(meth, *args, eng="dve", **kw):
                    kb.I(eng, meth, *args, reads=S5K, writes=S5K, **kw)

                kb.D(lre, din["lam_re"][sd].rearrange("(q g) p -> (g p) q", g=2), reads=list(F0) + S5K, writes=S5K, allow_slow_non_contiguous=True)
                kb.D(lim, din["lam_im"][sd].rearrange("(q g) p -> (g p) q", g=2), reads=list(F0), writes=[K], allow_slow_non_contiguous=True)
                for g2_ in range(2):
                    kb.D(stp[64 * g2_:64 * g2_ + 64, :],
                         din["lstep"][sd:sd + 1, :].rearrange("o (q g) -> o q g", g=2)[:, :, g2_].partition_broadcast(64),
                         reads=list(F0), writes=[K], allow_slow_non_contiguous=True)
                kb.D(bre, din["b_re"].rearrange("(q g) p h -> (g p) q h", g=2), reads=list(F0), writes=[K])
                kb.D(bim, din["b_im"].rearrange("(q g) p h -> (g p) q h", g=2), reads=list(F0), writes=[K])
                V("activation", stp, stp, AF.Exp, eng="act")
                V("tensor_tensor", mag, lre, stp, ALU.mult)
                V("tensor_tensor", th, lim, stp, ALU.mult)
                V("activation", imag, mag, AF.Exp, eng="act", scale=-1.0)
                V("activation", mag, mag, AF.Exp, eng="act")
                V("tensor_copy", u1, th)
                for m in (PI, 3 * PI, 5 * PI):
                    V("tensor_scalar", tmpc, th, m, -2 * PI, ALU.is_ge, ALU.mult)
                    V("tensor_tensor", u1, u1, tmpc, ALU.add)
                V("tensor_scalar", u1, u1, 0.125, 0.0, ALU.mult, ALU.add)
                V("tensor_tensor", th2, u1, u1, ALU.mult)
                V("tensor_scalar", sn, th2, -1.0 / 5040, 1.0 / 120, ALU.mult, ALU.add)
                V("tensor_tensor", sn, sn, th2, ALU.mult)
                V("tensor_scalar", sn, sn, -1.0 / 6, 0.0, ALU.add, ALU.add)
                V("tensor_tensor", sn, sn, th2, ALU.mult)
                V("tensor_scalar", sn, sn, 1.0, 0.0, ALU.add, ALU.add)
                V("tensor_tensor", sn, sn, u1, ALU.mult)
                V("tensor_scalar", csn, th2, 1.0 / 40320, -1.0 / 720, ALU.mult, ALU.add)
                V("tensor_tensor", csn, csn, th2, ALU.mult)
                V("tensor_scalar", csn, csn, 1.0 / 24, 0.0, ALU.add, ALU.add)
                V("tensor_tensor", csn, csn, th2, ALU.mult)
                V("tensor_scalar", csn, csn, -0.5, 0.0, ALU.add, ALU.add)
                V("tensor_tensor", csn, csn, th2, ALU.mult)
                V("tensor_scalar", csn, csn, 1.0, 0.0, ALU.add, ALU.add)
                for _ in range(3):
                    V("tensor_tensor", tmpc, csn, sn, ALU.mult)
                    V("tensor_tensor", th2, sn, sn, ALU.mult)
                    V("tensor_tensor", csn, csn, csn, ALU.mult)
                    V("tensor_tensor", csn, csn, th2, ALU.subtract)
                    V("tensor_scalar", sn, tmpc, 2.0, 0.0, ALU.mult, ALU.add)
                V("tensor_tensor", lbr, mag, csn, ALU.mult)
                V("tensor_tensor", lbi, mag, sn, ALU.mult)
                V("tensor_tensor", ivr, imag, csn, ALU.mult)
                V("scalar_tensor_tensor", ivi, imag, -1.0, sn, ALU.mult, ALU.mult)
                V("tensor_tensor", den, lre, lre, ALU.mult)
                V("tensor_tensor", u1, lim, lim, ALU.mult)
                V("tensor_tensor", den, den, u1, ALU.add)
                V("reciprocal", den, den)
                V("tensor_scalar", nr, lbr, -1.0, 0.0, ALU.add, ALU.add)
                V("tensor_tensor", qre, nr, lre, ALU.mult)
                V("tensor_tensor", u1, lbi, lim, ALU.mult)
                V("tensor_tensor", qre, qre, u1, ALU.add)
                V("tensor_tensor", qre, qre, den, ALU.mult)
                V("tensor_tensor", qim, lbi, lre, ALU.mult)
                V("tensor_tensor", u1, nr, lim, ALU.mult)
                V("tensor_tensor", qim, qim, u1, ALU.subtract)
                V("tensor_tensor", qim, qim, den, ALU.mult)
                qreb = qre.unsqueeze(2).to_broadcast([128, 16, 16]); qimb = qim.unsqueeze(2).to_broadcast([128, 16, 16])
                V("tensor_tensor", bbr, bre, qreb, ALU.mult)
                V("tensor_tensor", v1, bim, qimb, ALU.mult)
                V("tensor_tensor", bbr, bbr, v1, ALU.subtract)
                V("tensor_tensor", bbi, bim, qreb, ALU.mult)
                V("tensor_tensor", v1, bre, qimb, ALU.mult)
                V("tensor_tensor", bbi, bbi, v1, ALU.add)
                for src_, dstT in ((bbr, Bw_re), (bbi, Bw_im)):
                    for qq in range(4):
                        V("memset", bdw, 0.0)
                        for r_ in range(4):
                            for g2_ in range(2):
                                c0 = 32 * r_ + 16 * g2_
                                V("tensor_copy", bdw[64 * g2_:64 * g2_ + 64, r_, c0:c0 + 16], src_[64 * g2_:64 * g2_ + 64, qq * 4 + r_, :])
                        for r_ in range(4):
                            kb.I("pe", "transpose", PS[4][:, r_ * 128:(r_ + 1) * 128], bdw[:, r_, :], ident[:],
                                 reads=S5K + ["ident"], writes=[f"pX4_{r_}"])
                        kb.I("dve", "tensor_copy", dstT[:, qq * 4:(qq + 1) * 4, :], PS[4][:].rearrange("p (r c) -> p r c", r=4),
                             reads=bk(4) + S5K, writes=S5K)
                cpad = s5t["Xim"][:].rearrange("p (j c) -> p j c", c=128)
                for nm, dstC, sc in (("c_re", Cw_re, 1.0), ("c_im", Cw_imn, -1.0)):
                    csrc = din[nm].rearrange("g h p -> (g h) p").rearrange("(j r) p -> r j p", r=128)
                    for hh_ in range(2):
                        kb.D(cpad[:, :, 64 * hh_:64 * hh_ + 64], csrc, reads=S5K, writes=S5K)
                    for j_ in range(4):
                        kb.I("pe", "transpose", PS[4][:, j_ * 128:(j_ + 1) * 128], cpad[:, j_, :], ident[:], reads=S5K + ["ident"], writes=[f"pX4_{j_}"])
                    V("memset", dstC[:], 0.0)
                    for g2_ in range(2):
                        srcv = PS[4][64 * g2_:64 * g2_ + 64, :].rearrange("p (q g h) -> p q g h", g=2, h=16)[:, :, g2_, :]
                        kb.I("dve", "tensor_scalar", dstC[64 * g2_:64 * g2_ + 64, :, 16 * g2_:16 * g2_ + 16], srcv, sc, 0.0, ALU.mult, ALU.add,
                             reads=bk(4) + S5K, writes=S5K)
                for (Tr, Ti, sr, si) in ((F_re, F_im, lbr, lbi), (E_re, E_im, ivr, ivi)):
                    V("tensor_copy", Tr[:, :, 0:1], sr.unsqueeze(2))
                    V("tensor_copy", Ti[:, :, 0:1], si.unsqueeze(2))
                    n_ = 1
                    while n_ < 128:
                        for qh in range(2):
                            qsl = slice(8 * qh, 8 * qh + 8)
                            pr = Tr[:, qsl, n_ - 1:n_].to_broadcast([128, 8, n_]); pim = Ti[:, qsl, n_ - 1:n_].to_broadcast([128, 8, n_])
                            t1_ = s5t["t1"][:, 0:8 * n_].rearrange("p (q n) -> p q n", q=8)
                            t2_ = s5t["t2"][:, 0:8 * n_].rearrange("p (q n) -> p q n", q=8)
                            V("tensor_tensor", t1_, Tr[:, qsl, 0:n_], pr, ALU.mult)
                            V("tensor_tensor", t2_, Ti[:, qsl, 0:n_], pim, ALU.mult)
                            V("tensor_tensor", Tr[:, qsl, n_:2 * n_], t1_, t2_, ALU.subtract)
                            V("tensor_tensor", t1_, Tr[:, qsl, 0:n_], pim, ALU.mult)
                            V("tensor_tensor", t2_, Ti[:, qsl, 0:n_], pr, ALU.mult)
                            V("tensor_tensor", Ti[:, qsl, n_:2 * n_], t1_, t2_, ALU.add)
                        n_ *= 2
                if debug and sd == 0:
                    kb.D(dbg["d_F"], F_re[:].rearrange("p q t -> p (q t)"), reads=S5K, writes=["o_d_F"])
                V("tensor_copy", lre[:, 0:1], lre[:, 0:1])

        with contextlib.ExitStack() as p0:
            rec0 = []
            kb.rec = rec0
            stage = [sb(p0, f"stage{i}", [128, 2272]) for i in range(2)]
            for kt in range(8):
                s_ = stage[kt % 2]; sk = f"stage{kt % 2}"
                kb.D(s_[:], din["w_in"][kt * 128:(kt + 1) * 128, :], writes=[sk])
                kb.cp("act" if kt % 2 else "pool", win[:, kt, :], s_[:], reads=[sk], writes=["win"])
            modb = sb(p0, "modb", [128, 6 * D])
            cT = sb(p0, "cT", [128, 2, 8]); cs = sb(p0, "cs", [128, 2, 8])
            lbx = sb(p0, "lbx", [128, 8, 128]); lbc = sb(p0, "lbc", [128, 8, 128])
            modc = sb(p0, "modc", [128, 2 * D]); mbias = sb(p0, "mbias", [128, 6 * D])
            g1b = sb(p0, "g1b", [128, D]); tmpA = sb(p0, "tmpA", [128, D])
            wbuf = [sb(p0, f"wbuf{i}", [128, 512]) for i in range(4)]
            for j_ in range(2):
                kb.D(cT[:, j_, :], din["cc"][j_].rearrange("(k p) -> p k", p=128), writes=["cT"], allow_slow_non_contiguous=True)
            kb.D(mbias[:], din["mod_b"].partition_broadcast(128), writes=["mbias"])
            kb.D(g1b[:], din["g1"].partition_broadcast(128), writes=["g1b"])
            kb.I("act", "activation", cs[:], cT[:], AF.Silu, reads=["cT"], writes=["cs"])
            for kt in range(8):
                kb.I("dve", "tensor_copy", lbx[:, kt, :], cs[:, 0, kt:kt + 1].to_broadcast([128, 128]), reads=["cs"], writes=["lbx"])
                kb.I("pool", "tensor_copy", lbc[:, kt, :], cs[:, 1, kt:kt + 1].to_broadcast([128, 128]), reads=["cs"], writes=["lbc"])
            i = 0
            for n in range(12 if upto != "p0a" else 0):
                ns = slice(n * 512, (n + 1) * 512)
                for kt in range(8):
                    wb = wbuf[i % 4]; wk = f"wbuf{i % 4}"; i += 1
                    kb.D(wb[:], din["mod_w"][kt * 128:(kt + 1) * 128, ns], writes=[wk])
                    kb.I("pe", "matmul", PS[0][:], lbx[:, kt, :], wb[:], start=(kt == 0), stop=(kt == 7), reads=[wk, "lbx"], writes=bk(0))
                    if n < 4:
                        kb.I("pe", "matmul", PS[1][:], lbc[:, kt, :], wb[:], start=(kt == 0), stop=(kt == 7), reads=[wk, "lbc"], writes=bk(1))
                kb.I("dve", "tensor_tensor", modb[:, ns], PS[0][:], mbias[:, ns], ALU.add, reads=bk(0) + ["mbias"], writes=["modb"])
                if n < 4:
                    kb.I("dve", "tensor_tensor", modc[:, ns], PS[1][:], mbias[:, ns], ALU.add, reads=bk(1) + ["mbias"], writes=["modc"])
            for which, src_ in ((0, modb), (1, modc)) if upto not in ("p0a", "p0b") else ():
                kb.I("dve", "scalar_tensor_tensor", tmpA[:], src_[:, D:2 * D], 1.0, g1b[:], ALU.add, ALU.mult,
                     reads=["modb", "modc", "g1b"], writes=["tmpA"])
                for half in range(2):
                    for j in range(4):
                        kt = half * 4 + j
                        kb.I("pe", "transpose", PS[2][:, j * 128:(j + 1) * 128], tmpA[:, kt * 128:(kt + 1) * 128], ident[:],
                             reads=["tmpA", "ident"], writes=[f"pX2_{j}"])
                        kb.I("pe", "transpose", PS[3][:, j * 128:(j + 1) * 128], src_[:, kt * 128:(kt + 1) * 128], ident[:],
                             reads=["modb", "modc", "ident"], writes=[f"pX3_{j}"])
                    for j in range(4):
                        kt = half * 4 + j
                        kb.I("dve", "tensor_copy", AB[:, 2 * which, kt:kt + 1], PS[2][:, j * 128:j * 128 + 1], reads=[f"pX2_{j}"], writes=["AB"])
                        kb.I("dve", "tensor_copy", AB[:, 2 * which + 1, kt:kt + 1], PS[3][:, j * 128:j * 128 + 1], reads=[f"pX3_{j}"], writes=["AB"])
            if upto not in ("p0a", "p0b", "p0c"):
                kb.D(scr["mod"], modb[:], reads=["modb"], writes=["scr_mod"])
            if debug and upto not in ("p0a", "p0b", "p0c"):
                kb.D(dbg["d_mod"], modb[:], reads=["modb"], writes=["o_d_mod"])
                kb.D(dbg["d_AB"], AB[:].rearrange("p a k -> p (a k)"), reads=["AB"], writes=["o_d_AB"])
            kb.rec = None
            recS = []
            if not upto.startswith("p0"):
                kb.rec = recS
                s5_setup(0)
                kb.rec = None
            kb.merge(rec0, recS)
            kb.I("dve", "tensor_copy", tmpA[:, 0:1], tmpA[:, 0:1],
                 reads=list(kb.lastw.keys()), writes=list(kb.lastw.keys()) + ["fence0"])
        FENCE0 = ["fence0"]
        if upto.startswith("p0"):
            kb.mute = True

        with contextlib.ExitStack() as sw:
            F0 = FENCE0
            kkc = sb(sw, "kkc", [128, 4]); kac = sb(sw, "kac", [128, 4]); omka = sb(sw, "omka", [128, 4])
            rkc = sb(sw, "rkc", [128, 4]); w0c = sb(sw, "w0c", [128, 2, 4]); a0c = sb(sw, "a0c", [128, 2, 4])
            cw = sb(sw, "cw", [128, 12, 9]); s5dc = sb(sw, "s5dc", [128, 4])
            w2pad = sb(sw, "w2pad", [128, 2, 512]); a2pad = sb(sw, "a2pad", [128, 2, 512])
            diagd = sb(sw, "diagd", [128, 4, 128], BF16)
            for t, nm in ((kkc, "kkv"), (kac, "kav"), (rkc, "rkv"), (s5dc, "s5d")):
                kb.D(t[:], din[nm].rearrange("(f p) -> p f", p=128), reads=F0, writes=["cols"], allow_slow_non_contiguous=True)
            for t, nm in ((w0c, "w0"), (a0c, "a0")):
                for d_ in range(2):
                    kb.D(t[:, d_, :], din[nm][d_].rearrange("(f p) -> p f", p=128), reads=F0, writes=["cols"], allow_slow_non_contiguous=True)
            for tp in range(9):
                kb.D(cw[:, :, tp], din["conv"][tp].rearrange("(f p) -> p f", p=128), reads=F0, writes=["cols"], allow_slow_non_contiguous=True)
            kb.I("dve", "tensor_scalar", omka[:], kac[:], -1.0, 1.0, ALU.mult, ALU.add, reads=["cols"], writes=["cols2"])
            kb.I("pool", "memset", w2pad[:], 0.0, reads=F0, writes=["w2pad"])
            kb.I("pool", "memset", a2pad[:], 0.0, reads=F0, writes=["a2pad"])
            for d_ in range(2):
                kb.D(w2pad[32 * d_:32 * d_ + 32, d_, :], din["w2"][d_], reads=["w2pad"], writes=["w2pad"])
                kb.D(a2pad[64 + 32 * d_:96 + 32 * d_, d_, :], din["a2"][d_], reads=["a2pad"], writes=["a2pad"])
            for ct in range(4):
                kb.ts1("dve", diagd[:, ct, :], ident[:], s5dc[:, ct:ct + 1], ALU.mult, reads=["cols", "ident"], writes=["diagd"])

            ST = sb(sw, "ST", [128, 4, 64])
            xw = sb(sw, "xw", [128, 2, D]); junk = sb(sw, "junk", [128, D], BF16)
            ssum = sb(sw, "ssum", [128, 2]); rstd = sb(sw, "rstd", [128, 2])
            hT = sb(sw, "hT", [128, 8, 256], BF16)
            zrkv2 = [sb(sw, f"zrkv{i}", [128, 3, 264], BF16) for i in range(2)]; zl = sb(sw, "zl", [128, 128]); zg = sb(sw, "zg", [128, 128])
            dgb = [sb(sw, f"dgb{i}", [128, 9, 128], BF16) for i in range(2)]
            rkvc2 = [sb(sw, f"rkvc{i}", [128, 3, 128]) for i in range(2)]
            W2 = {}
            for nm in ("kk", "t1", "t2", "ld", "a", "cs", "Eni", "b", "kd", "bep", "kap"):
                W2[nm] = [sb(sw, f"{nm}{i}", [128, 128]) for i in range(2)]
            zub = [sb(sw, f"zub{i}", [128, 4, 128], BF16) for i in range(2)]
            BE = [sb(sw, f"BE{i}", [128, 4, 128]) for i in range(2)]; KAP = [sb(sw, f"KAP{i}", [128, 4, 128]) for i in range(2)]
            AR = [sb(sw, f"AR{i}", [128, 4, 256]) for i in range(2)]; gC = [sb(sw, f"gC{i}", [128, 4]) for i in range(2)]
            tot = sb(sw, "tot", [128, 4])
            TMt = [sb(sw, f"TMt{i}", [128, 4, 512]) for i in range(2)]
            UT = sb(sw, "UT", [128, 512]); yT = sb(sw, "yT", [128, 512])
            flpR = sb(sw, "flpR", [128, 512]); flpS = sb(sw, "flpS", [128, 512])
            NS = 4
            NB = [sb(sw, f"NB{i}", [128, 256]) for i in range(NS)]; KA = [sb(sw, f"KA{i}", [128, 256]) for i in range(NS)]
            Pq = [sb(sw, f"Pq{i}", [128, 2, 128], BF16) for i in range(NS)]; PTq = [sb(sw, f"PTq{i}", [128, 2, 128], BF16) for i in range(NS)]
            Xq = [sb(sw, f"Xq{i}", [128, 2, 64]) for i in range(NS)]; Xb = [sb(sw, f"Xb{i}", [128, 2, 64], BF16) for i in range(NS)]
            print("SBUF bytes remaining in sweep scope:", nc.sbuf_bytes_remaining)
            for cf in range(12):
                for tp in range(9):
                    kb.ts1("dve" if tp % 2 else "pool", dgb[cf % 2][:, tp, :], ident[:], cw[:, cf, tp:tp + 1], ALU.mult,
                           reads=["cols", "ident", f"dgb{cf % 2}"], writes=[f"dgb{cf % 2}"])
                kb.D(scr["dg"][cf], dgb[cf % 2][:].rearrange("p t c -> p (t c)"), reads=[f"dgb{cf % 2}"], writes=["scr_dg"])
            kb.I("pool", "memset", xw[:], 0.0, reads=F0, writes=["xw0", "xw1"])

            def load_window(sd, is_ctx, c):
                rev = (sd == 1)
                TT = T_CTX if is_ctx else T_LAT
                src = din["ctx"] if is_ctx else din["x"]
                for half in range(2):
                    lo_r = 128 * c - 64 + 128 * half
                    if not rev:
                        lo, hi = lo_r, lo_r + 128
                    else:
                        hi, lo = TT - lo_r, TT - lo_r - 128
                    a0_, a1_ = max(lo, 0), min(hi, TT)
                    if a1_ > a0_:
                        kb.D(xw[a0_ - lo:a1_ - lo, half, :], src[a0_:a1_, :], writes=[f"xw{half}"])

            def process_chunk(sd, is_ctx, c, own, slot, nxt=None, first=False, reuse=False):
                rev = (sd == 1)
                TT = T_CTX if is_ctx else T_LAT
                src = din["ctx"] if is_ctx else din["x"]
                ab = 2 if is_ctx else 0
                tid = antiI if rev else ident
                tk = "antiI" if rev else "ident"
                dbg_here = debug and sd == 0 and (not is_ctx) and c == 0
                SL = f"@{slot}"
                streams = {"front": [], "rwkv": [], "s5": []}
                kb.rec = streams["front"]
                AR_, BE_, KAP_, gC_, TM_, zub_ = AR[slot], BE[slot], KAP[slot], gC[slot], TMt[slot], zub[slot]
                kapT = TM_[:, 0, :]; bepT = TM_[:, 1, :]; vT = TM_[:, 2, :]; bvT = TM_[:, 3, :]
                TMK = "TM" + SL
                if not reuse:
                    if first:
                        load_window(sd, is_ctx, c)
                    kb.I("pool", "memset", ssum[:], 0.0, writes=["ssum"])
                    for half in range(2):
                        kb.I("act", "activation", junk[:], xw[:, half, :], AF.Square, accum_out=ssum[:, half:half + 1],
                             reads=[f"xw{half}"], writes=["junk", "ssum"])
                    kb.I("dve", "tensor_scalar", rstd[:], ssum[:], 1.0 / D, 1e-6, ALU.mult, ALU.add, reads=["ssum"], writes=["rstd"])
                    kb.I("act", "activation", rstd[:], rstd[:], AF.Sqrt, reads=["rstd"], writes=["rstd"])
                    kb.I("dve", "reciprocal", rstd[:], rstd[:], reads=["rstd"], writes=["rstd"])
                    for half in range(2):
                        kb.ts1("dve" if half == 0 else "pool", xw[:, half, :], xw[:, half, :], rstd[:, half:half + 1], ALU.mult,
                               reads=["rstd", f"xw{half}"], writes=[f"xw{half}"])
                    for kt in range(8):
                        b_ = kt % 2
                        for half in range(2):
                            kb.I("pe", "transpose", PS[b_][:, half * 128:(half + 1) * 128], xw[:, half, kt * 128:(kt + 1) * 128], tid[:],
                                 reads=[f"xw{half}", tk], writes=[f"pX{b_}_0"])
                        if kt % 2:
                            kb.I("dve", "tensor_scalar", hT[:, kt, :], PS[b_][:, 0:256], AB[:, ab, kt:kt + 1], AB[:, ab + 1, kt:kt + 1], ALU.mult, ALU.add,
                                 reads=[f"pX{b_}_0", "AB"], writes=[f"hT{kt}"])
                        else:
                            kb.I("act", "activation", hT[:, kt, :], PS[b_][:, 0:256], AF.Identity, bias=AB[:, ab + 1, kt:kt + 1], scale=AB[:, ab, kt:kt + 1],
                                 reads=[f"pX{b_}_0", "AB"], writes=[f"hT{kt}"])
                    if nxt is not None:
                        load_window(sd, nxt[0], nxt[1])
                    others = [("zl", zl[:, :], 1536, 128), ("zg", zg[0:96, :], 1664, 96)] + [(f"zu{j}" + SL, zub_[:, j, :], 1760 + 128 * j, 128) for j in range(4)]
                    for i_, (nm, dst, c0, m) in enumerate(others):
                        b_ = i_ % 2
                        pr_ = PS[b_][0:m, 0:128]
                        pk = [f"pX{b_}_0"]
                        for kt in range(8):
                            kb.I("pe", "matmul", pr_, win[:, kt, c0:c0 + m], hT[:, kt, 64:192], start=(kt == 0), stop=(kt == 7), reads=["win", f"hT{kt}"], writes=pk)
                        if nm == "zg":
                            kb.I("act", "activation", dst, pr_, AF.Sigmoid, reads=pk, writes=[nm])
                        else:
                            kb.cp("dve" if i_ % 2 else "act", dst, pr_, reads=pk, writes=[nm])
                    if own is not None and sd == 0:
                        kb.D(scr["sg"][own * 128:own * 128 + 96, :], zg[0:96, :], reads=["zg"], writes=[f"scr_sg{own}"])
                    kb.I("act", "activation", zl[0:64, :], zl[0:64, :], AF.Tanh, reads=["zl"], writes=["zl"])
                    if sd == 0 and own is not None and not is_ctx:
                        kb.D(scr["fz"][own * 128:(own + 1) * 128, :], zl[:, :], reads=["zl"], writes=[f"scr_fz{own}"])
                        kb.D(scr["fu"][own * 128:(own + 1) * 128, :], zub_[:].rearrange("p j t -> p (j t)"), reads=[f"zu{j}" + SL for j in range(4)], writes=[f"scr_fu{own}"])
                else:
                    kb.D(xw[:, 1, 512:640], scr["fz"][own * 128:(own + 1) * 128, :], reads=[f"scr_fz{own}"], writes=["xw1"])
                    kb.I("dve", "tensor_copy", zl[:, :], xw[:, 1, 512:640][:, ::-1], reads=["xw1"], writes=["zl"])
                    kb.D(junk[:, 0:512], scr["fu"][own * 128:(own + 1) * 128, :], reads=[f"scr_fu{own}"], writes=["junk"])
                    kb.I("dve", "tensor_copy", zub_[:], junk[:, 0:512].rearrange("p (j t) -> p j t", t=128)[:, :, ::-1], reads=["junk"], writes=[f"zu{j}" + SL for j in range(4)])
                sgn = -1 if rev else 1
                nk = 4 if own is not None else 3
                for ft in range(4):
                    fp_ = ft % 2
                    zrkv = zrkv2[fp_]; rkvc = rkvc2[fp_]
                    W = {nm: W2[nm][fp_] for nm in W2}
                    if reuse:
                        stg_ = xw[:, ft % 2, 0:512]
                        kb.D(stg_, scr["fr"][(own * 4 + ft) * 128:(own * 4 + ft + 1) * 128, :], reads=[f"scr_fr{own}_{ft}"], writes=[f"xw{ft % 2}"])
                        kb.I("dve", "tensor_copy", rkvc[:, :, :], stg_[:, 0:384].rearrange("p (j t) -> p j t", t=128)[:, :, ::-1],
                             reads=[f"xw{ft % 2}"], writes=[f"rkvc{j}_{fp_}" for j in range(3)])
                        kb.I("dve", "tensor_copy", W["kk"][:, :], stg_[:, 384:512][:, ::-1], reads=[f"xw{ft % 2}"], writes=[f"W_kk{fp_}"])
                    for j3 in (range(3) if not reuse else ()):
                        b_ = j3 % 2
                        cf = (j3 * 4 + ft) * 128
                        pr_ = PS[b_][:, 0:256]
                        pk = [f"pX{b_}_0"]
                        for kt in range(8):
                            kb.I("pe", "matmul", pr_, win[:, kt, cf:cf + 128], hT[:, kt, :], start=(kt == 0), stop=(kt == 7), reads=["win", f"hT{kt}"], writes=pk)
                        if is_ctx:
                            kb.cp("act" if j3 % 2 else "dve", zrkv[:, j3, 0:256], pr_, reads=pk, writes=[f"zrkv{j3}_{fp_}"])
                        else:
                            if c == 0 and ft < 2:
                                kb.I("pool", "memset", zrkv[:, j3, :].rearrange("p (r c) -> p r c", c=66)[:, :, 0:1], 0.0, reads=[f"zrkv{j3}_{fp_}"], writes=[f"zrkv{j3}_{fp_}"])
                                kb.I("pool", "memset", zrkv[:, j3, :].rearrange("p (r c) -> p r c", c=66)[:, :, 65:66], 0.0, reads=[f"zrkv{j3}_{fp_}"], writes=[f"zrkv{j3}_{fp_}"])
                            kb.cp("act" if j3 % 2 else "dve", zrkv[:, j3, :].rearrange("p (r c) -> p r c", c=66)[:, :, 1:65],
                                  pr_.rearrange("p (r c) -> p r c", c=64), reads=pk, writes=[f"zrkv{j3}_{fp_}"])
                    if dbg_here and ft == 0:
                        kb.D(dbg["d_z"], zrkv[:].rearrange("p f t -> p (f t)"), reads=[f"zrkv{j}_{fp_}" for j in range(3)], writes=["o_d_z"])
                    for j3 in (range(3) if not reuse else ()):
                        zk = f"zrkv{j3}_{fp_}"; ok = f"rkvc{j3}_{fp_}"
                        cf = j3 * 4 + ft
                        ds = (ft * 3 + j3) % 2
                        dk = f"dgb{ds}"
                        kb.D(dgb[ds][:].rearrange("p t c -> p (t c)"), scr["dg"][cf], reads=["scr_dg"], writes=[dk])
                        b_ = (j3 + 1) % 2
                        pc = PS[b_][:, 256:384]
                        taps = [(0, 0)] + [(dy, dx) for dy in ((0,) if is_ctx else (-1, 0, 1)) for dx in (-1, 0, 1) if (dy, dx) != (0, 0)]
                        mms = []
                        for (dy, dx) in taps:
                            wi = (1 + sgn * dy) * 3 + (1 + sgn * dx)
                            if is_ctx:
                                j0, j1 = 64, 192
                                if 128 * c + dx < 0:
                                    j0 = 65
                                if 128 * c + 127 + dx >= TT:
                                    j1 = 191
                                mms.append((PS[b_][:, 256 + j0 - 64:256 + j1 - 64], wi, zrkv[:, j3, j0 + dx:j1 + dx]))
                            else:
                                i0, i1 = 0, 2
                                if 2 * c + dy < 0:
                                    i0 = 1
                                if 2 * c + 1 + dy >= 64:
                                    i1 = 1
                                o0 = 66 * i0 + 1; o1 = 66 * (i1 - 1) + 65
                                sh = 66 * (1 + dy) + dx
                                mms.append((PS[b_][:, 256 + o0:256 + o1], wi, zrkv[:, j3, o0 + sh:o1 + sh]))
                        for ti, (o_ap, wi, i_ap) in enumerate(mms):
                            kb.I("pe", "matmul", o_ap, dgb[ds][:, wi, :], i_ap, start=(ti == 0), stop=(ti == len(mms) - 1),
                                 reads=[zk, dk], writes=[f"pX{b_}_0"])
                        if is_ctx:
                            kb.cp("act" if j3 % 2 == 0 else "dve", rkvc[:, j3, :], pc, reads=[f"pX{b_}_0"], writes=[ok])
                        else:
                            kb.cp("act" if j3 % 2 == 0 else "dve", rkvc[:, j3, :].rearrange("p (r c) -> p r c", c=64),
                                  PS[b_][:, 256:388].rearrange("p (r c) -> p r c", c=66)[:, :, 1:65], reads=[f"pX{b_}_0"], writes=[ok])
                    if dbg_here and ft == 0:
                        kb.D(dbg["d_rkvc"], rkvc[:].rearrange("p f t -> p (f t)"), reads=[f"rkvc{j}_{fp_}" for j in range(3)], writes=["o_d_rkvc"])
                    rc = rkvc[:, 0, :]; kc = rkvc[:, 1, :]; vc = rkvc[:, 2, :]
                    fk = f"w{ft}" + SL
                    g_ = {nm: W[nm][:, :] for nm in W}
                    CK = ["cols", "cols2"]

                    def V(meth, *args, eng="dve", r=(), w=(), **kw):
                        kb.I(eng, meth, *args, reads=list(r) + CK, writes=list(w), **kw)
                    ARa = AR_[:, ft, 0:128]; ARr = AR_[:, ft, 128:256]
                    fa, fr, fb, fkp, fg = fk + "a", fk + "r", fk + "b", fk + "k", fk + "g"
                    if not reuse:
                        V("tensor_scalar", g_["kk"], kc, kkc[:, ft:ft + 1], 0.0, ALU.mult, ALU.add, r=[f"rkvc1_{fp_}"], w=[f"W_kk{fp_}"])
                        V("tensor_tensor", g_["t1"], g_["kk"], g_["kk"], ALU.mult, r=[f"W_kk{fp_}"], w=[f"W_t1{fp_}"])
                        kb.I("pe", "matmul", PS[0][:, 384:512], bones[:], g_["t1"], start=True, stop=True, reads=[f"W_t1{fp_}", "bones"], writes=["pX0_0"])
                        kb.I("dve", "tensor_scalar", g_["t2"], PS[0][:, 384:512], 1e-12, 0.0, ALU.max, ALU.add, reads=["pX0_0"], writes=[f"W_t2{fp_}"])
                        V("activation", g_["t2"], g_["t2"], AF.Sqrt, eng="act", r=[f"W_t2{fp_}"], w=[f"W_t2{fp_}"])
                        V("reciprocal", g_["t2"], g_["t2"], r=[f"W_t2{fp_}"], w=[f"W_t2{fp_}"])
                        V("tensor_tensor", g_["kk"], g_["kk"], g_["t2"], ALU.mult, r=[f"W_kk{fp_}", f"W_t2{fp_}"], w=[f"W_kk{fp_}"])
                        if sd == 0 and own is not None and not is_ctx:
                            r0_ = (own * 4 + ft) * 128
                            kb.D(scr["fr"][r0_:r0_ + 128, 0:384], rkvc[:, :, :].rearrange("p j t -> p (j t)"), reads=[f"rkvc{j}_{fp_}" for j in range(3)], writes=[f"scr_fr{own}_{ft}"])
                            kb.D(scr["fr"][r0_:r0_ + 128, 384:512], g_["kk"], reads=[f"W_kk{fp_}"], writes=[f"scr_fr{own}_{ft}"])
                    kb.I("pe", "matmul", PS[1][:, 256:384], w2pad[:, sd, ft * 128:(ft + 1) * 128], zl[:, :], start=True, stop=True, reads=["zl", "w2pad"], writes=["pX1_0"])
                    kb.I("pe", "matmul", PS[1][:, 384:512], a2pad[:, sd, ft * 128:(ft + 1) * 128], zl[:, :], start=True, stop=True, reads=["zl", "a2pad"], writes=["pX1_0"])
                    kb.I("act", "activation", g_["ld"], PS[1][:, 256:384], AF.Sigmoid, bias=w0c[:, sd, ft:ft + 1], scale=1.0, reads=["pX1_0", "cols"], writes=[f"W_ld{fp_}"])
                    kb.I("act", "activation", g_["a"], PS[1][:, 384:512], AF.Sigmoid, bias=a0c[:, sd, ft:ft + 1], scale=1.0, reads=["pX1_0", "cols"], writes=[f"W_a{fp_}"])
                    V("tensor_tensor_scan", g_["cs"], ones[:], g_["ld"], 0.0, ALU.mult, ALU.add, r=[f"W_ld{fp_}", "ones"], w=[f"W_cs{fp_}"])
                    V("tensor_copy", tot[:, ft:ft + 1], g_["cs"][:, 127:128], r=[f"W_cs{fp_}"], w=[f"tot{ft}"])
                    V("tensor_tensor", g_["t1"], g_["cs"], g_["ld"], ALU.subtract, r=[f"W_cs{fp_}", f"W_ld{fp_}"], w=[f"W_t1{fp_}"])
                    V("activation", ARr, g_["cs"], AF.Exp, eng="act", scale=-KDEC, r=[f"W_cs{fp_}"], w=[fr])
                    V("activation", ARa, g_["t1"], AF.Exp, eng="act", scale=-KDEC, r=[f"W_t1{fp_}"], w=[fa])
                    V("activation", g_["Eni"], g_["cs"], AF.Exp, eng="act", scale=KDEC, r=[f"W_cs{fp_}"], w=[f"W_Eni{fp_}"])
                    V("activation", gC_[:, ft:ft + 1], tot[:, ft:ft + 1], AF.Exp, eng="act", scale=-KDEC, r=[f"tot{ft}"], w=[fg])
                    V("tensor_tensor", g_["b"], g_["kk"], g_["a"], ALU.mult, r=[f"W_kk{fp_}", f"W_a{fp_}"], w=[f"W_b{fp_}"])
                    V("tensor_scalar", g_["t2"], g_["a"], kac[:, ft:ft + 1], omka[:, ft:ft + 1], ALU.mult, ALU.add, r=[f"W_a{fp_}"], w=[f"W_t2{fp_}"])
                    V("tensor_tensor", g_["kd"], kc, g_["t2"], ALU.mult, r=[f"rkvc1_{fp_}", f"W_t2{fp_}"], w=[f"W_kd{fp_}"])
                    V("tensor_tensor", ARa, ARa, g_["kk"], ALU.mult, r=[fa, f"W_kk{fp_}"], w=[fa])
                    V("tensor_tensor", ARr, ARr, rc, ALU.mult, eng="pool", r=[fr, f"rkvc0_{fp_}"], w=[fr])
                    V("tensor_tensor", BE_[:, ft, :], g_["b"], g_["Eni"], ALU.mult, r=[f"W_b{fp_}", f"W_Eni{fp_}"], w=[fb])
                    V("tensor_tensor", KAP_[:, ft, :], g_["kd"], g_["Eni"], ALU.mult, eng="pool", r=[f"W_kd{fp_}", f"W_Eni{fp_}"], w=[fkp])
                    V("tensor_scalar", g_["bep"], BE_[:, ft, :], gC_[:, ft:ft + 1], -1.0, ALU.mult, ALU.mult, r=[fb, fg], w=[f"W_bep{fp_}"])
                    V("tensor_scalar", g_["kap"], KAP_[:, ft, :], gC_[:, ft:ft + 1], 0.0, ALU.mult, ALU.add, eng="pool", r=[fkp, fg], w=[f"W_kap{fp_}"])
                    tsrc = [(g_["kap"], f"W_kap{fp_}"), (g_["bep"], f"W_bep{fp_}"), (vc, f"rkvc2_{fp_}")]
                    if own is not None:
                        V("scalar_tensor_tensor", g_["t1"], rc, rkc[:, ft:ft + 1], g_["kd"], ALU.mult, ALU.mult, r=[f"rkvc0_{fp_}", f"W_kd{fp_}"], w=[f"W_t1{fp_}"])
                        kb.I("pe", "matmul", PS[0][:, 384:512], bones[:], g_["t1"], start=True, stop=True, reads=[f"W_t1{fp_}", "bones"], writes=["pX0_0"])
                        kb.I("dve", "tensor_tensor", g_["t2"], PS[0][:, 384:512], vc, ALU.mult, reads=["pX0_0", f"rkvc2_{fp_}"], writes=[f"W_t2{fp_}"])
                        tsrc.append((g_["t2"], f"W_t2{fp_}"))
                    b_ = ft % 2
                    for i_, (s_ap, s_k) in enumerate(tsrc):
                        kb.I("pe", "transpose", PS[b_][:, i_ * 128:(i_ + 1) * 128], s_ap, ident[:], reads=[s_k, "ident"], writes=[f"pX{b_}_0"])
                    kb.cp("act" if ft % 2 else "dve", TM_[:, 0:nk, ft * 128:(ft + 1) * 128], PS[b_][:, 0:nk * 128].rearrange("p (k t) -> p k t", k=nk),
                          reads=[f"pX{b_}_0"], writes=[TMK])
                kb.rec = streams["rwkv"]
                for g0 in range(0, 8, NS):
                    heads = list(range(g0, g0 + NS))

                    def hv(h):
                        ft = h // 2; Rs = slice(64 * (h % 2), 64 * (h % 2) + 64)
                        return ft, Rs, [f"w{ft}" + SL + x_ for x_ in "arbkg"]
                    BK = [2 + s_ for s_ in range(NS)]
                    bkk = [f"pX{b_}_0" for b_ in BK]
                    for s_, h in enumerate(heads):
                        ft, Rs, fk = hv(h)
                        be = BE_[Rs, ft, :]; ar = AR_[Rs, ft, :]; al = AR_[Rs, ft, 0:128]
                        kb.I("pe", "matmul", PS[BK[s_]][:, 0:256], be, ar, start=True, stop=True, reads=fk, writes=[bkk[s_]])
                        kb.I("pe", "matmul", PS[BK[s_]][:, 256:384], al, be, start=True, stop=True, reads=fk, writes=[bkk[s_]])
                    for s_, h in enumerate(heads):
                        kb.I("dve", "tensor_tensor", NB[s_][:], PS[BK[s_]][:, 0:256], mask_b[:], ALU.mult, reads=[bkk[s_], "mask_b"], writes=[f"NB{s_}"])
                        kb.I("dve", "tensor_tensor", PTq[s_][:, 0, :], PS[BK[s_]][:, 256:384], maskT[:], ALU.mult, reads=[bkk[s_], "maskT"], writes=[f"PT{s_}_0"])
                        kb.I("dve", "tensor_tensor", Pq[s_][:, 0, :], PS[BK[s_]][:, 0:128], mask_b[:, 0:128], ALU.mult, reads=[bkk[s_], "mask_b"], writes=[f"P{s_}_0"])
                    for s_, h in enumerate(heads):
                        ft, Rs, fk = hv(h)
                        kb.I("pe", "matmul", PS[BK[s_]][:, 0:256], KAP_[Rs, ft, :], AR_[Rs, ft, :], start=True, stop=True, reads=fk, writes=[bkk[s_]])
                    for s_, h in enumerate(heads):
                        kb.I("dve", "tensor_tensor", KA[s_][:], PS[BK[s_]][:, 0:256], mask_k[:], ALU.mult, reads=[bkk[s_], "mask_k"], writes=[f"KA{s_}"])
                    for s_, h in enumerate(heads):
                        ft, Rs, fk = hv(h)
                        al = AR_[Rs, ft, 0:128]
                        kb.I("pe", "matmul", PS[BK[s_]][:, 384:448], al, ST[Rs, ft, :], start=True, stop=False, reads=fk + [f"ST{ft}"], writes=[bkk[s_]])
                        kb.I("pe", "matmul", PS[BK[s_]][:, 384:448], KA[s_][:, 0:128], vT[:, h * 64:(h + 1) * 64], start=False, stop=True, reads=[f"KA{s_}", TMK], writes=[bkk[s_]])
                    for s_, h in enumerate(heads):
                        kb.cp("act", Xq[s_][:, 0, :], PS[BK[s_]][:, 384:448], reads=[bkk[s_]], writes=[f"X{s_}_0"])
                        kb.cp("act", Xb[s_][:, 0, :], PS[BK[s_]][:, 384:448], reads=[bkk[s_]], writes=[f"Xb{s_}_0"])
                    for j in range(7):
                        cu, nx = j % 2, (j + 1) % 2
                        for s_, h in enumerate(heads):
                            Pj = Pq[s_][:, cu, :]
                            Pk = f"P{s_}_{cu}"
                            PTj = PTq[s_][:, cu, :]; PTk = f"PT{s_}_{cu}"
                            if j < 6:
                                kb.I("pe", "matmul", PS[BK[s_]][:, 0:128], PTj, Pj, start=True, stop=True, reads=[PTk, Pk], writes=[bkk[s_]])
                            if j < 5:
                                kb.I("pe", "matmul", PS[BK[s_]][:, 128:256], Pj, PTj, start=True, stop=True, reads=[PTk, Pk], writes=[bkk[s_]])
                            kb.I("pe", "matmul", PS[BK[s_]][:, 448:512], Pj, Xb[s_][:, cu, :], start=True, stop=True, reads=[Pk, f"Xb{s_}_{cu}"], writes=[bkk[s_]])
                        for s_, h in enumerate(heads):
                            if j < 6:
                                kb.cp("act", Pq[s_][:, nx, :], PS[BK[s_]][:, 0:128], reads=[bkk[s_]], writes=[f"P{s_}_{nx}"])
                            if j < 5:
                                kb.cp("act", PTq[s_][:, nx, :], PS[BK[s_]][:, 128:256], reads=[bkk[s_]], writes=[f"PT{s_}_{nx}"])
                            dst_ap = UT[:, h * 64:(h + 1) * 64] if j == 6 else Xq[s_][:, nx, :]
                            dst_k = f"UT{h}" if j == 6 else f"X{s_}_{nx}"
                            if j < 6:
                                kb.I("dve", "tensor_tensor", Xb[s_][:, nx, :], Xq[s_][:, cu, :], PS[BK[s_]][:, 448:512], ALU.subtract if j == 0 else ALU.add,
                                     reads=[bkk[s_], f"X{s_}_{cu}"], writes=[f"Xb{s_}_{nx}"])
                            kb.I("dve", "tensor_tensor", dst_ap, Xq[s_][:, cu, :], PS[BK[s_]][:, 448:512], ALU.subtract if j == 0 else ALU.add,
                                 reads=[bkk[s_], f"X{s_}_{cu}"], writes=[dst_k])
                    if own is not None:
                        for s_, h in enumerate(heads):
                            ft, Rs, fk = hv(h)
                            rho = AR_[Rs, ft, 128:256]
                            yr_ = PS[BK[s_]][:, 256:320]
                            kb.I("pe", "matmul", yr_, rho, ST[Rs, ft, :], start=True, stop=False, reads=fk + [f"ST{ft}"], writes=[bkk[s_]])
                            kb.I("pe", "matmul", yr_, NB[s_][:, 128:256], UT[:, h * 64:(h + 1) * 64], start=False, stop=False, reads=[f"NB{s_}", f"UT{h}"], writes=[bkk[s_]])
                            kb.I("pe", "matmul", yr_, KA[s_][:, 128:256], vT[:, h * 64:(h + 1) * 64], start=False, stop=True, reads=[f"KA{s_}", TMK], writes=[bkk[s_]])
                        for s_, h in enumerate(heads):
                            kb.cp("act", yT[:, h * 64:(h + 1) * 64], PS[BK[s_]][:, 256:320], reads=[bkk[s_]], writes=[f"yT{h}"])
                    for ft in range(g0 // 2, (g0 + NS) // 2):
                        h1 = 2 * ft + 1
                        bnk = BK[h1 - g0]
                        cs_ = slice(ft * 128, (ft + 1) * 128)
                        kb.I("pe", "matmul", PS[bnk][:, 0:128], bepT[:, cs_], UT[:, cs_], start=True, stop=False, reads=[TMK, f"UT{h1 - 1}", f"UT{h1}"], writes=[f"pX{bnk}_0"])
                        kb.I("pe", "matmul", PS[bnk][:, 0:128], kapT[:, cs_], vT[:, cs_], start=False, stop=True, reads=[TMK], writes=[f"pX{bnk}_0"])
                        for jj in range(2):
                            rr_ = slice(64 * jj, 64 * jj + 64)
                            kb.I("dve", "scalar_tensor_tensor", ST[rr_, ft, :], ST[rr_, ft, :], gC_[rr_, ft:ft + 1], PS[bnk][rr_, 64 * jj:64 * jj + 64], ALU.mult, ALU.add,
                                 reads=[f"pX{bnk}_0", f"ST{ft}", f"w{ft}" + SL + "g"], writes=[f"ST{ft}"])
                if dbg_here:
                    kb.D(dbg["d_yT"], yT[:], reads=[f"yT{h}" for h in range(8)], writes=["o_d_yT"])
                    kb.D(dbg["d_ST"], ST[:].rearrange("p f v -> p (f v)"), reads=[f"ST{f}" for f in range(4)], writes=["o_d_ST"])
                if own is not None:
                    for nm, tile_ap, keys in (("yr", yT[:], [f"yT{h}" for h in range(8)]), ("bv", bvT, [TMK])):
                        dst = scr[f"{nm}{sd}"][own * 128:(own + 1) * 128, :]
                        if rev:
                            kb.I("pe", "matmul", PS[5][:], antiI[:], tile_ap, start=True, stop=True, reads=keys + ["antiI"], writes=["pX5_0"])
                            kb.cp("act", flpR[:], PS[5][:], reads=["pX5_0"], writes=["flpR"])
                            kb.D(dst, flpR[:], reads=["flpR"], writes=[f"scr_{nm}{sd}_{own}"])
                        else:
                            kb.D(dst, tile_ap, reads=keys, writes=[f"scr_{nm}{sd}_{own}"])
                kb.rec = streams["s5"]
                t1_, t2_ = s5t["t1"], s5t["t2"]
                for a_ in range(4):
                    zuk = f"zu{a_}" + SL
                    for r_ in range(4):
                        q_ = a_ * 4 + r_
                        kb.I("pe", "matmul", PS[6][:, r_ * 128:(r_ + 1) * 128], Bw_re[:, q_, :], zub_[:, a_, :], start=True, stop=True, reads=["s5tab", zuk], writes=["pX6_0"])
                        kb.I("pe", "matmul", PS[7][:, r_ * 128:(r_ + 1) * 128], Bw_im[:, q_, :], zub_[:, a_, :], start=True, stop=True, reads=["s5tab", zuk], writes=["pX7_0"])
                    k3 = ["pX6_0"]; k4 = ["pX7_0"]
                    qs = slice(a_ * 4, a_ * 4 + 4)
                    Er = E_re[:, qs, :].rearrange("p q t -> p (q t)"); Ei = E_im[:, qs, :].rearrange("p q t -> p (q t)")
                    Fr = F_re[:, qs, :].rearrange("p q t -> p (q t)"); Fi = F_im[:, qs, :].rearrange("p q t -> p (q t)")
                    g1_, g2_ = s5t["Gre"], s5t["Gim"]
                    kb.I("dve", "tensor_tensor", t1_[:], PS[6][:], Er, ALU.mult, reads=k3 + ["s5tab"], writes=["s5t1"])
                    kb.I("dve", "tensor_tensor", t2_[:], PS[7][:], Ei, ALU.mult, reads=k4 + ["s5tab"], writes=["s5t2"])
                    kb.I("dve", "tensor_tensor", g1_[:], PS[7][:], Er, ALU.mult, reads=k4 + ["s5tab"], writes=["s5Gre"])
                    kb.I("dve", "tensor_tensor", g2_[:], PS[6][:], Ei, ALU.mult, reads=k3 + ["s5tab"], writes=["s5Gim"])
                    kb.I("pool", "tensor_tensor", s5t["Xre"][:], t1_[:], t2_[:], ALU.subtract, reads=["s5t1", "s5t2"], writes=["s5Xre"])
                    kb.I("dve", "tensor_tensor", s5t["Xim"][:], g1_[:], g2_[:], ALU.add, reads=["s5Gre", "s5Gim"], writes=["s5Xim"])
                    if own is None:
                        kb.I("dve", "tensor_reduce", gs_re[:, qs], s5t["Xre"][:].rearrange("p (q t) -> p q t", q=4), AXX, ALU.add, reads=["s5Xre"], writes=["gs"])
                        kb.I("dve", "tensor_reduce", gs_im[:, qs], s5t["Xim"][:].rearrange("p (q t) -> p q t", q=4), AXX, ALU.add, reads=["s5Xim"], writes=["gs"])
                        continue
                    for r_ in range(4):
                        q_ = a_ * 4 + r_
                        cs_ = slice(r_ * 128, (r_ + 1) * 128)
                        kb.I("dve", "tensor_tensor_scan", s5t["Gre"][:, cs_], ones[:], s5t["Xre"][:, cs_], car_re[:, q_:q_ + 1], ALU.mult, ALU.add,
                             reads=["s5Xre", "car", "ones"], writes=["s5Gre"])
                        kb.I("dve", "tensor_tensor_scan", s5t["Gim"][:, cs_], ones[:], s5t["Xim"][:, cs_], car_im[:, q_:q_ + 1], ALU.mult, ALU.add,
                             reads=["s5Xim", "car", "ones"], writes=["s5Gim"])
                    x1_, x2_ = s5t["Xre"], s5t["Xim"]
                    kb.I("dve", "tensor_tensor", t1_[:], s5t["Gre"][:], Fr, ALU.mult, reads=["s5Gre", "s5tab"], writes=["s5t1"])
                    kb.I("dve", "tensor_tensor", t2_[:], s5t["Gim"][:], Fi, ALU.mult, reads=["s5Gim", "s5tab"], writes=["s5t2"])
                    kb.I("pool", "tensor_tensor", x1_[:], s5t["Gim"][:], Fr, ALU.mult, reads=["s5Gim", "s5tab", "s5Xre"], writes=["s5Xre"])
                    kb.I("pool", "tensor_tensor", x2_[:], s5t["Gre"][:], Fi, ALU.mult, reads=["s5Gre", "s5tab", "s5Xim"], writes=["s5Xim"])
                    kb.I("dve", "tensor_tensor", s5t["Hre"][:], t1_[:], t2_[:], ALU.subtract, reads=["s5t1", "s5t2"], writes=["s5Hre"])
                    kb.I("pool", "tensor_tensor", s5t["Him"][:], x1_[:], x2_[:], ALU.add, reads=["s5Xre", "s5Xim"], writes=["s5Him"])
                    kb.I("dve", "tensor_copy", car_re[:, qs], s5t["Hre"][:].rearrange("p (q t) -> p q t", q=4)[:, :, 127], reads=["s5Hre"], writes=["car"])
                    kb.I("dve", "tensor_copy", car_im[:, qs], s5t["Him"][:].rearrange("p (q t) -> p q t", q=4)[:, :, 127], reads=["s5Him"], writes=["car"])
                    if own is not None:
                        for r_ in range(4):
                            q_ = a_ * 4 + r_
                            cs_ = slice(r_ * 128, (r_ + 1) * 128)
                            yb = PS[6][:, r_ * 32:(r_ + 1) * 32]
                            if sd == 0:
                                kb.I("pe", "matmul", yb, zub_[:, a_, :], diagd[:, a_, r_ * 32:(r_ + 1) * 32], start=True, stop=False, reads=[zuk, "diagd"], writes=["pX6_0"])
                            kb.I("pe", "matmul", yb, s5t["Hre"][:, cs_], Cw_re[:, q_, :], start=(sd != 0), stop=False, reads=["s5Hre", "s5tab"], writes=["pX6_0"])
                            kb.I("pe", "matmul", yb, s5t["Him"][:, cs_], Cw_imn[:, q_, :], start=False, stop=True, reads=["s5Him", "s5tab"], writes=["pX6_0"])
                        kb.cp("act", flpS[:, a_ * 128:(a_ + 1) * 128], PS[6][:, 0:128], reads=["pX6_0"], writes=["flpS"])
                if own is None:
                    F127r = F_re[:, :, 127]; F127i = F_im[:, :, 127]
                    kb.I("dve", "tensor_tensor", gs_re[:], gs_re[:], car_re[:], ALU.add, reads=["gs", "car"], writes=["gs"])
                    kb.I("dve", "tensor_tensor", gs_im[:], gs_im[:], car_im[:], ALU.add, reads=["gs", "car"], writes=["gs"])
                    kb.I("dve", "tensor_tensor", gta[:], gs_re[:], F127r, ALU.mult, reads=["gs", "s5tab"], writes=["gta"])
                    kb.I("dve", "tensor_tensor", gtb[:], gs_im[:], F127i, ALU.mult, reads=["gs", "s5tab"], writes=["gtb"])
                    kb.I("dve", "tensor_tensor", car_re[:], gta[:], gtb[:], ALU.subtract, reads=["gta", "gtb"], writes=["car"])
                    kb.I("dve", "tensor_tensor", gta[:], gs_im[:], F127r, ALU.mult, reads=["gs", "s5tab", "car"], writes=["gta"])
                    kb.I("dve", "tensor_tensor", gtb[:], gs_re[:], F127i, ALU.mult, reads=["gs", "s5tab", "car"], writes=["gtb"])
                    kb.I("dve", "tensor_tensor", car_im[:], gta[:], gtb[:], ALU.add, reads=["gta", "gtb"], writes=["car"])
                if dbg_here:
                    kb.D(dbg["d_car"][:, 0:16], car_re[:], reads=["car"], writes=["o_d_car"])
                    kb.D(dbg["d_car"][:, 16:32], car_im[:], reads=["car"], writes=["o_d_car"])
                if own is not None:
                    if dbg_here:
                        kb.D(dbg["d_ys"], flpS[:], reads=["flpS"], writes=["o_d_ys"])
                    dst = scr[f"ys{sd}"][own * 128:(own + 1) * 128, :]
                    if rev:
                        kb.I("pe", "matmul", PS[6][:], antiI[:], flpS[:], start=True, stop=True, reads=["flpS", "antiI"], writes=["pX6_0"])
                        kb.cp("dve", flpS[:], PS[6][:], reads=["pX6_0"], writes=["flpS"])
                    kb.D(dst, flpS[:], reads=["flpS"], writes=[f"scr_ys{sd}_{own}"])
                kb.rec = None
                return streams

            for sd in range(2 if not upto.startswith("p0") else 0):
                if upto == "setup":
                    break
                pre = []
                if sd == 1:
                    kb.rec = pre
                    s5_setup(sd, F0)
                if upto == "s5setup":
                    break
                stk = [f"ST{f}" for f in range(4)]
                if sd == 0:
                    kb.I("pool", "memset", ST[:], 0.0, reads=stk, writes=stk)
                    kb.I("pool", "memset", car_re[:], 0.0, reads=["car"], writes=["car"])
                    kb.I("pool", "memset", car_im[:], 0.0, reads=["car"], writes=["car"])
                if sd == 0:
                    chunks = [(True, c, None) for c in range(n_ctx)] + [(False, c, c) for c in range(n_lat_chunks[0])]
                else:
                    stk = [f"ST{f}" for f in range(4)]
                    kb.D(scr["cc_in"][:, 0:256], ST[:].rearrange("p f v -> p (f v)"), reads=stk, writes=["cc_in"])
                    kb.D(scr["cc_in"][:, 256:272], car_re[:], reads=["car"], writes=["cc_in"])
                    kb.D(scr["cc_in"][:, 272:288], car_im[:], reads=["car"], writes=["cc_in"])
                    kb.I("pool", "collective_compute", "AllReduce", ALU.add, replica_groups=[[0, 1], [2, 3], [4, 5], [6, 7]],
                         ins=[scr["cc_in"]], outs=[scr["cc_out"]], reads=["cc_in"], writes=["cc_out"])
                    ccs = s5t["t1"]
                    kb.D(ccs[:, 0:288], scr["cc_out"], reads=["cc_out", "s5t1"], writes=["s5t1"])
                    kb.I("dve", "tensor_tensor", ST[:].rearrange("p f v -> p (f v)"), ccs[:, 0:256], ST[:].rearrange("p f v -> p (f v)"), ALU.subtract,
                         reads=["s5t1"] + stk, writes=stk)
                    kb.I("dve", "tensor_tensor", car_re[:], ccs[:, 256:272], car_re[:], ALU.subtract, reads=["s5t1", "car"], writes=["car"])
                    kb.I("dve", "tensor_tensor", car_im[:], ccs[:, 272:288], car_im[:], ALU.subtract, reads=["s5t1", "car"], writes=["car"])
                    chunks = [(False, c, 31 - c) for c in range(16, 16 + n_lat_chunks[1])]
                    kb.rec = None
                prev = None
                if sd == 0:
                    pend = []
                if pre:
                    pend.append(pre)
                for k_, (is_ctx, c, own) in enumerate(chunks):
                    nxt = chunks[k_ + 1][:2] if k_ + 1 < len(chunks) else None
                    cur = process_chunk(sd, is_ctx, c, own, k_ % 2, nxt=(nxt if sd == 0 else None), first=(k_ == 0), reuse=(sd == 1))
                    pend.append(cur["front"])
                    if prev is not None:
                        pend += [prev["rwkv"], prev["s5"]]
                    prev = cur
                    if len(pend) >= SCHED_WINDOW_STREAMS:
                        kb.merge(*pend)
                        pend = []
                if prev is not None:
                    pend += [prev["rwkv"], prev["s5"]]
                if sd == 1 or SCHED_WINDOW_STREAMS < 1000:
                    kb.merge(*pend)
                    pend = []
            kb.I("dve", "tensor_copy", ssum[:, 0:1], ssum[:, 0:1], reads=list(kb.lastw.keys()), writes=list(kb.lastw.keys()) + ["fence1"])
        FENCE = ["fence1"]
        wscope.close()

        if do_tail:
            with contextlib.ExitStack() as tl:
                wob = sb(tl, "wob", [128, 8, D], BF16)
                glb = sb(tl, "glb", [128, 4, 512], BF16); g2b = sb(tl, "g2b", [128, 512])
                stg = [sb(tl, f"stg{i}", [128, 1024]) for i in range(2)]
                jobs = [("w_out", wob, kt, 0, D) for kt in range(8)] + [("gluw", glb, kt, 0, 512) for kt in range(4)]
                for i, (nm, dst, kt, c0, ncol) in enumerate(jobs):
                    s_ = stg[i % 2]; sk = f"stg{i % 2}"
                    kb.D(s_[:, 0:ncol], din[nm][kt * 128:(kt + 1) * 128, c0:c0 + ncol], reads=FENCE, writes=[sk])
                    kb.cp(("dve", "act", "pool")[i % 3], dst[:, kt, c0:c0 + ncol], s_[:, 0:ncol], reads=[sk], writes=["tw"])
                kb.D(g2b[0:96, :], din["g2"], reads=FENCE, writes=["tw2"])
                rows = {}
                for nm in ("lnw", "lnb", "glub"):
                    rows[nm] = sb(tl, "row_" + nm, [128, IN_SHAPES[nm][1]])
                    kb.D(rows[nm][:], din[nm].partition_broadcast(128), reads=FENCE, writes=["rows"])
                gmix = sb(tl, "gmix", [128, D])

                def dbl(name, shape, dt=F32):
                    return [sb(tl, f"{name}_{i_}", shape, dt) for i_ in range(2)]
                x1 = dbl("x1", [128, D])
                kb.D(gmix[:], scr["mod"][:, 2 * D:3 * D], reads=FENCE + ["scr_mod"], writes=["modt"])
                tx = dbl("tx", [128, D]); yr_a = dbl("yr_a", [128, 512]); ys_a = dbl("ys_a", [128, 512]); bv_a = dbl("bv_a", [128, 512])
                yr_b = dbl("yr_b", [128, 512]); ys_b = dbl("ys_b", [128, 512]); bv_b = dbl("bv_b", [128, 512])
                sgt = dbl("sgt", [128, 128]); mix = dbl("mix", [128, D]); st8 = dbl("st8", [128, 8]); st8b = dbl("st8b", [128, 8])
                gt = dbl("gt", [128, 512]); gt2 = dbl("gt2", [128, 512]); zT = dbl("zT", [128, 4, 128], BF16); mixT = dbl("mixT", [128, 8, 128], BF16)
                TA = (tx, yr_a, ys_a, bv_a, yr_b, ys_b, bv_b, sgt, mix, st8, st8b, gt, gt2, zT, mixT, x1)
                recA = []
                kb.rec = recA
                for oc in range(NOWN):
                    sl_ = oc % 2
                    kb.ksuf = f"#{sl_}"
                    (tx, yr_a, ys_a, bv_a, yr_b, ys_b, bv_b, sgt, mix, st8, st8b, gt, gt2, zT, mixT, x1) = (t_[sl_] for t_ in TA)
                    rsl = slice(oc * 128, (oc + 1) * 128)
                    kb.D(tx[:], din["x"][rsl, :], reads=FENCE, writes=["tx"])
                    for t_, nm in ((yr_a, "yr0"), (yr_b, "yr1"), (ys_a, "ys0"), (ys_b, "ys1"), (bv_a, "bv0"), (bv_b, "bv1")):
                        kb.D(t_[:], scr[nm][rsl, :], reads=[f"scr_{nm}_{oc}"] + FENCE, writes=["t_" + nm])
                    kb.D(sgt[0:96, :], scr["sg"][oc * 128:oc * 128 + 96, :], reads=[f"scr_sg{oc}"] + FENCE, writes=["sgt"])
                    kb.I("dve", "tensor_tensor", yr_a[:], yr_a[:], yr_b[:], ALU.add, reads=["t_yr0", "t_yr1"], writes=["t_yr0"])
                    y3 = yr_a[:].rearrange("p (h n) -> p h n", n=64)
                    kb.I("dve", "tensor_reduce", st8[:], y3, AXX, ALU.add, reads=["t_yr0"], writes=["st8"])
                    kb.ts1("dve", st8[:], st8[:], 1.0 / 64, ALU.mult, reads=["st8"], writes=["st8"])
                    kb.I("dve", "tensor_tensor", y3, y3, st8[:].unsqueeze(2).to_broadcast([128, 8, 64]), ALU.subtract, reads=["st8", "t_yr0"], writes=["t_yr0"])
                    kb.I("pool", "tensor_tensor", gt2[:], yr_a[:], yr_a[:], ALU.mult, reads=["t_yr0"], writes=["gt2"])
                    kb.I("dve", "tensor_reduce", st8b[:], gt2[:].rearrange("p (h n) -> p h n", n=64), AXX, ALU.add, reads=["gt2"], writes=["st8b"])
                    kb.I("dve", "tensor_scalar", st8b[:], st8b[:], 1.0 / 64, 64e-5, ALU.mult, ALU.add, reads=["st8b"], writes=["st8b"])
                    kb.I("act", "activation", st8b[:], st8b[:], AF.Sqrt, reads=["st8b"], writes=["st8b"])
                    kb.I("dve", "reciprocal", st8b[:], st8b[:], reads=["st8b"], writes=["st8b"])
                    kb.I("dve", "tensor_tensor", y3, y3, st8b[:].unsqueeze(2).to_broadcast([128, 8, 64]), ALU.mult, reads=["st8b", "t_yr0"], writes=["t_yr0"])
                    kb.I("pool", "tensor_tensor", yr_a[:], yr_a[:], rows["lnw"][:], ALU.mult, reads=["rows", "t_yr0"], writes=["t_yr0"])
                    kb.I("pool", "tensor_tensor", yr_a[:], yr_a[:], rows["lnb"][:], ALU.add, reads=["rows", "t_yr0"], writes=["t_yr0"])
                    kb.I("dve", "tensor_tensor", bv_a[:], bv_a[:], bv_b[:], ALU.add, reads=["t_bv0", "t_bv1"], writes=["t_bv0"])
                    kb.I("dve", "tensor_tensor", yr_a[:], yr_a[:], bv_a[:], ALU.add, reads=["t_bv0", "t_yr0"], writes=["t_yr0"])
                    kb.I("pe", "matmul", PS[0][:], sgt[0:96, :], g2b[0:96, :], start=True, stop=True, reads=["sgt", "tw2"], writes=bk(0))
                    kb.I("dve", "tensor_tensor", mix[:, 0:512], yr_a[:], PS[0][:], ALU.mult, reads=bk(0) + ["t_yr0"], writes=["mixA"])
                    kb.I("dve", "tensor_tensor", ys_a[:], ys_a[:], ys_b[:], ALU.add, reads=["t_ys0", "t_ys1"], writes=["t_ys0"])
                    kb.I("pool", "tensor_tensor", gt[:], ys_a[:], ys_a[:], ALU.mult, reads=["t_ys0"], writes=["gt"])
                    kb.I("dve", "tensor_scalar", gt[:], gt[:], 0.044715, 1.0, ALU.mult, ALU.add, reads=["gt"], writes=["gt"])
                    kb.I("dve", "tensor_tensor", gt[:], gt[:], ys_a[:], ALU.mult, reads=["gt", "t_ys0"], writes=["gt"])
                    kb.I("act", "activation", gt[:], gt[:], AF.Tanh, scale=0.7978845608028654, reads=["gt"], writes=["gt"])
                    kb.I("dve", "tensor_scalar", gt[:], gt[:], 0.5, 0.5, ALU.mult, ALU.add, reads=["gt"], writes=["gt"])
                    kb.I("dve", "tensor_tensor", ys_a[:], ys_a[:], gt[:], ALU.mult, reads=["gt", "t_ys0"], writes=["t_ys0"])
                    for j in range(4):
                        kb.I("pe", "transpose", PS[1][:, j * 128:(j + 1) * 128], ys_a[:, j * 128:(j + 1) * 128], ident[:], reads=["t_ys0", "ident"], writes=[f"pX1_{j}"])
                    kb.cp("act", zT[:].rearrange("p j t -> p (j t)"), PS[1][:], reads=bk(1), writes=["zT"])
                    for j in range(4):
                        kb.I("pe", "matmul", PS[2][:], zT[:, j, :], glb[:, j, :], start=(j == 0), stop=(j == 3), reads=["zT", "tw"], writes=bk(2))
                    kb.I("dve", "tensor_tensor", gt[:], PS[2][:], rows["glub"][:], ALU.add, reads=bk(2) + ["rows", "gt"], writes=["gt"])
                    kb.I("act", "activation", gt[:], gt[:], AF.Sigmoid, reads=["gt"], writes=["gt"])
                    kb.I("dve", "tensor_tensor", mix[:, 512:1024], ys_a[:], gt[:], ALU.mult, reads=["gt", "t_ys0"], writes=["mixB"])
                    for half in range(2):
                        for j in range(4):
                            kt = half * 4 + j
                            kb.I("pe", "transpose", PS[3][:, j * 128:(j + 1) * 128], mix[:, kt * 128:(kt + 1) * 128], ident[:], reads=["mixA", "mixB", "ident"], writes=[f"pX3_{j}"])
                        kb.cp("act" if half else "dve", mixT[:, half * 4:half * 4 + 4, :].rearrange("p j t -> p (j t)"), PS[3][:], reads=bk(3), writes=["mixT"])
                    for nh in range(2):
                        ns = slice(nh * 512, (nh + 1) * 512)
                        for kt in range(8):
                            kb.I("pe", "matmul", PS[4 + nh][:], mixT[:, kt, :], wob[:, kt, ns], start=(kt == 0), stop=(kt == 7), reads=["mixT", "tw"], writes=bk(4 + nh))
                        kb.I("dve", "tensor_tensor", x1[:, ns], PS[4 + nh][:], gmix[:, ns], ALU.mult, reads=bk(4 + nh) + ["modt"], writes=["x1"])
                        kb.I("pool", "tensor_tensor", x1[:, ns], x1[:, ns], tx[:, ns], ALU.add, reads=["x1", "tx"], writes=["x1"])
                    if debug and oc == 0:
                        kb.D(dbg["d_x1"], x1[:], reads=["x1"], writes=["o_d_x1"])
                    kb.D(scr["x1"][rsl, :], x1[:], reads=["x1"], writes=[f"scr_x1_{oc}"])
                kb.rec = None
                kb.ksuf = None
                kb.merge(recA)
                st8 = TA[9][0]
                kb.I("dve", "tensor_copy", st8[:, 0:1], st8[:, 0:1], reads=list(kb.lastw.keys()), writes=list(kb.lastw.keys()) + ["fence2"])
            FENCE = ["fence2"]
            with contextlib.ExitStack() as tl:
                w1b = sb(tl, "w1b", [128, 8, DFF], BF16); w3b = sb(tl, "w3b", [128, 8, DFF], BF16)
                w2b = sb(tl, "w2b", [128, 22, D], BF16)
                stg = [sb(tl, f"stgb{i}", [128, 1024]) for i in range(2)]
                jobs = []
                for nm, dst in (("w1", w1b), ("w3", w3b)):
                    for kt in range(8):
                        for c0 in (0, 1024, 2048):
                            jobs.append((nm, dst, kt, c0, min(1024, DFF - c0)))
                jobs += [("w2f", w2b, kt, 0, D) for kt in range(22)]
                for i, (nm, dst, kt, c0, ncol) in enumerate(jobs):
                    s_ = stg[i % 2]; sk = f"stgb{i % 2}"
                    kb.D(s_[:, 0:ncol], din[nm][kt * 128:(kt + 1) * 128, c0:c0 + ncol], reads=FENCE, writes=[sk])
                    kb.cp(("dve", "act", "pool")[i % 3], dst[:, kt, c0:c0 + ncol], s_[:, 0:ncol], reads=[sk], writes=["tw"])
                rows = {"gf": sb(tl, "row_gf", [128, D])}
                kb.D(rows["gf"][:], din["gf"].partition_broadcast(128), reads=FENCE, writes=["rows"])
                A2 = sb(tl, "A2", [128, D]); sffn = sb(tl, "sffn", [128, D]); gffn = sb(tl, "gffn", [128, D])
                def dblb(name, shape, dt=F32):
                    return [sb(tl, f"{name}_{i_}", shape, dt) for i_ in range(2)]
                hh2 = dblb("hh", [128, D]); x12 = dblb("x1b", [128, D]); outt2 = dblb("outt", [128, D])
                hh = hh2[0]
                kb.D(A2[:], din["g2n"].partition_broadcast(128), reads=FENCE, writes=["A2"])
                kb.D(hh[:], scr["mod"][:, 4 * D:5 * D], reads=FENCE + ["scr_mod"], writes=["hh#0"])
                kb.D(sffn[:], scr["mod"][:, 3 * D:4 * D], reads=FENCE + ["scr_mod"], writes=["modt"])
                kb.D(gffn[:], scr["mod"][:, 5 * D:6 * D], reads=FENCE + ["scr_mod"], writes=["modt"])
                kb.I("dve", "scalar_tensor_tensor", A2[:], hh[:], 1.0, A2[:], ALU.add, ALU.mult, reads=["A2", "hh#0"], writes=["A2"])
                hhT2 = dblb("hhT", [128, 8, 128], BF16)
                actT2 = dblb("actT", [128, 22, 128], BF16); s12 = dblb("s1", [128, 256]); ss22 = dblb("ss2", [128, 2]); rs22 = dblb("rs2", [128, 2])
                print("SBUF bytes remaining in tail B scope:", nc.sbuf_bytes_remaining)
                recB = []
                kb.rec = recB
                for oc in range(NOWN):
                    sl_ = oc % 2
                    kb.ksuf = f"#{sl_}"
                    hh, x1, outt, hhT, actT, s1, ss2, rs2 = (t_[sl_] for t_ in (hh2, x12, outt2, hhT2, actT2, s12, ss22, rs22))
                    rsl = slice(oc * 128, (oc + 1) * 128)
                    kb.D(x1[:], scr["x1"][rsl, :], reads=[f"scr_x1_{oc}"], writes=["x1"])
                    kb.I("pool", "memset", ss2[:], 0.0, reads=FENCE, writes=["ss2"])
                    kb.I("act", "activation", hh[:], x1[:], AF.Square, accum_out=ss2[:, 0:1], reads=["x1", "A2"], writes=["hh", "ss2"])
                    kb.I("dve", "tensor_scalar", rs2[:, 0:1], ss2[:, 0:1], 1.0 / D, 1e-6, ALU.mult, ALU.add, reads=["ss2"], writes=["rs2"])
                    kb.I("act", "activation", rs2[:, 0:1], rs2[:, 0:1], AF.Sqrt, reads=["rs2"], writes=["rs2"])
                    kb.I("dve", "reciprocal", rs2[:, 0:1], rs2[:, 0:1], reads=["rs2"], writes=["rs2"])
                    kb.I("dve", "scalar_tensor_tensor", hh[:], x1[:], rs2[:, 0:1], A2[:], ALU.mult, ALU.mult, reads=["x1", "rs2", "A2", "hh"], writes=["hh"])
                    kb.I("pool", "tensor_tensor", hh[:], hh[:], sffn[:], ALU.add, reads=["hh", "modt"], writes=["hh"])
                    for half in range(2):
                        for j in range(4):
                            kt = half * 4 + j
                            kb.I("pe", "transpose", PS[3][:, j * 128:(j + 1) * 128], hh[:, kt * 128:(kt + 1) * 128], ident[:], reads=["hh", "ident"], writes=[f"pX3_{j}"])
                        kb.cp("act" if half else "dve", hhT[:, half * 4:half * 4 + 4, :].rearrange("p j t -> p (j t)"), PS[3][:], reads=bk(3), writes=["hhT"])
                    for ftf in range(22):
                        sl = ftf % 2
                        fs = slice(ftf * 128, (ftf + 1) * 128)
                        bq = (0, 1, 2)[ftf % 3]
                        pa = PS[bq][:, 0:128]; pb = PS[bq][:, 128:256]
                        s1_ = s1[:, (ftf % 2) * 128:(ftf % 2) * 128 + 128]
                        for kt in range(8):
                            kb.I("pe", "matmul", pa, w1b[:, kt, fs], hhT[:, kt, :], start=(kt == 0), stop=(kt == 7), reads=["hhT", "tw"], writes=[f"pX{bq}_0"])
                        for kt in range(8):
                            kb.I("pe", "matmul", pb, w3b[:, kt, fs], hhT[:, kt, :], start=(kt == 0), stop=(kt == 7), reads=["hhT", "tw"], writes=[f"pX{bq}_0"])
                        kb.I("act", "activation", s1_, pa, AF.Silu, reads=[f"pX{bq}_0"], writes=[f"s1_{ftf % 2}"])
                        kb.I("dve", "tensor_tensor", actT[:, ftf, :], s1_, pb, ALU.mult, reads=[f"pX{bq}_0", f"s1_{ftf % 2}"], writes=[f"actT{ftf}"])
                    for nh in range(2):
                        ns = slice(nh * 512, (nh + 1) * 512)
                        bd = 4 + 2 * sl_ + nh
                        for ftf in range(22):
                            kb.I("pe", "matmul", PS[bd][:], actT[:, ftf, :], w2b[:, ftf, ns], start=(ftf == 0), stop=(ftf == 21), reads=[f"actT{ftf}", "tw"], writes=bk(bd))
                        kb.I("dve", "tensor_tensor", outt[:, ns], PS[bd][:], gffn[:, ns], ALU.mult, reads=bk(bd) + ["modt"], writes=["outt"])
                        kb.I("pool", "tensor_tensor", outt[:, ns], outt[:, ns], x1[:, ns], ALU.add, reads=["outt", "x1"], writes=["outt"])
                    kb.I("act", "activation", hh[:], outt[:], AF.Square, accum_out=ss2[:, 1:2], reads=["outt", "hh"], writes=["hh", "ss2"])
                    kb.I("dve", "tensor_scalar", rs2[:, 1:2], ss2[:, 1:2], 1.0 / D, 1e-6, ALU.mult, ALU.add, reads=["ss2"], writes=["rs2"])
                    kb.I("act", "activation", rs2[:, 1:2], rs2[:, 1:2], AF.Sqrt, reads=["rs2"], writes=["rs2"])
                    kb.I("dve", "reciprocal", rs2[:, 1:2], rs2[:, 1:2], reads=["rs2"], writes=["rs2"])
                    kb.I("dve", "scalar_tensor_tensor", outt[:], outt[:], rs2[:, 1:2], rows["gf"][:], ALU.mult, ALU.mult, reads=["outt", "rs2", "rows"], writes=["outt"])
                    kb.D(out_d[rsl, :], outt[:], reads=["outt"], writes=[f"o_out{oc}"])
                kb.rec = None
                kb.ksuf = None
                kb.merge(recB)
                kb.emit(final_keys=[k for k in kb.lastw if k.startswith("o_")])
        else:
            kb.emit(final_keys=[k for k in kb.lastw if k.startswith("o_")] + FENCE)
    return nc


def make_in_maps(inp):
    f = np.float32
    ident = np.eye(128, dtype=f)
    antiI = np.ascontiguousarray(ident[::-1])
    strict = np.triu(np.ones((128, 128), f), 1)
    incl = np.triu(np.ones((128, 128), f), 0)
    consts = {
        "ident": ident, "antiI": antiI,
        "mask_b": np.concatenate([strict, -incl], axis=1), "mask_k": np.concatenate([strict, incl], axis=1),
        "maskT": np.ascontiguousarray(strict.T), "bones": np.kron(np.eye(2, dtype=f), np.ones((64, 64), f)),
    }
    maps = []
    for core in range(8):
        b, hf = core // 2, core % 2
        dsel = [1, 0] if hf else [0, 1]
        x = inp["x"][b]; ctx = inp["ctx"][b]
        conv = inp["rwkv_conv"][0]
        w_in = inp["w_in"][0]
        if hf:
            x = x[::-1]; ctx = ctx[::-1]; conv = conv[::-1, ::-1]
            perm = np.arange(2272)
            perm[1536:1568], perm[1568:1600] = np.arange(1568, 1600), np.arange(1536, 1568)
            perm[1600:1632], perm[1632:1664] = np.arange(1632, 1664), np.arange(1600, 1632)
            w_in = w_in[:, perm]
        m = {
            "x": x, "ctx": ctx, "cc": np.stack([inp["c"][b], inp["c_ctx"]]),
            "mod_w": inp["mod_w"][0], "mod_b": inp["mod_b"][0][None], "g1": inp["norm1_g"][0][None],
            "g2n": inp["norm2_g"][0][None], "gf": inp["final_g"][None], "w_in": w_in, "w_out": inp["w_out"][0],
            "conv": conv.reshape(9, 1536), "w0": inp["rwkv_w0"][0][dsel], "w2": inp["rwkv_w2"][0][dsel],
            "a0": inp["rwkv_a0"][0][dsel], "a2": inp["rwkv_a2"][0][dsel], "g2": inp["rwkv_g2"][0],
            "kkv": inp["rwkv_kk"][0], "kav": inp["rwkv_ka"][0], "rkv": inp["rwkv_rk"][0].reshape(512),
            "lnw": inp["rwkv_ln_w"][0][None], "lnb": inp["rwkv_ln_b"][0][None],
            "lam_re": inp["s5_lam_re"][0][dsel], "lam_im": inp["s5_lam_im"][0][dsel], "lstep": inp["s5_log_step"][0][dsel],
            "b_re": inp["s5_b_re"][0], "b_im": inp["s5_b_im"][0], "c_re": inp["s5_c_re"][0], "c_im": inp["s5_c_im"][0],
            "s5d": inp["s5_d"][0], "gluw": inp["s5_glu_w"][0], "glub": inp["s5_glu_b"][0][None],
            "w1": inp["ffn_w1"][0], "w3": inp["ffn_w3"][0], "w2f": inp["ffn_w2"][0],
        }
        m.update(consts)
        maps.append({k: np.ascontiguousarray(np.asarray(v, dtype=f)).reshape(IN_SHAPES[k]) for k, v in m.items()})
    return maps


def kernel(**inputs):
    inp = {k: np.asarray(v) for k, v in inputs.items()}
    nc = build_nc()
    maps = make_in_maps(inp)
    res = run_bass_kernel_spmd(nc, maps, core_ids=list(range(8)))
    out = np.zeros((4, T_LAT, D), np.float32)
    for core in range(8):
        b, hf = core // 2, core % 2
        o = np.asarray(res.results[core]["out"], dtype=np.float32)
        if hf:
            out[b, OWN:] = o[::-1]
        else:
            out[b, :OWN] = o
    return out
```

```python
import contextlib
import numpy as np
import concourse.bass as bass
import concourse.mybir as mybir
from concourse.bass_utils import run_bass_kernel_spmd

F32 = mybir.dt.float32
BF16 = mybir.dt.bfloat16
ALU = mybir.AluOpType
AF = mybir.ActivationFunctionType
AXX = mybir.AxisListType.X

SEM_CAP = 16000
N_DMA_SEM = 24
SCHED_WINDOW_STREAMS = 1000
SAME_ENG_WAIT = True

T_LAT, T_CTX, D, DFF = 4096, 256, 1024, 2816
OWN = 2048
NOWN = OWN // 128
PI = float(np.pi)
KDEC = 0.6065306597126334


class KB:
    ENGS = ("pe", "dve", "act", "pool", "sp")

    def __init__(self, nc):
        self.nc = nc
        self.ops = {e: [] for e in self.ENGS}
        self.lastw = {}
        self.readers = {}
        self.ndma = 0
        self.rr = 0

    @staticmethod
    def _norm(reads, writes):
        r2 = [k for k in reads if not k.startswith("pX")]
        w2 = [k for k in writes if not k.startswith("pX")]
        banks = {"bank" + k[2:].split("_")[0] for k in list(reads) + list(writes) if k.startswith("pX")}
        return r2, w2 + sorted(banks)

    def _deps(self, me, reads, writes):
        reads, writes = self._norm(reads, writes)
        deps = set()
        for k in reads:
            w = self.lastw.get(k)
            if w is not None:
                deps.add(w)
        for k in writes:
            w = self.lastw.get(k)
            if w is not None:
                deps.add(w)
            for r in self.readers.get(k, ()):
                deps.add(r)
        deps.discard(me)
        for k in reads:
            self.readers.setdefault(k, []).append(me)
        for k in writes:
            self.lastw[k] = me
            self.readers[k] = []
        return deps

    mute = False
    rec = None
    ksuf = None
    KGLOBAL = ("pX", "scr_", "o_", "fence", "rows", "tw", "modt", "ident", "A2")

    def _sfx(self, keys):
        if not self.ksuf:
            return list(keys)
        return [k if k.startswith(self.KGLOBAL) else k + self.ksuf for k in keys]

    @staticmethod
    def _est(op):
        def nfree(ap):
            n = 1
            for d in ap.shape[1:]:
                n *= d
            return n
        if op[0] == "D":
            ap = op[1]
            nbytes = nfree(ap) * ap.shape[0] * (2 if ap.dtype == BF16 else 4)
            return "sp", 0.08, 2.2 + nbytes / 1.0e5
        eng, meth, args = op[1], op[2], op[3]
        if meth == "collective_compute":
            return eng, 0.5, 30.0
        n = nfree(args[0])
        if eng == "pe":
            f32 = args[1].dtype == F32
            d = 0.09 + n * (0.0017 if f32 else 0.00045)
        elif eng == "dve":
            d = 0.25 + n * 0.00104 * (6.0 if meth == "reciprocal" else 1.0)
        elif eng == "act":
            d = 0.2 + n * 0.00104
        else:
            d = 0.45 + n * 0.0026
        return eng, d, d

    def merge(self, *streams):
        ops = [op for st_ in streams for op in st_]
        n = len(ops)
        lastw, readers = {}, {}
        preds = [set() for _ in range(n)]
        for i, op in enumerate(ops):
            r_, w_ = (op[5], op[6]) if op[0] == "I" else (op[4], op[5])
            r_, w_ = self._norm(r_, w_)
            for k in r_:
                if k in lastw:
                    preds[i].add(lastw[k])
            for k in w_:
                if k in lastw:
                    preds[i].add(lastw[k])
                preds[i].update(readers.get(k, ()))
            preds[i].discard(i)
            for k in r_:
                readers.setdefault(k, []).append(i)
            for k in w_:
                lastw[k] = i
                readers[k] = []
        succs = [[] for _ in range(n)]
        for i in range(n):
            for p in preds[i]:
                succs[p].append(i)
        est = [self._est(op) for op in ops]
        cp = [0.0] * n
        for i in range(n - 1, -1, -1):
            cp[i] = est[i][2] + max((cp[j] for j in succs[i]), default=0.0)
        npred = [len(p) for p in preds]
        ready = [i for i in range(n) if npred[i] == 0]
        fin = [0.0] * n
        eng_free = {e: 0.0 for e in self.ENGS}
        LAT = 1.0
        while ready:
            best, bkey = None, None
            for i in ready:
                e = est[i][0]
                t = eng_free[e]
                for p in preds[i]:
                    tp = fin[p] + (0.05 if est[p][0] == e else LAT)
                    if tp > t:
                        t = tp
                key = (t - 0.02 * cp[i], i)
                if bkey is None or key < bkey:
                    best, bkey, bt = i, key, t
            i = best
            ready.remove(i)
            e = est[i][0]
            eng_free[e] = bt + est[i][1]
            fin[i] = bt + est[i][2]
            op = ops[i]
            if op[0] == "I":
                self.I(op[1], op[2], *op[3], reads=op[5], writes=op[6], **op[4])
            else:
                self.D(op[1], op[2], reads=op[4], writes=op[5], **op[3])
            for j in succs[i]:
                npred[j] -= 1
                if npred[j] == 0:
                    ready.append(j)

    def I(self, eng, meth, *args, reads=(), writes=(), **kw):
        if self.mute:
            return
        if self.rec is not None:
            self.rec.append(("I", eng, meth, args, kw, self._sfx(reads), self._sfx(writes)))
            return
        idx = len(self.ops[eng])
        deps = self._deps((eng, idx), list(reads), list(writes))
        self.ops[eng].append(((meth, args, kw), deps, None))

    def D(self, out, in_, reads=(), writes=(), **kw):
        if self.mute:
            return
        if self.rec is not None:
            self.rec.append(("D", out, in_, dict(kw), self._sfx(reads), self._sfx(writes)))
            return
        k = self.ndma
        self.ndma += 1
        deps = self._deps(("dma", k), list(reads), list(writes))
        kw = dict(kw)
        kw["out"] = out
        kw["in_"] = in_
        self.ops["sp"].append((("dma_start", (), kw), deps, k))

    def cp(self, eng, out, in_, reads=(), writes=()):
        self.I(eng, "copy" if eng == "act" else "tensor_copy", out, in_, reads=reads, writes=writes)

    def ts1(self, eng, out, in0, s, op, reads=(), writes=()):
        self.I(eng, "tensor_scalar", out, in0, s, 0.0, op, ALU.add, reads=reads, writes=writes)

    def ew(self):
        self.rr += 1
        return "dve" if (self.rr % 3) else "pool"

    def ev(self):
        self.rr += 1
        return "dve" if (self.rr % 2) else "act"

    def emit(self, final_keys=()):
        nc = self.nc
        me = ("sp", len(self.ops["sp"]))
        deps = self._deps(me, list(final_keys), [])
        self.ops["sp"].append((None, deps, None))
        nsem = {e: (len(self.ops[e]) + SEM_CAP - 1) // SEM_CAP + 1 for e in self.ENGS}
        with contextlib.ExitStack() as st:
            sems = {e: [st.enter_context(nc.semaphore(f"s_{e}{i}")) for i in range(nsem[e])]
                    for e in self.ENGS}
            dsems = [st.enter_context(nc.semaphore(f"s_dma{i}")) for i in range(N_DMA_SEM)]
            block = st.enter_context(nc.Block())

            def waitspec(p):
                if p[0] == "dma":
                    k = p[1]
                    return ("d", k % N_DMA_SEM), dsems[k % N_DMA_SEM], 16 * (k // N_DMA_SEM + 1)
                e, i = p
                return (e, i // SEM_CAP), sems[e][i // SEM_CAP], i % SEM_CAP + 1

            def run(ename, eobj):
                waited = {}
                for idx, (fn, deps, dk) in enumerate(self.ops[ename]):
                    specs = []
                    for p in deps:
                        if p[0] == ename and (ename == "pe" or not SAME_ENG_WAIT):
                            continue
                        specs.append(waitspec(p))
                    if dk is not None and dk >= N_DMA_SEM:
                        specs.append((("d", dk % N_DMA_SEM), dsems[dk % N_DMA_SEM],
                                      16 * (dk // N_DMA_SEM)))
                    for sid, sem, val in specs:
                        if waited.get(sid, 0) >= val:
                            continue
                        waited[sid] = val
                        eobj.wait_ge(sem, val)
                    if fn is None:
                        continue
                    meth, args, kw = fn
                    ins = getattr(eobj, meth)(*args, **kw)
                    if dk is not None:
                        ins.then_inc(dsems[dk % N_DMA_SEM], 16)
                    else:
                        ins.then_inc(sems[ename][idx // SEM_CAP], 1)

            @block.tensor
            def _(e):
                run("pe", e)

            @block.vector
            def _(e):
                run("dve", e)

            @block.scalar
            def _(e):
                run("act", e)

            @block.gpsimd
            def _(e):
                run("pool", e)

            @block.sync
            def _(e):
                run("sp", e)


IN_SHAPES = {
    "x": [T_LAT, D], "ctx": [T_CTX, D], "cc": [2, D], "mod_w": [D, 6 * D], "mod_b": [1, 6 * D],
    "g1": [1, D], "g2n": [1, D], "gf": [1, D], "w_in": [D, 2272], "w_out": [D, D],
    "conv": [9, 1536], "w0": [2, 512], "w2": [2, 32, 512], "a0": [2, 512], "a2": [2, 32, 512],
    "g2": [96, 512], "kkv": [512], "kav": [512], "rkv": [512], "lnw": [1, 512], "lnb": [1, 512],
    "lam_re": [2, 32, 64], "lam_im": [2, 32, 64], "lstep": [2, 32],
    "b_re": [32, 64, 16], "b_im": [32, 64, 16], "c_re": [32, 16, 64], "c_im": [32, 16, 64],
    "s5d": [512], "gluw": [512, 512], "glub": [1, 512],
    "w1": [D, DFF], "w3": [D, DFF], "w2f": [DFF, D],
    "ident": [128, 128], "antiI": [128, 128], "mask_b": [128, 256], "mask_k": [128, 256],
    "maskT": [128, 128], "bones": [128, 128],
}

DBG_SHAPES = {"d_mod": [128, 6 * D], "d_AB": [128, 32], "d_z": [128, 3 * 256], "d_zu": [128, 512],
              "d_rkvc": [128, 3 * 128], "d_yT": [128, 512], "d_ST": [128, 256], "d_ys": [128, 512],
              "d_x1": [128, D], "d_car": [128, 32], "d_F": [128, 2048], "d_Bw": [128, 2048]}


def build_nc(n_lat_chunks=(16, 16), do_tail=True, debug=False, upto="full", n_ctx=2):
    nc = bass.Bass("TRN2", target_bir_lowering=False)
    din = {k: nc.dram_tensor(k, s, F32, kind="ExternalInput").ap() for k, s in IN_SHAPES.items()}
    out_d = nc.dram_tensor("out", [OWN, D], F32, kind="ExternalOutput").ap()
    scr = {}
    for nm in ("yr0", "yr1", "ys0", "ys1", "bv0", "bv1"):
        scr[nm] = nc.dram_tensor("scr_" + nm, [OWN, 512], F32, kind="Internal").ap()
    scr["sg"] = nc.dram_tensor("scr_sg", [NOWN * 128, 128], F32, kind="Internal").ap()
    scr["x1"] = nc.dram_tensor("scr_x1", [OWN, D], F32, kind="Internal").ap()
    scr["mod"] = nc.dram_tensor("scr_mod", [128, 6 * D], F32, kind="Internal").ap()
    scr["dg"] = nc.dram_tensor("scr_dg", [12, 128, 9 * 128], BF16, kind="Internal").ap()
    scr["fr"] = nc.dram_tensor("scr_fr", [NOWN * 4 * 128, 512], F32, kind="Internal").ap()
    scr["fz"] = nc.dram_tensor("scr_fz", [NOWN * 128, 128], F32, kind="Internal").ap()
    scr["fu"] = nc.dram_tensor("scr_fu", [NOWN * 128, 512], BF16, kind="Internal").ap()
    scr["cc_in"] = nc.dram_tensor("scr_cc_in", [128, 288], F32, kind="Internal").ap()
    scr["cc_out"] = nc.dram_tensor("scr_cc_out", [128, 288], F32, kind="Internal").ap()
    dbg = {}
    if debug:
        for nm, shp in DBG_SHAPES.items():
            dbg[nm] = nc.dram_tensor(nm, shp, F32, kind="ExternalOutput").ap()
    kb = KB(nc)
    cnt = [0]

    with contextlib.ExitStack() as g:
        def sb(st, name, shape, dt=F32):
            cnt[0] += 1
            return st.enter_context(nc.sbuf_tensor(f"sb{cnt[0]}_{name}", shape, dt))

        ident = sb(g, "ident", [128, 128]); antiI = sb(g, "antiI", [128, 128])
        mask_b = sb(g, "mask_b", [128, 256]); mask_k = sb(g, "mask_k", [128, 256])
        maskT = sb(g, "maskT", [128, 128]); bones = sb(g, "bones", [128, 128])
        ones = sb(g, "ones", [128, 128])
        for nm, t in (("ident", ident), ("antiI", antiI), ("mask_b", mask_b), ("mask_k", mask_k),
                      ("maskT", maskT), ("bones", bones)):
            kb.D(t[:], din[nm], writes=[nm])
        kb.I("pool", "memset", ones[:], 1.0, writes=["ones"])
        AB = sb(g, "AB", [128, 4, 8])
        cnt[0] += 1
        PS = [g.enter_context(nc.psum_tensor(f"psbank{i}", [128, 512], F32)) for i in range(8)]

        def bk(i):
            return [f"pX{i}_{j}" for j in range(4)]

        wscope = contextlib.ExitStack()
        win = sb(wscope, "win", [128, 8, 2272], BF16)
        car_re = sb(wscope, "car_re", [128, 16]); car_im = sb(wscope, "car_im", [128, 16])
        gs_re = sb(wscope, "gs_re", [128, 16]); gs_im = sb(wscope, "gs_im", [128, 16]); gta = sb(wscope, "gta", [128, 16]); gtb = sb(wscope, "gtb", [128, 16])
        Bw_re = sb(wscope, "Bw_re", [128, 16, 128], BF16); Bw_im = sb(wscope, "Bw_im", [128, 16, 128], BF16)
        Cw_re = sb(wscope, "Cw_re", [128, 16, 32]); Cw_imn = sb(wscope, "Cw_imn", [128, 16, 32])
        E_re = sb(wscope, "E_re", [128, 16, 128]); E_im = sb(wscope, "E_im", [128, 16, 128])
        F_re = sb(wscope, "F_re", [128, 16, 128]); F_im = sb(wscope, "F_im", [128, 16, 128])
        s5t = {nm: sb(wscope, "s5" + nm, [128, 512]) for nm in ("t1", "t2", "Xre", "Xim", "Gre", "Gim", "Hre", "Him")}

        def s5_setup(sd, F0=()):
            K = "s5s"
            S5K = ["s5t1", "s5t2", "s5Xre", "s5Xim", "s5Gre", "s5Gim", "s5Hre", "s5Him", "s5tab", K]
            with contextlib.ExitStack() as t_:
                sm = sb(t_, "s5sm", [128, 20, 16])
                (lre, lim, stp, th, th2, tmpc, mag, imag, sn, csn, lbr, lbi, den, nr, qre, qim, u1, ivr, ivi) = [sm[:, i_, :] for i_ in range(19)]
                bre = s5t["Gre"][:, 0:256].rearrange("p (q h) -> p q h", h=16); bim = s5t["Gre"][:, 256:512].rearrange("p (q h) -> p q h", h=16)
                bbr = s5t["Gim"][:, 0:256].rearrange("p (q h) -> p q h", h=16); bbi = s5t["Gim"][:, 256:512].rearrange("p (q h) -> p q h", h=16)
                v1 = s5t["Hre"][:, 0:256].rearrange("p (q h) -> p q h", h=16); cst = s5t["Hre"][:, 256:512].rearrange("p (q h) -> p q h", h=16)
                bdw = s5t["Xre"][:].rearrange("p (r c) -> p r c", c=128)

                def V(meth, *args, eng="dve", **kw):
                    kb.I(eng, meth, *args, reads=S5K, writes=S5K, **kw)

                kb.D(lre, din["lam_re"][sd].rearrange("(q g) p -> (g p) q", g=2), reads=list(F0) + S5K, writes=S5K, allow_slow_non_contiguous=True)
                kb.D(lim, din["lam_im"][sd].rearrange("(q g) p -> (g p) q", g=2), reads=list(F0), writes=[K], allow_slow_non_contiguous=True)
                for g2_ in range(2):
                    kb.D(stp[64 * g2_:64 * g2_ + 64, :],
                         din["lstep"][sd:sd + 1, :].rearrange("o (q g) -> o q g", g=2)[:, :, g2_].partition_broadcast(64),
                         reads=list(F0), writes=[K], allow_slow_non_contiguous=True)
                kb.D(bre, din["b_re"].rearrange("(q g) p h -> (g p) q h", g=2), reads=list(F0), writes=[K])
                kb.D(bim, din["b_im"].rearrange("(q g) p h -> (g p) q h", g=2), reads=list(F0), writes=[K])
                V("activation", stp, stp, AF.Exp, eng="act")
                V("tensor_tensor", mag, lre, stp, ALU.mult)
                V("tensor_tensor", th, lim, stp, ALU.mult)
                V("activation", imag, mag, AF.Exp, eng="act", scale=-1.0)
                V("activation", mag, mag, AF.Exp, eng="act")
                V("tensor_copy", u1, th)
                for m in (PI, 3 * PI, 5 * PI):
                    V("tensor_scalar", tmpc, th, m, -2 * PI, ALU.is_ge, ALU.mult)
                    V("tensor_tensor", u1, u1, tmpc, ALU.add)
                V("tensor_scalar", u1, u1, 0.125, 0.0, ALU.mult, ALU.add)
                V("tensor_tensor", th2, u1, u1, ALU.mult)
                V("tensor_scalar", sn, th2, -1.0 / 5040, 1.0 / 120, ALU.mult, ALU.add)
                V("tensor_tensor", sn, sn, th2, ALU.mult)
                V("tensor_scalar", sn, sn, -1.0 / 6, 0.0, ALU.add, ALU.add)
                V("tensor_tensor", sn, sn, th2, ALU.mult)
                V("tensor_scalar", sn, sn, 1.0, 0.0, ALU.add, ALU.add)
                V("tensor_tensor", sn, sn, u1, ALU.mult)
                V("tensor_scalar", csn, th2, 1.0 / 40320, -1.0 / 720, ALU.mult, ALU.add)
                V("tensor_tensor", csn, csn, th2, ALU.mult)
                V("tensor_scalar", csn, csn, 1.0 / 24, 0.0, ALU.add, ALU.add)
                V("tensor_tensor", csn, csn, th2, ALU.mult)
                V("tensor_scalar", csn, csn, -0.5, 0.0, ALU.add, ALU.add)
                V("tensor_tensor", csn, csn, th2, ALU.mult)
                V("tensor_scalar", csn, csn, 1.0, 0.0, ALU.add, ALU.add)
                for _ in range(3):
                    V("tensor_tensor", tmpc, csn, sn, ALU.mult)
                    V("tensor_tensor", th2, sn, sn, ALU.mult)
                    V("tensor_tensor", csn, csn, csn, ALU.mult)
                    V("tensor_tensor", csn, csn, th2, ALU.subtract)
                    V("tensor_scalar", sn, tmpc, 2.0, 0.0, ALU.mult, ALU.add)
                V("tensor_tensor", lbr, mag, csn, ALU.mult)
                V("tensor_tensor", lbi, mag, sn, ALU.mult)
                V("tensor_tensor", ivr, imag, csn, ALU.mult)
                V("scalar_tensor_tensor", ivi, imag, -1.0, sn, ALU.mult, ALU.mult)
                V("tensor_tensor", den, lre, lre, ALU.mult)
                V("tensor_tensor", u1, lim, lim, ALU.mult)
                V("tensor_tensor", den, den, u1, ALU.add)
                V("reciprocal", den, den)
                V("tensor_scalar", nr, lbr, -1.0, 0.0, ALU.add, ALU.add)
                V("tensor_tensor", qre, nr, lre, ALU.mult)
                V("tensor_tensor", u1, lbi, lim, ALU.mult)
                V("tensor_tensor", qre, qre, u1, ALU.add)
                V("tensor_tensor", qre, qre, den, ALU.mult)
                V("tensor_tensor", qim, lbi, lre, ALU.mult)
                V("tensor_tensor", u1, nr, lim, ALU.mult)
                V("tensor_tensor", qim, qim, u1, ALU.subtract)
                V("tensor_tensor", qim, qim, den, ALU.mult)
                qreb = qre.unsqueeze(2).to_broadcast([128, 16, 16]); qimb = qim.unsqueeze(2).to_broadcast([128, 16, 16])
                V("tensor_tensor", bbr, bre, qreb, ALU.mult)
                V("tensor_tensor", v1, bim, qimb, ALU.mult)
                V("tensor_tensor", bbr, bbr, v1, ALU.subtract)
                V("tensor_tensor", bbi, bim, qreb, ALU.mult)
                V("tensor_tensor", v1, bre, qimb, ALU.mult)
                V("tensor_tensor", bbi, bbi, v1, ALU.add)
                for src_, dstT in ((bbr, Bw_re), (bbi, Bw_im)):
                    for qq in range(4):
                        V("memset", bdw, 0.0)
                        for r_ in range(4):
                            for g2_ in range(2):
                                c0 = 32 * r_ + 16 * g2_
                                V("tensor_copy", bdw[64 * g2_:64 * g2_ + 64, r_, c0:c0 + 16], src_[64 * g2_:64 * g2_ + 64, qq * 4 + r_, :])
                        for r_ in range(4):
                            kb.I("pe", "transpose", PS[4][:, r_ * 128:(r_ + 1) * 128], bdw[:, r_, :], ident[:],
                                 reads=S5K + ["ident"], writes=[f"pX4_{r_}"])
                        kb.I("dve", "tensor_copy", dstT[:, qq * 4:(qq + 1) * 4, :], PS[4][:].rearrange("p (r c) -> p r c", r=4),
                             reads=bk(4) + S5K, writes=S5K)
                cpad = s5t["Xim"][:].rearrange("p (j c) -> p j c", c=128)
                for nm, dstC, sc in (("c_re", Cw_re, 1.0), ("c_im", Cw_imn, -1.0)):
                    csrc = din[nm].rearrange("g h p -> (g h) p").rearrange("(j r) p -> r j p", r=128)
                    for hh_ in range(2):
                        kb.D(cpad[:, :, 64 * hh_:64 * hh_ + 64], csrc, reads=S5K, writes=S5K)
                    for j_ in range(4):
                        kb.I("pe", "transpose", PS[4][:, j_ * 128:(j_ + 1) * 128], cpad[:, j_, :], ident[:], reads=S5K + ["ident"], writes=[f"pX4_{j_}"])
                    V("memset", dstC[:], 0.0)
                    for g2_ in range(2):
                        srcv = PS[4][64 * g2_:64 * g2_ + 64, :].rearrange("p (q g h) -> p q g h", g=2, h=16)[:, :, g2_, :]
                        kb.I("dve", "tensor_scalar", dstC[64 * g2_:64 * g2_ + 64, :, 16 * g2_:16 * g2_ + 16], srcv, sc, 0.0, ALU.mult, ALU.add,
                             reads=bk(4) + S5K, writes=S5K)
                for (Tr, Ti, sr, si) in ((F_re, F_im, lbr, lbi), (E_re, E_im, ivr, ivi)):
                    V("tensor_copy", Tr[:, :, 0:1], sr.unsqueeze(2))
                    V("tensor_copy", Ti[:, :, 0:1], si.unsqueeze(2))
                    n_ = 1
                    while n_ < 128:
                        for qh in range(2):
                            qsl = slice(8 * qh, 8 * qh + 8)
                            pr = Tr[:, qsl, n_ - 1:n_].to_broadcast([128, 8, n_]); pim = Ti[:, qsl, n_ - 1:n_].to_broadcast([128, 8, n_])
                            t1_ = s5t["t1"][:, 0:8 * n_].rearrange("p (q n) -> p q n", q=8)
                            t2_ = s5t["t2"][:, 0:8 * n_].rearrange("p (q n) -> p q n", q=8)
                            V("tensor_tensor", t1_, Tr[:, qsl, 0:n_], pr, ALU.mult)
                            V("tensor_tensor", t2_, Ti[:, qsl, 0:n_], pim, ALU.mult)
                            V("tensor_tensor", Tr[:, qsl, n_:2 * n_], t1_, t2_, ALU.subtract)
                            V("tensor_tensor", t1_, Tr[:, qsl, 0:n_], pim, ALU.mult)
                            V("tensor_tensor", t2_, Ti[:, qsl, 0:n_], pr, ALU.mult)
                            V("tensor_tensor", Ti[:, qsl, n_:2 * n_], t1_, t2_, ALU.add)
                        n_ *= 2
                if debug and sd == 0:
                    kb.D(dbg["d_F"], F_re[:].rearrange("p q t -> p (q t)"), reads=S5K, writes=["o_d_F"])
                V("tensor_copy", lre[:, 0:1], lre[:, 0:1])

        with contextlib.ExitStack() as p0:
            rec0 = []
            kb.rec = rec0
            stage = [sb(p0, f"stage{i}", [128, 2272]) for i in range(2)]
            for kt in range(8):
                s_ = stage[kt % 2]; sk = f"stage{kt % 2}"
                kb.D(s_[:], din["w_in"][kt * 128:(kt + 1) * 128, :], writes=[sk])
                kb.cp("act" if kt % 2 else "pool", win[:, kt, :], s_[:], reads=[sk], writes=["win"])
            modb = sb(p0, "modb", [128, 6 * D])
            cT = sb(p0, "cT", [128, 2, 8]); cs = sb(p0, "cs", [128, 2, 8])
            lbx = sb(p0, "lbx", [128, 8, 128]); lbc = sb(p0, "lbc", [128, 8, 128])
            modc = sb(p0, "modc", [128, 2 * D]); mbias = sb(p0, "mbias", [128, 6 * D])
            g1b = sb(p0, "g1b", [128, D]); tmpA = sb(p0, "tmpA", [128, D])
            wbuf = [sb(p0, f"wbuf{i}", [128, 512]) for i in range(4)]
            for j_ in range(2):
                kb.D(cT[:, j_, :], din["cc"][j_].rearrange("(k p) -> p k", p=128), writes=["cT"], allow_slow_non_contiguous=True)
            kb.D(mbias[:], din["mod_b"].partition_broadcast(128), writes=["mbias"])
            kb.D(g1b[:], din["g1"].partition_broadcast(128), writes=["g1b"])
            kb.I("act", "activation", cs[:], cT[:], AF.Silu, reads=["cT"], writes=["cs"])
            for kt in range(8):
                kb.I("dve", "tensor_copy", lbx[:, kt, :], cs[:, 0, kt:kt + 1].to_broadcast([128, 128]), reads=["cs"], writes=["lbx"])
                kb.I("pool", "tensor_copy", lbc[:, kt, :], cs[:, 1, kt:kt + 1].to_broadcast([128, 128]), reads=["cs"], writes=["lbc"])
            i = 0
            for n in range(12 if upto != "p0a" else 0):
                ns = slice(n * 512, (n + 1) * 512)
                for kt in range(8):
                    wb = wbuf[i % 4]; wk = f"wbuf{i % 4}"; i += 1
                    kb.D(wb[:], din["mod_w"][kt * 128:(kt + 1) * 128, ns], writes=[wk])
                    kb.I("pe", "matmul", PS[0][:], lbx[:, kt, :], wb[:], start=(kt == 0), stop=(kt == 7), reads=[wk, "lbx"], writes=bk(0))
                    if n < 4:
                        kb.I("pe", "matmul", PS[1][:], lbc[:, kt, :], wb[:], start=(kt == 0), stop=(kt == 7), reads=[wk, "lbc"], writes=bk(1))
                kb.I("dve", "tensor_tensor", modb[:, ns], PS[0][:], mbias[:, ns], ALU.add, reads=bk(0) + ["mbias"], writes=["modb"])
                if n < 4:
                    kb.I("dve", "tensor_tensor", modc[:, ns], PS[1][:], mbias[:, ns], ALU.add, reads=bk(1) + ["mbias"], writes=["modc"])
            for which, src_ in ((0, modb), (1, modc)) if upto not in ("p0a", "p0b") else ():
                kb.I("dve", "scalar_tensor_tensor", tmpA[:], src_[:, D:2 * D], 1.0, g1b[:], ALU.add, ALU.mult,
                     reads=["modb", "modc", "g1b"], writes=["tmpA"])
                for half in range(2):
                    for j in range(4):
                        kt = half * 4 + j
                        kb.I("pe", "transpose", PS[2][:, j * 128:(j + 1) * 128], tmpA[:, kt * 128:(kt + 1) * 128], ident[:],
                             reads=["tmpA", "ident"], writes=[f"pX2_{j}"])
                        kb.I("pe", "transpose", PS[3][:, j * 128:(j + 1) * 128], src_[:, kt * 128:(kt + 1) * 128], ident[:],
                             reads=["modb", "modc", "ident"], writes=[f"pX3_{j}"])
                    for j in range(4):
                        kt = half * 4 + j
                        kb.I("dve", "tensor_copy", AB[:, 2 * which, kt:kt + 1], PS[2][:, j * 128:j * 128 + 1], reads=[f"pX2_{j}"], writes=["AB"])
                        kb.I("dve", "tensor_copy", AB[:, 2 * which + 1, kt:kt + 1], PS[3][:, j * 128:j * 128 + 1], reads=[f"pX3_{j}"], writes=["AB"])
            if upto not in ("p0a", "p0b", "p0c"):
                kb.D(scr["mod"], modb[:], reads=["modb"], writes=["scr_mod"])
            if debug and upto not in ("p0a", "p0b", "p0c"):
                kb.D(dbg["d_mod"], modb[:], reads=["modb"], writes=["o_d_mod"])
                kb.D(dbg["d_AB"], AB[:].rearrange("p a k -> p (a k)"), reads=["AB"], writes=["o_d_AB"])
            kb.rec = None
            recS = []
            if not upto.startswith("p0"):
                kb.rec = recS
                s5_setup(0)
                kb.rec = None
            kb.merge(rec0, recS)
            kb.I("dve", "tensor_copy", tmpA[:, 0:1], tmpA[:, 0:1],
                 reads=list(kb.lastw.keys()), writes=list(kb.lastw.keys()) + ["fence0"])
        FENCE0 = ["fence0"]
        if upto.startswith("p0"):
            kb.mute = True

        with contextlib.ExitStack() as sw:
            F0 = FENCE0
            kkc = sb(sw, "kkc", [128, 4]); kac = sb(sw, "kac", [128, 4]); omka = sb(sw, "omka", [128, 4])
            rkc = sb(sw, "rkc", [128, 4]); w0c = sb(sw, "w0c", [128, 2, 4]); a0c = sb(sw, "a0c", [128, 2, 4])
            cw = sb(sw, "cw", [128, 12, 9]); s5dc = sb(sw, "s5dc", [128, 4])
            w2pad = sb(sw, "w2pad", [128, 2, 512]); a2pad = sb(sw, "a2pad", [128, 2, 512])
            diagd = sb(sw, "diagd", [128, 4, 128], BF16)
            for t, nm in ((kkc, "kkv"), (kac, "kav"), (rkc, "rkv"), (s5dc, "s5d")):
                kb.D(t[:], din[nm].rearrange("(f p) -> p f", p=128), reads=F0, writes=["cols"], allow_slow_non_contiguous=True)
            for t, nm in ((w0c, "w0"), (a0c, "a0")):
                for d_ in range(2):
                    kb.D(t[:, d_, :], din[nm][d_].rearrange("(f p) -> p f", p=128), reads=F0, writes=["cols"], allow_slow_non_contiguous=True)
            for tp in range(9):
                kb.D(cw[:, :, tp], din["conv"][tp].rearrange("(f p) -> p f", p=128), reads=F0, writes=["cols"], allow_slow_non_contiguous=True)
            kb.I("dve", "tensor_scalar", omka[:], kac[:], -1.0, 1.0, ALU.mult, ALU.add, reads=["cols"], writes=["cols2"])
            kb.I("pool", "memset", w2pad[:], 0.0, reads=F0, writes=["w2pad"])
            kb.I("pool", "memset", a2pad[:], 0.0, reads=F0, writes=["a2pad"])
            for d_ in range(2):
                kb.D(w2pad[32 * d_:32 * d_ + 32, d_, :], din["w2"][d_], reads=["w2pad"], writes=["w2pad"])
                kb.D(a2pad[64 + 32 * d_:96 + 32 * d_, d_, :], din["a2"][d_], reads=["a2pad"], writes=["a2pad"])
            for ct in range(4):
                kb.ts1("dve", diagd[:, ct, :], ident[:], s5dc[:, ct:ct + 1], ALU.mult, reads=["cols", "ident"], writes=["diagd"])

            ST = sb(sw, "ST", [128, 4, 64])
            xw = sb(sw, "xw", [128, 2, D]); junk = sb(sw, "junk", [128, D], BF16)
            ssum = sb(sw, "ssum", [128, 2]); rstd = sb(sw, "rstd", [128, 2])
            hT = sb(sw, "hT", [128, 8, 256], BF16)
            zrkv2 = [sb(sw, f"zrkv{i}", [128, 3, 264], BF16) for i in range(2)]; zl = sb(sw, "zl", [128, 128]); zg = sb(sw, "zg", [128, 128])
            dgb = [sb(sw, f"dgb{i}", [128, 9, 128], BF16) for i in range(2)]
            rkvc2 = [sb(sw, f"rkvc{i}", [128, 3, 128]) for i in range(2)]
            W2 = {}
            for nm in ("kk", "t1", "t2", "ld", "a", "cs", "Eni", "b", "kd", "bep", "kap"):
                W2[nm] = [sb(sw, f"{nm}{i}", [128, 128]) for i in range(2)]
            zub = [sb(sw, f"zub{i}", [128, 4, 128], BF16) for i in range(2)]
            BE = [sb(sw, f"BE{i}", [128, 4, 128]) for i in range(2)]; KAP = [sb(sw, f"KAP{i}", [128, 4, 128]) for i in range(2)]
            AR = [sb(sw, f"AR{i}", [128, 4, 256]) for i in range(2)]; gC = [sb(sw, f"gC{i}", [128, 4]) for i in range(2)]
            tot = sb(sw, "tot", [128, 4])
            TMt = [sb(sw, f"TMt{i}", [128, 4, 512]) for i in range(2)]
            UT = sb(sw, "UT", [128, 512]); yT = sb(sw, "yT", [128, 512])
            flpR = sb(sw, "flpR", [128, 512]); flpS = sb(sw, "flpS", [128, 512])
            NS = 4
            NB = [sb(sw, f"NB{i}", [128, 256]) for i in range(NS)]; KA = [sb(sw, f"KA{i}", [128, 256]) for i in range(NS)]
            Pq = [sb(sw, f"Pq{i}", [128, 2, 128], BF16) for i in range(NS)]; PTq = [sb(sw, f"PTq{i}", [128, 2, 128], BF16) for i in range(NS)]
            Xq = [sb(sw, f"Xq{i}", [128, 2, 64]) for i in range(NS)]; Xb = [sb(sw, f"Xb{i}", [128, 2, 64], BF16) for i in range(NS)]
            for cf in range(12):
                for tp in range(9):
                    kb.ts1("dve" if tp % 2 else "pool", dgb[cf % 2][:, tp, :], ident[:], cw[:, cf, tp:tp + 1], ALU.mult,
                           reads=["cols", "ident", f"dgb{cf % 2}"], writes=[f"dgb{cf % 2}"])
                kb.D(scr["dg"][cf], dgb[cf % 2][:].rearrange("p t c -> p (t c)"), reads=[f"dgb{cf % 2}"], writes=["scr_dg"])
            kb.I("pool", "memset", xw[:], 0.0, reads=F0, writes=["xw0", "xw1"])

            def load_window(sd, is_ctx, c):
                rev = (sd == 1)
                TT = T_CTX if is_ctx else T_LAT
                src = din["ctx"] if is_ctx else din["x"]
                for half in range(2):
                    lo_r = 128 * c - 64 + 128 * half
                    if not rev:
                        lo, hi = lo_r, lo_r + 128
                    else:
                        hi, lo = TT - lo_r, TT - lo_r - 128
                    a0_, a1_ = max(lo, 0), min(hi, TT)
                    if a1_ > a0_:
                        kb.D(xw[a0_ - lo:a1_ - lo, half, :], src[a0_:a1_, :], writes=[f"xw{half}"])

            def process_chunk(sd, is_ctx, c, own, slot, nxt=None, first=False, reuse=False):
                rev = (sd == 1)
                TT = T_CTX if is_ctx else T_LAT
                src = din["ctx"] if is_ctx else din["x"]
                ab = 2 if is_ctx else 0
                tid = antiI if rev else ident
                tk = "antiI" if rev else "ident"
                dbg_here = debug and sd == 0 and (not is_ctx) and c == 0
                SL = f"@{slot}"
                streams = {"front": [], "rwkv": [], "s5": []}
                kb.rec = streams["front"]
                AR_, BE_, KAP_, gC_, TM_, zub_ = AR[slot], BE[slot], KAP[slot], gC[slot], TMt[slot], zub[slot]
                kapT = TM_[:, 0, :]; bepT = TM_[:, 1, :]; vT = TM_[:, 2, :]; bvT = TM_[:, 3, :]
                TMK = "TM" + SL
                if not reuse:
                    if first:
                        load_window(sd, is_ctx, c)
                    kb.I("pool", "memset", ssum[:], 0.0, writes=["ssum"])
                    for half in range(2):
                        kb.I("act", "activation", junk[:], xw[:, half, :], AF.Square, accum_out=ssum[:, half:half + 1],
                             reads=[f"xw{half}"], writes=["junk", "ssum"])
                    kb.I("dve", "tensor_scalar", rstd[:], ssum[:], 1.0 / D, 1e-6, ALU.mult, ALU.add, reads=["ssum"], writes=["rstd"])
                    kb.I("act", "activation", rstd[:], rstd[:], AF.Sqrt, reads=["rstd"], writes=["rstd"])
                    kb.I("dve", "reciprocal", rstd[:], rstd[:], reads=["rstd"], writes=["rstd"])
                    for half in range(2):
                        kb.ts1("dve" if half == 0 else "pool", xw[:, half, :], xw[:, half, :], rstd[:, half:half + 1], ALU.mult,
                               reads=["rstd", f"xw{half}"], writes=[f"xw{half}"])
                    for kt in range(8):
                        b_ = kt % 2
                        for half in range(2):
                            kb.I("pe", "transpose", PS[b_][:, half * 128:(half + 1) * 128], xw[:, half, kt * 128:(kt + 1) * 128], tid[:],
                                 reads=[f"xw{half}", tk], writes=[f"pX{b_}_0"])
                        if kt % 2:
                            kb.I("dve", "tensor_scalar", hT[:, kt, :], PS[b_][:, 0:256], AB[:, ab, kt:kt + 1], AB[:, ab + 1, kt:kt + 1], ALU.mult, ALU.add,
                                 reads=[f"pX{b_}_0", "AB"], writes=[f"hT{kt}"])
                        else:
                            kb.I("act", "activation", hT[:, kt, :], PS[b_][:, 0:256], AF.Identity, bias=AB[:, ab + 1, kt:kt + 1], scale=AB[:, ab, kt:kt + 1],
                                 reads=[f"pX{b_}_0", "AB"], writes=[f"hT{kt}"])
                    if nxt is not None:
                        load_window(sd, nxt[0], nxt[1])
                    others = [("zl", zl[:, :], 1536, 128), ("zg", zg[0:96, :], 1664, 96)] + [(f"zu{j}" + SL, zub_[:, j, :], 1760 + 128 * j, 128) for j in range(4)]
                    for i_, (nm, dst, c0, m) in enumerate(others):
                        b_ = i_ % 2
                        pr_ = PS[b_][0:m, 0:128]
                        pk = [f"pX{b_}_0"]
                        for kt in range(8):
                            kb.I("pe", "matmul", pr_, win[:, kt, c0:c0 + m], hT[:, kt, 64:192], start=(kt == 0), stop=(kt == 7), reads=["win", f"hT{kt}"], writes=pk)
                        if nm == "zg":
                            kb.I("act", "activation", dst, pr_, AF.Sigmoid, reads=pk, writes=[nm])
                        else:
                            kb.cp("dve" if i_ % 2 else "act", dst, pr_, reads=pk, writes=[nm])
                    if own is not None and sd == 0:
                        kb.D(scr["sg"][own * 128:own * 128 + 96, :], zg[0:96, :], reads=["zg"], writes=[f"scr_sg{own}"])
                    kb.I("act", "activation", zl[0:64, :], zl[0:64, :], AF.Tanh, reads=["zl"], writes=["zl"])
                    if sd == 0 and own is not None and not is_ctx:
                        kb.D(scr["fz"][own * 128:(own + 1) * 128, :], zl[:, :], reads=["zl"], writes=[f"scr_fz{own}"])
                        kb.D(scr["fu"][own * 128:(own + 1) * 128, :], zub_[:].rearrange("p j t -> p (j t)"), reads=[f"zu{j}" + SL for j in range(4)], writes=[f"scr_fu{own}"])
                else:
                    kb.D(xw[:, 1, 512:640], scr["fz"][own * 128:(own + 1) * 128, :], reads=[f"scr_fz{own}"], writes=["xw1"])
                    kb.I("dve", "tensor_copy", zl[:, :], xw[:, 1, 512:640][:, ::-1], reads=["xw1"], writes=["zl"])
                    kb.D(junk[:, 0:512], scr["fu"][own * 128:(own + 1) * 128, :], reads=[f"scr_fu{own}"], writes=["junk"])
                    kb.I("dve", "tensor_copy", zub_[:], junk[:, 0:512].rearrange("p (j t) -> p j t", t=128)[:, :, ::-1], reads=["junk"], writes=[f"zu{j}" + SL for j in range(4)])
                sgn = -1 if rev else 1
                nk = 4 if own is not None else 3
                for ft in range(4):
                    fp_ = ft % 2
                    zrkv = zrkv2[fp_]; rkvc = rkvc2[fp_]
                    W = {nm: W2[nm][fp_] for nm in W2}
                    if reuse:
                        stg_ = xw[:, ft % 2, 0:512]
                        kb.D(stg_, scr["fr"][(own * 4 + ft) * 128:(own * 4 + ft + 1) * 128, :], reads=[f"scr_fr{own}_{ft}"], writes=[f"xw{ft % 2}"])
                        kb.I("dve", "tensor_copy", rkvc[:, :, :], stg_[:, 0:384].rearrange("p (j t) -> p j t", t=128)[:, :, ::-1],
                             reads=[f"xw{ft % 2}"], writes=[f"rkvc{j}_{fp_}" for j in range(3)])
                        kb.I("dve", "tensor_copy", W["kk"][:, :], stg_[:, 384:512][:, ::-1], reads=[f"xw{ft % 2}"], writes=[f"W_kk{fp_}"])
                    for j3 in (range(3) if not reuse else ()):
                        b_ = j3 % 2
                        cf = (j3 * 4 + ft) * 128
                        pr_ = PS[b_][:, 0:256]
                        pk = [f"pX{b_}_0"]
                        for kt in range(8):
                            kb.I("pe", "matmul", pr_, win[:, kt, cf:cf + 128], hT[:, kt, :], start=(kt == 0), stop=(kt == 7), reads=["win", f"hT{kt}"], writes=pk)
                        if is_ctx:
                            kb.cp("act" if j3 % 2 else "dve", zrkv[:, j3, 0:256], pr_, reads=pk, writes=[f"zrkv{j3}_{fp_}"])
                        else:
                            if c == 0 and ft < 2:
                                kb.I("pool", "memset", zrkv[:, j3, :].rearrange("p (r c) -> p r c", c=66)[:, :, 0:1], 0.0, reads=[f"zrkv{j3}_{fp_}"], writes=[f"zrkv{j3}_{fp_}"])
                                kb.I("pool", "memset", zrkv[:, j3, :].rearrange("p (r c) -> p r c", c=66)[:, :, 65:66], 0.0, reads=[f"zrkv{j3}_{fp_}"], writes=[f"zrkv{j3}_{fp_}"])
                            kb.cp("act" if j3 % 2 else "dve", zrkv[:, j3, :].rearrange("p (r c) -> p r c", c=66)[:, :, 1:65],
                                  pr_.rearrange("p (r c) -> p r c", c=64), reads=pk, writes=[f"zrkv{j3}_{fp_}"])
                    if dbg_here and ft == 0:
                        kb.D(dbg["d_z"], zrkv[:].rearrange("p f t -> p (f t)"), reads=[f"zrkv{j}_{fp_}" for j in range(3)], writes=["o_d_z"])
                    for j3 in (range(3) if not reuse else ()):
                        zk = f"zrkv{j3}_{fp_}"; ok = f"rkvc{j3}_{fp_}"
                        cf = j3 * 4 + ft
                        ds = (ft * 3 + j3) % 2
                        dk = f"dgb{ds}"
                        kb.D(dgb[ds][:].rearrange("p t c -> p (t c)"), scr["dg"][cf], reads=["scr_dg"], writes=[dk])
                        b_ = (j3 + 1) % 2
                        pc = PS[b_][:, 256:384]
                        taps = [(0, 0)] + [(dy, dx) for dy in ((0,) if is_ctx else (-1, 0, 1)) for dx in (-1, 0, 1) if (dy, dx) != (0, 0)]
                        mms = []
                        for (dy, dx) in taps:
                            wi = (1 + sgn * dy) * 3 + (1 + sgn * dx)
                            if is_ctx:
                                j0, j1 = 64, 192
                                if 128 * c + dx < 0:
                                    j0 = 65
                                if 128 * c + 127 + dx >= TT:
                                    j1 = 191
                                mms.append((PS[b_][:, 256 + j0 - 64:256 + j1 - 64], wi, zrkv[:, j3, j0 + dx:j1 + dx]))
                            else:
                                i0, i1 = 0, 2
                                if 2 * c + dy < 0:
                                    i0 = 1
                                if 2 * c + 1 + dy >= 64:
                                    i1 = 1
                                o0 = 66 * i0 + 1; o1 = 66 * (i1 - 1) + 65
                                sh = 66 * (1 + dy) + dx
                                mms.append((PS[b_][:, 256 + o0:256 + o1], wi, zrkv[:, j3, o0 + sh:o1 + sh]))
                        for ti, (o_ap, wi, i_ap) in enumerate(mms):
                            kb.I("pe", "matmul", o_ap, dgb[ds][:, wi, :], i_ap, start=(ti == 0), stop=(ti == len(mms) - 1),
                                 reads=[zk, dk], writes=[f"pX{b_}_0"])
                        if is_ctx:
                            kb.cp("act" if j3 % 2 == 0 else "dve", rkvc[:, j3, :], pc, reads=[f"pX{b_}_0"], writes=[ok])
                        else:
                            kb.cp("act" if j3 % 2 == 0 else "dve", rkvc[:, j3, :].rearrange("p (r c) -> p r c", c=64),
                                  PS[b_][:, 256:388].rearrange("p (r c) -> p r c", c=66)[:, :, 1:65], reads=[f"pX{b_}_0"], writes=[ok])
                    if dbg_here and ft == 0:
                        kb.D(dbg["d_rkvc"], rkvc[:].rearrange("p f t -> p (f t)"), reads=[f"rkvc{j}_{fp_}" for j in range(3)], writes=["o_d_rkvc"])
                    rc = rkvc[:, 0, :]; kc = rkvc[:, 1, :]; vc = rkvc[:, 2, :]
                    fk = f"w{ft}" + SL
                    g_ = {nm: W[nm][:, :] for nm in W}
                    CK = ["cols", "cols2"]

                    def V(meth, *args, eng="dve", r=(), w=(), **kw):
                        kb.I(eng, meth, *args, reads=list(r) + CK, writes=list(w), **kw)
                    ARa = AR_[:, ft, 0:128]; ARr = AR_[:, ft, 128:256]
                    fa, fr, fb, fkp, fg = fk + "a", fk + "r", fk + "b", fk + "k", fk + "g"
                    if not reuse:
                        V("tensor_scalar", g_["kk"], kc, kkc[:, ft:ft + 1], 0.0, ALU.mult, ALU.add, r=[f"rkvc1_{fp_}"], w=[f"W_kk{fp_}"])
                        V("tensor_tensor", g_["t1"], g_["kk"], g_["kk"], ALU.mult, r=[f"W_kk{fp_}"], w=[f"W_t1{fp_}"])
                        kb.I("pe", "matmul", PS[0][:, 384:512], bones[:], g_["t1"], start=True, stop=True, reads=[f"W_t1{fp_}", "bones"], writes=["pX0_0"])
                        kb.I("dve", "tensor_scalar", g_["t2"], PS[0][:, 384:512], 1e-12, 0.0, ALU.max, ALU.add, reads=["pX0_0"], writes=[f"W_t2{fp_}"])
                        V("activation", g_["t2"], g_["t2"], AF.Sqrt, eng="act", r=[f"W_t2{fp_}"], w=[f"W_t2{fp_}"])
                        V("reciprocal", g_["t2"], g_["t2"], r=[f"W_t2{fp_}"], w=[f"W_t2{fp_}"])
                        V("tensor_tensor", g_["kk"], g_["kk"], g_["t2"], ALU.mult, r=[f"W_kk{fp_}", f"W_t2{fp_}"], w=[f"W_kk{fp_}"])
                        if sd == 0 and own is not None and not is_ctx:
                            r0_ = (own * 4 + ft) * 128
                            kb.D(scr["fr"][r0_:r0_ + 128, 0:384], rkvc[:, :, :].rearrange("p j t -> p (j t)"), reads=[f"rkvc{j}_{fp_}" for j in range(3)], writes=[f"scr_fr{own}_{ft}"])
                            kb.D(scr["fr"][r0_:r0_ + 128, 384:512], g_["kk"], reads=[f"W_kk{fp_}"], writes=[f"scr_fr{own}_{ft}"])
                    kb.I("pe", "matmul", PS[1][:, 256:384], w2pad[:, sd, ft * 128:(ft + 1) * 128], zl[:, :], start=True, stop=True, reads=["zl", "w2pad"], writes=["pX1_0"])
                    kb.I("pe", "matmul", PS[1][:, 384:512], a2pad[:, sd, ft * 128:(ft + 1) * 128], zl[:, :], start=True, stop=True, reads=["zl", "a2pad"], writes=["pX1_0"])
                    kb.I("act", "activation", g_["ld"], PS[1][:, 256:384], AF.Sigmoid, bias=w0c[:, sd, ft:ft + 1], scale=1.0, reads=["pX1_0", "cols"], writes=[f"W_ld{fp_}"])
                    kb.I("act", "activation", g_["a"], PS[1][:, 384:512], AF.Sigmoid, bias=a0c[:, sd, ft:ft + 1], scale=1.0, reads=["pX1_0", "cols"], writes=[f"W_a{fp_}"])
                    V("tensor_tensor_scan", g_["cs"], ones[:], g_["ld"], 0.0, ALU.mult, ALU.add, r=[f"W_ld{fp_}", "ones"], w=[f"W_cs{fp_}"])
                    V("tensor_copy", tot[:, ft:ft + 1], g_["cs"][:, 127:128], r=[f"W_cs{fp_}"], w=[f"tot{ft}"])
                    V("tensor_tensor", g_["t1"], g_["cs"], g_["ld"], ALU.subtract, r=[f"W_cs{fp_}", f"W_ld{fp_}"], w=[f"W_t1{fp_}"])
                    V("activation", ARr, g_["cs"], AF.Exp, eng="act", scale=-KDEC, r=[f"W_cs{fp_}"], w=[fr])
                    V("activation", ARa, g_["t1"], AF.Exp, eng="act", scale=-KDEC, r=[f"W_t1{fp_}"], w=[fa])
                    V("activation", g_["Eni"], g_["cs"], AF.Exp, eng="act", scale=KDEC, r=[f"W_cs{fp_}"], w=[f"W_Eni{fp_}"])
                    V("activation", gC_[:, ft:ft + 1], tot[:, ft:ft + 1], AF.Exp, eng="act", scale=-KDEC, r=[f"tot{ft}"], w=[fg])
                    V("tensor_tensor", g_["b"], g_["kk"], g_["a"], ALU.mult, r=[f"W_kk{fp_}", f"W_a{fp_}"], w=[f"W_b{fp_}"])
                    V("tensor_scalar", g_["t2"], g_["a"], kac[:, ft:ft + 1], omka[:, ft:ft + 1], ALU.mult, ALU.add, r=[f"W_a{fp_}"], w=[f"W_t2{fp_}"])
                    V("tensor_tensor", g_["kd"], kc, g_["t2"], ALU.mult, r=[f"rkvc1_{fp_}", f"W_t2{fp_}"], w=[f"W_kd{fp_}"])
                    V("tensor_tensor", ARa, ARa, g_["kk"], ALU.mult, r=[fa, f"W_kk{fp_}"], w=[fa])
                    V("tensor_tensor", ARr, ARr, rc, ALU.mult, eng="pool", r=[fr, f"rkvc0_{fp_}"], w=[fr])
                    V("tensor_tensor", BE_[:, ft, :], g_["b"], g_["Eni"], ALU.mult, r=[f"W_b{fp_}", f"W_Eni{fp_}"], w=[fb])
                    V("tensor_tensor", KAP_[:, ft, :], g_["kd"], g_["Eni"], ALU.mult, eng="pool", r=[f"W_kd{fp_}", f"W_Eni{fp_}"], w=[fkp])
                    V("tensor_scalar", g_["bep"], BE_[:, ft, :], gC_[:, ft:ft + 1], -1.0, ALU.mult, ALU.mult, r=[fb, fg], w=[f"W_bep{fp_}"])
                    V("tensor_scalar", g_["kap"], KAP_[:, ft, :], gC_[:, ft:ft + 1], 0.0, ALU.mult, ALU.add, eng="pool", r=[fkp, fg], w=[f"W_kap{fp_}"])
                    tsrc = [(g_["kap"], f"W_kap{fp_}"), (g_["bep"], f"W_bep{fp_}"), (vc, f"rkvc2_{fp_}")]
                    if own is not None:
                        V("scalar_tensor_tensor", g_["t1"], rc, rkc[:, ft:ft + 1], g_["kd"], ALU.mult, ALU.mult, r=[f"rkvc0_{fp_}", f"W_kd{fp_}"], w=[f"W_t1{fp_}"])
                        kb.I("pe", "matmul", PS[0][:, 384:512], bones[:], g_["t1"], start=True, stop=True, reads=[f"W_t1{fp_}", "bones"], writes=["pX0_0"])
                        kb.I("dve", "tensor_tensor", g_["t2"], PS[0][:, 384:512], vc, ALU.mult, reads=["pX0_0", f"rkvc2_{fp_}"], writes=[f"W_t2{fp_}"])
                        tsrc.append((g_["t2"], f"W_t2{fp_}"))
                    b_ = ft % 2
                    for i_, (s_ap, s_k) in enumerate(tsrc):
                        kb.I("pe", "transpose", PS[b_][:, i_ * 128:(i_ + 1) * 128], s_ap, ident[:], reads=[s_k, "ident"], writes=[f"pX{b_}_0"])
                    kb.cp("act" if ft % 2 else "dve", TM_[:, 0:nk, ft * 128:(ft + 1) * 128], PS[b_][:, 0:nk * 128].rearrange("p (k t) -> p k t", k=nk),
                          reads=[f"pX{b_}_0"], writes=[TMK])
                kb.rec = streams["rwkv"]
                for g0 in range(0, 8, NS):
                    heads = list(range(g0, g0 + NS))

                    def hv(h):
                        ft = h // 2; Rs = slice(64 * (h % 2), 64 * (h % 2) + 64)
                        return ft, Rs, [f"w{ft}" + SL + x_ for x_ in "arbkg"]
                    BK = [2 + s_ for s_ in range(NS)]
                    bkk = [f"pX{b_}_0" for b_ in BK]
                    for s_, h in enumerate(heads):
                        ft, Rs, fk = hv(h)
                        be = BE_[Rs, ft, :]; ar = AR_[Rs, ft, :]; al = AR_[Rs, ft, 0:128]
                        kb.I("pe", "matmul", PS[BK[s_]][:, 0:256], be, ar, start=True, stop=True, reads=fk, writes=[bkk[s_]])
                        kb.I("pe", "matmul", PS[BK[s_]][:, 256:384], al, be, start=True, stop=True, reads=fk, writes=[bkk[s_]])
                    for s_, h in enumerate(heads):
                        kb.I("dve", "tensor_tensor", NB[s_][:], PS[BK[s_]][:, 0:256], mask_b[:], ALU.mult, reads=[bkk[s_], "mask_b"], writes=[f"NB{s_}"])
                        kb.I("dve", "tensor_tensor", PTq[s_][:, 0, :], PS[BK[s_]][:, 256:384], maskT[:], ALU.mult, reads=[bkk[s_], "maskT"], writes=[f"PT{s_}_0"])
                        kb.I("dve", "tensor_tensor", Pq[s_][:, 0, :], PS[BK[s_]][:, 0:128], mask_b[:, 0:128], ALU.mult, reads=[bkk[s_], "mask_b"], writes=[f"P{s_}_0"])
                    for s_, h in enumerate(heads):
                        ft, Rs, fk = hv(h)
                        kb.I("pe", "matmul", PS[BK[s_]][:, 0:256], KAP_[Rs, ft, :], AR_[Rs, ft, :], start=True, stop=True, reads=fk, writes=[bkk[s_]])
                    for s_, h in enumerate(heads):
                        kb.I("dve", "tensor_tensor", KA[s_][:], PS[BK[s_]][:, 0:256], mask_k[:], ALU.mult, reads=[bkk[s_], "mask_k"], writes=[f"KA{s_}"])
                    for s_, h in enumerate(heads):
                        ft, Rs, fk = hv(h)
                        al = AR_[Rs, ft, 0:128]
                        kb.I("pe", "matmul", PS[BK[s_]][:, 384:448], al, ST[Rs, ft, :], start=True, stop=False, reads=fk + [f"ST{ft}"], writes=[bkk[s_]])
                        kb.I("pe", "matmul", PS[BK[s_]][:, 384:448], KA[s_][:, 0:128], vT[:, h * 64:(h + 1) * 64], start=False, stop=True, reads=[f"KA{s_}", TMK], writes=[bkk[s_]])
                    for s_, h in enumerate(heads):
                        kb.cp("act", Xq[s_][:, 0, :], PS[BK[s_]][:, 384:448], reads=[bkk[s_]], writes=[f"X{s_}_0"])
                        kb.cp("act", Xb[s_][:, 0, :], PS[BK[s_]][:, 384:448], reads=[bkk[s_]], writes=[f"Xb{s_}_0"])
                    for j in range(7):
                        cu, nx = j % 2, (j + 1) % 2
                        for s_, h in enumerate(heads):
                            Pj = Pq[s_][:, cu, :]
                            Pk = f"P{s_}_{cu}"
                            PTj = PTq[s_][:, cu, :]; PTk = f"PT{s_}_{cu}"
                            if j < 6:
                                kb.I("pe", "matmul", PS[BK[s_]][:, 0:128], PTj, Pj, start=True, stop=True, reads=[PTk, Pk], writes=[bkk[s_]])
                            if j < 5:
                                kb.I("pe", "matmul", PS[BK[s_]][:, 128:256], Pj, PTj, start=True, stop=True, reads=[PTk, Pk], writes=[bkk[s_]])
                            kb.I("pe", "matmul", PS[BK[s_]][:, 448:512], Pj, Xb[s_][:, cu, :], start=True, stop=True, reads=[Pk, f"Xb{s_}_{cu}"], writes=[bkk[s_]])
                        for s_, h in enumerate(heads):
                            if j < 6:
                                kb.cp("act", Pq[s_][:, nx, :], PS[BK[s_]][:, 0:128], reads=[bkk[s_]], writes=[f"P{s_}_{nx}"])
                            if j < 5:
                                kb.cp("act", PTq[s_][:, nx, :], PS[BK[s_]][:, 128:256], reads=[bkk[s_]], writes=[f"PT{s_}_{nx}"])
                            dst_ap = UT[:, h * 64:(h + 1) * 64] if j == 6 else Xq[s_][:, nx, :]
                            dst_k = f"UT{h}" if j == 6 else f"X{s_}_{nx}"
                            if j < 6:
                                kb.I("dve", "tensor_tensor", Xb[s_][:, nx, :], Xq[s_][:, cu, :], PS[BK[s_]][:, 448:512], ALU.subtract if j == 0 else ALU.add,
                                     reads=[bkk[s_], f"X{s_}_{cu}"], writes=[f"Xb{s_}_{nx}"])
                            kb.I("dve", "tensor_tensor", dst_ap, Xq[s_][:, cu, :], PS[BK[s_]][:, 448:512], ALU.subtract if j == 0 else ALU.add,
                                 reads=[bkk[s_], f"X{s_}_{cu}"], writes=[dst_k])
                    if own is not None:
                        for s_, h in enumerate(heads):
                            ft, Rs, fk = hv(h)
                            rho = AR_[Rs, ft, 128:256]
                            yr_ = PS[BK[s_]][:, 256:320]
                            kb.I("pe", "matmul", yr_, rho, ST[Rs, ft, :], start=True, stop=False, reads=fk + [f"ST{ft}"], writes=[bkk[s_]])
                            kb.I("pe", "matmul", yr_, NB[s_][:, 128:256], UT[:, h * 64:(h + 1) * 64], start=False, stop=False, reads=[f"NB{s_}", f"UT{h}"], writes=[bkk[s_]])
                            kb.I("pe", "matmul", yr_, KA[s_][:, 128:256], vT[:, h * 64:(h + 1) * 64], start=False, stop=True, reads=[f"KA{s_}", TMK], writes=[bkk[s_]])
                        for s_, h in enumerate(heads):
                            kb.cp("act", yT[:, h * 64:(h + 1) * 64], PS[BK[s_]][:, 256:320], reads=[bkk[s_]], writes=[f"yT{h}"])
                    for ft in range(g0 // 2, (g0 + NS) // 2):
                        h1 = 2 * ft + 1
                        bnk = BK[h1 - g0]
                        cs_ = slice(ft * 128, (ft + 1) * 128)
                        kb.I("pe", "matmul", PS[bnk][:, 0:128], bepT[:, cs_], UT[:, cs_], start=True, stop=False, reads=[TMK, f"UT{h1 - 1}", f"UT{h1}"], writes=[f"pX{bnk}_0"])
                        kb.I("pe", "matmul", PS[bnk][:, 0:128], kapT[:, cs_], vT[:, cs_], start=False, stop=True, reads=[TMK], writes=[f"pX{bnk}_0"])
                        for jj in range(2):
                            rr_ = slice(64 * jj, 64 * jj + 64)
                            kb.I("dve", "scalar_tensor_tensor", ST[rr_, ft, :], ST[rr_, ft, :], gC_[rr_, ft:ft + 1], PS[bnk][rr_, 64 * jj:64 * jj + 64], ALU.mult, ALU.add,
                                 reads=[f"pX{bnk}_0", f"ST{ft}", f"w{ft}" + SL + "g"], writes=[f"ST{ft}"])
                if dbg_here:
                    kb.D(dbg["d_yT"], yT[:], reads=[f"yT{h}" for h in range(8)], writes=["o_d_yT"])
                    kb.D(dbg["d_ST"], ST[:].rearrange("p f v -> p (f v)"), reads=[f"ST{f}" for f in range(4)], writes=["o_d_ST"])
                if own is not None:
                    for nm, tile_ap, keys in (("yr", yT[:], [f"yT{h}" for h in range(8)]), ("bv", bvT, [TMK])):
                        dst = scr[f"{nm}{sd}"][own * 128:(own + 1) * 128, :]
                        if rev:
                            kb.I("pe", "matmul", PS[5][:], antiI[:], tile_ap, start=True, stop=True, reads=keys + ["antiI"], writes=["pX5_0"])
                            kb.cp("act", flpR[:], PS[5][:], reads=["pX5_0"], writes=["flpR"])
                            kb.D(dst, flpR[:], reads=["flpR"], writes=[f"scr_{nm}{sd}_{own}"])
                        else:
                            kb.D(dst, tile_ap, reads=keys, writes=[f"scr_{nm}{sd}_{own}"])
                kb.rec = streams["s5"]
                t1_, t2_ = s5t["t1"], s5t["t2"]
                for a_ in range(4):
                    zuk = f"zu{a_}" + SL
                    for r_ in range(4):
                        q_ = a_ * 4 + r_
                        kb.I("pe", "matmul", PS[6][:, r_ * 128:(r_ + 1) * 128], Bw_re[:, q_, :], zub_[:, a_, :], start=True, stop=True, reads=["s5tab", zuk], writes=["pX6_0"])
                        kb.I("pe", "matmul", PS[7][:, r_ * 128:(r_ + 1) * 128], Bw_im[:, q_, :], zub_[:, a_, :], start=True, stop=True, reads=["s5tab", zuk], writes=["pX7_0"])
                    k3 = ["pX6_0"]; k4 = ["pX7_0"]
                    qs = slice(a_ * 4, a_ * 4 + 4)
                    Er = E_re[:, qs, :].rearrange("p q t -> p (q t)"); Ei = E_im[:, qs, :].rearrange("p q t -> p (q t)")
                    Fr = F_re[:, qs, :].rearrange("p q t -> p (q t)"); Fi = F_im[:, qs, :].rearrange("p q t -> p (q t)")
                    g1_, g2_ = s5t["Gre"], s5t["Gim"]
                    kb.I("dve", "tensor_tensor", t1_[:], PS[6][:], Er, ALU.mult, reads=k3 + ["s5tab"], writes=["s5t1"])
                    kb.I("dve", "tensor_tensor", t2_[:], PS[7][:], Ei, ALU.mult, reads=k4 + ["s5tab"], writes=["s5t2"])
                    kb.I("dve", "tensor_tensor", g1_[:], PS[7][:], Er, ALU.mult, reads=k4 + ["s5tab"], writes=["s5Gre"])
                    kb.I("dve", "tensor_tensor", g2_[:], PS[6][:], Ei, ALU.mult, reads=k3 + ["s5tab"], writes=["s5Gim"])
                    kb.I("pool", "tensor_tensor", s5t["Xre"][:], t1_[:], t2_[:], ALU.subtract, reads=["s5t1", "s5t2"], writes=["s5Xre"])
                    kb.I("dve", "tensor_tensor", s5t["Xim"][:], g1_[:], g2_[:], ALU.add, reads=["s5Gre", "s5Gim"], writes=["s5Xim"])
                    if own is None:
                        kb.I("dve", "tensor_reduce", gs_re[:, qs], s5t["Xre"][:].rearrange("p (q t) -> p q t", q=4), AXX, ALU.add, reads=["s5Xre"], writes=["gs"])
                        kb.I("dve", "tensor_reduce", gs_im[:, qs], s5t["Xim"][:].rearrange("p (q t) -> p q t", q=4), AXX, ALU.add, reads=["s5Xim"], writes=["gs"])
                        continue
                    for r_ in range(4):
                        q_ = a_ * 4 + r_
                        cs_ = slice(r_ * 128, (r_ + 1) * 128)
                        kb.I("dve", "tensor_tensor_scan", s5t["Gre"][:, cs_], ones[:], s5t["Xre"][:, cs_], car_re[:, q_:q_ + 1], ALU.mult, ALU.add,
                             reads=["s5Xre", "car", "ones"], writes=["s5Gre"])
                        kb.I("dve", "tensor_tensor_scan", s5t["Gim"][:, cs_], ones[:], s5t["Xim"][:, cs_], car_im[:, q_:q_ + 1], ALU.mult, ALU.add,
                             reads=["s5Xim", "car", "ones"], writes=["s5Gim"])
                    x1_, x2_ = s5t["Xre"], s5t["Xim"]
                    kb.I("dve", "tensor_tensor", t1_[:], s5t["Gre"][:], Fr, ALU.mult, reads=["s5Gre", "s5tab"], writes=["s5t1"])
                    kb.I("dve", "tensor_tensor", t2_[:], s5t["Gim"][:], Fi, ALU.mult, reads=["s5Gim", "s5tab"], writes=["s5t2"])
                    kb.I("pool", "tensor_tensor", x1_[:], s5t["Gim"][:], Fr, ALU.mult, reads=["s5Gim", "s5tab", "s5Xre"], writes=["s5Xre"])
                    kb.I("pool", "tensor_tensor", x2_[:], s5t["Gre"][:], Fi, ALU.mult, reads=["s5Gre", "s5tab", "s5Xim"], writes=["s5Xim"])
                    kb.I("dve", "tensor_tensor", s5t["Hre"][:], t1_[:], t2_[:], ALU.subtract, reads=["s5t1", "s5t2"], writes=["s5Hre"])
                    kb.I("pool", "tensor_tensor", s5t["Him"][:], x1_[:], x2_[:], ALU.add, reads=["s5Xre", "s5Xim"], writes=["s5Him"])
                    kb.I("dve", "tensor_copy", car_re[:, qs], s5t["Hre"][:].rearrange("p (q t) -> p q t", q=4)[:, :, 127], reads=["s5Hre"], writes=["car"])
                    kb.I("dve", "tensor_copy", car_im[:, qs], s5t["Him"][:].rearrange("p (q t) -> p q t", q=4)[:, :, 127], reads=["s5Him"], writes=["car"])
                    if own is not None:
                        for r_ in range(4):
                            q_ = a_ * 4 + r_
                            cs_ = slice(r_ * 128, (r_ + 1) * 128)
                            yb = PS[6][:, r_ * 32:(r_ + 1) * 32]
                            if sd == 0:
                                kb.I("pe", "matmul", yb, zub_[:, a_, :], diagd[:, a_, r_ * 32:(r_ + 1) * 32], start=True, stop=False, reads=[zuk, "diagd"], writes=["pX6_0"])
                            kb.I("pe", "matmul", yb, s5t["Hre"][:, cs_], Cw_re[:, q_, :], start=(sd != 0), stop=False, reads=["s5Hre", "s5tab"], writes=["pX6_0"])
                            kb.I("pe", "matmul", yb, s5t["Him"][:, cs_], Cw_imn[:, q_, :], start=False, stop=True, reads=["s5Him", "s5tab"], writes=["pX6_0"])
                        kb.cp("act", flpS[:, a_ * 128:(a_ + 1) * 128], PS[6][:, 0:128], reads=["pX6_0"], writes=["flpS"])
                if own is None:
                    F127r = F_re[:, :, 127]; F127i = F_im[:, :, 127]
                    kb.I("dve", "tensor_tensor", gs_re[:], gs_re[:], car_re[:], ALU.add, reads=["gs", "car"], writes=["gs"])
                    kb.I("dve", "tensor_tensor", gs_im[:], gs_im[:], car_im[:], ALU.add, reads=["gs", "car"], writes=["gs"])
                    kb.I("dve", "tensor_tensor", gta[:], gs_re[:], F127r, ALU.mult, reads=["gs", "s5tab"], writes=["gta"])
                    kb.I("dve", "tensor_tensor", gtb[:], gs_im[:], F127i, ALU.mult, reads=["gs", "s5tab"], writes=["gtb"])
                    kb.I("dve", "tensor_tensor", car_re[:], gta[:], gtb[:], ALU.subtract, reads=["gta", "gtb"], writes=["car"])
                    kb.I("dve", "tensor_tensor", gta[:], gs_im[:], F127r, ALU.mult, reads=["gs", "s5tab", "car"], writes=["gta"])
                    kb.I("dve", "tensor_tensor", gtb[:], gs_re[:], F127i, ALU.mult, reads=["gs", "s5tab", "car"], writes=["gtb"])
                    kb.I("dve", "tensor_tensor", car_im[:], gta[:], gtb[:], ALU.add, reads=["gta", "gtb"], writes=["car"])
                if dbg_here:
                    kb.D(dbg["d_car"][:, 0:16], car_re[:], reads=["car"], writes=["o_d_car"])
                    kb.D(dbg["d_car"][:, 16:32], car_im[:], reads=["car"], writes=["o_d_car"])
                if own is not None:
                    if dbg_here:
                        kb.D(dbg["d_ys"], flpS[:], reads=["flpS"], writes=["o_d_ys"])
                    dst = scr[f"ys{sd}"][own * 128:(own + 1) * 128, :]
                    if rev:
                        kb.I("pe", "matmul", PS[6][:], antiI[:], flpS[:], start=True, stop=True, reads=["flpS", "antiI"], writes=["pX6_0"])
                        kb.cp("dve", flpS[:], PS[6][:], reads=["pX6_0"], writes=["flpS"])
                    kb.D(dst, flpS[:], reads=["flpS"], writes=[f"scr_ys{sd}_{own}"])
                kb.rec = None
                return streams

            for sd in range(2 if not upto.startswith("p0") else 0):
                if upto == "setup":
                    break
                pre = []
                if sd == 1:
                    kb.rec = pre
                    s5_setup(sd, F0)
                if upto == "s5setup":
                    break
                stk = [f"ST{f}" for f in range(4)]
                if sd == 0:
                    kb.I("pool", "memset", ST[:], 0.0, reads=stk, writes=stk)
                    kb.I("pool", "memset", car_re[:], 0.0, reads=["car"], writes=["car"])
                    kb.I("pool", "memset", car_im[:], 0.0, reads=["car"], writes=["car"])
                if sd == 0:
                    chunks = [(True, c, None) for c in range(n_ctx)] + [(False, c, c) for c in range(n_lat_chunks[0])]
                else:
                    stk = [f"ST{f}" for f in range(4)]
                    kb.D(scr["cc_in"][:, 0:256], ST[:].rearrange("p f v -> p (f v)"), reads=stk, writes=["cc_in"])
                    kb.D(scr["cc_in"][:, 256:272], car_re[:], reads=["car"], writes=["cc_in"])
                    kb.D(scr["cc_in"][:, 272:288], car_im[:], reads=["car"], writes=["cc_in"])
                    kb.I("pool", "collective_compute", "AllReduce", ALU.add, replica_groups=[[0, 1], [2, 3], [4, 5], [6, 7]],
                         ins=[scr["cc_in"]], outs=[scr["cc_out"]], reads=["cc_in"], writes=["cc_out"])
                    ccs = s5t["t1"]
                    kb.D(ccs[:, 0:288], scr["cc_out"], reads=["cc_out", "s5t1"], writes=["s5t1"])
                    kb.I("dve", "tensor_tensor", ST[:].rearrange("p f v -> p (f v)"), ccs[:, 0:256], ST[:].rearrange("p f v -> p (f v)"), ALU.subtract,
                         reads=["s5t1"] + stk, writes=stk)
                    kb.I("dve", "tensor_tensor", car_re[:], ccs[:, 256:272], car_re[:], ALU.subtract, reads=["s5t1", "car"], writes=["car"])
                    kb.I("dve", "tensor_tensor", car_im[:], ccs[:, 272:288], car_im[:], ALU.subtract, reads=["s5t1", "car"], writes=["car"])
                    chunks = [(False, c, 31 - c) for c in range(16, 16 + n_lat_chunks[1])]
                    kb.rec = None
                prev = None
                if sd == 0:
                    pend = []
                if pre:
                    pend.append(pre)
                for k_, (is_ctx, c, own) in enumerate(chunks):
                    nxt = chunks[k_ + 1][:2] if k_ + 1 < len(chunks) else None
                    cur = process_chunk(sd, is_ctx, c, own, k_ % 2, nxt=(nxt if sd == 0 else None), first=(k_ == 0), reuse=(sd == 1))
                    pend.append(cur["front"])
                    if prev is not None:
                        pend += [prev["rwkv"], prev["s5"]]
                    prev = cur
                    if len(pend) >= SCHED_WINDOW_STREAMS:
                        kb.merge(*pend)
                        pend = []
                if prev is not None:
                    pend += [prev["rwkv"], prev["s5"]]
                if sd == 1 or SCHED_WINDOW_STREAMS < 1000:
                    kb.merge(*pend)
                    pend = []
            kb.I("dve", "tensor_copy", ssum[:, 0:1], ssum[:, 0:1], reads=list(kb.lastw.keys()), writes=list(kb.lastw.keys()) + ["fence1"])
        FENCE = ["fence1"]
        wscope.close()

        if do_tail:
            with contextlib.ExitStack() as tl:
                wob = sb(tl, "wob", [128, 8, D], BF16)
                glb = sb(tl, "glb", [128, 4, 512], BF16); g2b = sb(tl, "g2b", [128, 512])
                stg = [sb(tl, f"stg{i}", [128, 1024]) for i in range(2)]
                jobs = [("w_out", wob, kt, 0, D) for kt in range(8)] + [("gluw", glb, kt, 0, 512) for kt in range(4)]
                for i, (nm, dst, kt, c0, ncol) in enumerate(jobs):
                    s_ = stg[i % 2]; sk = f"stg{i % 2}"
                    kb.D(s_[:, 0:ncol], din[nm][kt * 128:(kt + 1) * 128, c0:c0 + ncol], reads=FENCE, writes=[sk])
                    kb.cp(("dve", "act", "pool")[i % 3], dst[:, kt, c0:c0 + ncol], s_[:, 0:ncol], reads=[sk], writes=["tw"])
                kb.D(g2b[0:96, :], din["g2"], reads=FENCE, writes=["tw2"])
                rows = {}
                for nm in ("lnw", "lnb", "glub"):
                    rows[nm] = sb(tl, "row_" + nm, [128, IN_SHAPES[nm][1]])
                    kb.D(rows[nm][:], din[nm].partition_broadcast(128), reads=FENCE, writes=["rows"])
                gmix = sb(tl, "gmix", [128, D])

                def dbl(name, shape, dt=F32):
                    return [sb(tl, f"{name}_{i_}", shape, dt) for i_ in range(2)]
                x1 = dbl("x1", [128, D])
                kb.D(gmix[:], scr["mod"][:, 2 * D:3 * D], reads=FENCE + ["scr_mod"], writes=["modt"])
                tx = dbl("tx", [128, D]); yr_a = dbl("yr_a", [128, 512]); ys_a = dbl("ys_a", [128, 512]); bv_a = dbl("bv_a", [128, 512])
                yr_b = dbl("yr_b", [128, 512]); ys_b = dbl("ys_b", [128, 512]); bv_b = dbl("bv_b", [128, 512])
                sgt = dbl("sgt", [128, 128]); mix = dbl("mix", [128, D]); st8 = dbl("st8", [128, 8]); st8b = dbl("st8b", [128, 8])
                gt = dbl("gt", [128, 512]); gt2 = dbl("gt2", [128, 512]); zT = dbl("zT", [128, 4, 128], BF16); mixT = dbl("mixT", [128, 8, 128], BF16)
                TA = (tx, yr_a, ys_a, bv_a, yr_b, ys_b, bv_b, sgt, mix, st8, st8b, gt, gt2, zT, mixT, x1)
                recA = []
                kb.rec = recA
                for oc in range(NOWN):
                    sl_ = oc % 2
                    kb.ksuf = f"#{sl_}"
                    (tx, yr_a, ys_a, bv_a, yr_b, ys_b, bv_b, sgt, mix, st8, st8b, gt, gt2, zT, mixT, x1) = (t_[sl_] for t_ in TA)
                    rsl = slice(oc * 128, (oc + 1) * 128)
                    kb.D(tx[:], din["x"][rsl, :], reads=FENCE, writes=["tx"])
                    for t_, nm in ((yr_a, "yr0"), (yr_b, "yr1"), (ys_a, "ys0"), (ys_b, "ys1"), (bv_a, "bv0"), (bv_b, "bv1")):
                        kb.D(t_[:], scr[nm][rsl, :], reads=[f"scr_{nm}_{oc}"] + FENCE, writes=["t_" + nm])
                    kb.D(sgt[0:96, :], scr["sg"][oc * 128:oc * 128 + 96, :], reads=[f"scr_sg{oc}"] + FENCE, writes=["sgt"])
                    kb.I("dve", "tensor_tensor", yr_a[:], yr_a[:], yr_b[:], ALU.add, reads=["t_yr0", "t_yr1"], writes=["t_yr0"])
                    y3 = yr_a[:].rearrange("p (h n) -> p h n", n=64)
                    kb.I("dve", "tensor_reduce", st8[:], y3, AXX, ALU.add, reads=["t_yr0"], writes=["st8"])
                    kb.ts1("dve", st8[:], st8[:], 1.0 / 64, ALU.mult, reads=["st8"], writes=["st8"])
                    kb.I("dve", "tensor_tensor", y3, y3, st8[:].unsqueeze(2).to_broadcast([128, 8, 64]), ALU.subtract, reads=["st8", "t_yr0"], writes=["t_yr0"])
                    kb.I("pool", "tensor_tensor", gt2[:], yr_a[:], yr_a[:], ALU.mult, reads=["t_yr0"], writes=["gt2"])
                    kb.I("dve", "tensor_reduce", st8b[:], gt2[:].rearrange("p (h n) -> p h n", n=64), AXX, ALU.add, reads=["gt2"], writes=["st8b"])
                    kb.I("dve", "tensor_scalar", st8b[:], st8b[:], 1.0 / 64, 64e-5, ALU.mult, ALU.add, reads=["st8b"], writes=["st8b"])
                    kb.I("act", "activation", st8b[:], st8b[:], AF.Sqrt, reads=["st8b"], writes=["st8b"])
                    kb.I("dve", "reciprocal", st8b[:], st8b[:], reads=["st8b"], writes=["st8b"])
                    kb.I("dve", "tensor_tensor", y3, y3, st8b[:].unsqueeze(2).to_broadcast([128, 8, 64]), ALU.mult, reads=["st8b", "t_yr0"], writes=["t_yr0"])
                    kb.I("pool", "tensor_tensor", yr_a[:], yr_a[:], rows["lnw"][:], ALU.mult, reads=["rows", "t_yr0"], writes=["t_yr0"])
                    kb.I("pool", "tensor_tensor", yr_a[:], yr_a[:], rows["lnb"][:], ALU.add, reads=["rows", "t_yr0"], writes=["t_yr0"])
                    kb.I("dve", "tensor_tensor", bv_a[:], bv_a[:], bv_b[:], ALU.add, reads=["t_bv0", "t_bv1"], writes=["t_bv0"])
                    kb.I("dve", "tensor_tensor", yr_a[:], yr_a[:], bv_a[:], ALU.add, reads=["t_bv0", "t_yr0"], writes=["t_yr0"])
                    kb.I("pe", "matmul", PS[0][:], sgt[0:96, :], g2b[0:96, :], start=True, stop=True, reads=["sgt", "tw2"], writes=bk(0))
                    kb.I("dve", "tensor_tensor", mix[:, 0:512], yr_a[:], PS[0][:], ALU.mult, reads=bk(0) + ["t_yr0"], writes=["mixA"])
                    kb.I("dve", "tensor_tensor", ys_a[:], ys_a[:], ys_b[:], ALU.add, reads=["t_ys0", "t_ys1"], writes=["t_ys0"])
                    kb.I("pool", "tensor_tensor", gt[:], ys_a[:], ys_a[:], ALU.mult, reads=["t_ys0"], writes=["gt"])
                    kb.I("dve", "tensor_scalar", gt[:], gt[:], 0.044715, 1.0, ALU.mult, ALU.add, reads=["gt"], writes=["gt"])
                    kb.I("dve", "tensor_tensor", gt[:], gt[:], ys_a[:], ALU.mult, reads=["gt", "t_ys0"], writes=["gt"])
                    kb.I("act", "activation", gt[:], gt[:], AF.Tanh, scale=0.7978845608028654, reads=["gt"], writes=["gt"])
                    kb.I("dve", "tensor_scalar", gt[:], gt[:], 0.5, 0.5, ALU.mult, ALU.add, reads=["gt"], writes=["gt"])
                    kb.I("dve", "tensor_tensor", ys_a[:], ys_a[:], gt[:], ALU.mult, reads=["gt", "t_ys0"], writes=["t_ys0"])
                    for j in range(4):
                        kb.I("pe", "transpose", PS[1][:, j * 128:(j + 1) * 128], ys_a[:, j * 128:(j + 1) * 128], ident[:], reads=["t_ys0", "ident"], writes=[f"pX1_{j}"])
                    kb.cp("act", zT[:].rearrange("p j t -> p (j t)"), PS[1][:], reads=bk(1), writes=["zT"])
                    for j in range(4):
                        kb.I("pe", "matmul", PS[2][:], zT[:, j, :], glb[:, j, :], start=(j == 0), stop=(j == 3), reads=["zT", "tw"], writes=bk(2))
                    kb.I("dve", "tensor_tensor", gt[:], PS[2][:], rows["glub"][:], ALU.add, reads=bk(2) + ["rows", "gt"], writes=["gt"])
                    kb.I("act", "activation", gt[:], gt[:], AF.Sigmoid, reads=["gt"], writes=["gt"])
                    kb.I("dve", "tensor_tensor", mix[:, 512:1024], ys_a[:], gt[:], ALU.mult, reads=["gt", "t_ys0"], writes=["mixB"])
                    for half in range(2):
                        for j in range(4):
                            kt = half * 4 + j
                            kb.I("pe", "transpose", PS[3][:, j * 128:(j + 1) * 128], mix[:, kt * 128:(kt + 1) * 128], ident[:], reads=["mixA", "mixB", "ident"], writes=[f"pX3_{j}"])
                        kb.cp("act" if half else "dve", mixT[:, half * 4:half * 4 + 4, :].rearrange("p j t -> p (j t)"), PS[3][:], reads=bk(3), writes=["mixT"])
                    for nh in range(2):
                        ns = slice(nh * 512, (nh + 1) * 512)
                        for kt in range(8):
                            kb.I("pe", "matmul", PS[4 + nh][:], mixT[:, kt, :], wob[:, kt, ns], start=(kt == 0), stop=(kt == 7), reads=["mixT", "tw"], writes=bk(4 + nh))
                        kb.I("dve", "tensor_tensor", x1[:, ns], PS[4 + nh][:], gmix[:, ns], ALU.mult, reads=bk(4 + nh) + ["modt"], writes=["x1"])
                        kb.I("pool", "tensor_tensor", x1[:, ns], x1[:, ns], tx[:, ns], ALU.add, reads=["x1", "tx"], writes=["x1"])
                    if debug and oc == 0:
                        kb.D(dbg["d_x1"], x1[:], reads=["x1"], writes=["o_d_x1"])
                    kb.D(scr["x1"][rsl, :], x1[:], reads=["x1"], writes=[f"scr_x1_{oc}"])
                kb.rec = None
                kb.ksuf = None
                kb.merge(recA)
                st8 = TA[9][0]
                kb.I("dve", "tensor_copy", st8[:, 0:1], st8[:, 0:1], reads=list(kb.lastw.keys()), writes=list(kb.lastw.keys()) + ["fence2"])
            FENCE = ["fence2"]
            with contextlib.ExitStack() as tl:
                w1b = sb(tl, "w1b", [128, 8, DFF], BF16); w3b = sb(tl, "w3b", [128, 8, DFF], BF16)
                w2b = sb(tl, "w2b", [128, 22, D], BF16)
                stg = [sb(tl, f"stgb{i}", [128, 1024]) for i in range(2)]
                jobs = []
                for nm, dst in (("w1", w1b), ("w3", w3b)):
                    for kt in range(8):
                        for c0 in (0, 1024, 2048):
                            jobs.append((nm, dst, kt, c0, min(1024, DFF - c0)))
                jobs += [("w2f", w2b, kt, 0, D) for kt in range(22)]
                for i, (nm, dst, kt, c0, ncol) in enumerate(jobs):
                    s_ = stg[i % 2]; sk = f"stgb{i % 2}"
                    kb.D(s_[:, 0:ncol], din[nm][kt * 128:(kt + 1) * 128, c0:c0 + ncol], reads=FENCE, writes=[sk])
                    kb.cp(("dve", "act", "pool")[i % 3], dst[:, kt, c0:c0 + ncol], s_[:, 0:ncol], reads=[sk], writes=["tw"])
                rows = {"gf": sb(tl, "row_gf", [128, D])}
                kb.D(rows["gf"][:], din["gf"].partition_broadcast(128), reads=FENCE, writes=["rows"])
                A2 = sb(tl, "A2", [128, D]); sffn = sb(tl, "sffn", [128, D]); gffn = sb(tl, "gffn", [128, D])
                def dblb(name, shape, dt=F32):
                    return [sb(tl, f"{name}_{i_}", shape, dt) for i_ in range(2)]
                hh2 = dblb("hh", [128, D]); x12 = dblb("x1b", [128, D]); outt2 = dblb("outt", [128, D])
                hh = hh2[0]
                kb.D(A2[:], din["g2n"].partition_broadcast(128), reads=FENCE, writes=["A2"])
                kb.D(hh[:], scr["mod"][:, 4 * D:5 * D], reads=FENCE + ["scr_mod"], writes=["hh#0"])
                kb.D(sffn[:], scr["mod"][:, 3 * D:4 * D], reads=FENCE + ["scr_mod"], writes=["modt"])
                kb.D(gffn[:], scr["mod"][:, 5 * D:6 * D], reads=FENCE + ["scr_mod"], writes=["modt"])
                kb.I("dve", "scalar_tensor_tensor", A2[:], hh[:], 1.0, A2[:], ALU.add, ALU.mult, reads=["A2", "hh#0"], writes=["A2"])
                hhT2 = dblb("hhT", [128, 8, 128], BF16)
                actT2 = dblb("actT", [128, 22, 128], BF16); s12 = dblb("s1", [128, 256]); ss22 = dblb("ss2", [128, 2]); rs22 = dblb("rs2", [128, 2])
                recB = []
                kb.rec = recB
                for oc in range(NOWN):
                    sl_ = oc % 2
                    kb.ksuf = f"#{sl_}"
                    hh, x1, outt, hhT, actT, s1, ss2, rs2 = (t_[sl_] for t_ in (hh2, x12, outt2, hhT2, actT2, s12, ss22, rs22))
                    rsl = slice(oc * 128, (oc + 1) * 128)
                    kb.D(x1[:], scr["x1"][rsl, :], reads=[f"scr_x1_{oc}"], writes=["x1"])
                    kb.I("pool", "memset", ss2[:], 0.0, reads=FENCE, writes=["ss2"])
                    kb.I("act", "activation", hh[:], x1[:], AF.Square, accum_out=ss2[:, 0:1], reads=["x1", "A2"], writes=["hh", "ss2"])
                    kb.I("dve", "tensor_scalar", rs2[:, 0:1], ss2[:, 0:1], 1.0 / D, 1e-6, ALU.mult, ALU.add, reads=["ss2"], writes=["rs2"])
                    kb.I("act", "activation", rs2[:, 0:1], rs2[:, 0:1], AF.Sqrt, reads=["rs2"], writes=["rs2"])
                    kb.I("dve", "reciprocal", rs2[:, 0:1], rs2[:, 0:1], reads=["rs2"], writes=["rs2"])
                    kb.I("dve", "scalar_tensor_tensor", hh[:], x1[:], rs2[:, 0:1], A2[:], ALU.mult, ALU.mult, reads=["x1", "rs2", "A2", "hh"], writes=["hh"])
                    kb.I("pool", "tensor_tensor", hh[:], hh[:], sffn[:], ALU.add, reads=["hh", "modt"], writes=["hh"])
                    for half in range(2):
                        for j in range(4):
                            kt = half * 4 + j
                            kb.I("pe", "transpose", PS[3][:, j * 128:(j + 1) * 128], hh[:, kt * 128:(kt + 1) * 128], ident[:], reads=["hh", "ident"], writes=[f"pX3_{j}"])
                        kb.cp("act" if half else "dve", hhT[:, half * 4:half * 4 + 4, :].rearrange("p j t -> p (j t)"), PS[3][:], reads=bk(3), writes=["hhT"])
                    for ftf in range(22):
                        sl = ftf % 2
                        fs = slice(ftf * 128, (ftf + 1) * 128)
                        bq = (0, 1, 2)[ftf % 3]
                        pa = PS[bq][:, 0:128]; pb = PS[bq][:, 128:256]
                        s1_ = s1[:, (ftf % 2) * 128:(ftf % 2) * 128 + 128]
                        for kt in range(8):
                            kb.I("pe", "matmul", pa, w1b[:, kt, fs], hhT[:, kt, :], start=(kt == 0), stop=(kt == 7), reads=["hhT", "tw"], writes=[f"pX{bq}_0"])
                        for kt in range(8):
                            kb.I("pe", "matmul", pb, w3b[:, kt, fs], hhT[:, kt, :], start=(kt == 0), stop=(kt == 7), reads=["hhT", "tw"], writes=[f"pX{bq}_0"])
                        kb.I("act", "activation", s1_, pa, AF.Silu, reads=[f"pX{bq}_0"], writes=[f"s1_{ftf % 2}"])
                        kb.I("dve", "tensor_tensor", actT[:, ftf, :], s1_, pb, ALU.mult, reads=[f"pX{bq}_0", f"s1_{ftf % 2}"], writes=[f"actT{ftf}"])
                    for nh in range(2):
                        ns = slice(nh * 512, (nh + 1) * 512)
                        bd = 4 + 2 * sl_ + nh
                        for ftf in range(22):
                            kb.I("pe", "matmul", PS[bd][:], actT[:, ftf, :], w2b[:, ftf, ns], start=(ftf == 0), stop=(ftf == 21), reads=[f"actT{ftf}", "tw"], writes=bk(bd))
                        kb.I("dve", "tensor_tensor", outt[:, ns], PS[bd][:], gffn[:, ns], ALU.mult, reads=bk(bd) + ["modt"], writes=["outt"])
                        kb.I("pool", "tensor_tensor", outt[:, ns], outt[:, ns], x1[:, ns], ALU.add, reads=["outt", "x1"], writes=["outt"])
                    kb.I("act", "activation", hh[:], outt[:], AF.Square, accum_out=ss2[:, 1:2], reads=["outt", "hh"], writes=["hh", "ss2"])
                    kb.I("dve", "tensor_scalar", rs2[:, 1:2], ss2[:, 1:2], 1.0 / D, 1e-6, ALU.mult, ALU.add, reads=["ss2"], writes=["rs2"])
                    kb.I("act", "activation", rs2[:, 1:2], rs2[:, 1:2], AF.Sqrt, reads=["rs2"], writes=["rs2"])
                    kb.I("dve", "reciprocal", rs2[:, 1:2], rs2[:, 1:2], reads=["rs2"], writes=["rs2"])
                    kb.I("dve", "scalar_tensor_tensor", outt[:], outt[:], rs2[:, 1:2], rows["gf"][:], ALU.mult, ALU.mult, reads=["outt", "rs2", "rows"], writes=["outt"])
                    kb.D(out_d[rsl, :], outt[:], reads=["outt"], writes=[f"o_out{oc}"])
                kb.rec = None
                kb.ksuf = None
                kb.merge(recB)
                kb.emit(final_keys=[k for k in kb.lastw if k.startswith("o_")])
        else:
            kb.emit(final_keys=[k for k in kb.lastw if k.startswith("o_")] + FENCE)
    return nc


def make_in_maps(inp):
    f = np.float32
    ident = np.eye(128, dtype=f)
    antiI = np.ascontiguousarray(ident[::-1])
    strict = np.triu(np.ones((128, 128), f), 1)
    incl = np.triu(np.ones((128, 128), f), 0)
    consts = {
        "ident": ident, "antiI": antiI,
        "mask_b": np.concatenate([strict, -incl], axis=1), "mask_k": np.concatenate([strict, incl], axis=1),
        "maskT": np.ascontiguousarray(strict.T), "bones": np.kron(np.eye(2, dtype=f), np.ones((64, 64), f)),
    }
    maps = []
    for core in range(8):
        b, hf = core // 2, core % 2
        dsel = [1, 0] if hf else [0, 1]
        x = inp["x"][b]; ctx = inp["ctx"][b]
        conv = inp["rwkv_conv"][0]
        w_in = inp["w_in"][0]
        if hf:
            x = x[::-1]; ctx = ctx[::-1]; conv = conv[::-1, ::-1]
            perm = np.arange(2272)
            perm[1536:1568], perm[1568:1600] = np.arange(1568, 1600), np.arange(1536, 1568)
            perm[1600:1632], perm[1632:1664] = np.arange(1632, 1664), np.arange(1600, 1632)
            w_in = w_in[:, perm]
        m = {
            "x": x, "ctx": ctx, "cc": np.stack([inp["c"][b], inp["c_ctx"]]),
            "mod_w": inp["mod_w"][0], "mod_b": inp["mod_b"][0][None], "g1": inp["norm1_g"][0][None],
            "g2n": inp["norm2_g"][0][None], "gf": inp["final_g"][None], "w_in": w_in, "w_out": inp["w_out"][0],
            "conv": conv.reshape(9, 1536), "w0": inp["rwkv_w0"][0][dsel], "w2": inp["rwkv_w2"][0][dsel],
            "a0": inp["rwkv_a0"][0][dsel], "a2": inp["rwkv_a2"][0][dsel], "g2": inp["rwkv_g2"][0],
            "kkv": inp["rwkv_kk"][0], "kav": inp["rwkv_ka"][0], "rkv": inp["rwkv_rk"][0].reshape(512),
            "lnw": inp["rwkv_ln_w"][0][None], "lnb": inp["rwkv_ln_b"][0][None],
            "lam_re": inp["s5_lam_re"][0][dsel], "lam_im": inp["s5_lam_im"][0][dsel], "lstep": inp["s5_log_step"][0][dsel],
            "b_re": inp["s5_b_re"][0], "b_im": inp["s5_b_im"][0], "c_re": inp["s5_c_re"][0], "c_im": inp["s5_c_im"][0],
            "s5d": inp["s5_d"][0], "gluw": inp["s5_glu_w"][0], "glub": inp["s5_glu_b"][0][None],
            "w1": inp["ffn_w1"][0], "w3": inp["ffn_w3"][0], "w2f": inp["ffn_w2"][0],
        }
        m.update(consts)
        maps.append({k: np.ascontiguousarray(np.asarray(v, dtype=f)).reshape(IN_SHAPES[k]) for k, v in m.items()})
    return maps


def kernel(**inputs):
    inp = {k: np.asarray(v) for k, v in inputs.items()}
    nc = build_nc()
    maps = make_in_maps(inp)
    res = run_bass_kernel_spmd(nc, maps, core_ids=list(range(8)))
    out = np.zeros((4, T_LAT, D), np.float32)
    for core in range(8):
        b, hf = core // 2, core % 2
        o = np.asarray(res.results[core]["out"], dtype=np.float32)
        if hf:
            out[b, OWN:] = o[::-1]
        else:
            out[b, :OWN] = o
    return out
```
